# Optimizing a Trainium2 kernel written in Bass

```python
import jax, jax.numpy as jnp
from jax import lax
import numpy as np

D_MODEL = 2048
BATCH = 2
SEQ = 16384
DEPTH = 2

CTX_LEN = 256
GRID_W = 64
N_MIXERS = 2
N_A = (DEPTH + N_MIXERS - 1) // N_MIXERS
N_B = DEPTH // N_MIXERS
EPS = 1e-6
D_RNN = D_MODEL
LRU_HEADS = 16
LRU_HEAD_DIM = D_RNN // LRU_HEADS
LRU_CONV_W = 4
LRU_CONV_PAD_LEFT = 2
LRU_C = 8.0
D_SC = D_MODEL
SC_CONV_W = 3
SC_CONV_PAD_LEFT = 1
PEER_HEADS = 8
PEER_N_KEYS = 128
PEER_EXPERTS = PEER_N_KEYS * PEER_N_KEYS
PEER_TOPK = 16
PEER_D_QUERY = 256
PEER_D_KEY = PEER_D_QUERY // 2
PEER_BLOCK = 128

kernel_name = 'hybrid_rglru_shortconv_peer_dit'


def rms_norm(x, g):
    xf = x.astype(jnp.float32)
    y = xf * lax.rsqrt(jnp.mean(xf * xf, axis=-1, keepdims=True) + EPS)
    return (y * g.astype(jnp.float32)).astype(x.dtype)


def modulate(h, shift, scale):
    return h * (1 + scale) + shift


def depthwise_conv(x, w, b, pad_left):
    k_w, length = w.shape[0], x.shape[-2]
    pad = [(0, 0)] * (x.ndim - 2) + [(pad_left, k_w - 1 - pad_left), (0, 0)]
    xp = jnp.pad(x, pad)
    y = b + w[0] * xp[..., 0:length, :]
    for k in range(1, k_w):
        y = y + w[k] * xp[..., k:k + length, :]
    return y


def grid_conv(x, w, b, pad_left):
    bsz, length, ch = x.shape
    rows = length // GRID_W
    y = depthwise_conv(x.reshape(bsz, rows, GRID_W, ch), w, b, pad_left)
    return y.reshape(bsz, length, ch)


def rglru_coeffs(xc, w_a, b_a, w_x, b_x, lam):
    bsz, length, ch = xc.shape
    xh = xc.reshape(bsz, length, LRU_HEADS, LRU_HEAD_DIM)
    r = jax.nn.sigmoid(jnp.einsum('blhd,hde->blhe', xh, w_a).reshape(bsz, length, ch) + b_a)
    i = jax.nn.sigmoid(jnp.einsum('blhd,hde->blhe', xh, w_x).reshape(bsz, length, ch) + b_x)
    log_a = -LRU_C * r.astype(jnp.float32) * jax.nn.softplus(-lam.astype(jnp.float32))
    a = jnp.exp(log_a)
    b = jnp.sqrt(-jnp.expm1(2.0 * log_a)) * (i * xc).astype(jnp.float32)
    return a, b


def _combine(e1, e2):
    a1, b1 = e1
    a2, b2 = e2
    return a1 * a2, a2 * b1 + b2


def linear_scan(a, b, h0, reverse):
    if h0 is not None:
        edge = -1 if reverse else 0
        b = b.at[:, edge].add(a[:, edge] * h0)
    _, h = lax.associative_scan(_combine, (a, b), reverse=reverse, axis=1)
    return h


def rglru_mixer(h_lat, h_ctx, w_in, conv_w, conv_b, w_a, b_a, w_x, b_x, lam, w_out):
    gate, xb = jnp.split(h_lat @ w_in, 2, axis=-1)
    xc = grid_conv(xb, conv_w, conv_b, LRU_CONV_PAD_LEFT)
    xc_ctx = depthwise_conv(h_ctx @ w_in[:, D_RNN:], conv_w, conv_b, LRU_CONV_PAD_LEFT)
    ys = []
    for d, reverse in enumerate((False, True)):
        a_c, b_c = rglru_coeffs(xc_ctx, w_a[d], b_a[d], w_x[d], b_x[d], lam[d])
        h_c = linear_scan(a_c, b_c, None, reverse)
        h0 = h_c[:, 0] if reverse else h_c[:, -1]
        a_l, b_l = rglru_coeffs(xc, w_a[d], b_a[d], w_x[d], b_x[d], lam[d])
        ys.append(linear_scan(a_l, b_l, h0, reverse))
    y = (ys[0] + ys[1]).astype(h_lat.dtype)
    return (jax.nn.gelu(gate) * y) @ w_out


def shortconv_mixer(h_lat, w_in, conv_w, conv_b, w_out):
    bg, cg, v = jnp.split(h_lat @ w_in, 3, axis=-1)
    return (bg * grid_conv(cg * v, conv_w, conv_b, SC_CONV_PAD_LEFT)) @ w_out


def peer_ffn(h, w_q, sub_keys, u, v):
    bsz, length, dm = h.shape
    blocks = h.reshape(-1, PEER_BLOCK, dm)

    def block(xt):
        q = (xt @ w_q).reshape(PEER_BLOCK, PEER_HEADS, 2, PEER_D_KEY)
        s = jnp.einsum('thpd,hpkd->thpk', q, sub_keys)
        sv, si = lax.top_k(s, PEER_TOPK)
        cand = sv[:, :, 0, :, None] + sv[:, :, 1, None, :]
        cv, ci = lax.top_k(cand.reshape(PEER_BLOCK, PEER_HEADS, PEER_TOPK * PEER_TOPK), PEER_TOPK)
        i1 = jnp.take_along_axis(si[:, :, 0], ci // PEER_TOPK, axis=-1)
        i2 = jnp.take_along_axis(si[:, :, 1], ci % PEER_TOPK, axis=-1)
        expert = i1 * PEER_N_KEYS + i2
        g = jax.nn.softmax(cv.astype(jnp.float32), axis=-1)
        ue = jnp.take(u, expert, axis=0)
        ve = jnp.take(v, expert, axis=0)
        act = jax.nn.gelu(jnp.einsum('thkd,td->thk', ue, xt).astype(jnp.float32))
        return jnp.einsum('thk,thkd->td', (g * act).astype(xt.dtype), ve)

    return lax.map(block, blocks).reshape(bsz, length, dm)


def setup_inputs(seed: int = 0) -> dict:
    key = jax.random.key(seed)
    ks = jax.random.split(key, 26)

    def nrm(k, shape, scale):
        return scale * jax.random.normal(k, shape, jnp.float32)

    a0 = jax.random.uniform(ks[16], (N_A, 2, D_RNN), jnp.float32, 0.9, 0.999)
    s0 = a0 ** (1.0 / LRU_C)
    lru_lambda = jnp.log(s0) - jnp.log1p(-s0)
    return {
        'x': nrm(ks[0], (BATCH, SEQ, D_MODEL), 1.0),
        'c': nrm(ks[1], (BATCH, D_MODEL), 1.0),
        'ctx': nrm(ks[2], (BATCH, CTX_LEN, D_MODEL), 1.0),
        'c_ctx': nrm(ks[3], (D_MODEL,), 1.0),
        'w_mod': nrm(ks[4], (DEPTH, D_MODEL, 6 * D_MODEL), 0.5 * D_MODEL ** -0.5),
        'b_mod': nrm(ks[5], (DEPTH, 6 * D_MODEL), 0.02),
        'norm_mix_g': 1.0 + nrm(ks[6], (DEPTH, D_MODEL), 0.02),
        'norm_ffn_g': 1.0 + nrm(ks[7], (DEPTH, D_MODEL), 0.02),
        'norm_final_g': 1.0 + nrm(ks[8], (D_MODEL,), 0.02),
        'lru_w_in': nrm(ks[9], (N_A, D_MODEL, 2 * D_RNN), D_MODEL ** -0.5),
        'lru_conv_w': nrm(ks[10], (N_A, LRU_CONV_W, D_RNN), LRU_CONV_W ** -0.5),
        'lru_conv_b': nrm(ks[11], (N_A, D_RNN), 0.02),
        'lru_w_a': nrm(ks[12], (N_A, 2, LRU_HEADS, LRU_HEAD_DIM, LRU_HEAD_DIM), LRU_HEAD_DIM ** -0.5),
        'lru_b_a': nrm(ks[13], (N_A, 2, D_RNN), 0.02),
        'lru_w_x': nrm(ks[14], (N_A, 2, LRU_HEADS, LRU_HEAD_DIM, LRU_HEAD_DIM), LRU_HEAD_DIM ** -0.5),
        'lru_b_x': nrm(ks[15], (N_A, 2, D_RNN), 0.02),
        'lru_lambda': lru_lambda,
        'lru_w_out': nrm(ks[17], (N_A, D_RNN, D_MODEL), D_RNN ** -0.5),
        'sc_w_in': nrm(ks[18], (N_B, D_MODEL, 3 * D_SC), D_MODEL ** -0.5),
        'sc_conv_w': nrm(ks[19], (N_B, SC_CONV_W, D_SC), SC_CONV_W ** -0.5),
        'sc_conv_b': nrm(ks[20], (N_B, D_SC), 0.02),
        'sc_w_out': nrm(ks[21], (N_B, D_SC, D_MODEL), D_SC ** -0.5),
        'peer_w_q': nrm(ks[22], (DEPTH, D_MODEL, PEER_HEADS * PEER_D_QUERY), D_MODEL ** -0.5),
        'peer_sub_keys': nrm(ks[23], (DEPTH, PEER_HEADS, 2, PEER_N_KEYS, PEER_D_KEY), PEER_D_KEY ** -0.5),
        'peer_u': nrm(ks[24], (DEPTH, PEER_EXPERTS, D_MODEL), D_MODEL ** -0.5),
        'peer_v': nrm(ks[25], (DEPTH, PEER_EXPERTS, D_MODEL), PEER_HEADS ** -0.5),
    }


def reference(x, c, ctx, c_ctx, w_mod, b_mod, norm_mix_g, norm_ffn_g, norm_final_g,
              lru_w_in, lru_conv_w, lru_conv_b, lru_w_a, lru_b_a, lru_w_x, lru_b_x, lru_lambda, lru_w_out,
              sc_w_in, sc_conv_w, sc_conv_b, sc_w_out,
              peer_w_q, peer_sub_keys, peer_u, peer_v):
    silu_c = jax.nn.silu(c)
    silu_c_ctx = jax.nn.silu(c_ctx)
    for i in range(DEPTH):
        mixer, j = i % N_MIXERS, i // N_MIXERS
        mod = (silu_c @ w_mod[i] + b_mod[i])[:, None, :]
        sh1, sc1, g1, sh2, sc2, g2 = jnp.split(mod, 6, axis=-1)
        h = modulate(rms_norm(x, norm_mix_g[i]), sh1, sc1)
        if mixer == 0:
            mod_c = silu_c_ctx @ w_mod[i] + b_mod[i]
            h_ctx = modulate(rms_norm(ctx, norm_mix_g[i]), mod_c[:D_MODEL], mod_c[D_MODEL:2 * D_MODEL])
            out = rglru_mixer(h, h_ctx, lru_w_in[j], lru_conv_w[j], lru_conv_b[j], lru_w_a[j], lru_b_a[j],
                              lru_w_x[j], lru_b_x[j], lru_lambda[j], lru_w_out[j])
        else:
            out = shortconv_mixer(h, sc_w_in[j], sc_conv_w[j], sc_conv_b[j], sc_w_out[j])
        x = x + g1 * out
        h = modulate(rms_norm(x, norm_ffn_g[i]), sh2, sc2)
        x = x + g2 * peer_ffn(h, peer_w_q[i], peer_sub_keys[i], peer_u[i], peer_v[i])
    return rms_norm(x, norm_final_g)
```

```python
import numpy as np
from contextlib import ExitStack
import concourse.bass as bass
import concourse.mybir as mybir
from concourse.bass_utils import run_bass_kernel_spmd

F32 = mybir.dt.float32
BF16 = mybir.dt.bfloat16
U32 = mybir.dt.uint32
ALU = mybir.AluOpType
AF = mybir.ActivationFunctionType
AX = mybir.AxisListType

P = 128
D = 2048
KC = 16
T = 256
NTOK = 4096
NSB = NTOK // T
NCORE = 8
NEXP_C = 128
EPS = 1e-6
NV = 20
NEG = -1.0e30


class Buf:
    __slots__ = ("name", "writers", "readers")

    def __init__(self, name=""):
        self.name = name
        self.writers = {}
        self.readers = []


class Eng:
    def __init__(self, name, sem, self_sync):
        self.name = name
        self.sem = sem
        self.count = 0
        self.known = {}
        self.self_sync = self_sync
        self.prog = []


class Sched:
    def __init__(self, nc, stack):
        self.nc = nc
        self.stack = stack
        self.engs = {}
        self.slots = []
        for name, ss in (("pe", False), ("act", True), ("dve", True), ("pool", True), ("sp", False)):
            self.engs[name] = Eng(name, self.new_sem("e_" + name), ss)
        self.ninst = 0

    def new_sem(self, name):
        return self.stack.enter_context(self.nc.semaphore(name))

    def _waits(self, E, reads, writes):
        need = {}

        def add(tok):
            sem, val, en = tok
            if en == E.name and not E.self_sync:
                return
            k = id(sem)
            if k not in need or need[k][1] < val:
                need[k] = (sem, val)
        for b in reads:
            for tok in b.writers.values():
                add(tok)
        for b in writes:
            for tok in b.writers.values():
                add(tok)
            for tok in b.readers:
                add(tok)
        for k, (sem, val) in need.items():
            if E.known.get(k, 0) < val:
                E.prog.append(("w", sem, val))
                E.known[k] = val

    def _mark(self, tok, key, reads, writes):
        for b in reads:
            b.readers.append(tok)
        for b in writes:
            b.writers[key] = tok
            b.readers = []

    def op(self, ename, build, reads=(), writes=()):
        E = self.engs[ename]
        self._waits(E, reads, writes)
        E.count += 1
        E.prog.append(("o", build, E.sem, 1))
        self._mark((E.sem, E.count, E.name), E.name, reads, writes)
        self.ninst += 1

    def slot(self, name):
        s = {"sem": self.new_sem(name), "n": 0, "name": name}
        self.slots.append(s)
        return s

    def dma(self, qname, out, in_, slot, reads=(), writes=()):
        E = self.engs[qname]
        self._waits(E, reads, writes)
        E.prog.append(("o", (lambda h: h.dma_start(out=out, in_=in_)), slot["sem"], 16))
        slot["n"] += 1
        self._mark((slot["sem"], 16 * slot["n"], "dma"), "dma_" + slot["name"], reads, writes)
        self.ninst += 1

    def custom(self, qname, build, slot, reads=(), writes=(), inc=16):
        E = self.engs[qname]
        self._waits(E, reads, writes)
        E.prog.append(("o", build, slot["sem"], inc))
        slot["n"] += 1
        self._mark((slot["sem"], inc * slot["n"], "dma"), "dma_" + slot["name"], reads, writes)
        self.ninst += 1

    def barrier(self):
        for E in self.engs.values():
            for Fe in self.engs.values():
                if Fe is E or Fe.count == 0:
                    continue
                k = id(Fe.sem)
                if E.known.get(k, 0) < Fe.count:
                    E.prog.append(("w", Fe.sem, Fe.count))
                    E.known[k] = Fe.count
            for s in self.slots:
                k = id(s["sem"])
                if s["n"] and E.known.get(k, 0) < 16 * s["n"]:
                    E.prog.append(("w", s["sem"], 16 * s["n"]))
                    E.known[k] = 16 * s["n"]

    def emit(self):
        with self.nc.Block() as block:
            def run(E):
                def f(h):
                    for it in E.prog:
                        if it[0] == "w":
                            h.wait_ge(it[1], it[2])
                        else:
                            it[1](h).then_inc(it[2], it[3])
                return f
            block.tensor(run(self.engs["pe"]))
            block.scalar(run(self.engs["act"]))
            block.vector(run(self.engs["dve"]))
            block.gpsimd(run(self.engs["pool"]))
            block.sync(run(self.engs["sp"]))


class Tl:
    def __init__(self, t, name):
        self.t = t
        self.b = Buf(name)


class Ring:
    def __init__(self, tiles):
        self.tiles = tiles
        self.i = 0

    def next(self):
        t = self.tiles[self.i % len(self.tiles)]
        self.i += 1
        return t


class Stream:
    def __init__(self, kb, ring, srcs, queue="sp"):
        self.kb = kb
        self.ring = ring
        self.srcs = srcs
        self.R = len(ring)
        for tl in ring:
            if not hasattr(tl, "slot"):
                tl.slot = kb.S.slot("r_" + tl.b.name)
        self.issued = 0
        self.taken = 0
        self.queue = queue

    def _issue(self):
        n = self.issued
        tl = self.ring[n % self.R]
        src, sbuf = self.srcs[n]
        self.kb.S.dma(self.queue, tl.t[:], src, tl.slot,
                      reads=[sbuf] if sbuf is not None else [], writes=[tl.b])
        self.issued += 1

    def get(self):
        while self.issued < min(self.taken + self.R, len(self.srcs)):
            self._issue()
        tl = self.ring[self.taken % self.R]
        self.taken += 1
        return tl


class KB:
    def __init__(self, mode, nsb_run=NSB, dbg=None):
        self.mode = mode
        self.nsb_run = nsb_run
        self.dbg = dbg
        self.nc = bass.Bass("TRN2", target_bir_lowering=False)
        self.st = ExitStack()
        full = ["w_in", "wab", "w_out", "sc_w_in", "sc_w_out", "wq0", "wq1", "u0", "v0", "u1", "v1"]
        sub = {"s_setup": [], "x1": full[:3], "s_ctx": full[:3], "x2": full[:3] + ["wq0", "u0", "v0"],
               "x3": full[:5] + ["wq0", "u0", "v0"]}
        self.wlist = ["w_in_xb", "wab"] if mode == "p1" else sub.get(dbg, full)

    def din(self, name, shape, dt=F32):
        return self.nc.dram_tensor(name, list(shape), dt, kind="ExternalInput").ap()

    def dout(self, name, shape, dt=F32):
        return self.nc.dram_tensor(name, list(shape), dt, kind="ExternalOutput").ap()

    def dscr(self, name, shape, dt=BF16):
        return self.nc.dram_tensor(name, list(shape), dt, kind="Internal").ap()

    def sb(self, name, shape, dt=F32):
        return Tl(self.st.enter_context(self.nc.sbuf_tensor(name, list(shape), dt)), name)

    def ps(self, name, shape, dt=F32):
        return Tl(self.st.enter_context(self.nc.psum_tensor(name, list(shape), dt)), name)

    def act(self, out, in_, func, reads, writes, scale=1.0, bias=0.0, eng="act"):
        self.S.op(eng, lambda h: h.activation(out=out, in_=in_, func=func, scale=scale, bias=bias),
                  [r.b for r in reads], [w.b for w in writes])

    def tt(self, out, in0, in1, op, reads, writes, eng="dve"):
        self.S.op(eng, lambda h: h.tensor_tensor(out=out, in0=in0, in1=in1, op=op),
                  [r.b for r in reads], [w.b for w in writes])

    def ts(self, out, in0, s1, s2, op0, op1, reads, writes, eng="dve"):
        self.S.op(eng, lambda h: h.tensor_scalar(out=out, in0=in0, scalar1=s1, scalar2=s2, op0=op0, op1=op1),
                  [r.b for r in reads], [w.b for w in writes])

    def stt(self, out, in0, scalar, in1, op0, op1, reads, writes):
        self.S.op("dve", lambda h: h.scalar_tensor_tensor(out=out, in0=in0, scalar=scalar, in1=in1, op0=op0, op1=op1),
                  [r.b for r in reads], [w.b for w in writes])

    def cp(self, out, in_, reads, writes, eng="dve"):
        self.S.op(eng, lambda h: h.tensor_copy(out=out, in_=in_),
                  [r.b for r in reads], [w.b for w in writes])

    def mm(self, out, lhsT, rhs, start, stop, reads, writes):
        self.S.op("pe", lambda h: h.matmul(out, lhsT=lhsT, rhs=rhs, start=start, stop=stop),
                  [r.b for r in reads], [w.b for w in writes])

    def vop(self, fn, reads, writes, eng="dve"):
        self.S.op(eng, fn, [r.b for r in reads], [w.b for w in writes])

    def dma(self, out, in_, slot, reads=(), writes=(), q="sp"):
        self.S.dma(q, out, in_, slot, [r if isinstance(r, Buf) else r.b for r in reads],
                   [w if isinstance(w, Buf) else w.b for w in writes])

    def build(self):
        nc = self.nc
        mode = self.mode
        with self.st:
            self.S = Sched(nc, self.st)
            self.declare_io()
            self.alloc()
            self.setup()
            if mode == "p1":
                self.precast(["w_in_xb", "wab"])
                self.S.barrier()
                self.pass1(self.xT, 0)
            elif mode == "f":
                self.precast(self.wlist)
                self.S.barrier()
                for s_ in range(3):
                    self.pass1(self.xTo, s_ * NTOK)
                    self.cp(self.call.t[:, s_, :, :], self.carry.t[:, NSB, :, :], [self.carry], [self.call])
                    self.S.barrier()
                self.pass1(self.xT, 0)
                self.ctx_and_carries()
                self.pass2()
            else:
                self.precast(self.wlist)
                self.S.barrier()
                if self.dbg not in ("s_setup", "s_precast"):
                    self.ctx_and_carries()
                    if self.dbg != "s_ctx":
                        self.pass2()
            self.S.barrier()
            self.S.emit()
        return nc

    def declare_io(self):
        self.xT = self.din("xT", [P, KC, NTOK])
        self.consts = self.din("consts", [P, 3 * 128])
        self.vecs = self.din("vecs", [P, NV, KC])
        self.cvec = self.din("cvec", [P, KC, 2])
        self.w_mod = self.din("w_mod", [2, D, 6 * D])
        self.b_modT = self.din("b_modT", [P, 2, 96])
        self.lru_w_in = self.din("lru_w_in", [D, 2 * D])
        self.wab_in = self.din("wab", [16, P, 512])
        if self.mode == "p1":
            self.carry_out = self.dout("carry", [P, NSB + 1, 4, KC])
        else:
            self.ctxT = self.din("ctxT", [P, KC, 256])
            if self.mode == "f":
                self.xTo = self.din("xTo", [P, KC, 3 * NTOK])
            if self.mode == "p2":
                self.carry_own = self.din("carry_own", [P, NSB + 1, 4, KC])
                self.carry_all = self.din("carry_all", [P, NCORE, 4, KC])
            self.nslots = 3 if self.mode == "f" else NCORE
            self.cmask = self.din("cmask", [P, 2, self.nslots])
            wl = self.wlist
            if "w_out" in wl:
                self.lru_w_out = self.din("lru_w_out", [D, D])
            if "sc_w_in" in wl:
                self.sc_w_in = self.din("sc_w_in", [D, 3 * D])
                self.sc_w_out = self.din("sc_w_out", [D, D])
            self.has_peer = "wq0" in wl
            if self.has_peer:
                self.peer_w_q = self.din("peer_w_q", [2, D, D])
                self.skT_in = self.din("skT", [2, P, 16, 128])
                self.u_l = self.din("u_l", [2, NEXP_C, P, D])
                self.v_l = self.din("v_l", [2, NEXP_C, P, D])
            self.yT = self.dout("yT", [P, KC, NTOK])
        self.ws = {}
        self.ws["w_in"] = (self.dscr("ws_w_in", [32, P, D]), Buf("ws_w_in"))
        self.ws["wab"] = (self.dscr("ws_wab", [16, P, 512]), Buf("ws_wab"))
        if self.mode != "p1":
            self.ws["w_out"] = (self.dscr("ws_w_out", [16, P, D]), Buf("ws_w_out"))
            self.ws["sc_w_in"] = (self.dscr("ws_sc_w_in", [48, P, D]), Buf("ws_sc_w_in"))
            self.ws["sc_w_out"] = (self.dscr("ws_sc_w_out", [16, P, D]), Buf("ws_sc_w_out"))
            for i in range(2):
                self.ws[f"wq{i}"] = (self.dscr(f"ws_wq{i}", [16, P, D]), Buf(f"ws_wq{i}"))
                self.ws[f"u{i}"] = (self.dscr(f"ws_u{i}", [NEXP_C, P, D]), Buf(f"ws_u{i}"))
                self.ws[f"v{i}"] = (self.dscr(f"ws_v{i}", [NEXP_C, P, D]), Buf(f"ws_v{i}"))

    def alloc(self):
        sb, ps = self.sb, self.ps
        self.cst = sb("cst", [P, 3 * 128])
        self.ident = self.cst.t[:, 0:128]
        self.iota = self.cst.t[:, 128:256]
        self.ones_bf = sb("ones_bf", [P, 128], BF16)
        self.iota_bf = sb("iota_bf", [P, 128], BF16)
        self.vec = sb("vec", [P, NV, KC])
        self.cv = sb("cv", [P, KC, 2])
        self.modt = [sb(f"modt{i}", [P, 96, 2]) for i in range(2)]
        self.bmod = sb("bmod", [P, 2, 96])
        self.dv = sb("dv", [P, 16, KC])
        self.wabring = [sb(f"wabr{i}", [P, 4, 128], BF16) for i in range(2)]
        self.xres = sb("xres", [P, KC, T])
        self.hT = sb("hT", [P, KC, T], BF16)
        self.bfA = sb("bfA", [P, KC, T], BF16)
        self.rstd = sb("rstd", [P, T])
        self.tmpr = Ring([sb(f"tmp{i}", [P, T]) for i in range(11)])
        self.tmpb = Ring([sb(f"tmpb{i}", [P, T], BF16) for i in range(4)])
        self.arena = sb("arena", [P, 16384])
        self.wring = [sb(f"wr{i}", [P, D], BF16) for i in range(3)]
        self.carry = sb("carryt", [P, NSB + 1, 4, KC])
        self.hin = sb("hin", [P, 2, NSB + 1, KC])
        self.small = sb("small", [P, 8, KC])
        if self.mode != "p1":
            self.vring = [sb(f"vr{i}", [P, 4, D], BF16) for i in range(2)]
            self.skT = [sb(f"skT{i}", [P, 16, 128], BF16) for i in range(2)]
            self.X1 = sb("X1", [P, 2048])
            self.X2 = sb("X2", [P, 2048])
            self.sv = sb("sv", [P, 16, 16])
            self.si_u = sb("si_u", [P, 16, 16], U32)
            self.si_f = sb("si_f", [P, 16, 16])
            self.cvv = sb("cvv", [P, 8, 16])
            self.ci_u = sb("ci_u", [P, 8, 16], U32)
            self.ca_u = sb("ca_u", [P, 8, 16], U32)
            self.cb_u = sb("cb_u", [P, 8, 16], U32)
            self.caf = sb("caf", [P, 8, 16])
            self.cbf = sb("cbf", [P, 8, 16])
            self.slot3 = sb("slot3", [P, 3, 128])
            self.slotT = sb("slotT", [P, 3, T])
            self.zs = sb("zs", [P, 4, 8])
            self.cmk = sb("cmk", [P, 2, self.nslots])
            self.call = sb("call", [P, NCORE, 4, KC])
            self.WT = Tl(self.arena.t[:].bitcast(BF16).rearrange("p (c t) -> p c t", t=T), "WT")
            self.WT.b = self.arena.b
            self.ctxx = Tl(self.arena.t[:, 0:KC * 256].rearrange("p (k t) -> p k t", k=KC), "ctxx")
            self.ctxx.b = self.arena.b
            self.outT = Tl(self.arena.t[:, 0:KC * T].rearrange("p (k t) -> p k t", k=KC), "outT")
            self.outT.b = self.arena.b
        pa = [ps(f"psA{i}", [P, 512]) for i in range(3)]
        pst = []
        for half in range(2):
            for bnk in range(3):
                t_ = Tl(pa[bnk].t[:, half * 256:half * 256 + 256], f"psh{bnk}_{half}")
                t_.b = pa[bnk].b
                pst.append(t_)
        self.psring = Ring(pst)
        self.psS = [ps(f"psS{i}", [P, 512]) for i in range(4)]
        self.psM = ps("psM", [P, 512])

    def setup(self):
        S = self.S
        self._nld = 0

        def ld1(o, i, tl):
            self._nld += 1
            self.dma(o, i, S.slot(f"ld1_{self._nld}"), writes=[tl])
        self.ld1 = ld1
        ld1(self.cst.t[:], self.consts, self.cst)
        ld1(self.vec.t[:], self.vecs, self.vec)
        ld1(self.cv.t[:], self.cvec, self.cv)
        ld1(self.bmod.t[:], self.b_modT, self.bmod)
        self.cp(self.ones_bf.t[:], self.cst.t[:, 256:384], [self.cst], [self.ones_bf])
        self.cp(self.iota_bf.t[:], self.cst.t[:, 128:256], [self.cst], [self.iota_bf])
        self.act(self.cv.t[:], self.cv.t[:], AF.Silu, [self.cv], [self.cv])
        nlay = 1 if self.mode == "p1" else 2
        NB = 256
        wmr = [Tl(self.arena.t[:, i * 4096:(i + 1) * 4096].rearrange("p (k n) -> p k n", k=KC), f"wm{i}")
               for i in range(3)]
        wslots = [S.slot(f"wm{i}") for i in range(3)]
        cnt = 0
        for i in range(nlay):
            nblk = (3 * D // NB) if self.mode == "p1" else (6 * D // NB)
            for nb in range(nblk):
                wt = wmr[cnt % 3]
                src = self.w_mod[i, :, nb * NB:(nb + 1) * NB].rearrange("(k p) n -> p k n", p=P)
                self.dma(wt.t, src, wslots[cnt % 3], writes=[wt])
                cnt += 1
                for mi in range(NB // 128):
                    m = nb * (NB // 128) + mi
                    for k in range(KC):
                        self.mm(self.psM.t[:, 2 * m:2 * m + 2], wt.t[:, k, mi * 128:(mi + 1) * 128],
                                self.cv.t[:, k, :], k == 0, k == KC - 1, [wt, self.cv], [self.psM])
            nm = nblk * NB // 128
            self.tt(self.modt[i].t[:, 0:nm, :], self.psM.t[:, 0:2 * nm].rearrange("p (m c) -> p m c", c=2),
                    self.bmod.t[:, i, 0:nm].unsqueeze(2).broadcast_to([P, nm, 2]), ALU.add,
                    [self.psM, self.bmod], [self.modt[i]])
        dv = self.dv
        for i in range(nlay):
            self.stt(dv.t[:, 2 * i + 0, :], self.modt[i].t[:, 16:32, 0], 1.0, self.vec.t[:, 2 * i + 0, :],
                     ALU.add, ALU.mult, [self.modt[i], self.vec], [dv])
            if self.mode != "p1":
                self.stt(dv.t[:, 2 * i + 1, :], self.modt[i].t[:, 64:80, 0], 1.0, self.vec.t[:, 2 * i + 1, :],
                         ALU.add, ALU.mult, [self.modt[i], self.vec], [dv])
        self.stt(dv.t[:, 8, :], self.modt[0].t[:, 16:32, 1], 1.0, self.vec.t[:, 0, :],
                 ALU.add, ALU.mult, [self.modt[0], self.vec], [dv])
        self.act(dv.t[:, 9:11, :], self.vec.t[:, 14:16, :], AF.Exp, [self.vec], [dv], scale=-1.0)
        self.act(dv.t[:, 9:11, :], dv.t[:, 9:11, :], AF.Ln, [dv], [dv], bias=1.0)
        self.ts(dv.t[:, 9:11, :], dv.t[:, 9:11, :], -8.0, None, ALU.mult, ALU.bypass, [dv], [dv])
        self.ts(dv.t[:, 11:13, :], dv.t[:, 9:11, :], 2.0, None, ALU.mult, ALU.bypass, [dv], [dv])
        self.vop(lambda h: h.memset(dv.t[:, 13, :], 0.0), [], [dv], eng="pool")
        if self.mode != "p1":
            ld1(self.cmk.t[:], self.cmask, self.cmk)
            if self.mode == "p2":
                ld1(self.call.t[:], self.carry_all, self.call)
                ld1(self.carry.t[:], self.carry_own, self.carry)
        self.S.barrier()

    def A1(self, i, k): return self.dv.t[:, 2 * i, k:k + 1]
    def A2(self, i, k): return self.dv.t[:, 2 * i + 1, k:k + 1]
    def S1(self, i, k): return self.modt[i].t[:, 0 + k, 0:1]
    def G1(self, i, k): return self.modt[i].t[:, 32 + k, 0:1]
    def S2(self, i, k): return self.modt[i].t[:, 48 + k, 0:1]
    def G2(self, i, k): return self.modt[i].t[:, 80 + k, 0:1]
    def V(self, idx, k): return self.vec.t[:, idx, k:k + 1]

    def precast(self, which):
        S = self.S
        NST = 3
        fin = [Tl(self.arena.t[:, i * 2048:(i + 1) * 2048], f"pcin{i}") for i in range(NST)]
        fob = [Tl(self.arena.t[:, 6144 + i * 1024:6144 + (i + 1) * 1024].bitcast(BF16), f"pcout{i}")
               for i in range(NST)]
        sl_in = [S.slot(f"pci{i}") for i in range(NST)]
        sl_out = [S.slot(f"pco{i}") for i in range(NST)]
        engs = ["dve", "act", "pool"]
        jobs = []

        def add_W(W, key, cc_lo, cc_hi, base=0):
            dst, dbuf = self.ws[key]
            for cc in range(cc_lo, cc_hi):
                src = W[:, cc * 128:(cc + 1) * 128].rearrange("(k p) n -> p k n", p=P)
                jobs.append((src, dst[cc - base], dbuf, 16))

        def add_rows(R, key):
            dst, dbuf = self.ws[key]
            for c in range(NEXP_C):
                jobs.append((R[c], dst[c], dbuf, 0))
        for w in which:
            if w == "w_in_xb":
                add_W(self.lru_w_in, "w_in", 16, 32)
            elif w == "w_in":
                add_W(self.lru_w_in, "w_in", 0, 32)
            elif w == "wab":
                dst, dbuf = self.ws["wab"]
                for j in range(4):
                    jobs.append((self.wab_in[4 * j:4 * j + 4].rearrange("h p f -> p h f"),
                                 dst[4 * j:4 * j + 4].rearrange("h p f -> p h f"), dbuf, 4))
            elif w == "w_out":
                add_W(self.lru_w_out, "w_out", 0, 16)
            elif w == "sc_w_in":
                add_W(self.sc_w_in, "sc_w_in", 0, 48)
            elif w == "sc_w_out":
                add_W(self.sc_w_out, "sc_w_out", 0, 16)
            elif w in ("wq0", "wq1"):
                add_W(self.peer_w_q[int(w[2])], w, 0, 16)
            elif w in ("u0", "u1"):
                add_rows(self.u_l[int(w[1])], w)
            elif w in ("v0", "v1"):
                add_rows(self.v_l[int(w[1])], w)
        for n, (src, dst, dbuf, is3) in enumerate(jobs):
            i = n % NST
            o = fin[i].t.rearrange("p (k n) -> p k n", k=is3) if is3 else fin[i].t
            self.dma(o, src, sl_in[i], writes=[fin[i]])
            e = engs[n % 3]
            if e == "act":
                self.act(fob[i].t, fin[i].t, AF.Copy, [fin[i]], [fob[i]])
            else:
                self.cp(fob[i].t, fin[i].t, [fin[i]], [fob[i]], eng=e)
            oo = fob[i].t.rearrange("p (k n) -> p k n", k=4) if is3 == 4 else fob[i].t
            self.dma(dst, oo, sl_out[i], reads=[fob[i]], writes=[dbuf])
        self.S.barrier()
        if self.mode != "p1" and self.has_peer:
            for i in range(2):
                st = Tl(self.arena.t[:, i * 2048:(i + 1) * 2048], f"sks{i}")
                self.ld1(st.t, self.skT_in[i].rearrange("p a b -> p (a b)"), st)
                self.cp(self.skT[i].t[:].rearrange("p a b -> p (a b)"), st.t, [st], [self.skT[i]])

    def wstream(self, key, ccs):
        dst, dbuf = self.ws[key]
        return Stream(self, self.wring, [(dst[cc], dbuf) for cc in ccs])

    def norm_mod(self, xsrc, n, Afn, Sfn, out_tl, out_is_f32=False):
        sq = self.bfA
        self.act(sq.t[:, :, 0:n], xsrc.t[:, :, 0:n], AF.Square, [xsrc], [sq])
        pm = self.psM
        for k in range(KC):
            self.mm(pm.t[:, 0:n], self.ones_bf.t[:], sq.t[:, k, 0:n], k == 0, k == KC - 1,
                    [self.ones_bf, sq], [pm])
        r = self.rstd
        self.act(r.t[:, 0:n], pm.t[:, 0:n], AF.Sqrt, [pm], [r], scale=1.0 / D, bias=EPS)
        self.vop(lambda h: h.reciprocal(out=r.t[:, 0:n], in_=r.t[:, 0:n]), [r], [r])
        for k in range(KC):
            tm = self.tmpr.next()
            self.tt(tm.t[:, 0:n], xsrc.t[:, k, 0:n], r.t[:, 0:n], ALU.mult, [xsrc, r], [tm])
            self.act(out_tl.t[:, k, 0:n], tm.t[:, 0:n], AF.Identity, [tm], [out_tl], scale=Afn(k), bias=Sfn(k))

    def lru_block(self, xsrc, n, rowlen, Afn, Sfn, pass2, sbi, hin_f=None, hin_b=None, want_carry=True):
        self.norm_mod(xsrc, n, Afn, Sfn, self.hT)
        ccs = []
        for h in range(16):
            if pass2:
                ccs.append(h)
            ccs.append(16 + h)
        wsr = self.wstream("w_in", ccs)
        wabs = Stream(self, self.wabring, [(self.ws["wab"][0][h].rearrange("p (a b) -> p a b", a=4), self.ws["wab"][1])
                                           for h in range(16)])
        YT = self.bfA
        nrows = n // rowlen
        for h in range(16):
            wabt = wabs.get()
            gg = None
            if pass2:
                wt = wsr.get()
                pg = self.psring.next()
                for k in range(KC):
                    self.mm(pg.t[:, 0:n], wt.t[:, k * 128:(k + 1) * 128], self.hT.t[:, k, 0:n], k == 0, k == KC - 1,
                            [wt, self.hT], [pg])
                gg = self.tmpb.next()
                self.act(gg.t[:, 0:n], pg.t[:, 0:n], AF.Gelu_apprx_tanh, [pg], [gg])
            wt = wsr.get()
            px = self.psring.next()
            for k in range(KC):
                self.mm(px.t[:, 0:n], wt.t[:, k * 128:(k + 1) * 128], self.hT.t[:, k, 0:n], k == 0, k == KC - 1,
                        [wt, self.hT], [px])
            xc = self.tmpr.next()
            self.act(xc.t[:, 0:n], px.t[:, 0:n], AF.Identity, [px], [xc], scale=self.V(7, h), bias=self.V(9, h))
            pv = px.t[:, 0:n].rearrange("p (r c) -> p r c", c=rowlen)
            xv = xc.t[:, 0:n].rearrange("p (r c) -> p r c", c=rowlen)
            for tap, off in ((6, -1), (5, -2), (8, 1)):
                if off < 0:
                    o_, i_ = xv[:, :, -off:], pv[:, :, :rowlen + off]
                else:
                    o_, i_ = xv[:, :, :rowlen - off], pv[:, :, off:]
                self.stt(o_, i_, self.V(tap, h), o_, ALU.mult, ALU.add, [px, xc], [xc])
            xcb = self.tmpb.next()
            self.cp(xcb.t[:, 0:n], xc.t[:, 0:n], [xc], [xcb], eng="pool")
            hs = []
            for d in range(2):
                pr = self.psring.next()
                self.mm(pr.t[:, 0:n], wabt.t[:, d * 2 + 0, :], xcb.t[:, 0:n], True, True,
                        [wabt, xcb], [pr])
                pi = self.psring.next()
                self.mm(pi.t[:, 0:n], wabt.t[:, d * 2 + 1, :], xcb.t[:, 0:n], True, True,
                        [wabt, xcb], [pi])
                rr = self.tmpr.next()
                self.act(rr.t[:, 0:n], pr.t[:, 0:n], AF.Sigmoid, [pr], [rr], bias=self.V(10 + d, h))
                ii = self.tmpr.next()
                self.act(ii.t[:, 0:n], pi.t[:, 0:n], AF.Sigmoid, [pi], [ii], bias=self.V(12 + d, h))
                aa = self.tmpr.next()
                self.act(aa.t[:, 0:n], rr.t[:, 0:n], AF.Exp, [rr], [aa], scale=self.dv.t[:, 9 + d, h:h + 1])
                a2 = self.tmpr.next()
                self.act(a2.t[:, 0:n], rr.t[:, 0:n], AF.Exp, [rr], [a2], scale=self.dv.t[:, 11 + d, h:h + 1])
                self.act(a2.t[:, 0:n], a2.t[:, 0:n], AF.Sqrt, [a2], [a2], scale=-1.0, bias=1.0)
                self.tt(ii.t[:, 0:n], ii.t[:, 0:n], xc.t[:, 0:n], ALU.mult, [ii, xc], [ii])
                self.tt(ii.t[:, 0:n], ii.t[:, 0:n], a2.t[:, 0:n], ALU.mult, [ii, a2], [ii])
                first = 0 if d == 0 else n - 1
                hinp = hin_f if d == 0 else hin_b
                if hinp is not None:
                    self.stt(ii.t[:, first:first + 1], aa.t[:, first:first + 1], hinp(h), ii.t[:, first:first + 1],
                             ALU.mult, ALU.add, [aa, ii, self.hin], [ii])
                hh = self.tmpr.next()
                if d == 0:
                    o_, a_, b_ = hh.t[:, 0:n], aa.t[:, 0:n], ii.t[:, 0:n]
                else:
                    o_, a_, b_ = hh.t[:, 0:n][:, ::-1], aa.t[:, 0:n][:, ::-1], ii.t[:, 0:n][:, ::-1]
                self.vop(lambda hd, o_=o_, a_=a_, b_=b_: hd.tensor_tensor_scan(out=o_, data0=a_, data1=b_, initial=0.0,
                                                                           op0=ALU.mult, op1=ALU.add),
                         [aa, ii], [hh])
                if want_carry:
                    last = n - 1 if d == 0 else 0
                    ctl = self.carry
                    self.vop(lambda hd, rr=rr, d=d, h=h, ctl=ctl: hd.tensor_reduce(out=ctl.t[:, sbi, 2 * d, h:h + 1],
                                                                                 in_=rr.t[:, 0:n], axis=AX.X, op=ALU.add),
                             [rr], [ctl])
                    self.cp(self.carry.t[:, sbi, 2 * d + 1, h:h + 1], hh.t[:, last:last + 1], [hh], [self.carry],
                            eng="pool")
                hs.append(hh)
            if pass2:
                self.tt(hs[0].t[:, 0:n], hs[0].t[:, 0:n], hs[1].t[:, 0:n], ALU.add, [hs[0], hs[1]], [hs[0]])
                self.tt(YT.t[:, h, 0:n], hs[0].t[:, 0:n], gg.t[:, 0:n], ALU.mult, [hs[0], gg], [YT])
        if pass2:
            self.out_proj("w_out", YT, n, 0)

    def out_proj(self, key, YT, n, layer):
        wsr = self.wstream(key, list(range(16)))
        for m in range(16):
            wt = wsr.get()
            po = self.psring.next()
            for k in range(KC):
                self.mm(po.t[:, 0:n], wt.t[:, k * 128:(k + 1) * 128], YT.t[:, k, 0:n], k == 0, k == KC - 1,
                        [wt, YT], [po])
            self.stt(self.xres.t[:, m, 0:n], po.t[:, 0:n], self.G1(layer, m), self.xres.t[:, m, 0:n],
                     ALU.mult, ALU.add, [po, self.modt[layer], self.xres], [self.xres])

    def sc_block(self, n):
        self.norm_mod(self.xres, n, lambda k: self.A1(1, k), lambda k: self.S1(1, k), self.hT)
        ccs = []
        for ch in range(16):
            ccs += [ch, 16 + ch, 32 + ch]
        wsr = self.wstream("sc_w_in", ccs)
        YT = self.bfA
        rowlen = 64
        for ch in range(16):
            pp = []
            for j in range(3):
                wt = wsr.get()
                p_ = self.psring.next()
                for k in range(KC):
                    self.mm(p_.t[:, 0:n], wt.t[:, k * 128:(k + 1) * 128], self.hT.t[:, k, 0:n], k == 0, k == KC - 1,
                            [wt, self.hT], [p_])
                pp.append(p_)
            vs = self.tmpr.next()
            self.act(vs.t[:, 0:n], pp[2].t[:, 0:n], AF.Copy, [pp[2]], [vs])
            cvt = self.tmpr.next()
            self.tt(cvt.t[:, 0:n], pp[1].t[:, 0:n], vs.t[:, 0:n], ALU.mult, [pp[1], vs], [cvt])
            yc = self.tmpr.next()
            self.act(yc.t[:, 0:n], cvt.t[:, 0:n], AF.Identity, [cvt], [yc], scale=self.V(17, ch), bias=self.V(19, ch))
            cv_ = cvt.t[:, 0:n].rearrange("p (r c) -> p r c", c=rowlen)
            yv = yc.t[:, 0:n].rearrange("p (r c) -> p r c", c=rowlen)
            for tap, off in ((16, -1), (18, 1)):
                if off < 0:
                    o_, i_ = yv[:, :, -off:], cv_[:, :, :rowlen + off]
                else:
                    o_, i_ = yv[:, :, :rowlen - off], cv_[:, :, off:]
                self.stt(o_, i_, self.V(tap, ch), o_, ALU.mult, ALU.add, [cvt, yc], [yc])
            self.tt(YT.t[:, ch, 0:n], pp[0].t[:, 0:n], yc.t[:, 0:n], ALU.mult, [pp[0], yc], [YT])
        self.out_proj("sc_w_out", YT, n, 1)

    def peer_block(self, layer):
        n = T
        self.norm_mod(self.xres, n, lambda k: self.A2(layer, k), lambda k: self.S2(layer, k), self.hT)
        qT = self.bfA
        wsr = self.wstream(f"wq{layer}", list(range(16)))
        for hp in range(16):
            wt = wsr.get()
            pq = self.psring.next()
            for k in range(KC):
                self.mm(pq.t[:, 0:n], wt.t[:, k * 128:(k + 1) * 128], self.hT.t[:, k, 0:n], k == 0, k == KC - 1,
                        [wt, self.hT], [pq])
            self.act(qT.t[:, hp, 0:n], pq.t[:, 0:n], AF.Copy, [pq], [qT])
        for ts_ in range(T // 128):
            self.topk_sub(layer, qT, ts_)
            self.scatter_sub(ts_)
        self.sweep(layer)

    def topk_sub(self, layer, qT, ts_):
        t0 = ts_ * 128
        skT = self.skT[layer]
        s_sb = Tl(self.X1.t[:].rearrange("p (a b) -> p a b", b=128), "x"); s_sb.b = self.X1.b
        s_wk = Tl(self.X2.t[:].rearrange("p (a b) -> p a b", b=128), "x"); s_wk.b = self.X2.b
        for hp in range(16):
            pS = self.psS[hp // 4]
            self.mm(pS.t[:, (hp % 4) * 128:(hp % 4 + 1) * 128], qT.t[:, hp, t0:t0 + 128], skT.t[:, hp, :], True, True,
                    [qT, skT], [pS])
        for b4 in range(4):
            self.act(s_sb.t[:, b4 * 4:(b4 + 1) * 4, :].rearrange("p a b -> p (a b)"), self.psS[b4].t[:], AF.Copy,
                     [self.psS[b4]], [s_sb])
        sv, si_u = self.sv, self.si_u
        for hp in range(16):
            self.vop(lambda h, hp=hp: h.max(out=sv.t[:, hp, 0:8], in_=s_sb.t[:, hp, :]), [s_sb], [sv])
        for hp in range(16):
            self.vop(lambda h, hp=hp: h.max_index(out=si_u.t[:, hp, 0:8], in_max=sv.t[:, hp, 0:8],
                                                  in_values=s_sb.t[:, hp, :]), [s_sb, sv], [si_u])
        for hp in range(16):
            self.vop(lambda h, hp=hp: h.match_replace(out=s_wk.t[:, hp, :], in_to_replace=sv.t[:, hp, 0:8],
                                                      in_values=s_sb.t[:, hp, :], imm_value=NEG), [s_sb, sv], [s_wk])
        for hp in range(16):
            self.vop(lambda h, hp=hp: h.max(out=sv.t[:, hp, 8:16], in_=s_wk.t[:, hp, :]), [s_wk], [sv])
        for hp in range(16):
            self.vop(lambda h, hp=hp: h.max_index(out=si_u.t[:, hp, 8:16], in_max=sv.t[:, hp, 8:16],
                                                  in_values=s_wk.t[:, hp, :]), [s_wk, sv], [si_u])
        self.cp(self.si_f.t[:], si_u.t[:], [si_u], [self.si_f])
        cand = Tl(self.X1.t[:].rearrange("p (h a b) -> p h a b", h=8, a=16), "x"); cand.b = self.X1.b
        candw = Tl(self.X2.t[:].rearrange("p (h c) -> p h c", h=8), "x"); candw.b = self.X2.b
        self.tt(cand.t, sv.t[:, 0::2, :].unsqueeze(3).broadcast_to([P, 8, 16, 16]),
                sv.t[:, 1::2, :].unsqueeze(2).broadcast_to([P, 8, 16, 16]), ALU.add, [sv], [cand])
        c2 = Tl(self.X1.t[:].rearrange("p (h c) -> p h c", h=8), "x"); c2.b = self.X1.b
        cvv, ci_u = self.cvv, self.ci_u
        for h_ in range(8):
            self.vop(lambda h, h_=h_: h.max(out=cvv.t[:, h_, 0:8], in_=c2.t[:, h_, :]), [c2], [cvv])
        for h_ in range(8):
            self.vop(lambda h, h_=h_: h.max_index(out=ci_u.t[:, h_, 0:8], in_max=cvv.t[:, h_, 0:8],
                                                  in_values=c2.t[:, h_, :]), [c2, cvv], [ci_u])
        for h_ in range(8):
            self.vop(lambda h, h_=h_: h.match_replace(out=candw.t[:, h_, :], in_to_replace=cvv.t[:, h_, 0:8],
                                                      in_values=c2.t[:, h_, :], imm_value=NEG), [c2, cvv], [candw])
        for h_ in range(8):
            self.vop(lambda h, h_=h_: h.max(out=cvv.t[:, h_, 8:16], in_=candw.t[:, h_, :]), [candw], [cvv])
        for h_ in range(8):
            self.vop(lambda h, h_=h_: h.max_index(out=ci_u.t[:, h_, 8:16], in_max=cvv.t[:, h_, 8:16],
                                                  in_values=candw.t[:, h_, :]), [candw, cvv], [ci_u])
        self.ts(self.ca_u.t[:], ci_u.t[:], 4, None, ALU.logical_shift_right, ALU.bypass, [ci_u], [self.ca_u])
        self.ts(self.cb_u.t[:], ci_u.t[:], 15, None, ALU.bitwise_and, ALU.bypass, [ci_u], [self.cb_u])
        self.cp(self.caf.t[:], self.ca_u.t[:], [self.ca_u], [self.caf])
        self.cp(self.cbf.t[:], self.cb_u.t[:], [self.cb_u], [self.cbf])
        oh = Tl(self.X1.t[:].rearrange("p (h k a) -> p h k a", h=8, k=16), "x"); oh.b = self.X1.b
        io16 = self.iota[:, 0:16].unsqueeze(1).unsqueeze(1).broadcast_to([P, 8, 16, 16])
        for j, (cf, par) in enumerate(((self.caf, 0), (self.cbf, 1))):
            self.tt(oh.t, cf.t[:].unsqueeze(3).broadcast_to([P, 8, 16, 16]), io16, ALU.is_equal, [cf, self.cst], [oh])
            self.tt(oh.t, oh.t, self.si_f.t[:, par::2, :].unsqueeze(2).broadcast_to([P, 8, 16, 16]), ALU.mult,
                    [oh, self.si_f], [oh])
            self.vop(lambda h, j=j: h.tensor_reduce(out=self.slot3.t[:, j, :].rearrange("p (h k) -> p h k", h=8),
                                                    in_=oh.t, axis=AX.X, op=ALU.add), [oh], [self.slot3])
        zs = self.zs
        g3 = self.slot3.t[:, 2, :].rearrange("p (h k) -> p h k", h=8)
        self.tt(g3, cvv.t[:], cvv.t[:, :, 0:1].broadcast_to([P, 8, 16]), ALU.subtract, [cvv], [self.slot3])
        self.act(g3, g3, AF.Exp, [self.slot3], [self.slot3])
        self.vop(lambda h: h.tensor_reduce(out=zs.t[:, 0, :], in_=g3, axis=AX.X, op=ALU.add), [self.slot3], [zs])
        self.vop(lambda h: h.reciprocal(out=zs.t[:, 1, :], in_=zs.t[:, 0, :]), [zs], [zs])
        self.tt(g3, g3, zs.t[:, 1, :].unsqueeze(2).broadcast_to([P, 8, 16]), ALU.mult, [self.slot3, zs], [self.slot3])
        pm = self.psM
        for j in range(3):
            self.vop(lambda h, j=j: h.transpose(out=pm.t[:, j * 128:(j + 1) * 128], in_=self.slot3.t[:, j, :],
                                                identity=self.ident), [self.slot3, self.cst], [pm], eng="pe")
        self.cp(self.slotT.t[:, :, ts_ * 128:(ts_ + 1) * 128], pm.t[:, 0:384].rearrange("p (j t) -> p j t", j=3),
                [pm], [self.slotT])

    def scatter_sub(self, ts_):
        TG = 32
        WT = self.WT
        for g_ in range(128 // TG):
            t0 = ts_ * 128 + g_ * TG
            Bt = Tl(self.X1.t[:].bitcast(BF16).rearrange("p (t i) -> p t i", i=128)[:, 0:TG, :], "x"); Bt.b = self.X1.b
            At = Tl(self.X2.t[:].bitcast(BF16).rearrange("p (t i) -> p t i", i=128)[:, 0:TG, :], "x"); At.b = self.X2.b
            io = self.iota.unsqueeze(1).broadcast_to([P, TG, 128])
            i1v = self.slotT.t[:, 0, t0:t0 + TG].unsqueeze(2).broadcast_to([P, TG, 128])
            i2v = self.slotT.t[:, 1, t0:t0 + TG].unsqueeze(2).broadcast_to([P, TG, 128])
            gv = self.slotT.t[:, 2, t0:t0 + TG].unsqueeze(2).broadcast_to([P, TG, 128])
            self.tt(Bt.t, io, i2v, ALU.is_equal, [self.cst, self.slotT], [Bt])
            self.tt(At.t, io, i1v, ALU.is_equal, [self.cst, self.slotT], [At])
            self.tt(At.t, At.t, gv, ALU.mult, [At, self.slotT], [At], eng="pool")
            for q4 in range(TG // 4):
                pw = self.psS[(g_ * (TG // 4) + q4) % 4]
                for tt_ in range(4):
                    tl = q4 * 4 + tt_
                    self.mm(pw.t[:, tt_ * 128:(tt_ + 1) * 128], Bt.t[:, tl, :], At.t[:, tl, :], True, True,
                            [Bt, At], [pw])
                ta = t0 + q4 * 4
                src_ = pw.t[:].rearrange("p (t i) -> p i t", t=4)
                if q4 % 2:
                    self.act(WT.t[:, :, ta:ta + 4], src_, AF.Copy, [pw], [WT])
                else:
                    self.cp(WT.t[:, :, ta:ta + 4], src_, [pw], [WT])

    def sweep(self, layer):
        n = T
        PC = 4
        WT = self.WT
        udst, ubuf = self.ws[f"u{layer}"]
        vdst, vbuf = self.ws[f"v{layer}"]
        ust = Stream(self, self.wring, [(udst[c], ubuf) for c in range(NEXP_C)])
        vst = Stream(self, self.vring, [(vdst[c0:c0 + PC].rearrange("c p d -> p c d"), vbuf)
                                        for c0 in range(0, NEXP_C, PC)], queue="pool")
        for part in range(NEXP_C // PC):
            for cc in range(PC):
                c = part * PC + cc
                ut = ust.get()
                pu = self.psring.next()
                for k in range(KC):
                    self.mm(pu.t[:, 0:n], ut.t[:, k * 128:(k + 1) * 128], self.hT.t[:, k, 0:n], k == 0, k == KC - 1,
                            [ut, self.hT], [pu])
                ab = self.tmpb.next()
                self.act(ab.t[:, 0:n], pu.t[:, 0:n], AF.Gelu_apprx_tanh, [pu], [ab])
                self.tt(WT.t[:, c, :], WT.t[:, c, :], ab.t[:, 0:n], ALU.mult, [WT, ab], [WT])
            vt = vst.get()
            for m in range(16):
                pv = self.psring.next()
                for cc in range(PC):
                    c = part * PC + cc
                    self.mm(pv.t[:, 0:n], vt.t[:, cc, m * 128:(m + 1) * 128], WT.t[:, c, :], cc == 0, cc == PC - 1,
                            [vt, WT], [pv])
                self.stt(self.xres.t[:, m, 0:n], pv.t[:, 0:n], self.G2(layer, m), self.xres.t[:, m, 0:n],
                         ALU.mult, ALU.add, [pv, self.modt[layer], self.xres], [self.xres])

    def pass1(self, xsrc, base):
        S = self.S
        if not hasattr(self, "ldx1"):
            self.ldx1 = S.slot("ldx1")
        ldx = self.ldx1
        for sbi in range(NSB):
            self.dma(self.xres.t[:], xsrc[:, :, base + sbi * T:base + (sbi + 1) * T], ldx, writes=[self.xres])
            self.lru_block(self.xres, T, 64, lambda k: self.A1(0, k), lambda k: self.S1(0, k), False, sbi)
        self.S.barrier()
        self.chunk_carry()
        if self.mode == "p1":
            so = S.slot("st_carry")
            self.dma(self.carry_out, self.carry.t[:], so, reads=[self.carry])
        self.S.barrier()

    def exchange(self):
        S = self.S
        cc_in = self.dscr("cc_in", [P, 4 * KC], F32)
        cc_out = self.dscr("cc_out", [NCORE * P, 4 * KC], F32)
        bi, bo = Buf("cc_in"), Buf("cc_out")
        s1, s2, s3 = S.slot("cc1"), S.slot("cc2"), S.slot("cc3")
        self.dma(cc_in, self.carry.t[:, NSB, :, :].rearrange("p a b -> p (a b)"), s1, reads=[self.carry], writes=[bi])
        S.custom("pool", lambda h: h.collective_compute("AllGather", ALU.bypass, replica_groups=[list(range(NCORE))],
                                                        ins=[cc_in], outs=[cc_out]), s2, reads=[bi], writes=[bo])
        self.dma(self.call.t[:].rearrange("p r a b -> p r (a b)"), cc_out.rearrange("(r p) f -> p r f", p=P), s3,
                 reads=[bo], writes=[self.call])
        self.S.barrier()

    def chunk_carry(self):
        c = self.carry
        for d in range(2):
            self.tt(c.t[:, 0:NSB, 2 * d, :], c.t[:, 0:NSB, 2 * d, :],
                    self.dv.t[:, 9 + d, :].unsqueeze(1).broadcast_to([P, NSB, KC]), ALU.mult, [c, self.dv], [c])
            self.act(c.t[:, 0:NSB, 2 * d, :], c.t[:, 0:NSB, 2 * d, :], AF.Exp, [c], [c])
        self.cp(c.t[:, NSB, 0, :], c.t[:, 0, 0, :], [c], [c])
        self.cp(c.t[:, NSB, 1, :], c.t[:, 0, 1, :], [c], [c])
        for sb in range(1, NSB):
            self.tt(c.t[:, NSB, 1, :], c.t[:, NSB, 1, :], c.t[:, sb, 0, :], ALU.mult, [c], [c])
            self.tt(c.t[:, NSB, 1, :], c.t[:, NSB, 1, :], c.t[:, sb, 1, :], ALU.add, [c], [c])
            self.tt(c.t[:, NSB, 0, :], c.t[:, NSB, 0, :], c.t[:, sb, 0, :], ALU.mult, [c], [c])
        self.cp(c.t[:, NSB, 2, :], c.t[:, NSB - 1, 2, :], [c], [c])
        self.cp(c.t[:, NSB, 3, :], c.t[:, NSB - 1, 3, :], [c], [c])
        for sb in range(NSB - 2, -1, -1):
            self.tt(c.t[:, NSB, 3, :], c.t[:, NSB, 3, :], c.t[:, sb, 2, :], ALU.mult, [c], [c])
            self.tt(c.t[:, NSB, 3, :], c.t[:, NSB, 3, :], c.t[:, sb, 3, :], ALU.add, [c], [c])
            self.tt(c.t[:, NSB, 2, :], c.t[:, NSB, 2, :], c.t[:, sb, 2, :], ALU.mult, [c], [c])

    def ctx_and_carries(self):
        NC_ = NSB
        ldc = self.S.slot("ld_ctx")
        self.dma(self.ctxx.t, self.ctxT, ldc, writes=[self.ctxx])
        saved = self.carry
        ctxc = self.sb("ctxcarry", [P, 1, 4, KC])
        self.carry = ctxc
        self.lru_block(self.ctxx, 256, 256, lambda k: self.dv.t[:, 8, k:k + 1],
                       lambda k: self.modt[0].t[:, k, 1:2], False, 0)
        self.carry = saved
        hin = self.hin
        call = self.call
        cm = self.cmk
        sm = self.small
        self.cp(hin.t[:, 0, 0, :], ctxc.t[:, 0, 1, :], [ctxc], [hin])
        self.cp(hin.t[:, 1, NSB, :], ctxc.t[:, 0, 3, :], [ctxc], [hin])
        ns_ = self.nslots
        for d, order in ((0, range(ns_)), (1, range(ns_ - 1, -1, -1))):
            hsl = hin.t[:, 0, 0, :] if d == 0 else hin.t[:, 1, NSB, :]
            for cp_ in order:
                m = cm.t[:, d, cp_:cp_ + 1]
                self.ts(sm.t[:, 0, :], call.t[:, cp_, 2 * d, :], -1.0, m, ALU.add, ALU.mult, [call, cm], [sm])
                self.ts(sm.t[:, 0, :], sm.t[:, 0, :], 1.0, None, ALU.add, ALU.bypass, [sm], [sm])
                self.ts(sm.t[:, 1, :], call.t[:, cp_, 2 * d + 1, :], m, None, ALU.mult, ALU.bypass, [call, cm], [sm])
                self.tt(hsl, hsl, sm.t[:, 0, :], ALU.mult, [hin, sm], [hin])
                self.tt(hsl, hsl, sm.t[:, 1, :], ALU.add, [hin, sm], [hin])
        c = self.carry
        for sb in range(NSB):
            self.tt(hin.t[:, 0, sb + 1, :], hin.t[:, 0, sb, :], c.t[:, sb, 0, :], ALU.mult, [hin, c], [hin])
            self.tt(hin.t[:, 0, sb + 1, :], hin.t[:, 0, sb + 1, :], c.t[:, sb, 1, :], ALU.add, [hin, c], [hin])
        for sb in range(NSB - 1, -1, -1):
            self.tt(hin.t[:, 1, sb, :], hin.t[:, 1, sb + 1, :], c.t[:, sb, 2, :], ALU.mult, [hin, c], [hin])
            self.tt(hin.t[:, 1, sb, :], hin.t[:, 1, sb, :], c.t[:, sb, 3, :], ALU.add, [hin, c], [hin])
        self.S.barrier()

    def pass2(self):
        S = self.S
        ldx = S.slot("ldx")
        sto = S.slot("sto")
        for sbi in range(self.nsb_run):
            self.dma(self.xres.t[:], self.xT[:, :, sbi * T:(sbi + 1) * T], ldx, writes=[self.xres])
            self.lru_block(self.xres, T, 64, lambda k: self.A1(0, k), lambda k: self.S1(0, k), True, sbi,
                           hin_f=lambda h, sbi=sbi: self.hin.t[:, 0, sbi, h:h + 1],
                           hin_b=lambda h, sbi=sbi: self.hin.t[:, 1, sbi + 1, h:h + 1], want_carry=False)
            if self.dbg != "x1":
                self.peer_block(0)
            if self.dbg is None or self.dbg in ("x3", "x4"):
                self.sc_block(T)
            if self.dbg is None or self.dbg == "x4":
                self.peer_block(1)
            if self.dbg is None:
                self.norm_mod(self.xres, T, lambda k: self.V(4, k), lambda k: self.dv.t[:, 13, k:k + 1], self.outT)
                self.dma(self.yT[:, :, sbi * T:(sbi + 1) * T], self.outT.t[:], sto, reads=[self.outT])
            else:
                self.dma(self.yT[:, :, sbi * T:(sbi + 1) * T], self.xres.t[:], sto, reads=[self.xres])
        self.S.barrier()


def _fm(v):
    v = np.asarray(v, np.float32)
    return np.ascontiguousarray(np.moveaxis(v.reshape(v.shape[:-1] + (KC, P)), -1, 0))


def _consts():
    c = np.zeros((P, 384), np.float32)
    c[:, 0:128] = np.eye(P, dtype=np.float32)
    c[:, 128:256] = np.arange(128, dtype=np.float32)[None, :]
    c[:, 256:384] = 1.0
    return c


_NC_CACHE = {}


def _get_nc(mode, nsb_run=NSB, dbg=None):
    key = (mode, nsb_run, dbg)
    if key not in _NC_CACHE:
        _NC_CACHE[key] = KB(mode, nsb_run, dbg).build()
    return _NC_CACHE[key]


def _prep(inputs):
    f = lambda k: np.asarray(inputs[k], np.float32)
    x = f("x")
    shared = {}
    shared["consts"] = _consts()
    vec = np.zeros((NV, D), np.float32)
    vec[0] = f("norm_mix_g")[0]; vec[1] = f("norm_ffn_g")[0]
    vec[2] = f("norm_mix_g")[1]; vec[3] = f("norm_ffn_g")[1]
    vec[4] = f("norm_final_g")
    vec[5:9] = f("lru_conv_w")[0]; vec[9] = f("lru_conv_b")[0]
    vec[10:12] = f("lru_b_a")[0]; vec[12:14] = f("lru_b_x")[0]; vec[14:16] = f("lru_lambda")[0]
    vec[16:19] = f("sc_conv_w")[0]; vec[19] = f("sc_conv_b")[0]
    shared["vecs"] = _fm(vec)
    shared["w_mod"] = f("w_mod")
    shared["b_modT"] = np.ascontiguousarray(f("b_mod").reshape(2, 96, P).transpose(2, 0, 1))
    shared["lru_w_in"] = f("lru_w_in")[0]
    wab = np.stack([f("lru_w_a")[0], f("lru_w_x")[0]], axis=1)
    shared["wab"] = np.ascontiguousarray(wab.transpose(2, 3, 0, 1, 4)).reshape(16, P, 512)
    per_core = []
    for c in range(NCORE):
        b, j = divmod(c, 4)
        m = {}
        xs = x[b, j * NTOK:(j + 1) * NTOK]
        m["xT"] = np.ascontiguousarray(xs.reshape(NTOK, KC, P).transpose(2, 1, 0))
        m["cvec"] = np.ascontiguousarray(np.stack([_fm(f("c")[b]), _fm(f("c_ctx"))], axis=-1))
        per_core.append(m)
    return shared, per_core


def _prep2(inputs):
    f = lambda k: np.asarray(inputs[k], np.float32)
    sh = {}
    sh["lru_w_out"] = f("lru_w_out")[0]
    sh["sc_w_in"] = f("sc_w_in")[0]
    sh["sc_w_out"] = f("sc_w_out")[0]
    sh["peer_w_q"] = f("peer_w_q")
    sk = f("peer_sub_keys")
    sh["skT"] = np.ascontiguousarray(sk.reshape(2, 16, 128, 128).transpose(0, 3, 1, 2))
    u = f("peer_u")
    sh["u_l"] = np.ascontiguousarray(u.reshape(2, NEXP_C, P, KC, P).transpose(0, 1, 4, 3, 2)).reshape(2, NEXP_C, P, D)
    sh["v_l"] = f("peer_v").reshape(2, NEXP_C, P, D)
    ctx = f("ctx")
    pc = []
    for c in range(NCORE):
        b, j = divmod(c, 4)
        m = {}
        m["ctxT"] = np.ascontiguousarray(ctx[b].reshape(256, KC, P).transpose(2, 1, 0))
        cm = np.zeros((P, 2, NCORE), np.float32)
        for c2 in range(NCORE):
            b2, j2 = divmod(c2, 4)
            if b2 == b and j2 < j:
                cm[:, 0, c2] = 1.0
            if b2 == b and j2 > j:
                cm[:, 1, c2] = 1.0
        m["cmask"] = cm
        pc.append(m)
    return sh, pc


def _xT_chunk(x, b, j):
    xs = x[b, j * NTOK:(j + 1) * NTOK]
    return xs.reshape(NTOK, KC, P).transpose(2, 1, 0)


def kernel(**inputs):
    shared, pc = _prep(inputs)
    sh2, pc2 = _prep2(inputs)
    x = np.asarray(inputs["x"], np.float32)
    in_maps = []
    for c in range(NCORE):
        b, j = divmod(c, 4)
        others = [jj for jj in range(4) if jj != j]
        xTo = np.ascontiguousarray(np.concatenate([_xT_chunk(x, b, jj) for jj in others], axis=2))
        cm = np.zeros((P, 2, 3), np.float32)
        for s_, jj in enumerate(others):
            cm[:, 0, s_] = 1.0 if jj < j else 0.0
            cm[:, 1, s_] = 1.0 if jj > j else 0.0
        m = {**shared, **pc[c], **sh2, **pc2[c], "xTo": xTo, "cmask": cm}
        in_maps.append(m)
    nc = _get_nc("f")
    r = run_bass_kernel_spmd(nc, in_maps, core_ids=list(range(NCORE)))
    out = np.zeros((2, 4 * NTOK, D), np.float32)
    for c in range(NCORE):
        b, j = divmod(c, 4)
        yT = np.asarray(r.results[c]["yT"], np.float32)
        out[b, j * NTOK:(j + 1) * NTOK] = yT.transpose(2, 1, 0).reshape(NTOK, D)
    return out
```

```python
import math
import numpy as np
from contextlib import ExitStack
import concourse.bass as bass
import concourse.mybir as mybir
from concourse.bass_utils import run_bass_kernel_spmd

F32 = mybir.dt.float32
BF16 = mybir.dt.bfloat16
U32 = mybir.dt.uint32
ALU = mybir.AluOpType
AF = mybir.ActivationFunctionType
AX = mybir.AxisListType

P = 128
D = 2048
KC = 16
T = 256
NTOK = 4096
NSB = NTOK // T
NCORE = 8
NEXP_C = 128
EPS = 1e-6
NV = 20
NEG = -1.0e30


class Buf:
    __slots__ = ("name", "writers", "readers")

    def __init__(self, name=""):
        self.name = name
        self.writers = {}
        self.readers = []


class Eng:
    def __init__(self, name, sem, self_sync):
        self.name = name
        self.sem = sem
        self.count = 0
        self.known = {}
        self.self_sync = self_sync
        self.prog = []


class Sched:
    def __init__(self, nc, stack):
        self.nc = nc
        self.stack = stack
        self.engs = {}
        self.slots = []
        for name, ss in (("pe", False), ("act", True), ("dve", True), ("pool", True), ("sp", False)):
            self.engs[name] = Eng(name, self.new_sem("e_" + name), ss)
        self.ninst = 0

    def new_sem(self, name):
        return self.stack.enter_context(self.nc.semaphore(name))

    def _waits(self, E, reads, writes):
        need = {}

        def add(tok):
            sem, val, en = tok
            if en == E.name and not E.self_sync:
                return
            k = id(sem)
            if k not in need or need[k][1] < val:
                need[k] = (sem, val)
        for b in reads:
            for tok in b.writers.values():
                add(tok)
        for b in writes:
            for tok in b.writers.values():
                add(tok)
            for tok in b.readers:
                add(tok)
        for k, (sem, val) in need.items():
            if E.known.get(k, 0) < val:
                E.prog.append(("w", sem, val))
                E.known[k] = val

    def _mark(self, tok, key, reads, writes):
        for b in reads:
            b.readers.append(tok)
        for b in writes:
            b.writers[key] = tok
            b.readers = []

    def op(self, ename, build, reads=(), writes=()):
        E = self.engs[ename]
        self._waits(E, reads, writes)
        E.count += 1
        E.prog.append(("o", build, E.sem, 1))
        self._mark((E.sem, E.count, E.name), E.name, reads, writes)
        self.ninst += 1

    def slot(self, name):
        s = {"sem": self.new_sem(name), "n": 0, "name": name}
        self.slots.append(s)
        return s

    def dma(self, qname, out, in_, slot, reads=(), writes=()):
        E = self.engs[qname]
        self._waits(E, reads, writes)
        E.prog.append(("o", (lambda h: h.dma_start(out=out, in_=in_)), slot["sem"], 16))
        slot["n"] += 1
        self._mark((slot["sem"], 16 * slot["n"], "dma"), "dma_" + slot["name"], reads, writes)
        self.ninst += 1

    def custom(self, qname, build, slot, reads=(), writes=(), inc=16):
        E = self.engs[qname]
        self._waits(E, reads, writes)
        E.prog.append(("o", build, slot["sem"], inc))
        slot["n"] += 1
        self._mark((slot["sem"], inc * slot["n"], "dma"), "dma_" + slot["name"], reads, writes)
        self.ninst += 1

    def barrier(self):
        for E in self.engs.values():
            for Fe in self.engs.values():
                if Fe is E or Fe.count == 0:
                    continue
                k = id(Fe.sem)
                if E.known.get(k, 0) < Fe.count:
                    E.prog.append(("w", Fe.sem, Fe.count))
                    E.known[k] = Fe.count
            for s in self.slots:
                k = id(s["sem"])
                if s["n"] and E.known.get(k, 0) < 16 * s["n"]:
                    E.prog.append(("w", s["sem"], 16 * s["n"]))
                    E.known[k] = 16 * s["n"]

    def emit(self):
        with self.nc.Block() as block:
            def run(E):
                def f(h):
                    for it in E.prog:
                        if it[0] == "w":
                            h.wait_ge(it[1], it[2])
                        else:
                            it[1](h).then_inc(it[2], it[3])
                return f
            block.tensor(run(self.engs["pe"]))
            block.scalar(run(self.engs["act"]))
            block.vector(run(self.engs["dve"]))
            block.gpsimd(run(self.engs["pool"]))
            block.sync(run(self.engs["sp"]))


class Tl:
    def __init__(self, t, name):
        self.t = t
        self.b = Buf(name)


class Ring:
    def __init__(self, tiles):
        self.tiles = tiles
        self.i = 0

    def next(self):
        t = self.tiles[self.i % len(self.tiles)]
        self.i += 1
        return t


class Stream:
    def __init__(self, kb, ring, srcs, queue="sp"):
        self.kb = kb
        self.ring = ring
        self.srcs = srcs
        self.R = len(ring)
        for tl in ring:
            if not hasattr(tl, "slot"):
                tl.slot = kb.S.slot("r_" + tl.b.name)
        self.issued = 0
        self.taken = 0
        self.queue = queue

    def _issue(self):
        n = self.issued
        tl = self.ring[n % self.R]
        src, sbuf = self.srcs[n]
        self.kb.S.dma(self.queue, tl.t[:], src, tl.slot,
                      reads=[sbuf] if sbuf is not None else [], writes=[tl.b])
        self.issued += 1

    def get(self):
        while self.issued < min(self.taken + self.R, len(self.srcs)):
            self._issue()
        tl = self.ring[self.taken % self.R]
        self.taken += 1
        return tl


class KB:
    def __init__(self, mode, nsb_run=NSB, dbg=None):
        self.mode = mode
        self.nsb_run = nsb_run
        self.dbg = dbg
        self.nc = bass.Bass("TRN2", target_bir_lowering=False)
        self.st = ExitStack()
        full = ["w_in", "wab", "w_out", "sc_w_in", "sc_w_out", "wq0", "wq1", "u0", "v0", "u1", "v1"]
        sub = {"s_setup": [], "x1": full[:3], "s_ctx": full[:3], "x2": full[:3] + ["wq0", "u0", "v0"],
               "x3": full[:5] + ["wq0", "u0", "v0"]}
        self.wlist = ["w_in_xb", "wab"] if mode == "p1" else sub.get(dbg, full)

    def din(self, name, shape, dt=F32):
        return self.nc.dram_tensor(name, list(shape), dt, kind="ExternalInput").ap()

    def dout(self, name, shape, dt=F32):
        return self.nc.dram_tensor(name, list(shape), dt, kind="ExternalOutput").ap()

    def dscr(self, name, shape, dt=BF16):
        return self.nc.dram_tensor(name, list(shape), dt, kind="Internal").ap()

    def sb(self, name, shape, dt=F32):
        return Tl(self.st.enter_context(self.nc.sbuf_tensor(name, list(shape), dt)), name)

    def ps(self, name, shape, dt=F32):
        return Tl(self.st.enter_context(self.nc.psum_tensor(name, list(shape), dt)), name)

    def act(self, out, in_, func, reads, writes, scale=1.0, bias=0.0, eng="act"):
        self.S.op(eng, lambda h: h.activation(out=out, in_=in_, func=func, scale=scale, bias=bias),
                  [r.b for r in reads], [w.b for w in writes])

    def tt(self, out, in0, in1, op, reads, writes, eng="dve"):
        self.S.op(eng, lambda h: h.tensor_tensor(out=out, in0=in0, in1=in1, op=op),
                  [r.b for r in reads], [w.b for w in writes])

    def ts(self, out, in0, s1, s2, op0, op1, reads, writes, eng="dve"):
        self.S.op(eng, lambda h: h.tensor_scalar(out=out, in0=in0, scalar1=s1, scalar2=s2, op0=op0, op1=op1),
                  [r.b for r in reads], [w.b for w in writes])

    def stt(self, out, in0, scalar, in1, op0, op1, reads, writes):
        self.S.op("dve", lambda h: h.scalar_tensor_tensor(out=out, in0=in0, scalar=scalar, in1=in1, op0=op0, op1=op1),
                  [r.b for r in reads], [w.b for w in writes])

    def cp(self, out, in_, reads, writes, eng="dve"):
        self.S.op(eng, lambda h: h.tensor_copy(out=out, in_=in_),
                  [r.b for r in reads], [w.b for w in writes])

    def mm(self, out, lhsT, rhs, start, stop, reads, writes):
        self.S.op("pe", lambda h: h.matmul(out, lhsT=lhsT, rhs=rhs, start=start, stop=stop),
                  [r.b for r in reads], [w.b for w in writes])

    def vop(self, fn, reads, writes, eng="dve"):
        self.S.op(eng, fn, [r.b for r in reads], [w.b for w in writes])

    def dma(self, out, in_, slot, reads=(), writes=(), q="sp"):
        self.S.dma(q, out, in_, slot, [r if isinstance(r, Buf) else r.b for r in reads],
                   [w if isinstance(w, Buf) else w.b for w in writes])

    def build(self):
        nc = self.nc
        mode = self.mode
        with self.st:
            self.S = Sched(nc, self.st)
            self.declare_io()
            self.alloc()
            self.setup()
            if mode == "p1":
                self.precast(["w_in_xb", "wab"])
                self.precast_finish()
                self.S.barrier()
                self.pass1(self.xT, 0)
            elif mode == "f":
                self.precast(self.wlist)
                self.precast_run(36, bg=False)
                self.S.barrier()
                self.bg_per_block = -(-(len(self.pc_jobs) - 36) // (4 * NSB))
                for s_ in range(3):
                    self.pass1(self.xTo, s_ * NTOK)
                    self.cp(self.call.t[:, s_, :, :], self.carry.t[:, NSB, :, :], [self.carry], [self.call])
                    self.S.barrier()
                self.pass1(self.xT, 0)
                self.precast_finish()
                self.S.barrier()
                self.ctx_and_carries()
                self.pass2()
            else:
                self.precast(self.wlist)
                self.precast_finish()
                self.S.barrier()
                if self.dbg not in ("s_setup", "s_precast"):
                    self.ctx_and_carries()
                    if self.dbg != "s_ctx":
                        self.pass2()
            self.S.barrier()
            self.S.emit()
        return nc

    def declare_io(self):
        self.xT = self.din("xT", [P, KC, NTOK])
        self.consts = self.din("consts", [P, 3 * 128])
        self.vecs = self.din("vecs", [P, NV, KC])
        self.cvec = self.din("cvec", [P, KC, 2])
        self.w_mod = self.din("w_mod", [2, D, 6 * D])
        self.b_modT = self.din("b_modT", [P, 2, 96])
        self.lru_w_in = self.din("lru_w_in", [D, 2 * D])
        self.wab_in = self.din("wab", [16, P, 512])
        if self.mode == "p1":
            self.carry_out = self.dout("carry", [P, NSB + 1, 4, KC])
        else:
            self.ctxT = self.din("ctxT", [P, KC, 256])
            if self.mode == "f":
                self.xTo = self.din("xTo", [P, KC, 3 * NTOK])
            if self.mode == "p2":
                self.carry_own = self.din("carry_own", [P, NSB + 1, 4, KC])
                self.carry_all = self.din("carry_all", [P, NCORE, 4, KC])
            self.nslots = 3 if self.mode == "f" else NCORE
            self.cmask = self.din("cmask", [P, 2, self.nslots])
            wl = self.wlist
            if "w_out" in wl:
                self.lru_w_out = self.din("lru_w_out", [D, D])
            if "sc_w_in" in wl:
                self.sc_w_in = self.din("sc_w_in", [D, 3 * D])
                self.sc_w_out = self.din("sc_w_out", [D, D])
            self.has_peer = "wq0" in wl
            if self.has_peer:
                self.peer_w_q = self.din("peer_w_q", [2, D, D])
                self.skT_in = self.din("skT", [2, P, 16, 128])
                self.u_l = self.din("u_l", [2, NEXP_C, P, D])
                self.v_l = self.din("v_l", [2, NEXP_C, P, D])
            self.yT = self.dout("yT", [P, KC, NTOK])
        self.ws = {}
        self.ws["w_in"] = (self.dscr("ws_w_in", [32, P, D]), Buf("ws_w_in"))
        self.ws["wab"] = (self.dscr("ws_wab", [16, P, 512]), Buf("ws_wab"))
        if self.mode != "p1":
            self.ws["w_out"] = (self.dscr("ws_w_out", [16, P, D]), Buf("ws_w_out"))
            self.ws["sc_w_in"] = (self.dscr("ws_sc_w_in", [48, P, D]), Buf("ws_sc_w_in"))
            self.ws["sc_w_out"] = (self.dscr("ws_sc_w_out", [16, P, D]), Buf("ws_sc_w_out"))
            for i in range(2):
                self.ws[f"wq{i}"] = (self.dscr(f"ws_wq{i}", [16, P, D]), Buf(f"ws_wq{i}"))
                self.ws[f"u{i}"] = (self.dscr(f"ws_u{i}", [NEXP_C, P, D]), Buf(f"ws_u{i}"))
                self.ws[f"v{i}"] = (self.dscr(f"ws_v{i}", [NEXP_C, P, D]), Buf(f"ws_v{i}"))

    def alloc(self):
        sb, ps = self.sb, self.ps
        self.cst = sb("cst", [P, 3 * 128])
        self.ident = self.cst.t[:, 0:128]
        self.iota = self.cst.t[:, 128:256]
        self.ones_bf = sb("ones_bf", [P, 128], BF16)
        self.iota_bf = sb("iota_bf", [P, 128], BF16)
        self.vec = sb("vec", [P, NV, KC])
        self.cv = sb("cv", [P, KC, 2])
        self.modt = [sb(f"modt{i}", [P, 96, 2]) for i in range(2)]
        self.bmod = sb("bmod", [P, 2, 96])
        self.dv = sb("dv", [P, 24, KC])
        self.wabring = [sb(f"wabr{i}", [P, 4, 128], BF16) for i in range(2)]
        self.xres = sb("xres", [P, KC, T])
        self.hT = sb("hT", [P, KC, T], BF16)
        self.bfA = sb("bfA", [P, KC, T], BF16)
        self.rstd = sb("rstd", [P, T])
        self.tmpr = Ring([sb(f"tmp{i}", [P, T]) for i in range(11)])
        self.tmpb = Ring([sb(f"tmpb{i}", [P, T], BF16) for i in range(4)])
        self.arena = sb("arena", [P, 16384])
        self.wring = [sb(f"wr{i}", [P, D], BF16) for i in range(3)]
        self.carry = sb("carryt", [P, NSB + 1, 4, KC])
        self.hin = sb("hin", [P, 2, NSB + 1, KC])
        self.small = sb("small", [P, 8, KC])
        if self.mode != "p1":
            self.vring = [sb(f"vr{i}", [P, 4, D], BF16) for i in range(2)]
            self.skT = [sb(f"skT{i}", [P, 16, 128], BF16) for i in range(2)]
            self.X1 = sb("X1", [P, 2048])
            self.X2 = sb("X2", [P, 2048])
            self.sv = sb("sv", [P, 16, 16])
            self.si_u = sb("si_u", [P, 16, 16], U32)
            self.si_f = sb("si_f", [P, 16, 16])
            self.cvv = sb("cvv", [P, 8, 16])
            self.ci_u = sb("ci_u", [P, 8, 16], U32)
            self.ca_u = sb("ca_u", [P, 8, 16], U32)
            self.cb_u = sb("cb_u", [P, 8, 16], U32)
            self.caf = sb("caf", [P, 8, 16])
            self.cbf = sb("cbf", [P, 8, 16])
            self.slot3 = sb("slot3", [P, 3, 128])
            self.slotT = sb("slotT", [P, 3, T])
            self.zs = sb("zs", [P, 4, 8])
            self.cmk = sb("cmk", [P, 2, self.nslots])
            self.call = sb("call", [P, NCORE, 4, KC])
            self.WT = Tl(self.arena.t[:].bitcast(BF16).rearrange("p (c t) -> p c t", t=T), "WT")
            self.WT.b = self.arena.b
            self.ctxx = Tl(self.arena.t[:, 0:KC * 256].rearrange("p (k t) -> p k t", k=KC), "ctxx")
            self.ctxx.b = self.arena.b
            self.outT = Tl(self.arena.t[:, 0:KC * T].rearrange("p (k t) -> p k t", k=KC), "outT")
            self.outT.b = self.arena.b
        pa = [ps(f"psA{i}", [P, 512]) for i in range(3)]
        pst = []
        for half in range(2):
            for bnk in range(3):
                t_ = Tl(pa[bnk].t[:, half * 256:half * 256 + 256], f"psh{bnk}_{half}")
                t_.b = pa[bnk].b
                pst.append(t_)
        self.psring = Ring(pst)
        self.psS = [ps(f"psS{i}", [P, 512]) for i in range(4)]
        self.psM = ps("psM", [P, 512])

    def setup(self):
        S = self.S
        self._nld = 0

        def ld1(o, i, tl):
            self._nld += 1
            self.dma(o, i, S.slot(f"ld1_{self._nld}"), writes=[tl])
        self.ld1 = ld1
        ld1(self.cst.t[:], self.consts, self.cst)
        ld1(self.vec.t[:], self.vecs, self.vec)
        ld1(self.cv.t[:], self.cvec, self.cv)
        ld1(self.bmod.t[:], self.b_modT, self.bmod)
        self.cp(self.ones_bf.t[:], self.cst.t[:, 256:384], [self.cst], [self.ones_bf])
        self.cp(self.iota_bf.t[:], self.cst.t[:, 128:256], [self.cst], [self.iota_bf])
        self.act(self.cv.t[:], self.cv.t[:], AF.Silu, [self.cv], [self.cv])
        nlay = 1 if self.mode == "p1" else 2
        NB = 256
        wmr = [Tl(self.arena.t[:, i * 4096:(i + 1) * 4096].rearrange("p (k n) -> p k n", k=KC), f"wm{i}")
               for i in range(3)]
        wslots = [S.slot(f"wm{i}") for i in range(3)]
        cnt = 0
        for i in range(nlay):
            nblk = (3 * D // NB) if self.mode == "p1" else (6 * D // NB)
            for nb in range(nblk):
                wt = wmr[cnt % 3]
                src = self.w_mod[i, :, nb * NB:(nb + 1) * NB].rearrange("(k p) n -> p k n", p=P)
                self.dma(wt.t, src, wslots[cnt % 3], writes=[wt])
                cnt += 1
                for mi in range(NB // 128):
                    m = nb * (NB // 128) + mi
                    for k in range(KC):
                        self.mm(self.psM.t[:, 2 * m:2 * m + 2], wt.t[:, k, mi * 128:(mi + 1) * 128],
                                self.cv.t[:, k, :], k == 0, k == KC - 1, [wt, self.cv], [self.psM])
            nm = nblk * NB // 128
            self.tt(self.modt[i].t[:, 0:nm, :], self.psM.t[:, 0:2 * nm].rearrange("p (m c) -> p m c", c=2),
                    self.bmod.t[:, i, 0:nm].unsqueeze(2).broadcast_to([P, nm, 2]), ALU.add,
                    [self.psM, self.bmod], [self.modt[i]])
        dv = self.dv
        for i in range(nlay):
            self.stt(dv.t[:, 2 * i + 0, :], self.modt[i].t[:, 16:32, 0], 1.0, self.vec.t[:, 2 * i + 0, :],
                     ALU.add, ALU.mult, [self.modt[i], self.vec], [dv])
            if self.mode != "p1":
                self.stt(dv.t[:, 2 * i + 1, :], self.modt[i].t[:, 64:80, 0], 1.0, self.vec.t[:, 2 * i + 1, :],
                         ALU.add, ALU.mult, [self.modt[i], self.vec], [dv])
        self.stt(dv.t[:, 8, :], self.modt[0].t[:, 16:32, 1], 1.0, self.vec.t[:, 0, :],
                 ALU.add, ALU.mult, [self.modt[0], self.vec], [dv])
        self.act(dv.t[:, 9:11, :], self.vec.t[:, 14:16, :], AF.Exp, [self.vec], [dv], scale=-1.0)
        self.act(dv.t[:, 9:11, :], dv.t[:, 9:11, :], AF.Ln, [dv], [dv], bias=1.0)
        self.ts(dv.t[:, 9:11, :], dv.t[:, 9:11, :], -8.0, None, ALU.mult, ALU.bypass, [dv], [dv])
        self.ts(dv.t[:, 11:13, :], dv.t[:, 9:11, :], 2.0, None, ALU.mult, ALU.bypass, [dv], [dv])
        self.ts(dv.t[:, 14:16, :], self.vec.t[:, 10:12, :], 0.5, None, ALU.mult, ALU.bypass, [self.vec], [dv])
        self.ts(dv.t[:, 16:18, :], self.vec.t[:, 12:14, :], 0.5, None, ALU.mult, ALU.bypass, [self.vec], [dv])
        self.ts(dv.t[:, 18:20, :], dv.t[:, 9:11, :], 0.5, None, ALU.mult, ALU.bypass, [dv], [dv])
        self.ts(dv.t[:, 20:22, :], dv.t[:, 9:11, :], 0.5 * T, None, ALU.mult, ALU.bypass, [dv], [dv])
        self.vop(lambda h: h.memset(dv.t[:, 13, :], 0.0), [], [dv], eng="pool")
        if self.mode != "p1":
            ld1(self.cmk.t[:], self.cmask, self.cmk)
            if self.mode == "p2":
                ld1(self.call.t[:], self.carry_all, self.call)
                ld1(self.carry.t[:], self.carry_own, self.carry)
        self.S.barrier()

    def A1(self, i, k): return self.dv.t[:, 2 * i, k:k + 1]
    def A2(self, i, k): return self.dv.t[:, 2 * i + 1, k:k + 1]
    def S1(self, i, k): return self.modt[i].t[:, 0 + k, 0:1]
    def G1(self, i, k): return self.modt[i].t[:, 32 + k, 0:1]
    def S2(self, i, k): return self.modt[i].t[:, 48 + k, 0:1]
    def G2(self, i, k): return self.modt[i].t[:, 80 + k, 0:1]
    def V(self, idx, k): return self.vec.t[:, idx, k:k + 1]

    def precast(self, which):
        S = self.S
        NST = 3
        fin = [Tl(self.arena.t[:, i * 2048:(i + 1) * 2048], f"pcin{i}") for i in range(NST)]
        fob = [Tl(self.arena.t[:, 6144 + i * 1024:6144 + (i + 1) * 1024].bitcast(BF16), f"pcout{i}")
               for i in range(NST)]
        sl_in = [S.slot(f"pci{i}") for i in range(NST)]
        sl_out = [S.slot(f"pco{i}") for i in range(NST)]
        engs = ["dve", "act", "pool"]
        jobs = []

        def add_W(W, key, cc_lo, cc_hi, base=0):
            dst, dbuf = self.ws[key]
            for cc in range(cc_lo, cc_hi):
                src = W[:, cc * 128:(cc + 1) * 128].rearrange("(k p) n -> p k n", p=P)
                jobs.append((src, dst[cc - base], dbuf, 16))

        def add_rows(R, key):
            dst, dbuf = self.ws[key]
            for c in range(NEXP_C):
                jobs.append((R[c], dst[c], dbuf, 0))
        for w in which:
            if w == "w_in_xb":
                add_W(self.lru_w_in, "w_in", 16, 32)
            elif w == "w_in":
                add_W(self.lru_w_in, "w_in", 0, 32)
            elif w == "wab":
                dst, dbuf = self.ws["wab"]
                for j in range(4):
                    jobs.append((self.wab_in[4 * j:4 * j + 4].rearrange("h p f -> p h f"),
                                 dst[4 * j:4 * j + 4].rearrange("h p f -> p h f"), dbuf, 4))
            elif w == "w_out":
                add_W(self.lru_w_out, "w_out", 0, 16)
            elif w == "sc_w_in":
                add_W(self.sc_w_in, "sc_w_in", 0, 48)
            elif w == "sc_w_out":
                add_W(self.sc_w_out, "sc_w_out", 0, 16)
            elif w in ("wq0", "wq1"):
                add_W(self.peer_w_q[int(w[2])], w, 0, 16)
            elif w in ("u0", "u1"):
                add_rows(self.u_l[int(w[1])], w)
            elif w in ("v0", "v1"):
                add_rows(self.v_l[int(w[1])], w)
        self.pc_jobs = jobs
        self.pc_n = 0
        self.pc_in = 0
        self.pc_st = (fin, fob, sl_in, sl_out, NST)
        self.pc_st_bg = ([S.slot(f"pcib{i}") for i in range(NST)], [S.slot(f"pcob{i}") for i in range(NST)])

    def precast_run(self, k, bg):
        fin, fob, sl_in, sl_out, NST = self.pc_st
        if bg:
            sl_in, sl_out = self.pc_st_bg
        jobs = self.pc_jobs
        engs = ["pool"] if bg else ["dve", "act", "pool"]
        q = "pool" if bg else "sp"
        stop = min(self.pc_n + k, len(jobs))
        while self.pc_n < stop:
            n = self.pc_n
            while self.pc_in < min(n + NST, len(jobs)):
                m = self.pc_in
                src, dst, dbuf, is3 = jobs[m]
                i = m % NST
                o = fin[i].t.rearrange("p (k n) -> p k n", k=is3) if is3 else fin[i].t
                self.dma(o, src, sl_in[i], writes=[fin[i]], q=q)
                self.pc_in += 1
            src, dst, dbuf, is3 = jobs[n]
            i = n % NST
            e = engs[n % len(engs)]
            if e == "act":
                self.act(fob[i].t, fin[i].t, AF.Copy, [fin[i]], [fob[i]])
            else:
                self.cp(fob[i].t, fin[i].t, [fin[i]], [fob[i]], eng=e)
            oo = fob[i].t.rearrange("p (k n) -> p k n", k=4) if is3 == 4 else fob[i].t
            self.dma(dst, oo, sl_out[i], reads=[fob[i]], writes=[dbuf], q=q)
            self.pc_n += 1

    def precast_finish(self):
        self.precast_run(len(self.pc_jobs), bg=False)
        self.S.barrier()
        S = self.S
        if self.mode != "p1" and self.has_peer:
            for i in range(2):
                st = Tl(self.arena.t[:, i * 2048:(i + 1) * 2048], f"sks{i}")
                self.ld1(st.t, self.skT_in[i].rearrange("p a b -> p (a b)"), st)
                self.cp(self.skT[i].t[:].rearrange("p a b -> p (a b)"), st.t, [st], [self.skT[i]])

    def wstream(self, key, ccs):
        dst, dbuf = self.ws[key]
        return Stream(self, self.wring, [(dst[cc], dbuf) for cc in ccs])

    def norm_mod(self, xsrc, n, Afn, Sfn, out_tl, out_is_f32=False):
        sq = self.bfA
        self.act(sq.t[:, :, 0:n], xsrc.t[:, :, 0:n], AF.Square, [xsrc], [sq])
        pm = self.psM
        for k in range(KC):
            self.mm(pm.t[:, 0:n], self.ones_bf.t[:], sq.t[:, k, 0:n], k == 0, k == KC - 1,
                    [self.ones_bf, sq], [pm])
        r = self.rstd
        self.act(r.t[:, 0:n], pm.t[:, 0:n], AF.Ln, [pm], [r], scale=1.0 / D, bias=EPS)
        self.act(r.t[:, 0:n], r.t[:, 0:n], AF.Exp, [r], [r], scale=-0.5)
        for k in range(KC):
            tm = self.tmpr.next()
            self.tt(tm.t[:, 0:n], xsrc.t[:, k, 0:n], r.t[:, 0:n], ALU.mult, [xsrc, r], [tm])
            self.act(out_tl.t[:, k, 0:n], tm.t[:, 0:n], AF.Identity, [tm], [out_tl], scale=Afn(k), bias=Sfn(k))

    def lru_block(self, xsrc, n, rowlen, Afn, Sfn, pass2, sbi, hin_f=None, hin_b=None, want_carry=True):
        self.norm_mod(xsrc, n, Afn, Sfn, self.hT)
        ccs = (list(range(16)) if pass2 else []) + list(range(16, 32))
        wsr = self.wstream("w_in", ccs)
        wabs = Stream(self, self.wabring, [(self.ws["wab"][0][h].rearrange("p (a b) -> p a b", a=4), self.ws["wab"][1])
                                           for h in range(16)])
        YT = self.bfA
        if pass2:
            for h in range(16):
                wt = wsr.get()
                pg = self.psring.next()
                for k in range(KC):
                    self.mm(pg.t[:, 0:n], wt.t[:, k * 128:(k + 1) * 128], self.hT.t[:, k, 0:n], k == 0, k == KC - 1,
                            [wt, self.hT], [pg])
                self.act(YT.t[:, h, 0:n], pg.t[:, 0:n], AF.Gelu_apprx_tanh, [pg], [YT])
        LN_HALF = math.log(0.5)
        for h in range(16):
            wabt = wabs.get()
            wt = wsr.get()
            px = self.psring.next()
            for k in range(KC):
                self.mm(px.t[:, 0:n], wt.t[:, k * 128:(k + 1) * 128], self.hT.t[:, k, 0:n], k == 0, k == KC - 1,
                        [wt, self.hT], [px])
            xc = self.tmpr.next()
            self.act(xc.t[:, 0:n], px.t[:, 0:n], AF.Identity, [px], [xc], scale=self.V(7, h), bias=self.V(9, h))
            pv = px.t[:, 0:n].rearrange("p (r c) -> p r c", c=rowlen)
            xv = xc.t[:, 0:n].rearrange("p (r c) -> p r c", c=rowlen)
            for tap, off in ((6, -1), (5, -2), (8, 1)):
                if off < 0:
                    o_, i_ = xv[:, :, -off:], pv[:, :, :rowlen + off]
                else:
                    o_, i_ = xv[:, :, :rowlen - off], pv[:, :, off:]
                self.stt(o_, i_, self.V(tap, h), o_, ALU.mult, ALU.add, [px, xc], [xc])
            xcb = self.tmpb.next()
            self.act(xcb.t[:, 0:n], xc.t[:, 0:n], AF.Copy, [xc], [xcb])
            hs = []
            for d in range(2):
                pr = self.psring.next()
                self.mm(pr.t[:, 0:n], wabt.t[:, d * 2 + 0, :], xcb.t[:, 0:n], True, True, [wabt, xcb], [pr])
                pi = self.psring.next()
                self.mm(pi.t[:, 0:n], wabt.t[:, d * 2 + 1, :], xcb.t[:, 0:n], True, True, [wabt, xcb], [pi])
                rr = self.tmpr.next()
                self.act(rr.t[:, 0:n], pr.t[:, 0:n], AF.Tanh, [pr], [rr], scale=0.5, bias=self.dv.t[:, 14 + d, h:h + 1])
                ii = self.tmpr.next()
                self.act(ii.t[:, 0:n], pi.t[:, 0:n], AF.Tanh, [pi], [ii], scale=0.5, bias=self.dv.t[:, 16 + d, h:h + 1])
                aa = self.tmpr.next()
                self.act(aa.t[:, 0:n], rr.t[:, 0:n], AF.Exp, [rr], [aa], scale=self.dv.t[:, 18 + d, h:h + 1],
                         bias=self.dv.t[:, 18 + d, h:h + 1])
                a2 = self.tmpr.next()
                self.act(a2.t[:, 0:n], rr.t[:, 0:n], AF.Exp, [rr], [a2], scale=self.dv.t[:, 9 + d, h:h + 1],
                         bias=self.dv.t[:, 9 + d, h:h + 1])
                self.act(a2.t[:, 0:n], a2.t[:, 0:n], AF.Ln, [a2], [a2], scale=-1.0, bias=1.0)
                self.act(a2.t[:, 0:n], a2.t[:, 0:n], AF.Exp, [a2], [a2], scale=0.5, bias=LN_HALF)
                self.stt(ii.t[:, 0:n], ii.t[:, 0:n], 1.0, xc.t[:, 0:n], ALU.add, ALU.mult, [ii, xc], [ii])
                self.tt(ii.t[:, 0:n], ii.t[:, 0:n], a2.t[:, 0:n], ALU.mult, [ii, a2], [ii])
                first = 0 if d == 0 else n - 1
                hinp = hin_f if d == 0 else hin_b
                if hinp is not None:
                    self.stt(ii.t[:, first:first + 1], aa.t[:, first:first + 1], hinp(h), ii.t[:, first:first + 1],
                             ALU.mult, ALU.add, [aa, ii, self.hin], [ii])
                hh = self.tmpr.next()
                if d == 0:
                    o_, a_, b_ = hh.t[:, 0:n], aa.t[:, 0:n], ii.t[:, 0:n]
                else:
                    o_, a_, b_ = hh.t[:, 0:n][:, ::-1], aa.t[:, 0:n][:, ::-1], ii.t[:, 0:n][:, ::-1]
                self.vop(lambda hd, o_=o_, a_=a_, b_=b_: hd.tensor_tensor_scan(out=o_, data0=a_, data1=b_, initial=0.0,
                                                                           op0=ALU.mult, op1=ALU.add),
                         [aa, ii], [hh])
                if want_carry:
                    last = n - 1 if d == 0 else 0
                    ctl = self.carry
                    self.vop(lambda hd, rr=rr, d=d, h=h, ctl=ctl: hd.tensor_reduce(out=ctl.t[:, sbi, 2 * d, h:h + 1],
                                                                                 in_=rr.t[:, 0:n], axis=AX.X, op=ALU.add),
                             [rr], [ctl])
                    self.cp(ctl.t[:, sbi, 2 * d + 1, h:h + 1], hh.t[:, last:last + 1], [hh], [ctl])
                hs.append(hh)
            if pass2:
                self.tt(hs[0].t[:, 0:n], hs[0].t[:, 0:n], hs[1].t[:, 0:n], ALU.add, [hs[0], hs[1]], [hs[0]])
                self.tt(YT.t[:, h, 0:n], YT.t[:, h, 0:n], hs[0].t[:, 0:n], ALU.mult, [hs[0], YT], [YT])
        if pass2:
            self.out_proj("w_out", YT, n, 0)

    def out_proj(self, key, YT, n, layer):
        wsr = self.wstream(key, list(range(16)))
        for m in range(16):
            wt = wsr.get()
            po = self.psring.next()
            for k in range(KC):
                self.mm(po.t[:, 0:n], wt.t[:, k * 128:(k + 1) * 128], YT.t[:, k, 0:n], k == 0, k == KC - 1,
                        [wt, YT], [po])
            self.stt(self.xres.t[:, m, 0:n], po.t[:, 0:n], self.G1(layer, m), self.xres.t[:, m, 0:n],
                     ALU.mult, ALU.add, [po, self.modt[layer], self.xres], [self.xres])

    def sc_block(self, n):
        self.norm_mod(self.xres, n, lambda k: self.A1(1, k), lambda k: self.S1(1, k), self.hT)
        ccs = []
        for ch in range(16):
            ccs += [ch, 16 + ch, 32 + ch]
        wsr = self.wstream("sc_w_in", ccs)
        YT = self.bfA
        rowlen = 64
        for ch in range(16):
            pp = []
            for j in range(3):
                wt = wsr.get()
                p_ = self.psring.next()
                for k in range(KC):
                    self.mm(p_.t[:, 0:n], wt.t[:, k * 128:(k + 1) * 128], self.hT.t[:, k, 0:n], k == 0, k == KC - 1,
                            [wt, self.hT], [p_])
                pp.append(p_)
            vs = self.tmpr.next()
            self.act(vs.t[:, 0:n], pp[2].t[:, 0:n], AF.Copy, [pp[2]], [vs])
            cvt = self.tmpr.next()
            self.tt(cvt.t[:, 0:n], pp[1].t[:, 0:n], vs.t[:, 0:n], ALU.mult, [pp[1], vs], [cvt])
            yc = self.tmpr.next()
            self.act(yc.t[:, 0:n], cvt.t[:, 0:n], AF.Identity, [cvt], [yc], scale=self.V(17, ch), bias=self.V(19, ch))
            cv_ = cvt.t[:, 0:n].rearrange("p (r c) -> p r c", c=rowlen)
            yv = yc.t[:, 0:n].rearrange("p (r c) -> p r c", c=rowlen)
            for tap, off in ((16, -1), (18, 1)):
                if off < 0:
                    o_, i_ = yv[:, :, -off:], cv_[:, :, :rowlen + off]
                else:
                    o_, i_ = yv[:, :, :rowlen - off], cv_[:, :, off:]
                self.stt(o_, i_, self.V(tap, ch), o_, ALU.mult, ALU.add, [cvt, yc], [yc])
            self.tt(YT.t[:, ch, 0:n], pp[0].t[:, 0:n], yc.t[:, 0:n], ALU.mult, [pp[0], yc], [YT])
        self.out_proj("sc_w_out", YT, n, 1)

    def peer_block(self, layer):
        n = T
        self.norm_mod(self.xres, n, lambda k: self.A2(layer, k), lambda k: self.S2(layer, k), self.hT)
        qT = self.bfA
        wsr = self.wstream(f"wq{layer}", list(range(16)))
        for hp in range(16):
            wt = wsr.get()
            pq = self.psring.next()
            for k in range(KC):
                self.mm(pq.t[:, 0:n], wt.t[:, k * 128:(k + 1) * 128], self.hT.t[:, k, 0:n], k == 0, k == KC - 1,
                        [wt, self.hT], [pq])
            self.act(qT.t[:, hp, 0:n], pq.t[:, 0:n], AF.Copy, [pq], [qT])
        for ts_ in range(T // 128):
            self.topk_sub(layer, qT, ts_)
            self.scatter_sub(ts_)
        self.sweep(layer)

    def topk_sub(self, layer, qT, ts_):
        t0 = ts_ * 128
        skT = self.skT[layer]
        s_sb = Tl(self.X1.t[:].rearrange("p (a b) -> p a b", b=128), "x"); s_sb.b = self.X1.b
        s_wk = Tl(self.X2.t[:].rearrange("p (a b) -> p a b", b=128), "x"); s_wk.b = self.X2.b
        for hp in range(16):
            pS = self.psS[hp // 4]
            self.mm(pS.t[:, (hp % 4) * 128:(hp % 4 + 1) * 128], qT.t[:, hp, t0:t0 + 128], skT.t[:, hp, :], True, True,
                    [qT, skT], [pS])
        for b4 in range(4):
            self.act(s_sb.t[:, b4 * 4:(b4 + 1) * 4, :].rearrange("p a b -> p (a b)"), self.psS[b4].t[:], AF.Copy,
                     [self.psS[b4]], [s_sb])
        sv, si_u = self.sv, self.si_u
        for hp in range(16):
            self.vop(lambda h, hp=hp: h.max(out=sv.t[:, hp, 0:8], in_=s_sb.t[:, hp, :]), [s_sb], [sv])
        for hp in range(16):
            self.vop(lambda h, hp=hp: h.max_index(out=si_u.t[:, hp, 0:8], in_max=sv.t[:, hp, 0:8],
                                                  in_values=s_sb.t[:, hp, :]), [s_sb, sv], [si_u])
        for hp in range(16):
            self.vop(lambda h, hp=hp: h.match_replace(out=s_wk.t[:, hp, :], in_to_replace=sv.t[:, hp, 0:8],
                                                      in_values=s_sb.t[:, hp, :], imm_value=NEG), [s_sb, sv], [s_wk])
        for hp in range(16):
            self.vop(lambda h, hp=hp: h.max(out=sv.t[:, hp, 8:16], in_=s_wk.t[:, hp, :]), [s_wk], [sv])
        for hp in range(16):
            self.vop(lambda h, hp=hp: h.max_index(out=si_u.t[:, hp, 8:16], in_max=sv.t[:, hp, 8:16],
                                                  in_values=s_wk.t[:, hp, :]), [s_wk, sv], [si_u])
        self.cp(self.si_f.t[:], si_u.t[:], [si_u], [self.si_f])
        cand = Tl(self.X1.t[:].rearrange("p (h a b) -> p h a b", h=8, a=16), "x"); cand.b = self.X1.b
        candw = Tl(self.X2.t[:].rearrange("p (h c) -> p h c", h=8), "x"); candw.b = self.X2.b
        self.tt(cand.t, sv.t[:, 0::2, :].unsqueeze(3).broadcast_to([P, 8, 16, 16]),
                sv.t[:, 1::2, :].unsqueeze(2).broadcast_to([P, 8, 16, 16]), ALU.add, [sv], [cand])
        c2 = Tl(self.X1.t[:].rearrange("p (h c) -> p h c", h=8), "x"); c2.b = self.X1.b
        cvv, ci_u = self.cvv, self.ci_u
        for h_ in range(8):
            self.vop(lambda h, h_=h_: h.max(out=cvv.t[:, h_, 0:8], in_=c2.t[:, h_, :]), [c2], [cvv])
        for h_ in range(8):
            self.vop(lambda h, h_=h_: h.max_index(out=ci_u.t[:, h_, 0:8], in_max=cvv.t[:, h_, 0:8],
                                                  in_values=c2.t[:, h_, :]), [c2, cvv], [ci_u])
        for h_ in range(8):
            self.vop(lambda h, h_=h_: h.match_replace(out=candw.t[:, h_, :], in_to_replace=cvv.t[:, h_, 0:8],
                                                      in_values=c2.t[:, h_, :], imm_value=NEG), [c2, cvv], [candw])
        for h_ in range(8):
            self.vop(lambda h, h_=h_: h.max(out=cvv.t[:, h_, 8:16], in_=candw.t[:, h_, :]), [candw], [cvv])
        for h_ in range(8):
            self.vop(lambda h, h_=h_: h.max_index(out=ci_u.t[:, h_, 8:16], in_max=cvv.t[:, h_, 8:16],
                                                  in_values=candw.t[:, h_, :]), [candw, cvv], [ci_u])
        self.ts(self.ca_u.t[:], ci_u.t[:], 4, None, ALU.logical_shift_right, ALU.bypass, [ci_u], [self.ca_u])
        self.ts(self.cb_u.t[:], ci_u.t[:], 15, None, ALU.bitwise_and, ALU.bypass, [ci_u], [self.cb_u])
        self.cp(self.caf.t[:], self.ca_u.t[:], [self.ca_u], [self.caf])
        self.cp(self.cbf.t[:], self.cb_u.t[:], [self.cb_u], [self.cbf])
        oh = Tl(self.X1.t[:].rearrange("p (h k a) -> p h k a", h=8, k=16), "x"); oh.b = self.X1.b
        io16 = self.iota[:, 0:16].unsqueeze(1).unsqueeze(1).broadcast_to([P, 8, 16, 16])
        for j, (cf, par) in enumerate(((self.caf, 0), (self.cbf, 1))):
            self.tt(oh.t, cf.t[:].unsqueeze(3).broadcast_to([P, 8, 16, 16]), io16, ALU.is_equal, [cf, self.cst], [oh])
            self.tt(oh.t, oh.t, self.si_f.t[:, par::2, :].unsqueeze(2).broadcast_to([P, 8, 16, 16]), ALU.mult,
                    [oh, self.si_f], [oh])
            self.vop(lambda h, j=j: h.tensor_reduce(out=self.slot3.t[:, j, :].rearrange("p (h k) -> p h k", h=8),
                                                    in_=oh.t, axis=AX.X, op=ALU.add), [oh], [self.slot3])
        zs = self.zs
        g3 = self.slot3.t[:, 2, :].rearrange("p (h k) -> p h k", h=8)
        self.tt(g3, cvv.t[:], cvv.t[:, :, 0:1].broadcast_to([P, 8, 16]), ALU.subtract, [cvv], [self.slot3])
        self.act(g3, g3, AF.Exp, [self.slot3], [self.slot3])
        self.vop(lambda h: h.tensor_reduce(out=zs.t[:, 0, :], in_=g3, axis=AX.X, op=ALU.add), [self.slot3], [zs])
        self.vop(lambda h: h.reciprocal(out=zs.t[:, 1, :], in_=zs.t[:, 0, :]), [zs], [zs])
        self.tt(g3, g3, zs.t[:, 1, :].unsqueeze(2).broadcast_to([P, 8, 16]), ALU.mult, [self.slot3, zs], [self.slot3])
        pm = self.psM
        for j in range(3):
            self.vop(lambda h, j=j: h.transpose(out=pm.t[:, j * 128:(j + 1) * 128], in_=self.slot3.t[:, j, :],
                                                identity=self.ident), [self.slot3, self.cst], [pm], eng="pe")
        self.cp(self.slotT.t[:, :, ts_ * 128:(ts_ + 1) * 128], pm.t[:, 0:384].rearrange("p (j t) -> p j t", j=3),
                [pm], [self.slotT])

    def scatter_sub(self, ts_):
        TG = 32
        WT = self.WT
        for g_ in range(128 // TG):
            t0 = ts_ * 128 + g_ * TG
            Bt = Tl(self.X1.t[:].bitcast(BF16).rearrange("p (t i) -> p t i", i=128)[:, 0:TG, :], "x"); Bt.b = self.X1.b
            At = Tl(self.X2.t[:].bitcast(BF16).rearrange("p (t i) -> p t i", i=128)[:, 0:TG, :], "x"); At.b = self.X2.b
            io = self.iota.unsqueeze(1).broadcast_to([P, TG, 128])
            i1v = self.slotT.t[:, 0, t0:t0 + TG].unsqueeze(2).broadcast_to([P, TG, 128])
            i2v = self.slotT.t[:, 1, t0:t0 + TG].unsqueeze(2).broadcast_to([P, TG, 128])
            gv = self.slotT.t[:, 2, t0:t0 + TG].unsqueeze(2).broadcast_to([P, TG, 128])
            self.tt(Bt.t, io, i2v, ALU.is_equal, [self.cst, self.slotT], [Bt])
            self.tt(At.t, io, i1v, ALU.is_equal, [self.cst, self.slotT], [At])
            self.tt(At.t, At.t, gv, ALU.mult, [At, self.slotT], [At], eng="pool")
            for q4 in range(TG // 4):
                pw = self.psS[(g_ * (TG // 4) + q4) % 4]
                for tt_ in range(4):
                    tl = q4 * 4 + tt_
                    self.mm(pw.t[:, tt_ * 128:(tt_ + 1) * 128], Bt.t[:, tl, :], At.t[:, tl, :], True, True,
                            [Bt, At], [pw])
                ta = t0 + q4 * 4
                src_ = pw.t[:].rearrange("p (t i) -> p i t", t=4)
                if q4 % 2:
                    self.act(WT.t[:, :, ta:ta + 4], src_, AF.Copy, [pw], [WT])
                else:
                    self.cp(WT.t[:, :, ta:ta + 4], src_, [pw], [WT])

    def sweep(self, layer):
        n = T
        PC = 4
        WT = self.WT
        udst, ubuf = self.ws[f"u{layer}"]
        vdst, vbuf = self.ws[f"v{layer}"]
        ust = Stream(self, self.wring, [(udst[c], ubuf) for c in range(NEXP_C)])
        vst = Stream(self, self.vring, [(vdst[c0:c0 + PC].rearrange("c p d -> p c d"), vbuf)
                                        for c0 in range(0, NEXP_C, PC)], queue="pool")
        for part in range(NEXP_C // PC):
            for cc in range(PC):
                c = part * PC + cc
                ut = ust.get()
                pu = self.psring.next()
                for k in range(KC):
                    self.mm(pu.t[:, 0:n], ut.t[:, k * 128:(k + 1) * 128], self.hT.t[:, k, 0:n], k == 0, k == KC - 1,
                            [ut, self.hT], [pu])
                ab = self.tmpb.next()
                self.act(ab.t[:, 0:n], pu.t[:, 0:n], AF.Gelu_apprx_tanh, [pu], [ab])
                self.tt(WT.t[:, c, :], WT.t[:, c, :], ab.t[:, 0:n], ALU.mult, [WT, ab], [WT])
            vt = vst.get()
            for m in range(16):
                pv = self.psring.next()
                for cc in range(PC):
                    c = part * PC + cc
                    self.mm(pv.t[:, 0:n], vt.t[:, cc, m * 128:(m + 1) * 128], WT.t[:, c, :], cc == 0, cc == PC - 1,
                            [vt, WT], [pv])
                self.stt(self.xres.t[:, m, 0:n], pv.t[:, 0:n], self.G2(layer, m), self.xres.t[:, m, 0:n],
                         ALU.mult, ALU.add, [pv, self.modt[layer], self.xres], [self.xres])

    def pass1(self, xsrc, base):
        S = self.S
        if not hasattr(self, "ldx1"):
            self.ldx1 = S.slot("ldx1")
        ldx = self.ldx1
        for sbi in range(NSB):
            self.dma(self.xres.t[:], xsrc[:, :, base + sbi * T:base + (sbi + 1) * T], ldx, writes=[self.xres])
            self.lru_block(self.xres, T, 64, lambda k: self.A1(0, k), lambda k: self.S1(0, k), False, sbi)
            if getattr(self, "bg_per_block", 0):
                self.precast_run(self.bg_per_block, bg=True)
        self.S.barrier()
        self.chunk_carry()
        if self.mode == "p1":
            so = S.slot("st_carry")
            self.dma(self.carry_out, self.carry.t[:], so, reads=[self.carry])
        self.S.barrier()

    def exchange(self):
        S = self.S
        cc_in = self.dscr("cc_in", [P, 4 * KC], F32)
        cc_out = self.dscr("cc_out", [NCORE * P, 4 * KC], F32)
        bi, bo = Buf("cc_in"), Buf("cc_out")
        s1, s2, s3 = S.slot("cc1"), S.slot("cc2"), S.slot("cc3")
        self.dma(cc_in, self.carry.t[:, NSB, :, :].rearrange("p a b -> p (a b)"), s1, reads=[self.carry], writes=[bi])
        S.custom("pool", lambda h: h.collective_compute("AllGather", ALU.bypass, replica_groups=[list(range(NCORE))],
                                                        ins=[cc_in], outs=[cc_out]), s2, reads=[bi], writes=[bo])
        self.dma(self.call.t[:].rearrange("p r a b -> p r (a b)"), cc_out.rearrange("(r p) f -> p r f", p=P), s3,
                 reads=[bo], writes=[self.call])
        self.S.barrier()

    def chunk_carry(self):
        c = self.carry
        for d in range(2):
            self.tt(c.t[:, 0:NSB, 2 * d, :], c.t[:, 0:NSB, 2 * d, :],
                    self.dv.t[:, 18 + d, :].unsqueeze(1).broadcast_to([P, NSB, KC]), ALU.mult, [c, self.dv], [c])
            self.tt(c.t[:, 0:NSB, 2 * d, :], c.t[:, 0:NSB, 2 * d, :],
                    self.dv.t[:, 20 + d, :].unsqueeze(1).broadcast_to([P, NSB, KC]), ALU.add, [c, self.dv], [c])
            self.act(c.t[:, 0:NSB, 2 * d, :], c.t[:, 0:NSB, 2 * d, :], AF.Exp, [c], [c])
        self.cp(c.t[:, NSB, 0, :], c.t[:, 0, 0, :], [c], [c])
        self.cp(c.t[:, NSB, 1, :], c.t[:, 0, 1, :], [c], [c])
        for sb in range(1, NSB):
            self.tt(c.t[:, NSB, 1, :], c.t[:, NSB, 1, :], c.t[:, sb, 0, :], ALU.mult, [c], [c])
            self.tt(c.t[:, NSB, 1, :], c.t[:, NSB, 1, :], c.t[:, sb, 1, :], ALU.add, [c], [c])
            self.tt(c.t[:, NSB, 0, :], c.t[:, NSB, 0, :], c.t[:, sb, 0, :], ALU.mult, [c], [c])
        self.cp(c.t[:, NSB, 2, :], c.t[:, NSB - 1, 2, :], [c], [c])
        self.cp(c.t[:, NSB, 3, :], c.t[:, NSB - 1, 3, :], [c], [c])
        for sb in range(NSB - 2, -1, -1):
            self.tt(c.t[:, NSB, 3, :], c.t[:, NSB, 3, :], c.t[:, sb, 2, :], ALU.mult, [c], [c])
            self.tt(c.t[:, NSB, 3, :], c.t[:, NSB, 3, :], c.t[:, sb, 3, :], ALU.add, [c], [c])
            self.tt(c.t[:, NSB, 2, :], c.t[:, NSB, 2, :], c.t[:, sb, 2, :], ALU.mult, [c], [c])

    def ctx_and_carries(self):
        NC_ = NSB
        ldc = self.S.slot("ld_ctx")
        self.dma(self.ctxx.t, self.ctxT, ldc, writes=[self.ctxx])
        saved = self.carry
        ctxc = self.sb("ctxcarry", [P, 1, 4, KC])
        self.carry = ctxc
        self.lru_block(self.ctxx, 256, 256, lambda k: self.dv.t[:, 8, k:k + 1],
                       lambda k: self.modt[0].t[:, k, 1:2], False, 0)
        self.carry = saved
        hin = self.hin
        call = self.call
        cm = self.cmk
        sm = self.small
        self.cp(hin.t[:, 0, 0, :], ctxc.t[:, 0, 1, :], [ctxc], [hin])
        self.cp(hin.t[:, 1, NSB, :], ctxc.t[:, 0, 3, :], [ctxc], [hin])
        ns_ = self.nslots
        for d, order in ((0, range(ns_)), (1, range(ns_ - 1, -1, -1))):
            hsl = hin.t[:, 0, 0, :] if d == 0 else hin.t[:, 1, NSB, :]
            for cp_ in order:
                m = cm.t[:, d, cp_:cp_ + 1]
                self.ts(sm.t[:, 0, :], call.t[:, cp_, 2 * d, :], -1.0, m, ALU.add, ALU.mult, [call, cm], [sm])
                self.ts(sm.t[:, 0, :], sm.t[:, 0, :], 1.0, None, ALU.add, ALU.bypass, [sm], [sm])
                self.ts(sm.t[:, 1, :], call.t[:, cp_, 2 * d + 1, :], m, None, ALU.mult, ALU.bypass, [call, cm], [sm])
                self.tt(hsl, hsl, sm.t[:, 0, :], ALU.mult, [hin, sm], [hin])
                self.tt(hsl, hsl, sm.t[:, 1, :], ALU.add, [hin, sm], [hin])
        c = self.carry
        for sb in range(NSB):
            self.tt(hin.t[:, 0, sb + 1, :], hin.t[:, 0, sb, :], c.t[:, sb, 0, :], ALU.mult, [hin, c], [hin])
            self.tt(hin.t[:, 0, sb + 1, :], hin.t[:, 0, sb + 1, :], c.t[:, sb, 1, :], ALU.add, [hin, c], [hin])
        for sb in range(NSB - 1, -1, -1):
            self.tt(hin.t[:, 1, sb, :], hin.t[:, 1, sb + 1, :], c.t[:, sb, 2, :], ALU.mult, [hin, c], [hin])
            self.tt(hin.t[:, 1, sb, :], hin.t[:, 1, sb, :], c.t[:, sb, 3, :], ALU.add, [hin, c], [hin])
        self.S.barrier()

    def pass2(self):
        S = self.S
        ldx = S.slot("ldx")
        sto = S.slot("sto")
        for sbi in range(self.nsb_run):
            self.dma(self.xres.t[:], self.xT[:, :, sbi * T:(sbi + 1) * T], ldx, writes=[self.xres])
            self.lru_block(self.xres, T, 64, lambda k: self.A1(0, k), lambda k: self.S1(0, k), True, sbi,
                           hin_f=lambda h, sbi=sbi: self.hin.t[:, 0, sbi, h:h + 1],
                           hin_b=lambda h, sbi=sbi: self.hin.t[:, 1, sbi + 1, h:h + 1], want_carry=False)
            if self.dbg != "x1":
                self.peer_block(0)
            if self.dbg is None or self.dbg in ("x3", "x4"):
                self.sc_block(T)
            if self.dbg is None or self.dbg == "x4":
                self.peer_block(1)
            if self.dbg is None:
                self.norm_mod(self.xres, T, lambda k: self.V(4, k), lambda k: self.dv.t[:, 13, k:k + 1], self.outT)
                self.dma(self.yT[:, :, sbi * T:(sbi + 1) * T], self.outT.t[:], sto, reads=[self.outT])
            else:
                self.dma(self.yT[:, :, sbi * T:(sbi + 1) * T], self.xres.t[:], sto, reads=[self.xres])
        self.S.barrier()


def _fm(v):
    v = np.asarray(v, np.float32)
    return np.ascontiguousarray(np.moveaxis(v.reshape(v.shape[:-1] + (KC, P)), -1, 0))


def _consts():
    c = np.zeros((P, 384), np.float32)
    c[:, 0:128] = np.eye(P, dtype=np.float32)
    c[:, 128:256] = np.arange(128, dtype=np.float32)[None, :]
    c[:, 256:384] = 1.0
    return c


_NC_CACHE = {}


def _get_nc(mode, nsb_run=NSB, dbg=None):
    key = (mode, nsb_run, dbg)
    if key not in _NC_CACHE:
        _NC_CACHE[key] = KB(mode, nsb_run, dbg).build()
    return _NC_CACHE[key]


def _prep(inputs):
    f = lambda k: np.asarray(inputs[k], np.float32)
    x = f("x")
    shared = {}
    shared["consts"] = _consts()
    vec = np.zeros((NV, D), np.float32)
    vec[0] = f("norm_mix_g")[0]; vec[1] = f("norm_ffn_g")[0]
    vec[2] = f("norm_mix_g")[1]; vec[3] = f("norm_ffn_g")[1]
    vec[4] = f("norm_final_g")
    vec[5:9] = f("lru_conv_w")[0]; vec[9] = f("lru_conv_b")[0]
    vec[10:12] = f("lru_b_a")[0]; vec[12:14] = f("lru_b_x")[0]; vec[14:16] = f("lru_lambda")[0]
    vec[16:19] = f("sc_conv_w")[0]; vec[19] = f("sc_conv_b")[0]
    shared["vecs"] = _fm(vec)
    shared["w_mod"] = f("w_mod")
    shared["b_modT"] = np.ascontiguousarray(f("b_mod").reshape(2, 96, P).transpose(2, 0, 1))
    shared["lru_w_in"] = f("lru_w_in")[0]
    wab = np.stack([f("lru_w_a")[0], f("lru_w_x")[0]], axis=1)
    shared["wab"] = np.ascontiguousarray(wab.transpose(2, 3, 0, 1, 4)).reshape(16, P, 512)
    per_core = []
    for c in range(NCORE):
        b, j = divmod(c, 4)
        m = {}
        xs = x[b, j * NTOK:(j + 1) * NTOK]
        m["xT"] = np.ascontiguousarray(xs.reshape(NTOK, KC, P).transpose(2, 1, 0))
        m["cvec"] = np.ascontiguousarray(np.stack([_fm(f("c")[b]), _fm(f("c_ctx"))], axis=-1))
        per_core.append(m)
    return shared, per_core


def _prep2(inputs):
    f = lambda k: np.asarray(inputs[k], np.float32)
    sh = {}
    sh["lru_w_out"] = f("lru_w_out")[0]
    sh["sc_w_in"] = f("sc_w_in")[0]
    sh["sc_w_out"] = f("sc_w_out")[0]
    sh["peer_w_q"] = f("peer_w_q")
    sk = f("peer_sub_keys")
    sh["skT"] = np.ascontiguousarray(sk.reshape(2, 16, 128, 128).transpose(0, 3, 1, 2))
    u = f("peer_u")
    sh["u_l"] = np.ascontiguousarray(u.reshape(2, NEXP_C, P, KC, P).transpose(0, 1, 4, 3, 2)).reshape(2, NEXP_C, P, D)
    sh["v_l"] = f("peer_v").reshape(2, NEXP_C, P, D)
    ctx = f("ctx")
    pc = []
    for c in range(NCORE):
        b, j = divmod(c, 4)
        m = {}
        m["ctxT"] = np.ascontiguousarray(ctx[b].reshape(256, KC, P).transpose(2, 1, 0))
        cm = np.zeros((P, 2, NCORE), np.float32)
        for c2 in range(NCORE):
            b2, j2 = divmod(c2, 4)
            if b2 == b and j2 < j:
                cm[:, 0, c2] = 1.0
            if b2 == b and j2 > j:
                cm[:, 1, c2] = 1.0
        m["cmask"] = cm
        pc.append(m)
    return sh, pc


def _xT_chunk(x, b, j):
    xs = x[b, j * NTOK:(j + 1) * NTOK]
    return xs.reshape(NTOK, KC, P).transpose(2, 1, 0)


def kernel(**inputs):
    shared, pc = _prep(inputs)
    sh2, pc2 = _prep2(inputs)
    x = np.asarray(inputs["x"], np.float32)
    in_maps = []
    for c in range(NCORE):
        b, j = divmod(c, 4)
        others = [jj for jj in range(4) if jj != j]
        xTo = np.ascontiguousarray(np.concatenate([_xT_chunk(x, b, jj) for jj in others], axis=2))
        cm = np.zeros((P, 2, 3), np.float32)
        for s_, jj in enumerate(others):
            cm[:, 0, s_] = 1.0 if jj < j else 0.0
            cm[:, 1, s_] = 1.0 if jj > j else 0.0
        m = {**shared, **pc[c], **sh2, **pc2[c], "xTo": xTo, "cmask": cm}
        in_maps.append(m)
    nc = _get_nc("f")
    r = run_bass_kernel_spmd(nc, in_maps, core_ids=list(range(NCORE)))
    out = np.zeros((2, 4 * NTOK, D), np.float32)
    for c in range(NCORE):
        b, j = divmod(c, 4)
        yT = np.asarray(r.results[c]["yT"], np.float32)
        out[b, j * NTOK:(j + 1) * NTOK] = yT.transpose(2, 1, 0).reshape(NTOK, D)
    return out
```

```python
import math
import numpy as np
from contextlib import ExitStack
import concourse.bass as bass
import concourse.mybir as mybir
from concourse.bass_utils import run_bass_kernel_spmd

F32 = mybir.dt.float32
BF16 = mybir.dt.bfloat16
U32 = mybir.dt.uint32
ALU = mybir.AluOpType
AF = mybir.ActivationFunctionType
AX = mybir.AxisListType

P = 128
D = 2048
KC = 16
T = 256
NTOK = 4096
NSB = NTOK // T
NCORE = 8
NEXP_C = 128
EPS = 1e-6
NV = 20
NEG = -1.0e30


class Buf:
    __slots__ = ("name", "writers", "readers")

    def __init__(self, name=""):
        self.name = name
        self.writers = {}
        self.readers = []


class Eng:
    def __init__(self, name, sem, self_sync):
        self.name = name
        self.sem = sem
        self.count = 0
        self.known = {}
        self.self_sync = self_sync
        self.prog = []


class Sched:
    def __init__(self, nc, stack):
        self.nc = nc
        self.stack = stack
        self.engs = {}
        self.slots = []
        for name, ss in (("pe", False), ("act", True), ("dve", True), ("pool", True), ("sp", False)):
            self.engs[name] = Eng(name, self.new_sem("e_" + name), ss)
        self.ninst = 0

    def new_sem(self, name):
        return self.stack.enter_context(self.nc.semaphore(name))

    def _waits(self, E, reads, writes):
        need = {}

        def add(tok):
            sem, val, en = tok
            if en == E.name and not E.self_sync:
                return
            k = id(sem)
            if k not in need or need[k][1] < val:
                need[k] = (sem, val)
        for b in reads:
            for tok in b.writers.values():
                add(tok)
        for b in writes:
            for tok in b.writers.values():
                add(tok)
            for tok in b.readers:
                add(tok)
        for k, (sem, val) in need.items():
            if E.known.get(k, 0) < val:
                E.prog.append(("w", sem, val))
                E.known[k] = val

    def _mark(self, tok, key, reads, writes):
        for b in reads:
            b.readers.append(tok)
        for b in writes:
            b.writers[key] = tok
            b.readers = []

    def op(self, ename, build, reads=(), writes=()):
        E = self.engs[ename]
        self._waits(E, reads, writes)
        E.count += 1
        E.prog.append(("o", build, E.sem, 1))
        self._mark((E.sem, E.count, E.name), E.name, reads, writes)
        self.ninst += 1

    def slot(self, name):
        s = {"sem": self.new_sem(name), "n": 0, "name": name}
        self.slots.append(s)
        return s

    def dma(self, qname, out, in_, slot, reads=(), writes=()):
        E = self.engs[qname]
        self._waits(E, reads, writes)
        E.prog.append(("o", (lambda h: h.dma_start(out=out, in_=in_)), slot["sem"], 16))
        slot["n"] += 1
        self._mark((slot["sem"], 16 * slot["n"], "dma"), "dma_" + slot["name"], reads, writes)
        self.ninst += 1

    def custom(self, qname, build, slot, reads=(), writes=(), inc=16):
        E = self.engs[qname]
        self._waits(E, reads, writes)
        E.prog.append(("o", build, slot["sem"], inc))
        slot["n"] += 1
        self._mark((slot["sem"], inc * slot["n"], "dma"), "dma_" + slot["name"], reads, writes)
        self.ninst += 1

    def barrier(self):
        for E in self.engs.values():
            for Fe in self.engs.values():
                if Fe is E or Fe.count == 0:
                    continue
                k = id(Fe.sem)
                if E.known.get(k, 0) < Fe.count:
                    E.prog.append(("w", Fe.sem, Fe.count))
                    E.known[k] = Fe.count
            for s in self.slots:
                k = id(s["sem"])
                if s["n"] and E.known.get(k, 0) < 16 * s["n"]:
                    E.prog.append(("w", s["sem"], 16 * s["n"]))
                    E.known[k] = 16 * s["n"]

    def emit(self):
        with self.nc.Block() as block:
            def run(E):
                def f(h):
                    for it in E.prog:
                        if it[0] == "w":
                            h.wait_ge(it[1], it[2])
                        else:
                            it[1](h).then_inc(it[2], it[3])
                return f
            block.tensor(run(self.engs["pe"]))
            block.scalar(run(self.engs["act"]))
            block.vector(run(self.engs["dve"]))
            block.gpsimd(run(self.engs["pool"]))
            block.sync(run(self.engs["sp"]))


class Tl:
    def __init__(self, t, name):
        self.t = t
        self.b = Buf(name)


class Ring:
    def __init__(self, tiles):
        self.tiles = tiles
        self.i = 0

    def next(self):
        t = self.tiles[self.i % len(self.tiles)]
        self.i += 1
        return t


class Stream:
    def __init__(self, kb, ring, srcs, queue="sp"):
        self.kb = kb
        self.ring = ring
        self.srcs = srcs
        self.R = len(ring)
        for tl in ring:
            if not hasattr(tl, "slot"):
                tl.slot = kb.S.slot("r_" + tl.b.name)
        self.issued = 0
        self.taken = 0
        self.queue = queue

    def _issue(self):
        n = self.issued
        tl = self.ring[n % self.R]
        src, sbuf = self.srcs[n]
        self.kb.S.dma(self.queue, tl.t[:], src, tl.slot,
                      reads=[sbuf] if sbuf is not None else [], writes=[tl.b])
        self.issued += 1

    def get(self):
        while self.issued < min(self.taken + self.R, len(self.srcs)):
            self._issue()
        tl = self.ring[self.taken % self.R]
        self.taken += 1
        return tl


class KB:
    def __init__(self, mode, nsb_run=NSB, dbg=None):
        self.mode = mode
        self.nsb_run = nsb_run
        self.dbg = dbg
        self.nc = bass.Bass("TRN2", target_bir_lowering=False)
        self.st = ExitStack()
        full = ["w_in", "wab", "w_out", "sc_w_in", "sc_w_out", "wq0", "wq1", "u0", "v0", "u1", "v1"]
        sub = {"s_setup": [], "x1": full[:3], "s_ctx": full[:3], "x2": full[:3] + ["wq0", "u0", "v0"],
               "x3": full[:5] + ["wq0", "u0", "v0"]}
        self.wlist = ["w_in_xb", "wab"] if mode == "p1" else sub.get(dbg, full)

    def din(self, name, shape, dt=F32):
        return self.nc.dram_tensor(name, list(shape), dt, kind="ExternalInput").ap()

    def dout(self, name, shape, dt=F32):
        return self.nc.dram_tensor(name, list(shape), dt, kind="ExternalOutput").ap()

    def dscr(self, name, shape, dt=BF16):
        return self.nc.dram_tensor(name, list(shape), dt, kind="Internal").ap()

    def sb(self, name, shape, dt=F32):
        return Tl(self.st.enter_context(self.nc.sbuf_tensor(name, list(shape), dt)), name)

    def ps(self, name, shape, dt=F32):
        return Tl(self.st.enter_context(self.nc.psum_tensor(name, list(shape), dt)), name)

    def act(self, out, in_, func, reads, writes, scale=1.0, bias=0.0, eng="act"):
        self.S.op(eng, lambda h: h.activation(out=out, in_=in_, func=func, scale=scale, bias=bias),
                  [r.b for r in reads], [w.b for w in writes])

    def tt(self, out, in0, in1, op, reads, writes, eng="dve"):
        self.S.op(eng, lambda h: h.tensor_tensor(out=out, in0=in0, in1=in1, op=op),
                  [r.b for r in reads], [w.b for w in writes])

    def ts(self, out, in0, s1, s2, op0, op1, reads, writes, eng="dve"):
        self.S.op(eng, lambda h: h.tensor_scalar(out=out, in0=in0, scalar1=s1, scalar2=s2, op0=op0, op1=op1),
                  [r.b for r in reads], [w.b for w in writes])

    def stt(self, out, in0, scalar, in1, op0, op1, reads, writes):
        self.S.op("dve", lambda h: h.scalar_tensor_tensor(out=out, in0=in0, scalar=scalar, in1=in1, op0=op0, op1=op1),
                  [r.b for r in reads], [w.b for w in writes])

    def cp(self, out, in_, reads, writes, eng="dve"):
        self.S.op(eng, lambda h: h.tensor_copy(out=out, in_=in_),
                  [r.b for r in reads], [w.b for w in writes])

    def mm(self, out, lhsT, rhs, start, stop, reads, writes):
        self.S.op("pe", lambda h: h.matmul(out, lhsT=lhsT, rhs=rhs, start=start, stop=stop),
                  [r.b for r in reads], [w.b for w in writes])

    def vop(self, fn, reads, writes, eng="dve"):
        self.S.op(eng, fn, [r.b for r in reads], [w.b for w in writes])

    def dma(self, out, in_, slot, reads=(), writes=(), q="sp"):
        self.S.dma(q, out, in_, slot, [r if isinstance(r, Buf) else r.b for r in reads],
                   [w if isinstance(w, Buf) else w.b for w in writes])

    def build(self):
        nc = self.nc
        mode = self.mode
        with self.st:
            self.S = Sched(nc, self.st)
            self.declare_io()
            self.alloc()
            self.setup()
            if mode == "p1":
                self.precast(["w_in_xb", "wab"])
                self.precast_finish()
                self.S.barrier()
                self.pass1(self.xT, 0)
            elif mode == "f":
                self.precast(self.wlist)
                self.precast_run(36, bg=False)
                self.S.barrier()
                self.bg_per_block = -(-(len(self.pc_jobs) - 36) // (4 * NSB))
                for s_ in range(3):
                    self.pass1(self.xTo, s_ * NTOK)
                    self.cp(self.call.t[:, s_, :, :], self.carry.t[:, NSB, :, :], [self.carry], [self.call])
                    self.S.barrier()
                self.pass1(self.xT, 0)
                self.precast_finish()
                self.S.barrier()
                self.ctx_and_carries()
                self.pass2()
            else:
                self.precast(self.wlist)
                self.precast_finish()
                self.S.barrier()
                if self.dbg not in ("s_setup", "s_precast"):
                    self.ctx_and_carries()
                    if self.dbg != "s_ctx":
                        self.pass2()
            self.S.barrier()
            self.S.emit()
        return nc

    def declare_io(self):
        self.xT = self.din("xT", [P, KC, NTOK])
        self.consts = self.din("consts", [P, 3 * 128])
        self.vecs = self.din("vecs", [P, NV, KC])
        self.cvec = self.din("cvec", [P, KC, 2])
        self.w_mod = self.din("w_mod", [2, D, 6 * D])
        self.b_modT = self.din("b_modT", [P, 2, 96])
        self.lru_w_in = self.din("lru_w_in", [D, 2 * D])
        self.wab_in = self.din("wab", [16, P, 512])
        if self.mode == "p1":
            self.carry_out = self.dout("carry", [P, NSB + 1, 4, KC])
        else:
            self.ctxT = self.din("ctxT", [P, KC, 256])
            if self.mode == "f":
                self.xTo = self.din("xTo", [P, KC, 3 * NTOK])
            if self.mode == "p2":
                self.carry_own = self.din("carry_own", [P, NSB + 1, 4, KC])
                self.carry_all = self.din("carry_all", [P, NCORE, 4, KC])
            self.nslots = 3 if self.mode == "f" else NCORE
            self.cmask = self.din("cmask", [P, 2, self.nslots])
            wl = self.wlist
            if "w_out" in wl:
                self.lru_w_out = self.din("lru_w_out", [D, D])
            if "sc_w_in" in wl:
                self.sc_w_in = self.din("sc_w_in", [D, 3 * D])
                self.sc_w_out = self.din("sc_w_out", [D, D])
            self.has_peer = "wq0" in wl
            if self.has_peer:
                self.peer_w_q = self.din("peer_w_q", [2, D, D])
                self.skT_in = self.din("skT", [2, P, 16, 128])
                self.u_l = self.din("u_l", [2, NEXP_C, P, D])
                self.v_l = self.din("v_l", [2, NEXP_C, P, D])
            self.yT = self.dout("yT", [P, KC, NTOK])
        self.ws = {}
        self.ws["w_in"] = (self.dscr("ws_w_in", [32, P, D]), Buf("ws_w_in"))
        self.ws["wab"] = (self.dscr("ws_wab", [16, P, 512]), Buf("ws_wab"))
        if self.mode != "p1":
            self.ws["w_out"] = (self.dscr("ws_w_out", [16, P, D]), Buf("ws_w_out"))
            self.ws["sc_w_in"] = (self.dscr("ws_sc_w_in", [48, P, D]), Buf("ws_sc_w_in"))
            self.ws["sc_w_out"] = (self.dscr("ws_sc_w_out", [16, P, D]), Buf("ws_sc_w_out"))
            for i in range(2):
                self.ws[f"wq{i}"] = (self.dscr(f"ws_wq{i}", [16, P, D]), Buf(f"ws_wq{i}"))
                self.ws[f"u{i}"] = (self.dscr(f"ws_u{i}", [NEXP_C, P, D]), Buf(f"ws_u{i}"))
                self.ws[f"v{i}"] = (self.dscr(f"ws_v{i}", [NEXP_C, P, D]), Buf(f"ws_v{i}"))

    def alloc(self):
        sb, ps = self.sb, self.ps
        self.cst = sb("cst", [P, 3 * 128])
        self.ident = self.cst.t[:, 0:128]
        self.iota = self.cst.t[:, 128:256]
        self.ones_bf = sb("ones_bf", [P, 128], BF16)
        self.iota_bf = sb("iota_bf", [P, 128], BF16)
        self.vec = sb("vec", [P, NV, KC])
        self.cv = sb("cv", [P, KC, 2])
        self.modt = [sb(f"modt{i}", [P, 96, 2]) for i in range(2)]
        self.bmod = sb("bmod", [P, 2, 96])
        self.dv = sb("dv", [P, 24, KC])
        self.wabring = [sb(f"wabr{i}", [P, 4, 128], BF16) for i in range(2)]
        self.xres = sb("xres", [P, KC, T])
        self.hT = sb("hT", [P, KC, T], BF16)
        self.bfA = sb("bfA", [P, KC, T], BF16)
        self.rstd = sb("rstd", [P, T])
        self.tmpr = Ring([sb(f"tmp{i}", [P, T]) for i in range(11)])
        self.tmpb = Ring([sb(f"tmpb{i}", [P, T], BF16) for i in range(4)])
        self.arena = sb("arena", [P, 16384])
        self.wring = [sb(f"wr{i}", [P, D], BF16) for i in range(3)]
        self.carry = sb("carryt", [P, NSB + 1, 4, KC])
        self.hin = sb("hin", [P, 2, NSB + 1, KC])
        self.small = sb("small", [P, 8, KC])
        if self.mode != "p1":
            self.vring = [sb(f"vr{i}", [P, 4, D], BF16) for i in range(2)]
            self.skT = [sb(f"skT{i}", [P, 16, 128], BF16) for i in range(2)]
            self.X1 = sb("X1", [P, 2048])
            self.X2 = sb("X2", [P, 2048])
            self.sv = sb("sv", [P, 16, 16])
            self.si_u = sb("si_u", [P, 16, 16], U32)
            self.si_f = sb("si_f", [P, 16, 16])
            self.cvv = sb("cvv", [P, 8, 16])
            self.ci_u = sb("ci_u", [P, 8, 16], U32)
            self.ca_u = sb("ca_u", [P, 8, 16], U32)
            self.cb_u = sb("cb_u", [P, 8, 16], U32)
            self.caf = sb("caf", [P, 8, 16])
            self.cbf = sb("cbf", [P, 8, 16])
            self.slot3 = sb("slot3", [P, 3, 128])
            self.slotT = sb("slotT", [P, 3, T])
            self.zs = sb("zs", [P, 4, 8])
            self.cmk = sb("cmk", [P, 2, self.nslots])
            self.call = sb("call", [P, NCORE, 4, KC])
            self.WT = Tl(self.arena.t[:].bitcast(BF16).rearrange("p (c t) -> p c t", t=T), "WT")
            self.WT.b = self.arena.b
            self.ctxx = Tl(self.arena.t[:, 0:KC * 256].rearrange("p (k t) -> p k t", k=KC), "ctxx")
            self.ctxx.b = self.arena.b
            self.outT = Tl(self.arena.t[:, 0:KC * T].rearrange("p (k t) -> p k t", k=KC), "outT")
            self.outT.b = self.arena.b
        pa = [ps(f"psA{i}", [P, 512]) for i in range(3)]
        pst = []
        for half in range(2):
            for bnk in range(3):
                t_ = Tl(pa[bnk].t[:, half * 256:half * 256 + 256], f"psh{bnk}_{half}")
                t_.b = pa[bnk].b
                pst.append(t_)
        self.psring = Ring(pst)
        self.psS = [ps(f"psS{i}", [P, 512]) for i in range(4)]
        self.psM = ps("psM", [P, 512])

    def setup(self):
        S = self.S
        self._nld = 0

        def ld1(o, i, tl):
            self._nld += 1
            self.dma(o, i, S.slot(f"ld1_{self._nld}"), writes=[tl])
        self.ld1 = ld1
        ld1(self.cst.t[:], self.consts, self.cst)
        ld1(self.vec.t[:], self.vecs, self.vec)
        ld1(self.cv.t[:], self.cvec, self.cv)
        ld1(self.bmod.t[:], self.b_modT, self.bmod)
        self.cp(self.ones_bf.t[:], self.cst.t[:, 256:384], [self.cst], [self.ones_bf])
        self.cp(self.iota_bf.t[:], self.cst.t[:, 128:256], [self.cst], [self.iota_bf])
        self.act(self.cv.t[:], self.cv.t[:], AF.Silu, [self.cv], [self.cv])
        nlay = 1 if self.mode == "p1" else 2
        NB = 256
        wmr = [Tl(self.arena.t[:, i * 4096:(i + 1) * 4096].rearrange("p (k n) -> p k n", k=KC), f"wm{i}")
               for i in range(3)]
        wslots = [S.slot(f"wm{i}") for i in range(3)]
        cnt = 0
        for i in range(nlay):
            nblk = (3 * D // NB) if self.mode == "p1" else (6 * D // NB)
            for nb in range(nblk):
                wt = wmr[cnt % 3]
                src = self.w_mod[i, :, nb * NB:(nb + 1) * NB].rearrange("(k p) n -> p k n", p=P)
                self.dma(wt.t, src, wslots[cnt % 3], writes=[wt])
                cnt += 1
                for mi in range(NB // 128):
                    m = nb * (NB // 128) + mi
                    for k in range(KC):
                        self.mm(self.psM.t[:, 2 * m:2 * m + 2], wt.t[:, k, mi * 128:(mi + 1) * 128],
                                self.cv.t[:, k, :], k == 0, k == KC - 1, [wt, self.cv], [self.psM])
            nm = nblk * NB // 128
            self.tt(self.modt[i].t[:, 0:nm, :], self.psM.t[:, 0:2 * nm].rearrange("p (m c) -> p m c", c=2),
                    self.bmod.t[:, i, 0:nm].unsqueeze(2).broadcast_to([P, nm, 2]), ALU.add,
                    [self.psM, self.bmod], [self.modt[i]])
        dv = self.dv
        for i in range(nlay):
            self.stt(dv.t[:, 2 * i + 0, :], self.modt[i].t[:, 16:32, 0], 1.0, self.vec.t[:, 2 * i + 0, :],
                     ALU.add, ALU.mult, [self.modt[i], self.vec], [dv])
            if self.mode != "p1":
                self.stt(dv.t[:, 2 * i + 1, :], self.modt[i].t[:, 64:80, 0], 1.0, self.vec.t[:, 2 * i + 1, :],
                         ALU.add, ALU.mult, [self.modt[i], self.vec], [dv])
        self.stt(dv.t[:, 8, :], self.modt[0].t[:, 16:32, 1], 1.0, self.vec.t[:, 0, :],
                 ALU.add, ALU.mult, [self.modt[0], self.vec], [dv])
        self.act(dv.t[:, 9:11, :], self.vec.t[:, 14:16, :], AF.Exp, [self.vec], [dv], scale=-1.0)
        self.act(dv.t[:, 9:11, :], dv.t[:, 9:11, :], AF.Ln, [dv], [dv], bias=1.0)
        self.ts(dv.t[:, 9:11, :], dv.t[:, 9:11, :], -8.0, None, ALU.mult, ALU.bypass, [dv], [dv])
        self.ts(dv.t[:, 11:13, :], dv.t[:, 9:11, :], 2.0, None, ALU.mult, ALU.bypass, [dv], [dv])
        self.ts(dv.t[:, 14:16, :], self.vec.t[:, 10:12, :], -1.0, None, ALU.mult, ALU.bypass, [self.vec], [dv])
        self.ts(dv.t[:, 16:18, :], self.vec.t[:, 12:14, :], -1.0, None, ALU.mult, ALU.bypass, [self.vec], [dv])
        self.ts(dv.t[:, 18:20, :], dv.t[:, 9:11, :], 0.5, None, ALU.mult, ALU.bypass, [dv], [dv])
        self.ts(dv.t[:, 20:22, :], dv.t[:, 9:11, :], 0.5 * T, None, ALU.mult, ALU.bypass, [dv], [dv])
        self.vop(lambda h: h.memset(dv.t[:, 13, :], 0.0), [], [dv], eng="pool")
        if self.mode != "p1":
            ld1(self.cmk.t[:], self.cmask, self.cmk)
            if self.mode == "p2":
                ld1(self.call.t[:], self.carry_all, self.call)
                ld1(self.carry.t[:], self.carry_own, self.carry)
        self.S.barrier()

    def A1(self, i, k): return self.dv.t[:, 2 * i, k:k + 1]
    def A2(self, i, k): return self.dv.t[:, 2 * i + 1, k:k + 1]
    def S1(self, i, k): return self.modt[i].t[:, 0 + k, 0:1]
    def G1(self, i, k): return self.modt[i].t[:, 32 + k, 0:1]
    def S2(self, i, k): return self.modt[i].t[:, 48 + k, 0:1]
    def G2(self, i, k): return self.modt[i].t[:, 80 + k, 0:1]
    def V(self, idx, k): return self.vec.t[:, idx, k:k + 1]

    def precast(self, which):
        S = self.S
        NST = 3
        fin = [Tl(self.arena.t[:, i * 2048:(i + 1) * 2048], f"pcin{i}") for i in range(NST)]
        fob = [Tl(self.arena.t[:, 6144 + i * 1024:6144 + (i + 1) * 1024].bitcast(BF16), f"pcout{i}")
               for i in range(NST)]
        sl_in = [S.slot(f"pci{i}") for i in range(NST)]
        sl_out = [S.slot(f"pco{i}") for i in range(NST)]
        engs = ["dve", "act", "pool"]
        jobs = []

        def add_W(W, key, cc_lo, cc_hi, base=0):
            dst, dbuf = self.ws[key]
            for cc in range(cc_lo, cc_hi):
                src = W[:, cc * 128:(cc + 1) * 128].rearrange("(k p) n -> p k n", p=P)
                jobs.append((src, dst[cc - base], dbuf, 16))

        def add_rows(R, key):
            dst, dbuf = self.ws[key]
            for c in range(NEXP_C):
                jobs.append((R[c], dst[c], dbuf, 0))
        for w in which:
            if w == "w_in_xb":
                add_W(self.lru_w_in, "w_in", 16, 32)
            elif w == "w_in":
                add_W(self.lru_w_in, "w_in", 0, 32)
            elif w == "wab":
                dst, dbuf = self.ws["wab"]
                for j in range(4):
                    jobs.append((self.wab_in[4 * j:4 * j + 4].rearrange("h p f -> p h f"),
                                 dst[4 * j:4 * j + 4].rearrange("h p f -> p h f"), dbuf, 4))
            elif w == "w_out":
                add_W(self.lru_w_out, "w_out", 0, 16)
            elif w == "sc_w_in":
                add_W(self.sc_w_in, "sc_w_in", 0, 48)
            elif w == "sc_w_out":
                add_W(self.sc_w_out, "sc_w_out", 0, 16)
            elif w in ("wq0", "wq1"):
                add_W(self.peer_w_q[int(w[2])], w, 0, 16)
            elif w in ("u0", "u1"):
                add_rows(self.u_l[int(w[1])], w)
            elif w in ("v0", "v1"):
                add_rows(self.v_l[int(w[1])], w)
        self.pc_jobs = jobs
        self.pc_n = 0
        self.pc_in = 0
        self.pc_st = (fin, fob, sl_in, sl_out, NST)
        self.pc_st_bg = ([S.slot(f"pcib{i}") for i in range(NST)], [S.slot(f"pcob{i}") for i in range(NST)])

    def precast_run(self, k, bg):
        fin, fob, sl_in, sl_out, NST = self.pc_st
        if bg:
            sl_in, sl_out = self.pc_st_bg
        jobs = self.pc_jobs
        engs = ["pool"] if bg else ["dve", "act", "pool"]
        q = "pool" if bg else "sp"
        stop = min(self.pc_n + k, len(jobs))
        while self.pc_n < stop:
            n = self.pc_n
            while self.pc_in < min(n + NST, len(jobs)):
                m = self.pc_in
                src, dst, dbuf, is3 = jobs[m]
                i = m % NST
                o = fin[i].t.rearrange("p (k n) -> p k n", k=is3) if is3 else fin[i].t
                self.dma(o, src, sl_in[i], writes=[fin[i]], q=q)
                self.pc_in += 1
            src, dst, dbuf, is3 = jobs[n]
            i = n % NST
            e = engs[n % len(engs)]
            if e == "act":
                self.act(fob[i].t, fin[i].t, AF.Copy, [fin[i]], [fob[i]])
            else:
                self.cp(fob[i].t, fin[i].t, [fin[i]], [fob[i]], eng=e)
            oo = fob[i].t.rearrange("p (k n) -> p k n", k=4) if is3 == 4 else fob[i].t
            self.dma(dst, oo, sl_out[i], reads=[fob[i]], writes=[dbuf], q=q)
            self.pc_n += 1

    def precast_finish(self):
        self.precast_run(len(self.pc_jobs), bg=False)
        self.S.barrier()
        S = self.S
        if self.mode != "p1" and self.has_peer:
            for i in range(2):
                st = Tl(self.arena.t[:, i * 2048:(i + 1) * 2048], f"sks{i}")
                self.ld1(st.t, self.skT_in[i].rearrange("p a b -> p (a b)"), st)
                self.cp(self.skT[i].t[:].rearrange("p a b -> p (a b)"), st.t, [st], [self.skT[i]])

    def wstream(self, key, ccs):
        dst, dbuf = self.ws[key]
        return Stream(self, self.wring, [(dst[cc], dbuf) for cc in ccs])

    def norm_mod(self, xsrc, n, Afn, Sfn, out_tl, out_is_f32=False):
        sq = self.bfA
        self.act(sq.t[:, :, 0:n], xsrc.t[:, :, 0:n], AF.Square, [xsrc], [sq])
        pm = self.psM
        for k in range(KC):
            self.mm(pm.t[:, 0:n], self.ones_bf.t[:], sq.t[:, k, 0:n], k == 0, k == KC - 1,
                    [self.ones_bf, sq], [pm])
        r = self.rstd
        self.act(r.t[:, 0:n], pm.t[:, 0:n], AF.Ln, [pm], [r], scale=1.0 / D, bias=EPS)
        self.act(r.t[:, 0:n], r.t[:, 0:n], AF.Exp, [r], [r], scale=-0.5)
        for k in range(KC):
            tm = self.tmpr.next()
            self.tt(tm.t[:, 0:n], xsrc.t[:, k, 0:n], r.t[:, 0:n], ALU.mult, [xsrc, r], [tm])
            self.act(out_tl.t[:, k, 0:n], tm.t[:, 0:n], AF.Identity, [tm], [out_tl], scale=Afn(k), bias=Sfn(k))

    def lru_block(self, xsrc, n, rowlen, Afn, Sfn, pass2, sbi, hin_f=None, hin_b=None, want_carry=True):
        self.norm_mod(xsrc, n, Afn, Sfn, self.hT)
        ccs = (list(range(16)) if pass2 else []) + list(range(16, 32))
        wsr = self.wstream("w_in", ccs)
        wabs = Stream(self, self.wabring, [(self.ws["wab"][0][h].rearrange("p (a b) -> p a b", a=4), self.ws["wab"][1])
                                           for h in range(16)])
        YT = self.bfA
        if pass2:
            for h in range(16):
                wt = wsr.get()
                pg = self.psring.next()
                for k in range(KC):
                    self.mm(pg.t[:, 0:n], wt.t[:, k * 128:(k + 1) * 128], self.hT.t[:, k, 0:n], k == 0, k == KC - 1,
                            [wt, self.hT], [pg])
                self.act(YT.t[:, h, 0:n], pg.t[:, 0:n], AF.Gelu_apprx_tanh, [pg], [YT])
        LN_HALF = math.log(0.5)
        for h in range(16):
            wabt = wabs.get()
            wt = wsr.get()
            px = self.psring.next()
            for k in range(KC):
                self.mm(px.t[:, 0:n], wt.t[:, k * 128:(k + 1) * 128], self.hT.t[:, k, 0:n], k == 0, k == KC - 1,
                        [wt, self.hT], [px])
            xc = self.tmpr.next()
            self.act(xc.t[:, 0:n], px.t[:, 0:n], AF.Identity, [px], [xc], scale=self.V(7, h), bias=self.V(9, h))
            pv = px.t[:, 0:n].rearrange("p (r c) -> p r c", c=rowlen)
            xv = xc.t[:, 0:n].rearrange("p (r c) -> p r c", c=rowlen)
            for tap, off in ((6, -1), (5, -2), (8, 1)):
                if off < 0:
                    o_, i_ = xv[:, :, -off:], pv[:, :, :rowlen + off]
                else:
                    o_, i_ = xv[:, :, :rowlen - off], pv[:, :, off:]
                self.stt(o_, i_, self.V(tap, h), o_, ALU.mult, ALU.add, [px, xc], [xc])
            xcb = self.tmpb.next()
            self.act(xcb.t[:, 0:n], xc.t[:, 0:n], AF.Copy, [xc], [xcb])
            hs = []
            for d in range(2):
                pr = self.psring.next()
                self.mm(pr.t[:, 0:n], wabt.t[:, d * 2 + 0, :], xcb.t[:, 0:n], True, True, [wabt, xcb], [pr])
                pi = self.psring.next()
                self.mm(pi.t[:, 0:n], wabt.t[:, d * 2 + 1, :], xcb.t[:, 0:n], True, True, [wabt, xcb], [pi])
                rr = self.tmpr.next()
                self.act(rr.t[:, 0:n], pr.t[:, 0:n], AF.Exp, [pr], [rr], scale=-1.0, bias=self.dv.t[:, 14 + d, h:h + 1])
                ii = self.tmpr.next()
                self.act(ii.t[:, 0:n], pi.t[:, 0:n], AF.Exp, [pi], [ii], scale=-1.0, bias=self.dv.t[:, 16 + d, h:h + 1])
                self.ts(rr.t[:, 0:n], rr.t[:, 0:n], 1.0, None, ALU.add, ALU.bypass, [rr], [rr])
                self.vop(lambda hd, rr=rr: hd.reciprocal(out=rr.t[:, 0:n], in_=rr.t[:, 0:n]), [rr], [rr])
                self.act(ii.t[:, 0:n], ii.t[:, 0:n], AF.Ln, [ii], [ii], bias=1.0)
                self.act(ii.t[:, 0:n], ii.t[:, 0:n], AF.Exp, [ii], [ii], scale=-1.0)
                aa = self.tmpr.next()
                self.act(aa.t[:, 0:n], rr.t[:, 0:n], AF.Exp, [rr], [aa], scale=self.dv.t[:, 9 + d, h:h + 1])
                a2 = self.tmpr.next()
                self.act(a2.t[:, 0:n], rr.t[:, 0:n], AF.Exp, [rr], [a2], scale=self.dv.t[:, 11 + d, h:h + 1])
                self.act(a2.t[:, 0:n], a2.t[:, 0:n], AF.Ln, [a2], [a2], scale=-1.0, bias=1.0)
                self.act(a2.t[:, 0:n], a2.t[:, 0:n], AF.Exp, [a2], [a2], scale=0.5)
                self.tt(ii.t[:, 0:n], ii.t[:, 0:n], xc.t[:, 0:n], ALU.mult, [ii, xc], [ii])
                self.tt(ii.t[:, 0:n], ii.t[:, 0:n], a2.t[:, 0:n], ALU.mult, [ii, a2], [ii])
                first = 0 if d == 0 else n - 1
                hinp = hin_f if d == 0 else hin_b
                if hinp is not None:
                    self.stt(ii.t[:, first:first + 1], aa.t[:, first:first + 1], hinp(h), ii.t[:, first:first + 1],
                             ALU.mult, ALU.add, [aa, ii, self.hin], [ii])
                hh = self.tmpr.next()
                if d == 0:
                    o_, a_, b_ = hh.t[:, 0:n], aa.t[:, 0:n], ii.t[:, 0:n]
                else:
                    o_, a_, b_ = hh.t[:, 0:n][:, ::-1], aa.t[:, 0:n][:, ::-1], ii.t[:, 0:n][:, ::-1]
                self.vop(lambda hd, o_=o_, a_=a_, b_=b_: hd.tensor_tensor_scan(out=o_, data0=a_, data1=b_, initial=0.0,
                                                                           op0=ALU.mult, op1=ALU.add),
                         [aa, ii], [hh])
                if want_carry:
                    last = n - 1 if d == 0 else 0
                    ctl = self.carry
                    self.vop(lambda hd, rr=rr, d=d, h=h, ctl=ctl: hd.tensor_reduce(out=ctl.t[:, sbi, 2 * d, h:h + 1],
                                                                                 in_=rr.t[:, 0:n], axis=AX.X, op=ALU.add),
                             [rr], [ctl])
                    self.cp(ctl.t[:, sbi, 2 * d + 1, h:h + 1], hh.t[:, last:last + 1], [hh], [ctl])
                hs.append(hh)
            if pass2:
                self.tt(hs[0].t[:, 0:n], hs[0].t[:, 0:n], hs[1].t[:, 0:n], ALU.add, [hs[0], hs[1]], [hs[0]])
                self.tt(YT.t[:, h, 0:n], YT.t[:, h, 0:n], hs[0].t[:, 0:n], ALU.mult, [hs[0], YT], [YT])
        if pass2:
            self.out_proj("w_out", YT, n, 0)

    def out_proj(self, key, YT, n, layer):
        wsr = self.wstream(key, list(range(16)))
        for m in range(16):
            wt = wsr.get()
            po = self.psring.next()
            for k in range(KC):
                self.mm(po.t[:, 0:n], wt.t[:, k * 128:(k + 1) * 128], YT.t[:, k, 0:n], k == 0, k == KC - 1,
                        [wt, YT], [po])
            self.stt(self.xres.t[:, m, 0:n], po.t[:, 0:n], self.G1(layer, m), self.xres.t[:, m, 0:n],
                     ALU.mult, ALU.add, [po, self.modt[layer], self.xres], [self.xres])

    def sc_block(self, n):
        self.norm_mod(self.xres, n, lambda k: self.A1(1, k), lambda k: self.S1(1, k), self.hT)
        ccs = []
        for ch in range(16):
            ccs += [ch, 16 + ch, 32 + ch]
        wsr = self.wstream("sc_w_in", ccs)
        YT = self.bfA
        rowlen = 64
        for ch in range(16):
            pp = []
            for j in range(3):
                wt = wsr.get()
                p_ = self.psring.next()
                for k in range(KC):
                    self.mm(p_.t[:, 0:n], wt.t[:, k * 128:(k + 1) * 128], self.hT.t[:, k, 0:n], k == 0, k == KC - 1,
                            [wt, self.hT], [p_])
                pp.append(p_)
            vs = self.tmpr.next()
            self.act(vs.t[:, 0:n], pp[2].t[:, 0:n], AF.Copy, [pp[2]], [vs])
            cvt = self.tmpr.next()
            self.tt(cvt.t[:, 0:n], pp[1].t[:, 0:n], vs.t[:, 0:n], ALU.mult, [pp[1], vs], [cvt])
            yc = self.tmpr.next()
            self.act(yc.t[:, 0:n], cvt.t[:, 0:n], AF.Identity, [cvt], [yc], scale=self.V(17, ch), bias=self.V(19, ch))
            cv_ = cvt.t[:, 0:n].rearrange("p (r c) -> p r c", c=rowlen)
            yv = yc.t[:, 0:n].rearrange("p (r c) -> p r c", c=rowlen)
            for tap, off in ((16, -1), (18, 1)):
                if off < 0:
                    o_, i_ = yv[:, :, -off:], cv_[:, :, :rowlen + off]
                else:
                    o_, i_ = yv[:, :, :rowlen - off], cv_[:, :, off:]
                self.stt(o_, i_, self.V(tap, ch), o_, ALU.mult, ALU.add, [cvt, yc], [yc])
            self.tt(YT.t[:, ch, 0:n], pp[0].t[:, 0:n], yc.t[:, 0:n], ALU.mult, [pp[0], yc], [YT])
        self.out_proj("sc_w_out", YT, n, 1)

    def peer_block(self, layer):
        n = T
        self.norm_mod(self.xres, n, lambda k: self.A2(layer, k), lambda k: self.S2(layer, k), self.hT)
        qT = self.bfA
        wsr = self.wstream(f"wq{layer}", list(range(16)))
        for hp in range(16):
            wt = wsr.get()
            pq = self.psring.next()
            for k in range(KC):
                self.mm(pq.t[:, 0:n], wt.t[:, k * 128:(k + 1) * 128], self.hT.t[:, k, 0:n], k == 0, k == KC - 1,
                        [wt, self.hT], [pq])
            self.act(qT.t[:, hp, 0:n], pq.t[:, 0:n], AF.Copy, [pq], [qT])
        for ts_ in range(T // 128):
            self.topk_sub(layer, qT, ts_)
            self.scatter_sub(ts_)
        self.sweep(layer)

    def topk_sub(self, layer, qT, ts_):
        t0 = ts_ * 128
        skT = self.skT[layer]
        s_sb = Tl(self.X1.t[:].rearrange("p (a b) -> p a b", b=128), "x"); s_sb.b = self.X1.b
        s_wk = Tl(self.X2.t[:].rearrange("p (a b) -> p a b", b=128), "x"); s_wk.b = self.X2.b
        for hp in range(16):
            pS = self.psS[hp // 4]
            self.mm(pS.t[:, (hp % 4) * 128:(hp % 4 + 1) * 128], qT.t[:, hp, t0:t0 + 128], skT.t[:, hp, :], True, True,
                    [qT, skT], [pS])
        for b4 in range(4):
            self.act(s_sb.t[:, b4 * 4:(b4 + 1) * 4, :].rearrange("p a b -> p (a b)"), self.psS[b4].t[:], AF.Copy,
                     [self.psS[b4]], [s_sb])
        sv, si_u = self.sv, self.si_u
        for hp in range(16):
            self.vop(lambda h, hp=hp: h.max(out=sv.t[:, hp, 0:8], in_=s_sb.t[:, hp, :]), [s_sb], [sv])
        for hp in range(16):
            self.vop(lambda h, hp=hp: h.max_index(out=si_u.t[:, hp, 0:8], in_max=sv.t[:, hp, 0:8],
                                                  in_values=s_sb.t[:, hp, :]), [s_sb, sv], [si_u])
        for hp in range(16):
            self.vop(lambda h, hp=hp: h.match_replace(out=s_wk.t[:, hp, :], in_to_replace=sv.t[:, hp, 0:8],
                                                      in_values=s_sb.t[:, hp, :], imm_value=NEG), [s_sb, sv], [s_wk])
        for hp in range(16):
            self.vop(lambda h, hp=hp: h.max(out=sv.t[:, hp, 8:16], in_=s_wk.t[:, hp, :]), [s_wk], [sv])
        for hp in range(16):
            self.vop(lambda h, hp=hp: h.max_index(out=si_u.t[:, hp, 8:16], in_max=sv.t[:, hp, 8:16],
                                                  in_values=s_wk.t[:, hp, :]), [s_wk, sv], [si_u])
        self.cp(self.si_f.t[:], si_u.t[:], [si_u], [self.si_f])
        cand = Tl(self.X1.t[:].rearrange("p (h a b) -> p h a b", h=8, a=16), "x"); cand.b = self.X1.b
        candw = Tl(self.X2.t[:].rearrange("p (h c) -> p h c", h=8), "x"); candw.b = self.X2.b
        self.tt(cand.t, sv.t[:, 0::2, :].unsqueeze(3).broadcast_to([P, 8, 16, 16]),
                sv.t[:, 1::2, :].unsqueeze(2).broadcast_to([P, 8, 16, 16]), ALU.add, [sv], [cand])
        c2 = Tl(self.X1.t[:].rearrange("p (h c) -> p h c", h=8), "x"); c2.b = self.X1.b
        cvv, ci_u = self.cvv, self.ci_u
        for h_ in range(8):
            self.vop(lambda h, h_=h_: h.max(out=cvv.t[:, h_, 0:8], in_=c2.t[:, h_, :]), [c2], [cvv])
        for h_ in range(8):
            self.vop(lambda h, h_=h_: h.max_index(out=ci_u.t[:, h_, 0:8], in_max=cvv.t[:, h_, 0:8],
                                                  in_values=c2.t[:, h_, :]), [c2, cvv], [ci_u])
        for h_ in range(8):
            self.vop(lambda h, h_=h_: h.match_replace(out=candw.t[:, h_, :], in_to_replace=cvv.t[:, h_, 0:8],
                                                      in_values=c2.t[:, h_, :], imm_value=NEG), [c2, cvv], [candw])
        for h_ in range(8):
            self.vop(lambda h, h_=h_: h.max(out=cvv.t[:, h_, 8:16], in_=candw.t[:, h_, :]), [candw], [cvv])
        for h_ in range(8):
            self.vop(lambda h, h_=h_: h.max_index(out=ci_u.t[:, h_, 8:16], in_max=cvv.t[:, h_, 8:16],
                                                  in_values=candw.t[:, h_, :]), [candw, cvv], [ci_u])
        self.ts(self.ca_u.t[:], ci_u.t[:], 4, None, ALU.logical_shift_right, ALU.bypass, [ci_u], [self.ca_u])
        self.ts(self.cb_u.t[:], ci_u.t[:], 15, None, ALU.bitwise_and, ALU.bypass, [ci_u], [self.cb_u])
        self.cp(self.caf.t[:], self.ca_u.t[:], [self.ca_u], [self.caf])
        self.cp(self.cbf.t[:], self.cb_u.t[:], [self.cb_u], [self.cbf])
        oh = Tl(self.X1.t[:].rearrange("p (h k a) -> p h k a", h=8, k=16), "x"); oh.b = self.X1.b
        io16 = self.iota[:, 0:16].unsqueeze(1).unsqueeze(1).broadcast_to([P, 8, 16, 16])
        for j, (cf, par) in enumerate(((self.caf, 0), (self.cbf, 1))):
            self.tt(oh.t, cf.t[:].unsqueeze(3).broadcast_to([P, 8, 16, 16]), io16, ALU.is_equal, [cf, self.cst], [oh])
            self.tt(oh.t, oh.t, self.si_f.t[:, par::2, :].unsqueeze(2).broadcast_to([P, 8, 16, 16]), ALU.mult,
                    [oh, self.si_f], [oh])
            self.vop(lambda h, j=j: h.tensor_reduce(out=self.slot3.t[:, j, :].rearrange("p (h k) -> p h k", h=8),
                                                    in_=oh.t, axis=AX.X, op=ALU.add), [oh], [self.slot3])
        zs = self.zs
        g3 = self.slot3.t[:, 2, :].rearrange("p (h k) -> p h k", h=8)
        self.tt(g3, cvv.t[:], cvv.t[:, :, 0:1].broadcast_to([P, 8, 16]), ALU.subtract, [cvv], [self.slot3])
        self.act(g3, g3, AF.Exp, [self.slot3], [self.slot3])
        self.vop(lambda h: h.tensor_reduce(out=zs.t[:, 0, :], in_=g3, axis=AX.X, op=ALU.add), [self.slot3], [zs])
        self.vop(lambda h: h.reciprocal(out=zs.t[:, 1, :], in_=zs.t[:, 0, :]), [zs], [zs])
        self.tt(g3, g3, zs.t[:, 1, :].unsqueeze(2).broadcast_to([P, 8, 16]), ALU.mult, [self.slot3, zs], [self.slot3])
        pm = self.psM
        for j in range(3):
            self.vop(lambda h, j=j: h.transpose(out=pm.t[:, j * 128:(j + 1) * 128], in_=self.slot3.t[:, j, :],
                                                identity=self.ident), [self.slot3, self.cst], [pm], eng="pe")
        self.cp(self.slotT.t[:, :, ts_ * 128:(ts_ + 1) * 128], pm.t[:, 0:384].rearrange("p (j t) -> p j t", j=3),
                [pm], [self.slotT])

    def scatter_sub(self, ts_):
        TG = 32
        WT = self.WT
        for g_ in range(128 // TG):
            t0 = ts_ * 128 + g_ * TG
            Bt = Tl(self.X1.t[:].bitcast(BF16).rearrange("p (t i) -> p t i", i=128)[:, 0:TG, :], "x"); Bt.b = self.X1.b
            At = Tl(self.X2.t[:].bitcast(BF16).rearrange("p (t i) -> p t i", i=128)[:, 0:TG, :], "x"); At.b = self.X2.b
            io = self.iota.unsqueeze(1).broadcast_to([P, TG, 128])
            i1v = self.slotT.t[:, 0, t0:t0 + TG].unsqueeze(2).broadcast_to([P, TG, 128])
            i2v = self.slotT.t[:, 1, t0:t0 + TG].unsqueeze(2).broadcast_to([P, TG, 128])
            gv = self.slotT.t[:, 2, t0:t0 + TG].unsqueeze(2).broadcast_to([P, TG, 128])
            self.tt(Bt.t, io, i2v, ALU.is_equal, [self.cst, self.slotT], [Bt])
            self.tt(At.t, io, i1v, ALU.is_equal, [self.cst, self.slotT], [At])
            self.tt(At.t, At.t, gv, ALU.mult, [At, self.slotT], [At], eng="pool")
            for q4 in range(TG // 4):
                pw = self.psS[(g_ * (TG // 4) + q4) % 4]
                for tt_ in range(4):
                    tl = q4 * 4 + tt_
                    self.mm(pw.t[:, tt_ * 128:(tt_ + 1) * 128], Bt.t[:, tl, :], At.t[:, tl, :], True, True,
                            [Bt, At], [pw])
                ta = t0 + q4 * 4
                src_ = pw.t[:].rearrange("p (t i) -> p i t", t=4)
                if q4 % 2:
                    self.act(WT.t[:, :, ta:ta + 4], src_, AF.Copy, [pw], [WT])
                else:
                    self.cp(WT.t[:, :, ta:ta + 4], src_, [pw], [WT])

    def sweep(self, layer):
        n = T
        PC = 4
        WT = self.WT
        udst, ubuf = self.ws[f"u{layer}"]
        vdst, vbuf = self.ws[f"v{layer}"]
        ust = Stream(self, self.wring, [(udst[c], ubuf) for c in range(NEXP_C)])
        vst = Stream(self, self.vring, [(vdst[c0:c0 + PC].rearrange("c p d -> p c d"), vbuf)
                                        for c0 in range(0, NEXP_C, PC)], queue="pool")
        xb_ = self.xres.b
        subs = []
        for m in range(16):
            sb_ = Buf(f"xres_m{m}")
            sb_.writers = dict(xb_.writers)
            sb_.readers = list(xb_.readers)
            subs.append(sb_)
        modb = self.modt[layer].b
        wb_ = WT.b
        wtb = []
        for c in range(NEXP_C):
            b_ = Buf(f"wt_c{c}")
            b_.writers = dict(wb_.writers)
            b_.readers = list(wb_.readers)
            wtb.append(b_)

        def u_side(part):
            for cc in range(PC):
                c = part * PC + cc
                ut = ust.get()
                pu = self.psring.next()
                for k in range(KC):
                    self.mm(pu.t[:, 0:n], ut.t[:, k * 128:(k + 1) * 128], self.hT.t[:, k, 0:n], k == 0, k == KC - 1,
                            [ut, self.hT], [pu])
                ab = self.tmpb.next()
                self.act(ab.t[:, 0:n], pu.t[:, 0:n], AF.Gelu_apprx_tanh, [pu], [ab])
                self.S.op("dve", lambda h, c=c, ab=ab: h.tensor_tensor(out=WT.t[:, c, :], in0=WT.t[:, c, :],
                                                                         in1=ab.t[:, 0:n], op=ALU.mult),
                          [wtb[c], ab.b], [wtb[c]])

        def v_side(part):
            vt = vst.get()
            for m in range(16):
                pv = self.psring.next()
                for cc in range(PC):
                    c = part * PC + cc
                    self.S.op("pe", lambda h, pv=pv, vt=vt, cc=cc, m=m, c=c: h.matmul(
                        pv.t[:, 0:n], lhsT=vt.t[:, cc, m * 128:(m + 1) * 128], rhs=WT.t[:, c, :],
                        start=(cc == 0), stop=(cc == PC - 1)), [vt.b, wtb[c]], [pv.b])
                xo = self.xres.t[:, m, 0:n]
                if m % 2 == 0:
                    self.S.op("dve", lambda h, xo=xo, pv=pv, m=m: h.scalar_tensor_tensor(
                        out=xo, in0=pv.t[:, 0:n], scalar=self.G2(layer, m), in1=xo, op0=ALU.mult, op1=ALU.add),
                        [pv.b, modb, subs[m]], [subs[m]])
                else:
                    tm = self.tmpr.next()
                    self.act(tm.t[:, 0:n], pv.t[:, 0:n], AF.Identity, [pv], [tm], scale=self.G2(layer, m))
                    self.S.op("pool", lambda h, xo=xo, tm=tm: h.tensor_tensor(out=xo, in0=xo, in1=tm.t[:, 0:n], op=ALU.add),
                              [tm.b, subs[m]], [subs[m]])

        nparts = NEXP_C // PC
        u_side(0)
        for part in range(nparts):
            if part + 1 < nparts:
                u_side(part + 1)
            v_side(part)
        for sb_ in subs:
            for k_, tok in sb_.writers.items():
                if k_ not in xb_.writers or xb_.writers[k_][1] < tok[1]:
                    xb_.writers[k_] = tok
            xb_.readers.extend(sb_.readers)
        for b_ in wtb:
            for k_, tok in b_.writers.items():
                if k_ not in wb_.writers or wb_.writers[k_][1] < tok[1]:
                    wb_.writers[k_] = tok
            wb_.readers.extend(b_.readers)

    def pass1(self, xsrc, base):
        S = self.S
        if not hasattr(self, "ldx1"):
            self.ldx1 = S.slot("ldx1")
        ldx = self.ldx1
        for sbi in range(NSB):
            self.dma(self.xres.t[:], xsrc[:, :, base + sbi * T:base + (sbi + 1) * T], ldx, writes=[self.xres])
            self.lru_block(self.xres, T, 64, lambda k: self.A1(0, k), lambda k: self.S1(0, k), False, sbi)
            if getattr(self, "bg_per_block", 0):
                self.precast_run(self.bg_per_block, bg=True)
        self.S.barrier()
        self.chunk_carry()
        if self.mode == "p1":
            so = S.slot("st_carry")
            self.dma(self.carry_out, self.carry.t[:], so, reads=[self.carry])
        self.S.barrier()

    def exchange(self):
        S = self.S
        cc_in = self.dscr("cc_in", [P, 4 * KC], F32)
        cc_out = self.dscr("cc_out", [NCORE * P, 4 * KC], F32)
        bi, bo = Buf("cc_in"), Buf("cc_out")
        s1, s2, s3 = S.slot("cc1"), S.slot("cc2"), S.slot("cc3")
        self.dma(cc_in, self.carry.t[:, NSB, :, :].rearrange("p a b -> p (a b)"), s1, reads=[self.carry], writes=[bi])
        S.custom("pool", lambda h: h.collective_compute("AllGather", ALU.bypass, replica_groups=[list(range(NCORE))],
                                                        ins=[cc_in], outs=[cc_out]), s2, reads=[bi], writes=[bo])
        self.dma(self.call.t[:].rearrange("p r a b -> p r (a b)"), cc_out.rearrange("(r p) f -> p r f", p=P), s3,
                 reads=[bo], writes=[self.call])
        self.S.barrier()

    def chunk_carry(self):
        c = self.carry
        for d in range(2):
            self.tt(c.t[:, 0:NSB, 2 * d, :], c.t[:, 0:NSB, 2 * d, :],
                    self.dv.t[:, 9 + d, :].unsqueeze(1).broadcast_to([P, NSB, KC]), ALU.mult, [c, self.dv], [c])
            self.act(c.t[:, 0:NSB, 2 * d, :], c.t[:, 0:NSB, 2 * d, :], AF.Exp, [c], [c])
        self.cp(c.t[:, NSB, 0, :], c.t[:, 0, 0, :], [c], [c])
        self.cp(c.t[:, NSB, 1, :], c.t[:, 0, 1, :], [c], [c])
        for sb in range(1, NSB):
            self.tt(c.t[:, NSB, 1, :], c.t[:, NSB, 1, :], c.t[:, sb, 0, :], ALU.mult, [c], [c])
            self.tt(c.t[:, NSB, 1, :], c.t[:, NSB, 1, :], c.t[:, sb, 1, :], ALU.add, [c], [c])
            self.tt(c.t[:, NSB, 0, :], c.t[:, NSB, 0, :], c.t[:, sb, 0, :], ALU.mult, [c], [c])
        self.cp(c.t[:, NSB, 2, :], c.t[:, NSB - 1, 2, :], [c], [c])
        self.cp(c.t[:, NSB, 3, :], c.t[:, NSB - 1, 3, :], [c], [c])
        for sb in range(NSB - 2, -1, -1):
            self.tt(c.t[:, NSB, 3, :], c.t[:, NSB, 3, :], c.t[:, sb, 2, :], ALU.mult, [c], [c])
            self.tt(c.t[:, NSB, 3, :], c.t[:, NSB, 3, :], c.t[:, sb, 3, :], ALU.add, [c], [c])
            self.tt(c.t[:, NSB, 2, :], c.t[:, NSB, 2, :], c.t[:, sb, 2, :], ALU.mult, [c], [c])

    def ctx_and_carries(self):
        NC_ = NSB
        ldc = self.S.slot("ld_ctx")
        self.dma(self.ctxx.t, self.ctxT, ldc, writes=[self.ctxx])
        saved = self.carry
        ctxc = self.sb("ctxcarry", [P, 1, 4, KC])
        self.carry = ctxc
        self.lru_block(self.ctxx, 256, 256, lambda k: self.dv.t[:, 8, k:k + 1],
                       lambda k: self.modt[0].t[:, k, 1:2], False, 0)
        self.carry = saved
        hin = self.hin
        call = self.call
        cm = self.cmk
        sm = self.small
        self.cp(hin.t[:, 0, 0, :], ctxc.t[:, 0, 1, :], [ctxc], [hin])
        self.cp(hin.t[:, 1, NSB, :], ctxc.t[:, 0, 3, :], [ctxc], [hin])
        ns_ = self.nslots
        for d, order in ((0, range(ns_)), (1, range(ns_ - 1, -1, -1))):
            hsl = hin.t[:, 0, 0, :] if d == 0 else hin.t[:, 1, NSB, :]
            for cp_ in order:
                m = cm.t[:, d, cp_:cp_ + 1]
                self.ts(sm.t[:, 0, :], call.t[:, cp_, 2 * d, :], -1.0, m, ALU.add, ALU.mult, [call, cm], [sm])
                self.ts(sm.t[:, 0, :], sm.t[:, 0, :], 1.0, None, ALU.add, ALU.bypass, [sm], [sm])
                self.ts(sm.t[:, 1, :], call.t[:, cp_, 2 * d + 1, :], m, None, ALU.mult, ALU.bypass, [call, cm], [sm])
                self.tt(hsl, hsl, sm.t[:, 0, :], ALU.mult, [hin, sm], [hin])
                self.tt(hsl, hsl, sm.t[:, 1, :], ALU.add, [hin, sm], [hin])
        c = self.carry
        for sb in range(NSB):
            self.tt(hin.t[:, 0, sb + 1, :], hin.t[:, 0, sb, :], c.t[:, sb, 0, :], ALU.mult, [hin, c], [hin])
            self.tt(hin.t[:, 0, sb + 1, :], hin.t[:, 0, sb + 1, :], c.t[:, sb, 1, :], ALU.add, [hin, c], [hin])
        for sb in range(NSB - 1, -1, -1):
            self.tt(hin.t[:, 1, sb, :], hin.t[:, 1, sb + 1, :], c.t[:, sb, 2, :], ALU.mult, [hin, c], [hin])
            self.tt(hin.t[:, 1, sb, :], hin.t[:, 1, sb, :], c.t[:, sb, 3, :], ALU.add, [hin, c], [hin])
        self.S.barrier()

    def pass2(self):
        S = self.S
        ldx = S.slot("ldx")
        sto = S.slot("sto")
        for sbi in range(self.nsb_run):
            self.dma(self.xres.t[:], self.xT[:, :, sbi * T:(sbi + 1) * T], ldx, writes=[self.xres])
            self.lru_block(self.xres, T, 64, lambda k: self.A1(0, k), lambda k: self.S1(0, k), True, sbi,
                           hin_f=lambda h, sbi=sbi: self.hin.t[:, 0, sbi, h:h + 1],
                           hin_b=lambda h, sbi=sbi: self.hin.t[:, 1, sbi + 1, h:h + 1], want_carry=False)
            if self.dbg != "x1":
                self.peer_block(0)
            if self.dbg is None or self.dbg in ("x3", "x4"):
                self.sc_block(T)
            if self.dbg is None or self.dbg == "x4":
                self.peer_block(1)
            if self.dbg is None:
                self.norm_mod(self.xres, T, lambda k: self.V(4, k), lambda k: self.dv.t[:, 13, k:k + 1], self.outT)
                self.dma(self.yT[:, :, sbi * T:(sbi + 1) * T], self.outT.t[:], sto, reads=[self.outT])
            else:
                self.dma(self.yT[:, :, sbi * T:(sbi + 1) * T], self.xres.t[:], sto, reads=[self.xres])
        self.S.barrier()


def _fm(v):
    v = np.asarray(v, np.float32)
    return np.ascontiguousarray(np.moveaxis(v.reshape(v.shape[:-1] + (KC, P)), -1, 0))


def _consts():
    c = np.zeros((P, 384), np.float32)
    c[:, 0:128] = np.eye(P, dtype=np.float32)
    c[:, 128:256] = np.arange(128, dtype=np.float32)[None, :]
    c[:, 256:384] = 1.0
    return c


_NC_CACHE = {}


def _get_nc(mode, nsb_run=NSB, dbg=None):
    key = (mode, nsb_run, dbg)
    if key not in _NC_CACHE:
        _NC_CACHE[key] = KB(mode, nsb_run, dbg).build()
    return _NC_CACHE[key]


def _prep(inputs):
    f = lambda k: np.asarray(inputs[k], np.float32)
    x = f("x")
    shared = {}
    shared["consts"] = _consts()
    vec = np.zeros((NV, D), np.float32)
    vec[0] = f("norm_mix_g")[0]; vec[1] = f("norm_ffn_g")[0]
    vec[2] = f("norm_mix_g")[1]; vec[3] = f("norm_ffn_g")[1]
    vec[4] = f("norm_final_g")
    vec[5:9] = f("lru_conv_w")[0]; vec[9] = f("lru_conv_b")[0]
    vec[10:12] = f("lru_b_a")[0]; vec[12:14] = f("lru_b_x")[0]; vec[14:16] = f("lru_lambda")[0]
    vec[16:19] = f("sc_conv_w")[0]; vec[19] = f("sc_conv_b")[0]
    shared["vecs"] = _fm(vec)
    shared["w_mod"] = f("w_mod")
    shared["b_modT"] = np.ascontiguousarray(f("b_mod").reshape(2, 96, P).transpose(2, 0, 1))
    shared["lru_w_in"] = f("lru_w_in")[0]
    wab = np.stack([f("lru_w_a")[0], f("lru_w_x")[0]], axis=1)
    shared["wab"] = np.ascontiguousarray(wab.transpose(2, 3, 0, 1, 4)).reshape(16, P, 512)
    per_core = []
    for c in range(NCORE):
        b, j = divmod(c, 4)
        m = {}
        xs = x[b, j * NTOK:(j + 1) * NTOK]
        m["xT"] = np.ascontiguousarray(xs.reshape(NTOK, KC, P).transpose(2, 1, 0))
        m["cvec"] = np.ascontiguousarray(np.stack([_fm(f("c")[b]), _fm(f("c_ctx"))], axis=-1))
        per_core.append(m)
    return shared, per_core


def _prep2(inputs):
    f = lambda k: np.asarray(inputs[k], np.float32)
    sh = {}
    sh["lru_w_out"] = f("lru_w_out")[0]
    sh["sc_w_in"] = f("sc_w_in")[0]
    sh["sc_w_out"] = f("sc_w_out")[0]
    sh["peer_w_q"] = f("peer_w_q")
    sk = f("peer_sub_keys")
    sh["skT"] = np.ascontiguousarray(sk.reshape(2, 16, 128, 128).transpose(0, 3, 1, 2))
    u = f("peer_u")
    sh["u_l"] = np.ascontiguousarray(u.reshape(2, NEXP_C, P, KC, P).transpose(0, 1, 4, 3, 2)).reshape(2, NEXP_C, P, D)
    sh["v_l"] = f("peer_v").reshape(2, NEXP_C, P, D)
    ctx = f("ctx")
    pc = []
    for c in range(NCORE):
        b, j = divmod(c, 4)
        m = {}
        m["ctxT"] = np.ascontiguousarray(ctx[b].reshape(256, KC, P).transpose(2, 1, 0))
        cm = np.zeros((P, 2, NCORE), np.float32)
        for c2 in range(NCORE):
            b2, j2 = divmod(c2, 4)
            if b2 == b and j2 < j:
                cm[:, 0, c2] = 1.0
            if b2 == b and j2 > j:
                cm[:, 1, c2] = 1.0
        m["cmask"] = cm
        pc.append(m)
    return sh, pc


def _xT_chunk(x, b, j):
    xs = x[b, j * NTOK:(j + 1) * NTOK]
    return xs.reshape(NTOK, KC, P).transpose(2, 1, 0)


def kernel(**inputs):
    shared, pc = _prep(inputs)
    sh2, pc2 = _prep2(inputs)
    x = np.asarray(inputs["x"], np.float32)
    in_maps = []
    for c in range(NCORE):
        b, j = divmod(c, 4)
        others = [jj for jj in range(4) if jj != j]
        xTo = np.ascontiguousarray(np.concatenate([_xT_chunk(x, b, jj) for jj in others], axis=2))
        cm = np.zeros((P, 2, 3), np.float32)
        for s_, jj in enumerate(others):
            cm[:, 0, s_] = 1.0 if jj < j else 0.0
            cm[:, 1, s_] = 1.0 if jj > j else 0.0
        m = {**shared, **pc[c], **sh2, **pc2[c], "xTo": xTo, "cmask": cm}
        in_maps.append(m)
    nc = _get_nc("f")
    r = run_bass_kernel_spmd(nc, in_maps, core_ids=list(range(NCORE)))
    out = np.zeros((2, 4 * NTOK, D), np.float32)
    for c in range(NCORE):
        b, j = divmod(c, 4)
        yT = np.asarray(r.results[c]["yT"], np.float32)
        out[b, j * NTOK:(j + 1) * NTOK] = yT.transpose(2, 1, 0).reshape(NTOK, D)
    return out
```

```python
import math
import numpy as np
from contextlib import ExitStack
import concourse.bass as bass
import concourse.mybir as mybir
from concourse.bass_utils import run_bass_kernel_spmd

F32 = mybir.dt.float32
BF16 = mybir.dt.bfloat16
U32 = mybir.dt.uint32
ALU = mybir.AluOpType
AF = mybir.ActivationFunctionType
AX = mybir.AxisListType

P = 128
D = 2048
KC = 16
T = 256
NTOK = 4096
NSB = NTOK // T
NCORE = 8
NEXP_C = 128
EPS = 1e-6
NV = 20
NEG = -1.0e30


class Buf:
    __slots__ = ("name", "writers", "readers")

    def __init__(self, name=""):
        self.name = name
        self.writers = {}
        self.readers = []


class Eng:
    def __init__(self, name, sem, self_sync):
        self.name = name
        self.sem = sem
        self.count = 0
        self.known = {}
        self.self_sync = self_sync
        self.prog = []


class Sched:
    def __init__(self, nc, stack):
        self.nc = nc
        self.stack = stack
        self.engs = {}
        self.slots = []
        for name, ss in (("pe", False), ("act", True), ("dve", True), ("pool", True), ("sp", False)):
            self.engs[name] = Eng(name, self.new_sem("e_" + name), ss)
        self.ninst = 0

    def new_sem(self, name):
        return self.stack.enter_context(self.nc.semaphore(name))

    def _waits(self, E, reads, writes):
        need = {}

        def add(tok):
            sem, val, en = tok
            if en == E.name and not E.self_sync:
                return
            k = id(sem)
            if k not in need or need[k][1] < val:
                need[k] = (sem, val)
        for b in reads:
            for tok in b.writers.values():
                add(tok)
        for b in writes:
            for tok in b.writers.values():
                add(tok)
            for tok in b.readers:
                add(tok)
        for k, (sem, val) in need.items():
            if E.known.get(k, 0) < val:
                E.prog.append(("w", sem, val))
                E.known[k] = val

    def _mark(self, tok, key, reads, writes):
        for b in reads:
            b.readers.append(tok)
        for b in writes:
            b.writers[key] = tok
            b.readers = []

    def op(self, ename, build, reads=(), writes=()):
        E = self.engs[ename]
        self._waits(E, reads, writes)
        E.count += 1
        E.prog.append(("o", build, E.sem, 1))
        self._mark((E.sem, E.count, E.name), E.name, reads, writes)
        self.ninst += 1

    def slot(self, name):
        s = {"sem": self.new_sem(name), "n": 0, "name": name}
        self.slots.append(s)
        return s

    def dma(self, qname, out, in_, slot, reads=(), writes=()):
        E = self.engs[qname]
        self._waits(E, reads, writes)
        E.prog.append(("o", (lambda h: h.dma_start(out=out, in_=in_)), slot["sem"], 16))
        slot["n"] += 1
        self._mark((slot["sem"], 16 * slot["n"], "dma"), "dma_" + slot["name"], reads, writes)
        self.ninst += 1

    def custom(self, qname, build, slot, reads=(), writes=(), inc=16):
        E = self.engs[qname]
        self._waits(E, reads, writes)
        E.prog.append(("o", build, slot["sem"], inc))
        slot["n"] += 1
        self._mark((slot["sem"], inc * slot["n"], "dma"), "dma_" + slot["name"], reads, writes)
        self.ninst += 1

    def barrier(self):
        for E in self.engs.values():
            for Fe in self.engs.values():
                if Fe is E or Fe.count == 0:
                    continue
                k = id(Fe.sem)
                if E.known.get(k, 0) < Fe.count:
                    E.prog.append(("w", Fe.sem, Fe.count))
                    E.known[k] = Fe.count
            for s in self.slots:
                k = id(s["sem"])
                if s["n"] and E.known.get(k, 0) < 16 * s["n"]:
                    E.prog.append(("w", s["sem"], 16 * s["n"]))
                    E.known[k] = 16 * s["n"]

    def emit(self):
        with self.nc.Block() as block:
            def run(E):
                def f(h):
                    for it in E.prog:
                        if it[0] == "w":
                            h.wait_ge(it[1], it[2])
                        else:
                            it[1](h).then_inc(it[2], it[3])
                return f
            block.tensor(run(self.engs["pe"]))
            block.scalar(run(self.engs["act"]))
            block.vector(run(self.engs["dve"]))
            block.gpsimd(run(self.engs["pool"]))
            block.sync(run(self.engs["sp"]))


class Tl:
    def __init__(self, t, name):
        self.t = t
        self.b = Buf(name)


class Ring:
    def __init__(self, tiles):
        self.tiles = tiles
        self.i = 0

    def next(self):
        t = self.tiles[self.i % len(self.tiles)]
        self.i += 1
        return t


class Stream:
    def __init__(self, kb, ring, srcs, queue="sp"):
        self.kb = kb
        self.ring = ring
        self.srcs = srcs
        self.R = len(ring)
        for tl in ring:
            if not hasattr(tl, "slot"):
                tl.slot = kb.S.slot("r_" + tl.b.name)
        self.issued = 0
        self.taken = 0
        self.queue = queue

    def _issue(self):
        n = self.issued
        tl = self.ring[n % self.R]
        src, sbuf = self.srcs[n]
        self.kb.S.dma(self.queue, tl.t[:], src, tl.slot,
                      reads=[sbuf] if sbuf is not None else [], writes=[tl.b])
        self.issued += 1

    def get(self):
        while self.issued < min(self.taken + self.R, len(self.srcs)):
            self._issue()
        tl = self.ring[self.taken % self.R]
        self.taken += 1
        return tl


class KB:
    def __init__(self, mode, nsb_run=NSB, dbg=None):
        self.mode = mode
        self.nsb_run = nsb_run
        self.dbg = dbg
        self.nc = bass.Bass("TRN2", target_bir_lowering=False)
        self.st = ExitStack()
        full = ["w_in", "wab", "w_out", "sc_w_in", "sc_w_out", "wq0", "wq1", "u0", "v0", "u1", "v1"]
        sub = {"s_setup": [], "x1": full[:3], "s_ctx": full[:3], "x2": full[:3] + ["wq0", "u0", "v0"],
               "x3": full[:5] + ["wq0", "u0", "v0"]}
        self.wlist = ["w_in_xb", "wab"] if mode == "p1" else sub.get(dbg, full)

    def din(self, name, shape, dt=F32):
        return self.nc.dram_tensor(name, list(shape), dt, kind="ExternalInput").ap()

    def dout(self, name, shape, dt=F32):
        return self.nc.dram_tensor(name, list(shape), dt, kind="ExternalOutput").ap()

    def dscr(self, name, shape, dt=BF16):
        return self.nc.dram_tensor(name, list(shape), dt, kind="Internal").ap()

    def sb(self, name, shape, dt=F32):
        return Tl(self.st.enter_context(self.nc.sbuf_tensor(name, list(shape), dt)), name)

    def ps(self, name, shape, dt=F32):
        return Tl(self.st.enter_context(self.nc.psum_tensor(name, list(shape), dt)), name)

    def act(self, out, in_, func, reads, writes, scale=1.0, bias=0.0, eng="act"):
        self.S.op(eng, lambda h: h.activation(out=out, in_=in_, func=func, scale=scale, bias=bias),
                  [r.b for r in reads], [w.b for w in writes])

    def tt(self, out, in0, in1, op, reads, writes, eng="dve"):
        self.S.op(eng, lambda h: h.tensor_tensor(out=out, in0=in0, in1=in1, op=op),
                  [r.b for r in reads], [w.b for w in writes])

    def ts(self, out, in0, s1, s2, op0, op1, reads, writes, eng="dve"):
        self.S.op(eng, lambda h: h.tensor_scalar(out=out, in0=in0, scalar1=s1, scalar2=s2, op0=op0, op1=op1),
                  [r.b for r in reads], [w.b for w in writes])

    def stt(self, out, in0, scalar, in1, op0, op1, reads, writes):
        self.S.op("dve", lambda h: h.scalar_tensor_tensor(out=out, in0=in0, scalar=scalar, in1=in1, op0=op0, op1=op1),
                  [r.b for r in reads], [w.b for w in writes])

    def cp(self, out, in_, reads, writes, eng="dve"):
        self.S.op(eng, lambda h: h.tensor_copy(out=out, in_=in_),
                  [r.b for r in reads], [w.b for w in writes])

    def mm(self, out, lhsT, rhs, start, stop, reads, writes):
        self.S.op("pe", lambda h: h.matmul(out, lhsT=lhsT, rhs=rhs, start=start, stop=stop),
                  [r.b for r in reads], [w.b for w in writes])

    def vop(self, fn, reads, writes, eng="dve"):
        self.S.op(eng, fn, [r.b for r in reads], [w.b for w in writes])

    def dma(self, out, in_, slot, reads=(), writes=(), q="sp"):
        self.S.dma(q, out, in_, slot, [r if isinstance(r, Buf) else r.b for r in reads],
                   [w if isinstance(w, Buf) else w.b for w in writes])

    def build(self):
        nc = self.nc
        mode = self.mode
        with self.st:
            self.S = Sched(nc, self.st)
            self.declare_io()
            self.alloc()
            self.setup()
            if mode == "p1":
                self.precast(["w_in_xb", "wab"])
                self.precast_finish()
                self.S.barrier()
                self.pass1(self.xT, 0)
            elif mode == "f":
                self.precast(self.wlist)
                self.precast_run(36, bg=False)
                self.S.barrier()
                self.bg_per_block = -(-(len(self.pc_jobs) - 36) // (4 * NSB))
                for s_ in range(3):
                    self.pass1(self.xTo, s_ * NTOK)
                    self.cp(self.call.t[:, s_, :, :], self.carry.t[:, NSB, :, :], [self.carry], [self.call])
                    self.S.barrier()
                self.pass1(self.xT, 0)
                self.precast_finish()
                self.S.barrier()
                self.ctx_and_carries()
                self.pass2()
            else:
                self.precast(self.wlist)
                self.precast_finish()
                self.S.barrier()
                if self.dbg not in ("s_setup", "s_precast"):
                    self.ctx_and_carries()
                    if self.dbg != "s_ctx":
                        self.pass2()
            self.S.barrier()
            self.S.emit()
        return nc

    def declare_io(self):
        self.xT = self.din("xT", [P, KC, NTOK])
        self.consts = self.din("consts", [P, 3 * 128])
        self.vecs = self.din("vecs", [P, NV, KC])
        self.cvec = self.din("cvec", [P, KC, 2])
        self.w_mod = self.din("w_mod", [2, D, 6 * D])
        self.b_modT = self.din("b_modT", [P, 2, 96])
        self.lru_w_in = self.din("lru_w_in", [D, 2 * D])
        self.wab_in = self.din("wab", [16, P, 512])
        if self.mode == "p1":
            self.carry_out = self.dout("carry", [P, NSB + 1, 4, KC])
        else:
            self.ctxT = self.din("ctxT", [P, KC, 256])
            if self.mode == "f":
                self.xTo = self.din("xTo", [P, KC, 3 * NTOK])
            if self.mode == "p2":
                self.carry_own = self.din("carry_own", [P, NSB + 1, 4, KC])
                self.carry_all = self.din("carry_all", [P, NCORE, 4, KC])
            self.nslots = 3 if self.mode == "f" else NCORE
            self.cmask = self.din("cmask", [P, 2, self.nslots])
            wl = self.wlist
            if "w_out" in wl:
                self.lru_w_out = self.din("lru_w_out", [D, D])
            if "sc_w_in" in wl:
                self.sc_w_in = self.din("sc_w_in", [D, 3 * D])
                self.sc_w_out = self.din("sc_w_out", [D, D])
            self.has_peer = "wq0" in wl
            if self.has_peer:
                self.peer_w_q = self.din("peer_w_q", [2, D, D])
                self.skT_in = self.din("skT", [2, P, 16, 128])
                self.u_l = self.din("u_l", [2, NEXP_C, P, D])
                self.v_l = self.din("v_l", [2, NEXP_C, P, D])
            self.yT = self.dout("yT", [P, KC, NTOK])
        self.ws = {}
        self.ws["w_in"] = (self.dscr("ws_w_in", [32, P, D]), Buf("ws_w_in"))
        self.ws["wab"] = (self.dscr("ws_wab", [16, P, 512]), Buf("ws_wab"))
        if self.mode != "p1":
            self.ws["w_out"] = (self.dscr("ws_w_out", [16, P, D]), Buf("ws_w_out"))
            self.ws["sc_w_in"] = (self.dscr("ws_sc_w_in", [48, P, D]), Buf("ws_sc_w_in"))
            self.ws["sc_w_out"] = (self.dscr("ws_sc_w_out", [16, P, D]), Buf("ws_sc_w_out"))
            for i in range(2):
                self.ws[f"wq{i}"] = (self.dscr(f"ws_wq{i}", [16, P, D]), Buf(f"ws_wq{i}"))
                self.ws[f"u{i}"] = (self.dscr(f"ws_u{i}", [NEXP_C, P, D]), Buf(f"ws_u{i}"))
                self.ws[f"v{i}"] = (self.dscr(f"ws_v{i}", [NEXP_C, P, D]), Buf(f"ws_v{i}"))

    def alloc(self):
        sb, ps = self.sb, self.ps
        self.cst = sb("cst", [P, 3 * 128])
        self.ident = self.cst.t[:, 0:128]
        self.iota = self.cst.t[:, 128:256]
        self.ones_bf = sb("ones_bf", [P, 128], BF16)
        self.iota_bf = sb("iota_bf", [P, 128], BF16)
        self.vec = sb("vec", [P, NV, KC])
        self.cv = sb("cv", [P, KC, 2])
        self.modt = [sb(f"modt{i}", [P, 96, 2]) for i in range(2)]
        self.bmod = sb("bmod", [P, 2, 96])
        self.dv = sb("dv", [P, 24, KC])
        self.wabring = [sb(f"wabr{i}", [P, 4, 128], BF16) for i in range(2)]
        self.xres = sb("xres", [P, KC, T])
        self.hT = sb("hT", [P, KC, T], BF16)
        self.bfA = sb("bfA", [P, KC, T], BF16)
        self.rstd = sb("rstd", [P, T])
        self.tmpr = Ring([sb(f"tmp{i}", [P, T]) for i in range(11)])
        self.tmpb = Ring([sb(f"tmpb{i}", [P, T], BF16) for i in range(4)])
        self.arena = sb("arena", [P, 16384])
        self.wring = [sb(f"wr{i}", [P, D], BF16) for i in range(3)]
        self.carry = sb("carryt", [P, NSB + 1, 4, KC])
        self.hin = sb("hin", [P, 2, NSB + 1, KC])
        self.small = sb("small", [P, 8, KC])
        if self.mode != "p1":
            self.vring = [sb(f"vr{i}", [P, 4, D], BF16) for i in range(2)]
            self.skT = [sb(f"skT{i}", [P, 16, 128], BF16) for i in range(2)]
            self.X1 = sb("X1", [P, 2048])
            self.X2 = sb("X2", [P, 2048])
            self.sv = sb("sv", [P, 16, 16])
            self.si_u = sb("si_u", [P, 16, 16], U32)
            self.si_f = sb("si_f", [P, 16, 16])
            self.cvv = sb("cvv", [P, 8, 16])
            self.ci_u = sb("ci_u", [P, 8, 16], U32)
            self.ca_u = sb("ca_u", [P, 8, 16], U32)
            self.cb_u = sb("cb_u", [P, 8, 16], U32)
            self.caf = sb("caf", [P, 8, 16])
            self.cbf = sb("cbf", [P, 8, 16])
            self.slot3 = sb("slot3", [P, 3, 128])
            self.slotT = sb("slotT", [P, 3, T])
            self.zs = sb("zs", [P, 4, 8])
            self.cmk = sb("cmk", [P, 2, self.nslots])
            self.call = sb("call", [P, NCORE, 4, KC])
            self.WT = Tl(self.arena.t[:].bitcast(BF16).rearrange("p (c t) -> p c t", t=T), "WT")
            self.WT.b = self.arena.b
            self.ctxx = Tl(self.arena.t[:, 0:KC * 256].rearrange("p (k t) -> p k t", k=KC), "ctxx")
            self.ctxx.b = self.arena.b
            self.outT = Tl(self.arena.t[:, 0:KC * T].rearrange("p (k t) -> p k t", k=KC), "outT")
            self.outT.b = self.arena.b
        pa = [ps(f"psA{i}", [P, 512]) for i in range(3)]
        pst = []
        for half in range(2):
            for bnk in range(3):
                t_ = Tl(pa[bnk].t[:, half * 256:half * 256 + 256], f"psh{bnk}_{half}")
                t_.b = pa[bnk].b
                pst.append(t_)
        self.psring = Ring(pst)
        self._pa = pa
        self.psS = [ps(f"psS{i}", [P, 512]) for i in range(4)]
        self.psM = ps("psM", [P, 512])
        big = []
        banks = self._pa + self.psS
        for half in range(2):
            for bk in banks:
                t_ = Tl(bk.t[:, half * 256:half * 256 + 256], f"pb_{bk.b.name}_{half}")
                t_.b = bk.b
                big.append(t_)
        self.psbig = Ring(big)
        self.p1ring = Ring([Tl(self.arena.t[:, 9216 + i * 256:9216 + (i + 1) * 256], f"p1t{i}") for i in range(28)])

    def setup(self):
        S = self.S
        self._nld = 0

        def ld1(o, i, tl):
            self._nld += 1
            self.dma(o, i, S.slot(f"ld1_{self._nld}"), writes=[tl])
        self.ld1 = ld1
        ld1(self.cst.t[:], self.consts, self.cst)
        ld1(self.vec.t[:], self.vecs, self.vec)
        ld1(self.cv.t[:], self.cvec, self.cv)
        ld1(self.bmod.t[:], self.b_modT, self.bmod)
        self.cp(self.ones_bf.t[:], self.cst.t[:, 256:384], [self.cst], [self.ones_bf])
        self.cp(self.iota_bf.t[:], self.cst.t[:, 128:256], [self.cst], [self.iota_bf])
        self.act(self.cv.t[:], self.cv.t[:], AF.Silu, [self.cv], [self.cv])
        nlay = 1 if self.mode == "p1" else 2
        NB = 256
        wmr = [Tl(self.arena.t[:, i * 4096:(i + 1) * 4096].rearrange("p (k n) -> p k n", k=KC), f"wm{i}")
               for i in range(3)]
        wslots = [S.slot(f"wm{i}") for i in range(3)]
        cnt = 0
        for i in range(nlay):
            nblk = (3 * D // NB) if self.mode == "p1" else (6 * D // NB)
            for nb in range(nblk):
                wt = wmr[cnt % 3]
                src = self.w_mod[i, :, nb * NB:(nb + 1) * NB].rearrange("(k p) n -> p k n", p=P)
                self.dma(wt.t, src, wslots[cnt % 3], writes=[wt])
                cnt += 1
                for mi in range(NB // 128):
                    m = nb * (NB // 128) + mi
                    for k in range(KC):
                        self.mm(self.psM.t[:, 2 * m:2 * m + 2], wt.t[:, k, mi * 128:(mi + 1) * 128],
                                self.cv.t[:, k, :], k == 0, k == KC - 1, [wt, self.cv], [self.psM])
            nm = nblk * NB // 128
            self.tt(self.modt[i].t[:, 0:nm, :], self.psM.t[:, 0:2 * nm].rearrange("p (m c) -> p m c", c=2),
                    self.bmod.t[:, i, 0:nm].unsqueeze(2).broadcast_to([P, nm, 2]), ALU.add,
                    [self.psM, self.bmod], [self.modt[i]])
        dv = self.dv
        for i in range(nlay):
            self.stt(dv.t[:, 2 * i + 0, :], self.modt[i].t[:, 16:32, 0], 1.0, self.vec.t[:, 2 * i + 0, :],
                     ALU.add, ALU.mult, [self.modt[i], self.vec], [dv])
            if self.mode != "p1":
                self.stt(dv.t[:, 2 * i + 1, :], self.modt[i].t[:, 64:80, 0], 1.0, self.vec.t[:, 2 * i + 1, :],
                         ALU.add, ALU.mult, [self.modt[i], self.vec], [dv])
        self.stt(dv.t[:, 8, :], self.modt[0].t[:, 16:32, 1], 1.0, self.vec.t[:, 0, :],
                 ALU.add, ALU.mult, [self.modt[0], self.vec], [dv])
        self.act(dv.t[:, 9:11, :], self.vec.t[:, 14:16, :], AF.Exp, [self.vec], [dv], scale=-1.0)
        self.act(dv.t[:, 9:11, :], dv.t[:, 9:11, :], AF.Ln, [dv], [dv], bias=1.0)
        self.ts(dv.t[:, 9:11, :], dv.t[:, 9:11, :], -8.0, None, ALU.mult, ALU.bypass, [dv], [dv])
        self.ts(dv.t[:, 11:13, :], dv.t[:, 9:11, :], 2.0, None, ALU.mult, ALU.bypass, [dv], [dv])
        self.ts(dv.t[:, 14:16, :], self.vec.t[:, 10:12, :], -1.0, None, ALU.mult, ALU.bypass, [self.vec], [dv])
        self.ts(dv.t[:, 16:18, :], self.vec.t[:, 12:14, :], -1.0, None, ALU.mult, ALU.bypass, [self.vec], [dv])
        self.ts(dv.t[:, 18:20, :], dv.t[:, 9:11, :], 0.5, None, ALU.mult, ALU.bypass, [dv], [dv])
        self.ts(dv.t[:, 20:22, :], dv.t[:, 9:11, :], 0.5 * T, None, ALU.mult, ALU.bypass, [dv], [dv])
        self.vop(lambda h: h.memset(dv.t[:, 13, :], 0.0), [], [dv], eng="pool")
        if self.mode != "p1":
            ld1(self.cmk.t[:], self.cmask, self.cmk)
            if self.mode == "p2":
                ld1(self.call.t[:], self.carry_all, self.call)
                ld1(self.carry.t[:], self.carry_own, self.carry)
        self.S.barrier()

    def A1(self, i, k): return self.dv.t[:, 2 * i, k:k + 1]
    def A2(self, i, k): return self.dv.t[:, 2 * i + 1, k:k + 1]
    def S1(self, i, k): return self.modt[i].t[:, 0 + k, 0:1]
    def G1(self, i, k): return self.modt[i].t[:, 32 + k, 0:1]
    def S2(self, i, k): return self.modt[i].t[:, 48 + k, 0:1]
    def G2(self, i, k): return self.modt[i].t[:, 80 + k, 0:1]
    def V(self, idx, k): return self.vec.t[:, idx, k:k + 1]

    def precast(self, which):
        S = self.S
        NST = 3
        fin = [Tl(self.arena.t[:, i * 2048:(i + 1) * 2048], f"pcin{i}") for i in range(NST)]
        fob = [Tl(self.arena.t[:, 6144 + i * 1024:6144 + (i + 1) * 1024].bitcast(BF16), f"pcout{i}")
               for i in range(NST)]
        sl_in = [S.slot(f"pci{i}") for i in range(NST)]
        sl_out = [S.slot(f"pco{i}") for i in range(NST)]
        engs = ["dve", "act", "pool"]
        jobs = []

        def add_W(W, key, cc_lo, cc_hi, base=0):
            dst, dbuf = self.ws[key]
            for cc in range(cc_lo, cc_hi):
                src = W[:, cc * 128:(cc + 1) * 128].rearrange("(k p) n -> p k n", p=P)
                jobs.append((src, dst[cc - base], dbuf, 16))

        def add_rows(R, key):
            dst, dbuf = self.ws[key]
            for c in range(NEXP_C):
                jobs.append((R[c], dst[c], dbuf, 0))
        for w in which:
            if w == "w_in_xb":
                add_W(self.lru_w_in, "w_in", 16, 32)
            elif w == "w_in":
                add_W(self.lru_w_in, "w_in", 0, 32)
            elif w == "wab":
                dst, dbuf = self.ws["wab"]
                for j in range(4):
                    jobs.append((self.wab_in[4 * j:4 * j + 4].rearrange("h p f -> p h f"),
                                 dst[4 * j:4 * j + 4].rearrange("h p f -> p h f"), dbuf, 4))
            elif w == "w_out":
                add_W(self.lru_w_out, "w_out", 0, 16)
            elif w == "sc_w_in":
                add_W(self.sc_w_in, "sc_w_in", 0, 48)
            elif w == "sc_w_out":
                add_W(self.sc_w_out, "sc_w_out", 0, 16)
            elif w in ("wq0", "wq1"):
                add_W(self.peer_w_q[int(w[2])], w, 0, 16)
            elif w in ("u0", "u1"):
                add_rows(self.u_l[int(w[1])], w)
            elif w in ("v0", "v1"):
                add_rows(self.v_l[int(w[1])], w)
        self.pc_jobs = jobs
        self.pc_n = 0
        self.pc_in = 0
        self.pc_st = (fin, fob, sl_in, sl_out, NST)
        self.pc_st_bg = ([S.slot(f"pcib{i}") for i in range(NST)], [S.slot(f"pcob{i}") for i in range(NST)])

    def precast_run(self, k, bg):
        fin, fob, sl_in, sl_out, NST = self.pc_st
        if bg:
            sl_in, sl_out = self.pc_st_bg
        jobs = self.pc_jobs
        engs = ["pool"] if bg else ["dve", "act", "pool"]
        q = "pool" if bg else "sp"
        stop = min(self.pc_n + k, len(jobs))
        while self.pc_n < stop:
            n = self.pc_n
            while self.pc_in < min(n + NST, len(jobs)):
                m = self.pc_in
                src, dst, dbuf, is3 = jobs[m]
                i = m % NST
                o = fin[i].t.rearrange("p (k n) -> p k n", k=is3) if is3 else fin[i].t
                self.dma(o, src, sl_in[i], writes=[fin[i]], q=q)
                self.pc_in += 1
            src, dst, dbuf, is3 = jobs[n]
            i = n % NST
            e = engs[n % len(engs)]
            if e == "act":
                self.act(fob[i].t, fin[i].t, AF.Copy, [fin[i]], [fob[i]])
            else:
                self.cp(fob[i].t, fin[i].t, [fin[i]], [fob[i]], eng=e)
            oo = fob[i].t.rearrange("p (k n) -> p k n", k=4) if is3 == 4 else fob[i].t
            self.dma(dst, oo, sl_out[i], reads=[fob[i]], writes=[dbuf], q=q)
            self.pc_n += 1

    def precast_finish(self):
        self.precast_run(len(self.pc_jobs), bg=False)
        self.S.barrier()
        S = self.S
        if self.mode != "p1" and self.has_peer:
            for i in range(2):
                st = Tl(self.arena.t[:, i * 2048:(i + 1) * 2048], f"sks{i}")
                self.ld1(st.t, self.skT_in[i].rearrange("p a b -> p (a b)"), st)
                self.cp(self.skT[i].t[:].rearrange("p a b -> p (a b)"), st.t, [st], [self.skT[i]])

    def wstream(self, key, ccs):
        dst, dbuf = self.ws[key]
        return Stream(self, self.wring, [(dst[cc], dbuf) for cc in ccs])

    def norm_mod(self, xsrc, n, Afn, Sfn, out_tl, out_is_f32=False):
        sq = self.bfA
        self.act(sq.t[:, :, 0:n], xsrc.t[:, :, 0:n], AF.Square, [xsrc], [sq])
        pm = self.psM
        for k in range(KC):
            self.mm(pm.t[:, 0:n], self.ones_bf.t[:], sq.t[:, k, 0:n], k == 0, k == KC - 1,
                    [self.ones_bf, sq], [pm])
        r = self.rstd
        self.act(r.t[:, 0:n], pm.t[:, 0:n], AF.Ln, [pm], [r], scale=1.0 / D, bias=EPS)
        self.act(r.t[:, 0:n], r.t[:, 0:n], AF.Exp, [r], [r], scale=-0.5)
        for k in range(KC):
            tm = self.tmpr.next()
            self.tt(tm.t[:, 0:n], xsrc.t[:, k, 0:n], r.t[:, 0:n], ALU.mult, [xsrc, r], [tm])
            self.act(out_tl.t[:, k, 0:n], tm.t[:, 0:n], AF.Identity, [tm], [out_tl], scale=Afn(k), bias=Sfn(k))

    def lru_block(self, xsrc, n, rowlen, Afn, Sfn, pass2, sbi, hin_f=None, hin_b=None, want_carry=True):
        self.norm_mod(xsrc, n, Afn, Sfn, self.hT)
        ccs = (list(range(16)) if pass2 else []) + list(range(16, 32))
        wsr = self.wstream("w_in", ccs)
        wabs = Stream(self, self.wabring, [(self.ws["wab"][0][h].rearrange("p (a b) -> p a b", a=4), self.ws["wab"][1])
                                           for h in range(16)])
        YT = self.bfA
        if pass2:
            for h in range(16):
                wt = wsr.get()
                pg = self.psring.next()
                for k in range(KC):
                    self.mm(pg.t[:, 0:n], wt.t[:, k * 128:(k + 1) * 128], self.hT.t[:, k, 0:n], k == 0, k == KC - 1,
                            [wt, self.hT], [pg])
                self.act(YT.t[:, h, 0:n], pg.t[:, 0:n], AF.Gelu_apprx_tanh, [pg], [YT])
        LN_HALF = math.log(0.5)
        for h in range(16):
            wabt = wabs.get()
            wt = wsr.get()
            px = self.psring.next()
            for k in range(KC):
                self.mm(px.t[:, 0:n], wt.t[:, k * 128:(k + 1) * 128], self.hT.t[:, k, 0:n], k == 0, k == KC - 1,
                        [wt, self.hT], [px])
            xc = self.tmpr.next()
            self.act(xc.t[:, 0:n], px.t[:, 0:n], AF.Identity, [px], [xc], scale=self.V(7, h), bias=self.V(9, h))
            pv = px.t[:, 0:n].rearrange("p (r c) -> p r c", c=rowlen)
            xv = xc.t[:, 0:n].rearrange("p (r c) -> p r c", c=rowlen)
            for tap, off in ((6, -1), (5, -2), (8, 1)):
                if off < 0:
                    o_, i_ = xv[:, :, -off:], pv[:, :, :rowlen + off]
                else:
                    o_, i_ = xv[:, :, :rowlen - off], pv[:, :, off:]
                self.stt(o_, i_, self.V(tap, h), o_, ALU.mult, ALU.add, [px, xc], [xc])
            xcb = self.tmpb.next()
            self.act(xcb.t[:, 0:n], xc.t[:, 0:n], AF.Copy, [xc], [xcb])
            hs = []
            for d in range(2):
                pr = self.psring.next()
                self.mm(pr.t[:, 0:n], wabt.t[:, d * 2 + 0, :], xcb.t[:, 0:n], True, True, [wabt, xcb], [pr])
                pi = self.psring.next()
                self.mm(pi.t[:, 0:n], wabt.t[:, d * 2 + 1, :], xcb.t[:, 0:n], True, True, [wabt, xcb], [pi])
                rr = self.tmpr.next()
                self.act(rr.t[:, 0:n], pr.t[:, 0:n], AF.Exp, [pr], [rr], scale=-1.0, bias=self.dv.t[:, 14 + d, h:h + 1])
                ii = self.tmpr.next()
                self.act(ii.t[:, 0:n], pi.t[:, 0:n], AF.Exp, [pi], [ii], scale=-1.0, bias=self.dv.t[:, 16 + d, h:h + 1])
                self.ts(rr.t[:, 0:n], rr.t[:, 0:n], 1.0, None, ALU.add, ALU.bypass, [rr], [rr])
                self.vop(lambda hd, rr=rr: hd.reciprocal(out=rr.t[:, 0:n], in_=rr.t[:, 0:n]), [rr], [rr])
                self.act(ii.t[:, 0:n], ii.t[:, 0:n], AF.Ln, [ii], [ii], bias=1.0)
                self.act(ii.t[:, 0:n], ii.t[:, 0:n], AF.Exp, [ii], [ii], scale=-1.0)
                aa = self.tmpr.next()
                self.act(aa.t[:, 0:n], rr.t[:, 0:n], AF.Exp, [rr], [aa], scale=self.dv.t[:, 9 + d, h:h + 1])
                a2 = self.tmpr.next()
                self.act(a2.t[:, 0:n], rr.t[:, 0:n], AF.Exp, [rr], [a2], scale=self.dv.t[:, 11 + d, h:h + 1])
                self.act(a2.t[:, 0:n], a2.t[:, 0:n], AF.Ln, [a2], [a2], scale=-1.0, bias=1.0)
                self.act(a2.t[:, 0:n], a2.t[:, 0:n], AF.Exp, [a2], [a2], scale=0.5)
                self.tt(ii.t[:, 0:n], ii.t[:, 0:n], xc.t[:, 0:n], ALU.mult, [ii, xc], [ii])
                self.tt(ii.t[:, 0:n], ii.t[:, 0:n], a2.t[:, 0:n], ALU.mult, [ii, a2], [ii])
                first = 0 if d == 0 else n - 1
                hinp = hin_f if d == 0 else hin_b
                if hinp is not None:
                    self.stt(ii.t[:, first:first + 1], aa.t[:, first:first + 1], hinp(h), ii.t[:, first:first + 1],
                             ALU.mult, ALU.add, [aa, ii, self.hin], [ii])
                hh = self.tmpr.next()
                if d == 0:
                    o_, a_, b_ = hh.t[:, 0:n], aa.t[:, 0:n], ii.t[:, 0:n]
                else:
                    o_, a_, b_ = hh.t[:, 0:n][:, ::-1], aa.t[:, 0:n][:, ::-1], ii.t[:, 0:n][:, ::-1]
                self.vop(lambda hd, o_=o_, a_=a_, b_=b_: hd.tensor_tensor_scan(out=o_, data0=a_, data1=b_, initial=0.0,
                                                                           op0=ALU.mult, op1=ALU.add),
                         [aa, ii], [hh])
                if want_carry:
                    last = n - 1 if d == 0 else 0
                    ctl = self.carry
                    self.vop(lambda hd, rr=rr, d=d, h=h, ctl=ctl: hd.tensor_reduce(out=ctl.t[:, sbi, 2 * d, h:h + 1],
                                                                                 in_=rr.t[:, 0:n], axis=AX.X, op=ALU.add),
                             [rr], [ctl])
                    self.cp(ctl.t[:, sbi, 2 * d + 1, h:h + 1], hh.t[:, last:last + 1], [hh], [ctl])
                hs.append(hh)
            if pass2:
                self.tt(hs[0].t[:, 0:n], hs[0].t[:, 0:n], hs[1].t[:, 0:n], ALU.add, [hs[0], hs[1]], [hs[0]])
                self.tt(YT.t[:, h, 0:n], YT.t[:, h, 0:n], hs[0].t[:, 0:n], ALU.mult, [hs[0], YT], [YT])
        if pass2:
            self.out_proj("w_out", YT, n, 0)

    def lru_p1(self, xsrc, sbi):
        n, rowlen = T, 64
        self.norm_mod(xsrc, n, lambda k: self.A1(0, k), lambda k: self.S1(0, k), self.hT)
        wsr = self.wstream("w_in", list(range(16, 32)))
        wabs = Stream(self, self.wabring, [(self.ws["wab"][0][h].rearrange("p (a b) -> p a b", a=4), self.ws["wab"][1])
                                           for h in range(16)])
        ctl = self.carry
        G = 2
        R = self.p1ring

        def s0(h, c):
            wt = wsr.get()
            c["px"] = px = self.psbig.next()
            for k in range(KC):
                self.mm(px.t[:, 0:n], wt.t[:, k * 128:(k + 1) * 128], self.hT.t[:, k, 0:n], k == 0, k == KC - 1,
                        [wt, self.hT], [px])

        def s1(h, c):
            c["xc"] = xc = R.next()
            self.act(xc.t[:, 0:n], c["px"].t[:, 0:n], AF.Identity, [c["px"]], [xc], scale=self.V(7, h), bias=self.V(9, h))

        def s2(h, c):
            px, xc = c["px"], c["xc"]
            pv = px.t[:, 0:n].rearrange("p (r c) -> p r c", c=rowlen)
            xv = xc.t[:, 0:n].rearrange("p (r c) -> p r c", c=rowlen)
            for tap, off in ((6, -1), (5, -2), (8, 1)):
                if off < 0:
                    o_, i_ = xv[:, :, -off:], pv[:, :, :rowlen + off]
                else:
                    o_, i_ = xv[:, :, :rowlen - off], pv[:, :, off:]
                self.stt(o_, i_, self.V(tap, h), o_, ALU.mult, ALU.add, [px, xc], [xc])

        def s3(h, c):
            c["xcb"] = xcb = self.tmpb.next()
            self.act(xcb.t[:, 0:n], c["xc"].t[:, 0:n], AF.Copy, [c["xc"]], [xcb])

        def s4(h, c):
            wabt = wabs.get()
            xcb = c["xcb"]
            for d in range(2):
                c["pr", d] = pr = self.psbig.next()
                self.mm(pr.t[:, 0:n], wabt.t[:, d * 2 + 0, :], xcb.t[:, 0:n], True, True, [wabt, xcb], [pr])
                c["pi", d] = pi = self.psbig.next()
                self.mm(pi.t[:, 0:n], wabt.t[:, d * 2 + 1, :], xcb.t[:, 0:n], True, True, [wabt, xcb], [pi])

        def s5(h, c):
            for d in range(2):
                c["rr", d] = rr = R.next()
                self.act(rr.t[:, 0:n], c["pr", d].t[:, 0:n], AF.Exp, [c["pr", d]], [rr], scale=-1.0,
                         bias=self.dv.t[:, 14 + d, h:h + 1])
                c["ii", d] = ii = R.next()
                self.act(ii.t[:, 0:n], c["pi", d].t[:, 0:n], AF.Exp, [c["pi", d]], [ii], scale=-1.0,
                         bias=self.dv.t[:, 16 + d, h:h + 1])

        def s6(h, c):
            for d in range(2):
                rr, ii = c["rr", d], c["ii", d]
                self.act(rr.t[:, 0:n], rr.t[:, 0:n], AF.Ln, [rr], [rr], bias=1.0)
                self.act(ii.t[:, 0:n], ii.t[:, 0:n], AF.Ln, [ii], [ii], bias=1.0)

        def s7(h, c):
            for d in range(2):
                rr, ii = c["rr", d], c["ii", d]
                self.act(rr.t[:, 0:n], rr.t[:, 0:n], AF.Exp, [rr], [rr], scale=-1.0)
                self.act(ii.t[:, 0:n], ii.t[:, 0:n], AF.Exp, [ii], [ii], scale=-1.0)

        def s8(h, c):
            for d in range(2):
                rr = c["rr", d]
                c["aa", d] = aa = R.next()
                self.act(aa.t[:, 0:n], rr.t[:, 0:n], AF.Exp, [rr], [aa], scale=self.dv.t[:, 9 + d, h:h + 1])
                c["a2", d] = a2 = R.next()
                self.act(a2.t[:, 0:n], rr.t[:, 0:n], AF.Exp, [rr], [a2], scale=self.dv.t[:, 11 + d, h:h + 1])
                self.vop(lambda hd, rr=rr, d=d, h=h: hd.tensor_reduce(out=ctl.t[:, sbi, 2 * d, h:h + 1],
                                                                    in_=rr.t[:, 0:n], axis=AX.X, op=ALU.add),
                         [rr], [ctl])
                ii = c["ii", d]
                self.tt(ii.t[:, 0:n], ii.t[:, 0:n], c["xc"].t[:, 0:n], ALU.mult, [ii, c["xc"]], [ii])

        def s9(h, c):
            for d in range(2):
                a2 = c["a2", d]
                self.act(a2.t[:, 0:n], a2.t[:, 0:n], AF.Ln, [a2], [a2], scale=-1.0, bias=1.0)

        def s10(h, c):
            for d in range(2):
                a2 = c["a2", d]
                self.act(a2.t[:, 0:n], a2.t[:, 0:n], AF.Exp, [a2], [a2], scale=0.5)

        def s11(h, c):
            for d in range(2):
                ii, a2 = c["ii", d], c["a2", d]
                self.tt(ii.t[:, 0:n], ii.t[:, 0:n], a2.t[:, 0:n], ALU.mult, [ii, a2], [ii])

        def s12(h, c):
            for d in range(2):
                aa, ii, hh = c["aa", d], c["ii", d], c["a2", d]
                if d == 0:
                    o_, a_, b_ = hh.t[:, 0:n], aa.t[:, 0:n], ii.t[:, 0:n]
                else:
                    o_, a_, b_ = hh.t[:, 0:n][:, ::-1], aa.t[:, 0:n][:, ::-1], ii.t[:, 0:n][:, ::-1]
                self.vop(lambda hd, o_=o_, a_=a_, b_=b_: hd.tensor_tensor_scan(out=o_, data0=a_, data1=b_, initial=0.0,
                                                                           op0=ALU.mult, op1=ALU.add),
                         [aa, ii], [hh])

        def s13(h, c):
            for d in range(2):
                hh = c["a2", d]
                last = n - 1 if d == 0 else 0
                self.cp(ctl.t[:, sbi, 2 * d + 1, h:h + 1], hh.t[:, last:last + 1], [hh], [ctl])

        stages = [s0, s1, s2, s3, s4, s5, s6, s7, s8, s9, s10, s11, s12, s13]
        SK = 5
        cx = {h: {} for h in range(16)}
        ns = len(stages)
        for step in range(ns + 15 * SK):
            for h in range(16):
                si = step - h * SK
                if 0 <= si < ns:
                    stages[si](h, cx[h])

    def out_proj(self, key, YT, n, layer):
        wsr = self.wstream(key, list(range(16)))
        for m in range(16):
            wt = wsr.get()
            po = self.psring.next()
            for k in range(KC):
                self.mm(po.t[:, 0:n], wt.t[:, k * 128:(k + 1) * 128], YT.t[:, k, 0:n], k == 0, k == KC - 1,
                        [wt, YT], [po])
            self.stt(self.xres.t[:, m, 0:n], po.t[:, 0:n], self.G1(layer, m), self.xres.t[:, m, 0:n],
                     ALU.mult, ALU.add, [po, self.modt[layer], self.xres], [self.xres])

    def sc_block(self, n):
        self.norm_mod(self.xres, n, lambda k: self.A1(1, k), lambda k: self.S1(1, k), self.hT)
        ccs = []
        for ch in range(16):
            ccs += [ch, 16 + ch, 32 + ch]
        wsr = self.wstream("sc_w_in", ccs)
        YT = self.bfA
        rowlen = 64
        for ch in range(16):
            pp = []
            for j in range(3):
                wt = wsr.get()
                p_ = self.psring.next()
                for k in range(KC):
                    self.mm(p_.t[:, 0:n], wt.t[:, k * 128:(k + 1) * 128], self.hT.t[:, k, 0:n], k == 0, k == KC - 1,
                            [wt, self.hT], [p_])
                pp.append(p_)
            vs = self.tmpr.next()
            self.act(vs.t[:, 0:n], pp[2].t[:, 0:n], AF.Copy, [pp[2]], [vs])
            cvt = self.tmpr.next()
            self.tt(cvt.t[:, 0:n], pp[1].t[:, 0:n], vs.t[:, 0:n], ALU.mult, [pp[1], vs], [cvt])
            yc = self.tmpr.next()
            self.act(yc.t[:, 0:n], cvt.t[:, 0:n], AF.Identity, [cvt], [yc], scale=self.V(17, ch), bias=self.V(19, ch))
            cv_ = cvt.t[:, 0:n].rearrange("p (r c) -> p r c", c=rowlen)
            yv = yc.t[:, 0:n].rearrange("p (r c) -> p r c", c=rowlen)
            for tap, off in ((16, -1), (18, 1)):
                if off < 0:
                    o_, i_ = yv[:, :, -off:], cv_[:, :, :rowlen + off]
                else:
                    o_, i_ = yv[:, :, :rowlen - off], cv_[:, :, off:]
                self.stt(o_, i_, self.V(tap, ch), o_, ALU.mult, ALU.add, [cvt, yc], [yc])
            self.tt(YT.t[:, ch, 0:n], pp[0].t[:, 0:n], yc.t[:, 0:n], ALU.mult, [pp[0], yc], [YT])
        self.out_proj("sc_w_out", YT, n, 1)

    def peer_block(self, layer):
        n = T
        self.norm_mod(self.xres, n, lambda k: self.A2(layer, k), lambda k: self.S2(layer, k), self.hT)
        qT = self.bfA
        wsr = self.wstream(f"wq{layer}", list(range(16)))
        for hp in range(16):
            wt = wsr.get()
            pq = self.psring.next()
            for k in range(KC):
                self.mm(pq.t[:, 0:n], wt.t[:, k * 128:(k + 1) * 128], self.hT.t[:, k, 0:n], k == 0, k == KC - 1,
                        [wt, self.hT], [pq])
            self.act(qT.t[:, hp, 0:n], pq.t[:, 0:n], AF.Copy, [pq], [qT])
        for ts_ in range(T // 128):
            self.topk_sub(layer, qT, ts_)
            self.scatter_sub(ts_)
        self.sweep(layer)

    def topk_sub(self, layer, qT, ts_):
        t0 = ts_ * 128
        skT = self.skT[layer]
        s_sb = Tl(self.X1.t[:].rearrange("p (a b) -> p a b", b=128), "x"); s_sb.b = self.X1.b
        s_wk = Tl(self.X2.t[:].rearrange("p (a b) -> p a b", b=128), "x"); s_wk.b = self.X2.b
        for hp in range(16):
            pS = self.psS[hp // 4]
            self.mm(pS.t[:, (hp % 4) * 128:(hp % 4 + 1) * 128], qT.t[:, hp, t0:t0 + 128], skT.t[:, hp, :], True, True,
                    [qT, skT], [pS])
        for b4 in range(4):
            self.act(s_sb.t[:, b4 * 4:(b4 + 1) * 4, :].rearrange("p a b -> p (a b)"), self.psS[b4].t[:], AF.Copy,
                     [self.psS[b4]], [s_sb])
        sv, si_u = self.sv, self.si_u
        for hp in range(16):
            self.vop(lambda h, hp=hp: h.max(out=sv.t[:, hp, 0:8], in_=s_sb.t[:, hp, :]), [s_sb], [sv])
        for hp in range(16):
            self.vop(lambda h, hp=hp: h.max_index(out=si_u.t[:, hp, 0:8], in_max=sv.t[:, hp, 0:8],
                                                  in_values=s_sb.t[:, hp, :]), [s_sb, sv], [si_u])
        for hp in range(16):
            self.vop(lambda h, hp=hp: h.match_replace(out=s_wk.t[:, hp, :], in_to_replace=sv.t[:, hp, 0:8],
                                                      in_values=s_sb.t[:, hp, :], imm_value=NEG), [s_sb, sv], [s_wk])
        for hp in range(16):
            self.vop(lambda h, hp=hp: h.max(out=sv.t[:, hp, 8:16], in_=s_wk.t[:, hp, :]), [s_wk], [sv])
        for hp in range(16):
            self.vop(lambda h, hp=hp: h.max_index(out=si_u.t[:, hp, 8:16], in_max=sv.t[:, hp, 8:16],
                                                  in_values=s_wk.t[:, hp, :]), [s_wk, sv], [si_u])
        self.cp(self.si_f.t[:], si_u.t[:], [si_u], [self.si_f])
        cand = Tl(self.X1.t[:].rearrange("p (h a b) -> p h a b", h=8, a=16), "x"); cand.b = self.X1.b
        candw = Tl(self.X2.t[:].rearrange("p (h c) -> p h c", h=8), "x"); candw.b = self.X2.b
        self.tt(cand.t, sv.t[:, 0::2, :].unsqueeze(3).broadcast_to([P, 8, 16, 16]),
                sv.t[:, 1::2, :].unsqueeze(2).broadcast_to([P, 8, 16, 16]), ALU.add, [sv], [cand])
        c2 = Tl(self.X1.t[:].rearrange("p (h c) -> p h c", h=8), "x"); c2.b = self.X1.b
        cvv, ci_u = self.cvv, self.ci_u
        for h_ in range(8):
            self.vop(lambda h, h_=h_: h.max(out=cvv.t[:, h_, 0:8], in_=c2.t[:, h_, :]), [c2], [cvv])
        for h_ in range(8):
            self.vop(lambda h, h_=h_: h.max_index(out=ci_u.t[:, h_, 0:8], in_max=cvv.t[:, h_, 0:8],
                                                  in_values=c2.t[:, h_, :]), [c2, cvv], [ci_u])
        for h_ in range(8):
            self.vop(lambda h, h_=h_: h.match_replace(out=candw.t[:, h_, :], in_to_replace=cvv.t[:, h_, 0:8],
                                                      in_values=c2.t[:, h_, :], imm_value=NEG), [c2, cvv], [candw])
        for h_ in range(8):
            self.vop(lambda h, h_=h_: h.max(out=cvv.t[:, h_, 8:16], in_=candw.t[:, h_, :]), [candw], [cvv])
        for h_ in range(8):
            self.vop(lambda h, h_=h_: h.max_index(out=ci_u.t[:, h_, 8:16], in_max=cvv.t[:, h_, 8:16],
                                                  in_values=candw.t[:, h_, :]), [candw, cvv], [ci_u])
        self.ts(self.ca_u.t[:], ci_u.t[:], 4, None, ALU.logical_shift_right, ALU.bypass, [ci_u], [self.ca_u])
        self.ts(self.cb_u.t[:], ci_u.t[:], 15, None, ALU.bitwise_and, ALU.bypass, [ci_u], [self.cb_u])
        self.cp(self.caf.t[:], self.ca_u.t[:], [self.ca_u], [self.caf])
        self.cp(self.cbf.t[:], self.cb_u.t[:], [self.cb_u], [self.cbf])
        oh = Tl(self.X1.t[:].rearrange("p (h k a) -> p h k a", h=8, k=16), "x"); oh.b = self.X1.b
        io16 = self.iota[:, 0:16].unsqueeze(1).unsqueeze(1).broadcast_to([P, 8, 16, 16])
        for j, (cf, par) in enumerate(((self.caf, 0), (self.cbf, 1))):
            self.tt(oh.t, cf.t[:].unsqueeze(3).broadcast_to([P, 8, 16, 16]), io16, ALU.is_equal, [cf, self.cst], [oh])
            self.tt(oh.t, oh.t, self.si_f.t[:, par::2, :].unsqueeze(2).broadcast_to([P, 8, 16, 16]), ALU.mult,
                    [oh, self.si_f], [oh])
            self.vop(lambda h, j=j: h.tensor_reduce(out=self.slot3.t[:, j, :].rearrange("p (h k) -> p h k", h=8),
                                                    in_=oh.t, axis=AX.X, op=ALU.add), [oh], [self.slot3])
        zs = self.zs
        g3 = self.slot3.t[:, 2, :].rearrange("p (h k) -> p h k", h=8)
        self.tt(g3, cvv.t[:], cvv.t[:, :, 0:1].broadcast_to([P, 8, 16]), ALU.subtract, [cvv], [self.slot3])
        self.act(g3, g3, AF.Exp, [self.slot3], [self.slot3])
        self.vop(lambda h: h.tensor_reduce(out=zs.t[:, 0, :], in_=g3, axis=AX.X, op=ALU.add), [self.slot3], [zs])
        self.vop(lambda h: h.reciprocal(out=zs.t[:, 1, :], in_=zs.t[:, 0, :]), [zs], [zs])
        self.tt(g3, g3, zs.t[:, 1, :].unsqueeze(2).broadcast_to([P, 8, 16]), ALU.mult, [self.slot3, zs], [self.slot3])
        pm = self.psM
        for j in range(3):
            self.vop(lambda h, j=j: h.transpose(out=pm.t[:, j * 128:(j + 1) * 128], in_=self.slot3.t[:, j, :],
                                                identity=self.ident), [self.slot3, self.cst], [pm], eng="pe")
        self.cp(self.slotT.t[:, :, ts_ * 128:(ts_ + 1) * 128], pm.t[:, 0:384].rearrange("p (j t) -> p j t", j=3),
                [pm], [self.slotT])

    def scatter_sub(self, ts_):
        TG = 32
        WT = self.WT
        for g_ in range(128 // TG):
            t0 = ts_ * 128 + g_ * TG
            Bt = Tl(self.X1.t[:].bitcast(BF16).rearrange("p (t i) -> p t i", i=128)[:, 0:TG, :], "x"); Bt.b = self.X1.b
            At = Tl(self.X2.t[:].bitcast(BF16).rearrange("p (t i) -> p t i", i=128)[:, 0:TG, :], "x"); At.b = self.X2.b
            io = self.iota.unsqueeze(1).broadcast_to([P, TG, 128])
            i1v = self.slotT.t[:, 0, t0:t0 + TG].unsqueeze(2).broadcast_to([P, TG, 128])
            i2v = self.slotT.t[:, 1, t0:t0 + TG].unsqueeze(2).broadcast_to([P, TG, 128])
            gv = self.slotT.t[:, 2, t0:t0 + TG].unsqueeze(2).broadcast_to([P, TG, 128])
            self.tt(Bt.t, io, i2v, ALU.is_equal, [self.cst, self.slotT], [Bt])
            self.tt(At.t, io, i1v, ALU.is_equal, [self.cst, self.slotT], [At])
            self.tt(At.t, At.t, gv, ALU.mult, [At, self.slotT], [At], eng="pool")
            for q4 in range(TG // 4):
                pw = self.psS[(g_ * (TG // 4) + q4) % 4]
                for tt_ in range(4):
                    tl = q4 * 4 + tt_
                    self.mm(pw.t[:, tt_ * 128:(tt_ + 1) * 128], Bt.t[:, tl, :], At.t[:, tl, :], True, True,
                            [Bt, At], [pw])
                ta = t0 + q4 * 4
                src_ = pw.t[:].rearrange("p (t i) -> p i t", t=4)
                if q4 % 2:
                    self.act(WT.t[:, :, ta:ta + 4], src_, AF.Copy, [pw], [WT])
                else:
                    self.cp(WT.t[:, :, ta:ta + 4], src_, [pw], [WT])

    def sweep(self, layer):
        n = T
        PC = 4
        WT = self.WT
        udst, ubuf = self.ws[f"u{layer}"]
        vdst, vbuf = self.ws[f"v{layer}"]
        ust = Stream(self, self.wring, [(udst[c], ubuf) for c in range(NEXP_C)])
        vst = Stream(self, self.vring, [(vdst[c0:c0 + PC].rearrange("c p d -> p c d"), vbuf)
                                        for c0 in range(0, NEXP_C, PC)], queue="pool")
        xb_ = self.xres.b
        subs = []
        for m in range(16):
            sb_ = Buf(f"xres_m{m}")
            sb_.writers = dict(xb_.writers)
            sb_.readers = list(xb_.readers)
            subs.append(sb_)
        modb = self.modt[layer].b
        wb_ = WT.b
        wtb = []
        for c in range(NEXP_C):
            b_ = Buf(f"wt_c{c}")
            b_.writers = dict(wb_.writers)
            b_.readers = list(wb_.readers)
            wtb.append(b_)

        def u_side(part):
            for cc in range(PC):
                c = part * PC + cc
                ut = ust.get()
                pu = self.psring.next()
                for k in range(KC):
                    self.mm(pu.t[:, 0:n], ut.t[:, k * 128:(k + 1) * 128], self.hT.t[:, k, 0:n], k == 0, k == KC - 1,
                            [ut, self.hT], [pu])
                ab = self.tmpb.next()
                self.act(ab.t[:, 0:n], pu.t[:, 0:n], AF.Gelu_apprx_tanh, [pu], [ab])
                self.S.op("dve", lambda h, c=c, ab=ab: h.tensor_tensor(out=WT.t[:, c, :], in0=WT.t[:, c, :],
                                                                         in1=ab.t[:, 0:n], op=ALU.mult),
                          [wtb[c], ab.b], [wtb[c]])

        def v_side(part):
            vt = vst.get()
            for m in range(16):
                pv = self.psring.next()
                for cc in range(PC):
                    c = part * PC + cc
                    self.S.op("pe", lambda h, pv=pv, vt=vt, cc=cc, m=m, c=c: h.matmul(
                        pv.t[:, 0:n], lhsT=vt.t[:, cc, m * 128:(m + 1) * 128], rhs=WT.t[:, c, :],
                        start=(cc == 0), stop=(cc == PC - 1)), [vt.b, wtb[c]], [pv.b])
                xo = self.xres.t[:, m, 0:n]
                if m % 2 == 0:
                    self.S.op("dve", lambda h, xo=xo, pv=pv, m=m: h.scalar_tensor_tensor(
                        out=xo, in0=pv.t[:, 0:n], scalar=self.G2(layer, m), in1=xo, op0=ALU.mult, op1=ALU.add),
                        [pv.b, modb, subs[m]], [subs[m]])
                else:
                    tm = self.tmpr.next()
                    self.act(tm.t[:, 0:n], pv.t[:, 0:n], AF.Identity, [pv], [tm], scale=self.G2(layer, m))
                    self.S.op("pool", lambda h, xo=xo, tm=tm: h.tensor_tensor(out=xo, in0=xo, in1=tm.t[:, 0:n], op=ALU.add),
                              [tm.b, subs[m]], [subs[m]])

        nparts = NEXP_C // PC
        u_side(0)
        for part in range(nparts):
            if part + 1 < nparts:
                u_side(part + 1)
            v_side(part)
        for sb_ in subs:
            for k_, tok in sb_.writers.items():
                if k_ not in xb_.writers or xb_.writers[k_][1] < tok[1]:
                    xb_.writers[k_] = tok
            xb_.readers.extend(sb_.readers)
        for b_ in wtb:
            for k_, tok in b_.writers.items():
                if k_ not in wb_.writers or wb_.writers[k_][1] < tok[1]:
                    wb_.writers[k_] = tok
            wb_.readers.extend(b_.readers)

    def pass1(self, xsrc, base):
        S = self.S
        if not hasattr(self, "ldx1"):
            self.ldx1 = S.slot("ldx1")
        ldx = self.ldx1
        for sbi in range(NSB):
            self.dma(self.xres.t[:], xsrc[:, :, base + sbi * T:base + (sbi + 1) * T], ldx, writes=[self.xres])
            self.lru_p1(self.xres, sbi)
            if getattr(self, "bg_per_block", 0):
                self.precast_run(self.bg_per_block, bg=True)
        self.S.barrier()
        self.chunk_carry()
        if self.mode == "p1":
            so = S.slot("st_carry")
            self.dma(self.carry_out, self.carry.t[:], so, reads=[self.carry])
        self.S.barrier()

    def exchange(self):
        S = self.S
        cc_in = self.dscr("cc_in", [P, 4 * KC], F32)
        cc_out = self.dscr("cc_out", [NCORE * P, 4 * KC], F32)
        bi, bo = Buf("cc_in"), Buf("cc_out")
        s1, s2, s3 = S.slot("cc1"), S.slot("cc2"), S.slot("cc3")
        self.dma(cc_in, self.carry.t[:, NSB, :, :].rearrange("p a b -> p (a b)"), s1, reads=[self.carry], writes=[bi])
        S.custom("pool", lambda h: h.collective_compute("AllGather", ALU.bypass, replica_groups=[list(range(NCORE))],
                                                        ins=[cc_in], outs=[cc_out]), s2, reads=[bi], writes=[bo])
        self.dma(self.call.t[:].rearrange("p r a b -> p r (a b)"), cc_out.rearrange("(r p) f -> p r f", p=P), s3,
                 reads=[bo], writes=[self.call])
        self.S.barrier()

    def chunk_carry(self):
        c = self.carry
        for d in range(2):
            self.tt(c.t[:, 0:NSB, 2 * d, :], c.t[:, 0:NSB, 2 * d, :],
                    self.dv.t[:, 9 + d, :].unsqueeze(1).broadcast_to([P, NSB, KC]), ALU.mult, [c, self.dv], [c])
            self.act(c.t[:, 0:NSB, 2 * d, :], c.t[:, 0:NSB, 2 * d, :], AF.Exp, [c], [c])
        self.cp(c.t[:, NSB, 0, :], c.t[:, 0, 0, :], [c], [c])
        self.cp(c.t[:, NSB, 1, :], c.t[:, 0, 1, :], [c], [c])
        for sb in range(1, NSB):
            self.tt(c.t[:, NSB, 1, :], c.t[:, NSB, 1, :], c.t[:, sb, 0, :], ALU.mult, [c], [c])
            self.tt(c.t[:, NSB, 1, :], c.t[:, NSB, 1, :], c.t[:, sb, 1, :], ALU.add, [c], [c])
            self.tt(c.t[:, NSB, 0, :], c.t[:, NSB, 0, :], c.t[:, sb, 0, :], ALU.mult, [c], [c])
        self.cp(c.t[:, NSB, 2, :], c.t[:, NSB - 1, 2, :], [c], [c])
        self.cp(c.t[:, NSB, 3, :], c.t[:, NSB - 1, 3, :], [c], [c])
        for sb in range(NSB - 2, -1, -1):
            self.tt(c.t[:, NSB, 3, :], c.t[:, NSB, 3, :], c.t[:, sb, 2, :], ALU.mult, [c], [c])
            self.tt(c.t[:, NSB, 3, :], c.t[:, NSB, 3, :], c.t[:, sb, 3, :], ALU.add, [c], [c])
            self.tt(c.t[:, NSB, 2, :], c.t[:, NSB, 2, :], c.t[:, sb, 2, :], ALU.mult, [c], [c])

    def ctx_and_carries(self):
        NC_ = NSB
        ldc = self.S.slot("ld_ctx")
        self.dma(self.ctxx.t, self.ctxT, ldc, writes=[self.ctxx])
        saved = self.carry
        ctxc = self.sb("ctxcarry", [P, 1, 4, KC])
        self.carry = ctxc
        self.lru_block(self.ctxx, 256, 256, lambda k: self.dv.t[:, 8, k:k + 1],
                       lambda k: self.modt[0].t[:, k, 1:2], False, 0)
        self.carry = saved
        hin = self.hin
        call = self.call
        cm = self.cmk
        sm = self.small
        self.cp(hin.t[:, 0, 0, :], ctxc.t[:, 0, 1, :], [ctxc], [hin])
        self.cp(hin.t[:, 1, NSB, :], ctxc.t[:, 0, 3, :], [ctxc], [hin])
        ns_ = self.nslots
        for d, order in ((0, range(ns_)), (1, range(ns_ - 1, -1, -1))):
            hsl = hin.t[:, 0, 0, :] if d == 0 else hin.t[:, 1, NSB, :]
            for cp_ in order:
                m = cm.t[:, d, cp_:cp_ + 1]
                self.ts(sm.t[:, 0, :], call.t[:, cp_, 2 * d, :], -1.0, m, ALU.add, ALU.mult, [call, cm], [sm])
                self.ts(sm.t[:, 0, :], sm.t[:, 0, :], 1.0, None, ALU.add, ALU.bypass, [sm], [sm])
                self.ts(sm.t[:, 1, :], call.t[:, cp_, 2 * d + 1, :], m, None, ALU.mult, ALU.bypass, [call, cm], [sm])
                self.tt(hsl, hsl, sm.t[:, 0, :], ALU.mult, [hin, sm], [hin])
                self.tt(hsl, hsl, sm.t[:, 1, :], ALU.add, [hin, sm], [hin])
        c = self.carry
        for sb in range(NSB):
            self.tt(hin.t[:, 0, sb + 1, :], hin.t[:, 0, sb, :], c.t[:, sb, 0, :], ALU.mult, [hin, c], [hin])
            self.tt(hin.t[:, 0, sb + 1, :], hin.t[:, 0, sb + 1, :], c.t[:, sb, 1, :], ALU.add, [hin, c], [hin])
        for sb in range(NSB - 1, -1, -1):
            self.tt(hin.t[:, 1, sb, :], hin.t[:, 1, sb + 1, :], c.t[:, sb, 2, :], ALU.mult, [hin, c], [hin])
            self.tt(hin.t[:, 1, sb, :], hin.t[:, 1, sb, :], c.t[:, sb, 3, :], ALU.add, [hin, c], [hin])
        self.S.barrier()

    def pass2(self):
        S = self.S
        ldx = S.slot("ldx")
        sto = S.slot("sto")
        for sbi in range(self.nsb_run):
            self.dma(self.xres.t[:], self.xT[:, :, sbi * T:(sbi + 1) * T], ldx, writes=[self.xres])
            self.lru_block(self.xres, T, 64, lambda k: self.A1(0, k), lambda k: self.S1(0, k), True, sbi,
                           hin_f=lambda h, sbi=sbi: self.hin.t[:, 0, sbi, h:h + 1],
                           hin_b=lambda h, sbi=sbi: self.hin.t[:, 1, sbi + 1, h:h + 1], want_carry=False)
            if self.dbg != "x1":
                self.peer_block(0)
            if self.dbg is None or self.dbg in ("x3", "x4"):
                self.sc_block(T)
            if self.dbg is None or self.dbg == "x4":
                self.peer_block(1)
            if self.dbg is None:
                self.norm_mod(self.xres, T, lambda k: self.V(4, k), lambda k: self.dv.t[:, 13, k:k + 1], self.outT)
                self.dma(self.yT[:, :, sbi * T:(sbi + 1) * T], self.outT.t[:], sto, reads=[self.outT])
            else:
                self.dma(self.yT[:, :, sbi * T:(sbi + 1) * T], self.xres.t[:], sto, reads=[self.xres])
        self.S.barrier()


def _fm(v):
    v = np.asarray(v, np.float32)
    return np.ascontiguousarray(np.moveaxis(v.reshape(v.shape[:-1] + (KC, P)), -1, 0))


def _consts():
    c = np.zeros((P, 384), np.float32)
    c[:, 0:128] = np.eye(P, dtype=np.float32)
    c[:, 128:256] = np.arange(128, dtype=np.float32)[None, :]
    c[:, 256:384] = 1.0
    return c


_NC_CACHE = {}


def _get_nc(mode, nsb_run=NSB, dbg=None):
    key = (mode, nsb_run, dbg)
    if key not in _NC_CACHE:
        _NC_CACHE[key] = KB(mode, nsb_run, dbg).build()
    return _NC_CACHE[key]


def _prep(inputs):
    f = lambda k: np.asarray(inputs[k], np.float32)
    x = f("x")
    shared = {}
    shared["consts"] = _consts()
    vec = np.zeros((NV, D), np.float32)
    vec[0] = f("norm_mix_g")[0]; vec[1] = f("norm_ffn_g")[0]
    vec[2] = f("norm_mix_g")[1]; vec[3] = f("norm_ffn_g")[1]
    vec[4] = f("norm_final_g")
    vec[5:9] = f("lru_conv_w")[0]; vec[9] = f("lru_conv_b")[0]
    vec[10:12] = f("lru_b_a")[0]; vec[12:14] = f("lru_b_x")[0]; vec[14:16] = f("lru_lambda")[0]
    vec[16:19] = f("sc_conv_w")[0]; vec[19] = f("sc_conv_b")[0]
    shared["vecs"] = _fm(vec)
    shared["w_mod"] = f("w_mod")
    shared["b_modT"] = np.ascontiguousarray(f("b_mod").reshape(2, 96, P).transpose(2, 0, 1))
    shared["lru_w_in"] = f("lru_w_in")[0]
    wab = np.stack([f("lru_w_a")[0], f("lru_w_x")[0]], axis=1)
    shared["wab"] = np.ascontiguousarray(wab.transpose(2, 3, 0, 1, 4)).reshape(16, P, 512)
    per_core = []
    for c in range(NCORE):
        b, j = divmod(c, 4)
        m = {}
        xs = x[b, j * NTOK:(j + 1) * NTOK]
        m["xT"] = np.ascontiguousarray(xs.reshape(NTOK, KC, P).transpose(2, 1, 0))
        m["cvec"] = np.ascontiguousarray(np.stack([_fm(f("c")[b]), _fm(f("c_ctx"))], axis=-1))
        per_core.append(m)
    return shared, per_core


def _prep2(inputs):
    f = lambda k: np.asarray(inputs[k], np.float32)
    sh = {}
    sh["lru_w_out"] = f("lru_w_out")[0]
    sh["sc_w_in"] = f("sc_w_in")[0]
    sh["sc_w_out"] = f("sc_w_out")[0]
    sh["peer_w_q"] = f("peer_w_q")
    sk = f("peer_sub_keys")
    sh["skT"] = np.ascontiguousarray(sk.reshape(2, 16, 128, 128).transpose(0, 3, 1, 2))
    u = f("peer_u")
    sh["u_l"] = np.ascontiguousarray(u.reshape(2, NEXP_C, P, KC, P).transpose(0, 1, 4, 3, 2)).reshape(2, NEXP_C, P, D)
    sh["v_l"] = f("peer_v").reshape(2, NEXP_C, P, D)
    ctx = f("ctx")
    pc = []
    for c in range(NCORE):
        b, j = divmod(c, 4)
        m = {}
        m["ctxT"] = np.ascontiguousarray(ctx[b].reshape(256, KC, P).transpose(2, 1, 0))
        cm = np.zeros((P, 2, NCORE), np.float32)
        for c2 in range(NCORE):
            b2, j2 = divmod(c2, 4)
            if b2 == b and j2 < j:
                cm[:, 0, c2] = 1.0
            if b2 == b and j2 > j:
                cm[:, 1, c2] = 1.0
        m["cmask"] = cm
        pc.append(m)
    return sh, pc


def _xT_chunk(x, b, j):
    xs = x[b, j * NTOK:(j + 1) * NTOK]
    return xs.reshape(NTOK, KC, P).transpose(2, 1, 0)


def kernel(**inputs):
    shared, pc = _prep(inputs)
    sh2, pc2 = _prep2(inputs)
    x = np.asarray(inputs["x"], np.float32)
    in_maps = []
    for c in range(NCORE):
        b, j = divmod(c, 4)
        others = [jj for jj in range(4) if jj != j]
        xTo = np.ascontiguousarray(np.concatenate([_xT_chunk(x, b, jj) for jj in others], axis=2))
        cm = np.zeros((P, 2, 3), np.float32)
        for s_, jj in enumerate(others):
            cm[:, 0, s_] = 1.0 if jj < j else 0.0
            cm[:, 1, s_] = 1.0 if jj > j else 0.0
        m = {**shared, **pc[c], **sh2, **pc2[c], "xTo": xTo, "cmask": cm}
        in_maps.append(m)
    nc = _get_nc("f")
    r = run_bass_kernel_spmd(nc, in_maps, core_ids=list(range(NCORE)))
    out = np.zeros((2, 4 * NTOK, D), np.float32)
    for c in range(NCORE):
        b, j = divmod(c, 4)
        yT = np.asarray(r.results[c]["yT"], np.float32)
        out[b, j * NTOK:(j + 1) * NTOK] = yT.transpose(2, 1, 0).reshape(NTOK, D)
    return out
```

```python
import math
import numpy as np
from contextlib import ExitStack
import concourse.bass as bass
import concourse.mybir as mybir
from concourse.bass_utils import run_bass_kernel_spmd

F32 = mybir.dt.float32
BF16 = mybir.dt.bfloat16
U32 = mybir.dt.uint32
ALU = mybir.AluOpType
AF = mybir.ActivationFunctionType
AX = mybir.AxisListType

P = 128
D = 2048
KC = 16
T = 256
NTOK = 4096
NSB = NTOK // T
NCORE = 8
NEXP_C = 128
EPS = 1e-6
NV = 20
NEG = -1.0e30


class Buf:
    __slots__ = ("name", "writers", "readers")

    def __init__(self, name=""):
        self.name = name
        self.writers = {}
        self.readers = []


class Eng:
    def __init__(self, name, sem, self_sync):
        self.name = name
        self.sem = sem
        self.count = 0
        self.known = {}
        self.self_sync = self_sync
        self.prog = []


class Sched:
    def __init__(self, nc, stack):
        self.nc = nc
        self.stack = stack
        self.engs = {}
        self.slots = []
        for name, ss in (("pe", False), ("act", True), ("dve", True), ("pool", True), ("sp", False)):
            self.engs[name] = Eng(name, self.new_sem("e_" + name), ss)
        self.ninst = 0

    def new_sem(self, name):
        return self.stack.enter_context(self.nc.semaphore(name))

    def _waits(self, E, reads, writes):
        need = {}

        def add(tok):
            sem, val, en = tok
            if en == E.name and not E.self_sync:
                return
            k = id(sem)
            if k not in need or need[k][1] < val:
                need[k] = (sem, val)
        for b in reads:
            for tok in b.writers.values():
                add(tok)
        for b in writes:
            for tok in b.writers.values():
                add(tok)
            for tok in b.readers:
                add(tok)
        for k, (sem, val) in need.items():
            if E.known.get(k, 0) < val:
                E.prog.append(("w", sem, val))
                E.known[k] = val

    def _mark(self, tok, key, reads, writes):
        for b in reads:
            b.readers.append(tok)
        for b in writes:
            b.writers[key] = tok
            b.readers = []

    def op(self, ename, build, reads=(), writes=()):
        E = self.engs[ename]
        self._waits(E, reads, writes)
        E.count += 1
        E.prog.append(("o", build, E.sem, 1))
        self._mark((E.sem, E.count, E.name), E.name, reads, writes)
        self.ninst += 1

    def slot(self, name):
        s = {"sem": self.new_sem(name), "n": 0, "name": name}
        self.slots.append(s)
        return s

    def dma(self, qname, out, in_, slot, reads=(), writes=()):
        E = self.engs[qname]
        self._waits(E, reads, writes)
        E.prog.append(("o", (lambda h: h.dma_start(out=out, in_=in_)), slot["sem"], 16))
        slot["n"] += 1
        self._mark((slot["sem"], 16 * slot["n"], "dma"), "dma_" + slot["name"], reads, writes)
        self.ninst += 1

    def custom(self, qname, build, slot, reads=(), writes=(), inc=16):
        E = self.engs[qname]
        self._waits(E, reads, writes)
        E.prog.append(("o", build, slot["sem"], inc))
        slot["n"] += 1
        self._mark((slot["sem"], inc * slot["n"], "dma"), "dma_" + slot["name"], reads, writes)
        self.ninst += 1

    def barrier(self):
        for E in self.engs.values():
            for Fe in self.engs.values():
                if Fe is E or Fe.count == 0:
                    continue
                k = id(Fe.sem)
                if E.known.get(k, 0) < Fe.count:
                    E.prog.append(("w", Fe.sem, Fe.count))
                    E.known[k] = Fe.count
            for s in self.slots:
                k = id(s["sem"])
                if s["n"] and E.known.get(k, 0) < 16 * s["n"]:
                    E.prog.append(("w", s["sem"], 16 * s["n"]))
                    E.known[k] = 16 * s["n"]

    def emit(self):
        with self.nc.Block() as block:
            def run(E):
                def f(h):
                    for it in E.prog:
                        if it[0] == "w":
                            h.wait_ge(it[1], it[2])
                        else:
                            it[1](h).then_inc(it[2], it[3])
                return f
            block.tensor(run(self.engs["pe"]))
            block.scalar(run(self.engs["act"]))
            block.vector(run(self.engs["dve"]))
            block.gpsimd(run(self.engs["pool"]))
            block.sync(run(self.engs["sp"]))


class Tl:
    def __init__(self, t, name):
        self.t = t
        self.b = Buf(name)


class Ring:
    def __init__(self, tiles):
        self.tiles = tiles
        self.i = 0

    def next(self):
        t = self.tiles[self.i % len(self.tiles)]
        self.i += 1
        return t


class Stream:
    def __init__(self, kb, ring, srcs, queue="sp"):
        self.kb = kb
        self.ring = ring
        self.srcs = srcs
        self.R = len(ring)
        for tl in ring:
            if not hasattr(tl, "slot"):
                tl.slot = kb.S.slot("r_" + tl.b.name)
        self.issued = 0
        self.taken = 0
        self.queue = queue

    def _issue(self):
        n = self.issued
        tl = self.ring[n % self.R]
        src, sbuf = self.srcs[n]
        self.kb.S.dma(self.queue, tl.t[:], src, tl.slot,
                      reads=[sbuf] if sbuf is not None else [], writes=[tl.b])
        self.issued += 1

    def get(self):
        while self.issued < min(self.taken + self.R, len(self.srcs)):
            self._issue()
        tl = self.ring[self.taken % self.R]
        self.taken += 1
        return tl


class KB:
    def __init__(self, mode, nsb_run=NSB, dbg=None):
        self.mode = mode
        self.nsb_run = nsb_run
        self.dbg = dbg
        self.nc = bass.Bass("TRN2", target_bir_lowering=False)
        self.st = ExitStack()
        full = ["w_in", "wab", "w_out", "sc_w_in", "sc_w_out", "wq0", "wq1", "u0", "v0", "u1", "v1"]
        sub = {"s_setup": [], "x1": full[:3], "s_ctx": full[:3], "x2": full[:3] + ["wq0", "u0", "v0"],
               "x3": full[:5] + ["wq0", "u0", "v0"]}
        self.wlist = ["w_in_xb", "wab"] if mode == "p1" else sub.get(dbg, full)

    def din(self, name, shape, dt=F32):
        return self.nc.dram_tensor(name, list(shape), dt, kind="ExternalInput").ap()

    def dout(self, name, shape, dt=F32):
        return self.nc.dram_tensor(name, list(shape), dt, kind="ExternalOutput").ap()

    def dscr(self, name, shape, dt=BF16):
        return self.nc.dram_tensor(name, list(shape), dt, kind="Internal").ap()

    def sb(self, name, shape, dt=F32):
        return Tl(self.st.enter_context(self.nc.sbuf_tensor(name, list(shape), dt)), name)

    def ps(self, name, shape, dt=F32):
        return Tl(self.st.enter_context(self.nc.psum_tensor(name, list(shape), dt)), name)

    def act(self, out, in_, func, reads, writes, scale=1.0, bias=0.0, eng="act"):
        self.S.op(eng, lambda h: h.activation(out=out, in_=in_, func=func, scale=scale, bias=bias),
                  [r.b for r in reads], [w.b for w in writes])

    def tt(self, out, in0, in1, op, reads, writes, eng="dve"):
        self.S.op(eng, lambda h: h.tensor_tensor(out=out, in0=in0, in1=in1, op=op),
                  [r.b for r in reads], [w.b for w in writes])

    def ts(self, out, in0, s1, s2, op0, op1, reads, writes, eng="dve"):
        self.S.op(eng, lambda h: h.tensor_scalar(out=out, in0=in0, scalar1=s1, scalar2=s2, op0=op0, op1=op1),
                  [r.b for r in reads], [w.b for w in writes])

    def stt(self, out, in0, scalar, in1, op0, op1, reads, writes):
        self.S.op("dve", lambda h: h.scalar_tensor_tensor(out=out, in0=in0, scalar=scalar, in1=in1, op0=op0, op1=op1),
                  [r.b for r in reads], [w.b for w in writes])

    def cp(self, out, in_, reads, writes, eng="dve"):
        self.S.op(eng, lambda h: h.tensor_copy(out=out, in_=in_),
                  [r.b for r in reads], [w.b for w in writes])

    def mm(self, out, lhsT, rhs, start, stop, reads, writes):
        self.S.op("pe", lambda h: h.matmul(out, lhsT=lhsT, rhs=rhs, start=start, stop=stop),
                  [r.b for r in reads], [w.b for w in writes])

    def vop(self, fn, reads, writes, eng="dve"):
        self.S.op(eng, fn, [r.b for r in reads], [w.b for w in writes])

    def dma(self, out, in_, slot, reads=(), writes=(), q="sp"):
        self.S.dma(q, out, in_, slot, [r if isinstance(r, Buf) else r.b for r in reads],
                   [w if isinstance(w, Buf) else w.b for w in writes])

    def build(self):
        nc = self.nc
        mode = self.mode
        with self.st:
            self.S = Sched(nc, self.st)
            self.declare_io()
            self.alloc()
            self.setup()
            if mode == "p1":
                self.precast(["w_in_xb", "wab"])
                self.precast_finish()
                self.S.barrier()
                self.pass1(self.xT, 0)
            elif mode == "f":
                self.precast(self.wlist)
                self.precast_run(36, bg=False)
                self.S.barrier()
                self.bg_per_block = -(-(len(self.pc_jobs) - 36) // (4 * NSB))
                for s_ in range(3):
                    self.pass1(self.xTo, s_ * NTOK)
                    self.cp(self.call.t[:, s_, :, :], self.carry.t[:, NSB, :, :], [self.carry], [self.call])
                    self.S.barrier()
                self.pass1(self.xT, 0)
                self.precast_finish()
                self.S.barrier()
                self.ctx_and_carries()
                self.pass2()
            else:
                self.precast(self.wlist)
                self.precast_finish()
                self.S.barrier()
                if self.dbg not in ("s_setup", "s_precast"):
                    self.ctx_and_carries()
                    if self.dbg != "s_ctx":
                        self.pass2()
            self.S.barrier()
            self.S.emit()
        return nc

    def declare_io(self):
        self.xT = self.din("xT", [P, KC, NTOK])
        self.consts = self.din("consts", [P, 3 * 128])
        self.vecs = self.din("vecs", [P, NV, KC])
        self.cvec = self.din("cvec", [P, KC, 2])
        self.w_mod = self.din("w_mod", [2, D, 6 * D])
        self.b_modT = self.din("b_modT", [P, 2, 96])
        self.lru_w_in = self.din("lru_w_in", [D, 2 * D])
        self.wab_in = self.din("wab", [16, P, 512])
        if self.mode == "p1":
            self.carry_out = self.dout("carry", [P, NSB + 1, 4, KC])
        else:
            self.ctxT = self.din("ctxT", [P, KC, 256])
            if self.mode == "f":
                self.xTo = self.din("xTo", [P, KC, 3 * NTOK])
            if self.mode == "p2":
                self.carry_own = self.din("carry_own", [P, NSB + 1, 4, KC])
                self.carry_all = self.din("carry_all", [P, NCORE, 4, KC])
            self.nslots = 3 if self.mode == "f" else NCORE
            self.cmask = self.din("cmask", [P, 2, self.nslots])
            wl = self.wlist
            if "w_out" in wl:
                self.lru_w_out = self.din("lru_w_out", [D, D])
            if "sc_w_in" in wl:
                self.sc_w_in = self.din("sc_w_in", [D, 3 * D])
                self.sc_w_out = self.din("sc_w_out", [D, D])
            self.has_peer = "wq0" in wl
            if self.has_peer:
                self.peer_w_q = self.din("peer_w_q", [2, D, D])
                self.skT_in = self.din("skT", [2, P, 16, 128])
                self.u_l = self.din("u_l", [2, NEXP_C, P, D])
                self.v_l = self.din("v_l", [2, NEXP_C, P, D])
            self.yT = self.dout("yT", [P, KC, NTOK])
        self.ws = {}
        self.ws["w_in"] = (self.dscr("ws_w_in", [32, P, D]), Buf("ws_w_in"))
        self.ws["wab"] = (self.dscr("ws_wab", [16, P, 512]), Buf("ws_wab"))
        if self.mode != "p1":
            self.ws["w_out"] = (self.dscr("ws_w_out", [16, P, D]), Buf("ws_w_out"))
            self.ws["sc_w_in"] = (self.dscr("ws_sc_w_in", [48, P, D]), Buf("ws_sc_w_in"))
            self.ws["sc_w_out"] = (self.dscr("ws_sc_w_out", [16, P, D]), Buf("ws_sc_w_out"))
            for i in range(2):
                self.ws[f"wq{i}"] = (self.dscr(f"ws_wq{i}", [16, P, D]), Buf(f"ws_wq{i}"))
                self.ws[f"u{i}"] = (self.dscr(f"ws_u{i}", [NEXP_C, P, D]), Buf(f"ws_u{i}"))
                self.ws[f"v{i}"] = (self.dscr(f"ws_v{i}", [NEXP_C, P, D]), Buf(f"ws_v{i}"))

    def alloc(self):
        sb, ps = self.sb, self.ps
        self.cst = sb("cst", [P, 3 * 128])
        self.ident = self.cst.t[:, 0:128]
        self.iota = self.cst.t[:, 128:256]
        self.ones_bf = sb("ones_bf", [P, 128], BF16)
        self.iota_bf = sb("iota_bf", [P, 128], BF16)
        self.vec = sb("vec", [P, NV, KC])
        self.cv = sb("cv", [P, KC, 2])
        self.modt = [sb(f"modt{i}", [P, 96, 2]) for i in range(2)]
        self.bmod = sb("bmod", [P, 2, 96])
        self.dv = sb("dv", [P, 24, KC])
        self.wabring = [sb(f"wabr{i}", [P, 4, 128], BF16) for i in range(2)]
        self.xres = sb("xres", [P, KC, T])
        self.hT = sb("hT", [P, KC, T], BF16)
        self.bfA = sb("bfA", [P, KC, T], BF16)
        self.rstd = sb("rstd", [P, T])
        self.tmpr = Ring([sb(f"tmp{i}", [P, T]) for i in range(11)])
        self.tmpb = Ring([sb(f"tmpb{i}", [P, T], BF16) for i in range(4)])
        self.arena = sb("arena", [P, 16384])
        self.wring = [sb(f"wr{i}", [P, D], BF16) for i in range(3)]
        self.carry = sb("carryt", [P, NSB + 1, 4, KC])
        self.hin = sb("hin", [P, 2, NSB + 1, KC])
        self.small = sb("small", [P, 8, KC])
        if self.mode != "p1":
            self.vring = [sb(f"vr{i}", [P, 4, D], BF16) for i in range(2)]
            self.skT = [sb(f"skT{i}", [P, 16, 128], BF16) for i in range(2)]
            self.X1 = sb("X1", [P, 2048])
            self.X2 = sb("X2", [P, 2048])
            self.sv = sb("sv", [P, 16, 16])
            self.si_u = sb("si_u", [P, 16, 16], U32)
            self.si_f = sb("si_f", [P, 16, 16])
            self.cvv = sb("cvv", [P, 8, 16])
            self.ci_u = sb("ci_u", [P, 8, 16], U32)
            self.ca_u = sb("ca_u", [P, 8, 16], U32)
            self.cb_u = sb("cb_u", [P, 8, 16], U32)
            self.caf = sb("caf", [P, 8, 16])
            self.cbf = sb("cbf", [P, 8, 16])
            self.slot3 = sb("slot3", [P, 3, 128])
            self.slotT = sb("slotT", [P, 3, T])
            self.zs = sb("zs", [P, 4, 8])
            self.cmk = sb("cmk", [P, 2, self.nslots])
            self.call = sb("call", [P, NCORE, 4, KC])
            self.WT = Tl(self.arena.t[:].bitcast(BF16).rearrange("p (c t) -> p c t", t=T), "WT")
            self.WT.b = self.arena.b
            self.ctxx = Tl(self.arena.t[:, 0:KC * 256].rearrange("p (k t) -> p k t", k=KC), "ctxx")
            self.ctxx.b = self.arena.b
            self.outT = Tl(self.arena.t[:, 0:KC * T].rearrange("p (k t) -> p k t", k=KC), "outT")
            self.outT.b = self.arena.b
        pa = [ps(f"psA{i}", [P, 512]) for i in range(3)]
        pst = []
        for half in range(2):
            for bnk in range(3):
                t_ = Tl(pa[bnk].t[:, half * 256:half * 256 + 256], f"psh{bnk}_{half}")
                t_.b = pa[bnk].b
                pst.append(t_)
        self.psring = Ring(pst)
        self._pa = pa
        self.psS = [ps(f"psS{i}", [P, 512]) for i in range(4)]
        self.psM = ps("psM", [P, 512])
        big = []
        banks = self._pa + self.psS
        for half in range(2):
            for bk in banks:
                t_ = Tl(bk.t[:, half * 256:half * 256 + 256], f"pb_{bk.b.name}_{half}")
                t_.b = bk.b
                big.append(t_)
        self.psbig = Ring(big)
        self.p1ring = Ring([Tl(self.arena.t[:, 9216 + i * 256:9216 + (i + 1) * 256], f"p1t{i}") for i in range(28)])

    def setup(self):
        S = self.S
        self._nld = 0

        def ld1(o, i, tl):
            self._nld += 1
            self.dma(o, i, S.slot(f"ld1_{self._nld}"), writes=[tl])
        self.ld1 = ld1
        ld1(self.cst.t[:], self.consts, self.cst)
        ld1(self.vec.t[:], self.vecs, self.vec)
        ld1(self.cv.t[:], self.cvec, self.cv)
        ld1(self.bmod.t[:], self.b_modT, self.bmod)
        self.cp(self.ones_bf.t[:], self.cst.t[:, 256:384], [self.cst], [self.ones_bf])
        self.cp(self.iota_bf.t[:], self.cst.t[:, 128:256], [self.cst], [self.iota_bf])
        self.act(self.cv.t[:], self.cv.t[:], AF.Silu, [self.cv], [self.cv])
        nlay = 1 if self.mode == "p1" else 2
        NB = 256
        wmr = [Tl(self.arena.t[:, i * 4096:(i + 1) * 4096].rearrange("p (k n) -> p k n", k=KC), f"wm{i}")
               for i in range(3)]
        wslots = [S.slot(f"wm{i}") for i in range(3)]
        cnt = 0
        for i in range(nlay):
            nblk = (3 * D // NB) if self.mode == "p1" else (6 * D // NB)
            for nb in range(nblk):
                wt = wmr[cnt % 3]
                src = self.w_mod[i, :, nb * NB:(nb + 1) * NB].rearrange("(k p) n -> p k n", p=P)
                self.dma(wt.t, src, wslots[cnt % 3], writes=[wt])
                cnt += 1
                for mi in range(NB // 128):
                    m = nb * (NB // 128) + mi
                    for k in range(KC):
                        self.mm(self.psM.t[:, 2 * m:2 * m + 2], wt.t[:, k, mi * 128:(mi + 1) * 128],
                                self.cv.t[:, k, :], k == 0, k == KC - 1, [wt, self.cv], [self.psM])
            nm = nblk * NB // 128
            self.tt(self.modt[i].t[:, 0:nm, :], self.psM.t[:, 0:2 * nm].rearrange("p (m c) -> p m c", c=2),
                    self.bmod.t[:, i, 0:nm].unsqueeze(2).broadcast_to([P, nm, 2]), ALU.add,
                    [self.psM, self.bmod], [self.modt[i]])
        dv = self.dv
        for i in range(nlay):
            self.stt(dv.t[:, 2 * i + 0, :], self.modt[i].t[:, 16:32, 0], 1.0, self.vec.t[:, 2 * i + 0, :],
                     ALU.add, ALU.mult, [self.modt[i], self.vec], [dv])
            if self.mode != "p1":
                self.stt(dv.t[:, 2 * i + 1, :], self.modt[i].t[:, 64:80, 0], 1.0, self.vec.t[:, 2 * i + 1, :],
                         ALU.add, ALU.mult, [self.modt[i], self.vec], [dv])
        self.stt(dv.t[:, 8, :], self.modt[0].t[:, 16:32, 1], 1.0, self.vec.t[:, 0, :],
                 ALU.add, ALU.mult, [self.modt[0], self.vec], [dv])
        self.act(dv.t[:, 9:11, :], self.vec.t[:, 14:16, :], AF.Exp, [self.vec], [dv], scale=-1.0)
        self.act(dv.t[:, 9:11, :], dv.t[:, 9:11, :], AF.Ln, [dv], [dv], bias=1.0)
        self.ts(dv.t[:, 9:11, :], dv.t[:, 9:11, :], -8.0, None, ALU.mult, ALU.bypass, [dv], [dv])
        self.ts(dv.t[:, 11:13, :], dv.t[:, 9:11, :], 2.0, None, ALU.mult, ALU.bypass, [dv], [dv])
        self.ts(dv.t[:, 14:16, :], self.vec.t[:, 10:12, :], -1.0, None, ALU.mult, ALU.bypass, [self.vec], [dv])
        self.ts(dv.t[:, 16:18, :], self.vec.t[:, 12:14, :], -1.0, None, ALU.mult, ALU.bypass, [self.vec], [dv])
        self.ts(dv.t[:, 18:20, :], dv.t[:, 9:11, :], 0.5, None, ALU.mult, ALU.bypass, [dv], [dv])
        self.ts(dv.t[:, 20:22, :], dv.t[:, 9:11, :], 0.5 * T, None, ALU.mult, ALU.bypass, [dv], [dv])
        self.vop(lambda h: h.memset(dv.t[:, 13, :], 0.0), [], [dv], eng="pool")
        if self.mode != "p1":
            ld1(self.cmk.t[:], self.cmask, self.cmk)
            if self.mode == "p2":
                ld1(self.call.t[:], self.carry_all, self.call)
                ld1(self.carry.t[:], self.carry_own, self.carry)
        self.S.barrier()

    def A1(self, i, k): return self.dv.t[:, 2 * i, k:k + 1]
    def A2(self, i, k): return self.dv.t[:, 2 * i + 1, k:k + 1]
    def S1(self, i, k): return self.modt[i].t[:, 0 + k, 0:1]
    def G1(self, i, k): return self.modt[i].t[:, 32 + k, 0:1]
    def S2(self, i, k): return self.modt[i].t[:, 48 + k, 0:1]
    def G2(self, i, k): return self.modt[i].t[:, 80 + k, 0:1]
    def V(self, idx, k): return self.vec.t[:, idx, k:k + 1]

    def precast(self, which):
        S = self.S
        NST = 3
        fin = [Tl(self.arena.t[:, i * 2048:(i + 1) * 2048], f"pcin{i}") for i in range(NST)]
        fob = [Tl(self.arena.t[:, 6144 + i * 1024:6144 + (i + 1) * 1024].bitcast(BF16), f"pcout{i}")
               for i in range(NST)]
        sl_in = [S.slot(f"pci{i}") for i in range(NST)]
        sl_out = [S.slot(f"pco{i}") for i in range(NST)]
        engs = ["dve", "act", "pool"]
        jobs = []

        def add_W(W, key, cc_lo, cc_hi, base=0):
            dst, dbuf = self.ws[key]
            for cc in range(cc_lo, cc_hi):
                src = W[:, cc * 128:(cc + 1) * 128].rearrange("(k p) n -> p k n", p=P)
                jobs.append((src, dst[cc - base], dbuf, 16))

        def add_rows(R, key):
            dst, dbuf = self.ws[key]
            for c in range(NEXP_C):
                jobs.append((R[c], dst[c], dbuf, 0))
        for w in which:
            if w == "w_in_xb":
                add_W(self.lru_w_in, "w_in", 16, 32)
            elif w == "w_in":
                add_W(self.lru_w_in, "w_in", 0, 32)
            elif w == "wab":
                dst, dbuf = self.ws["wab"]
                for j in range(4):
                    jobs.append((self.wab_in[4 * j:4 * j + 4].rearrange("h p f -> p h f"),
                                 dst[4 * j:4 * j + 4].rearrange("h p f -> p h f"), dbuf, 4))
            elif w == "w_out":
                add_W(self.lru_w_out, "w_out", 0, 16)
            elif w == "sc_w_in":
                add_W(self.sc_w_in, "sc_w_in", 0, 48)
            elif w == "sc_w_out":
                add_W(self.sc_w_out, "sc_w_out", 0, 16)
            elif w in ("wq0", "wq1"):
                add_W(self.peer_w_q[int(w[2])], w, 0, 16)
            elif w in ("u0", "u1"):
                add_rows(self.u_l[int(w[1])], w)
            elif w in ("v0", "v1"):
                add_rows(self.v_l[int(w[1])], w)
        self.pc_jobs = jobs
        self.pc_n = 0
        self.pc_in = 0
        self.pc_st = (fin, fob, sl_in, sl_out, NST)
        self.pc_st_bg = ([S.slot(f"pcib{i}") for i in range(NST)], [S.slot(f"pcob{i}") for i in range(NST)])

    def precast_run(self, k, bg):
        fin, fob, sl_in, sl_out, NST = self.pc_st
        if bg:
            sl_in, sl_out = self.pc_st_bg
        jobs = self.pc_jobs
        engs = ["pool"] if bg else ["dve", "act", "pool"]
        q = "pool" if bg else "sp"
        stop = min(self.pc_n + k, len(jobs))
        while self.pc_n < stop:
            n = self.pc_n
            while self.pc_in < min(n + NST, len(jobs)):
                m = self.pc_in
                src, dst, dbuf, is3 = jobs[m]
                i = m % NST
                o = fin[i].t.rearrange("p (k n) -> p k n", k=is3) if is3 else fin[i].t
                self.dma(o, src, sl_in[i], writes=[fin[i]], q=q)
                self.pc_in += 1
            src, dst, dbuf, is3 = jobs[n]
            i = n % NST
            e = engs[n % len(engs)]
            if e == "act":
                self.act(fob[i].t, fin[i].t, AF.Copy, [fin[i]], [fob[i]])
            else:
                self.cp(fob[i].t, fin[i].t, [fin[i]], [fob[i]], eng=e)
            oo = fob[i].t.rearrange("p (k n) -> p k n", k=4) if is3 == 4 else fob[i].t
            self.dma(dst, oo, sl_out[i], reads=[fob[i]], writes=[dbuf], q=q)
            self.pc_n += 1

    def precast_finish(self):
        self.precast_run(len(self.pc_jobs), bg=False)
        self.S.barrier()
        S = self.S
        if self.mode != "p1" and self.has_peer:
            for i in range(2):
                st = Tl(self.arena.t[:, i * 2048:(i + 1) * 2048], f"sks{i}")
                self.ld1(st.t, self.skT_in[i].rearrange("p a b -> p (a b)"), st)
                self.cp(self.skT[i].t[:].rearrange("p a b -> p (a b)"), st.t, [st], [self.skT[i]])

    def wstream(self, key, ccs):
        dst, dbuf = self.ws[key]
        return Stream(self, self.wring, [(dst[cc], dbuf) for cc in ccs])

    def norm_mod(self, xsrc, n, Afn, Sfn, out_tl, out_is_f32=False):
        sq = self.bfA
        self.act(sq.t[:, :, 0:n], xsrc.t[:, :, 0:n], AF.Square, [xsrc], [sq])
        pm = self.psM
        for k in range(KC):
            self.mm(pm.t[:, 0:n], self.ones_bf.t[:], sq.t[:, k, 0:n], k == 0, k == KC - 1,
                    [self.ones_bf, sq], [pm])
        r = self.rstd
        self.act(r.t[:, 0:n], pm.t[:, 0:n], AF.Ln, [pm], [r], scale=1.0 / D, bias=EPS)
        self.act(r.t[:, 0:n], r.t[:, 0:n], AF.Exp, [r], [r], scale=-0.5)
        for k in range(KC):
            tm = self.tmpr.next()
            self.tt(tm.t[:, 0:n], xsrc.t[:, k, 0:n], r.t[:, 0:n], ALU.mult, [xsrc, r], [tm])
            self.act(out_tl.t[:, k, 0:n], tm.t[:, 0:n], AF.Identity, [tm], [out_tl], scale=Afn(k), bias=Sfn(k))

    def lru_block(self, xsrc, n, rowlen, Afn, Sfn, pass2, sbi, hin_f=None, hin_b=None, want_carry=True):
        self.norm_mod(xsrc, n, Afn, Sfn, self.hT)
        ccs = (list(range(16)) if pass2 else []) + list(range(16, 32))
        wsr = self.wstream("w_in", ccs)
        wabs = Stream(self, self.wabring, [(self.ws["wab"][0][h].rearrange("p (a b) -> p a b", a=4), self.ws["wab"][1])
                                           for h in range(16)])
        YT = self.bfA
        if pass2:
            for h in range(16):
                wt = wsr.get()
                pg = self.psring.next()
                for k in range(KC):
                    self.mm(pg.t[:, 0:n], wt.t[:, k * 128:(k + 1) * 128], self.hT.t[:, k, 0:n], k == 0, k == KC - 1,
                            [wt, self.hT], [pg])
                self.act(YT.t[:, h, 0:n], pg.t[:, 0:n], AF.Gelu_apprx_tanh, [pg], [YT])
        LN_HALF = math.log(0.5)
        for h in range(16):
            wabt = wabs.get()
            wt = wsr.get()
            px = self.psring.next()
            for k in range(KC):
                self.mm(px.t[:, 0:n], wt.t[:, k * 128:(k + 1) * 128], self.hT.t[:, k, 0:n], k == 0, k == KC - 1,
                        [wt, self.hT], [px])
            xc = self.tmpr.next()
            self.act(xc.t[:, 0:n], px.t[:, 0:n], AF.Identity, [px], [xc], scale=self.V(7, h), bias=self.V(9, h))
            pv = px.t[:, 0:n].rearrange("p (r c) -> p r c", c=rowlen)
            xv = xc.t[:, 0:n].rearrange("p (r c) -> p r c", c=rowlen)
            for tap, off in ((6, -1), (5, -2), (8, 1)):
                if off < 0:
                    o_, i_ = xv[:, :, -off:], pv[:, :, :rowlen + off]
                else:
                    o_, i_ = xv[:, :, :rowlen - off], pv[:, :, off:]
                self.stt(o_, i_, self.V(tap, h), o_, ALU.mult, ALU.add, [px, xc], [xc])
            xcb = self.tmpb.next()
            self.act(xcb.t[:, 0:n], xc.t[:, 0:n], AF.Copy, [xc], [xcb])
            hs = []
            for d in range(2):
                pr = self.psring.next()
                self.mm(pr.t[:, 0:n], wabt.t[:, d * 2 + 0, :], xcb.t[:, 0:n], True, True, [wabt, xcb], [pr])
                pi = self.psring.next()
                self.mm(pi.t[:, 0:n], wabt.t[:, d * 2 + 1, :], xcb.t[:, 0:n], True, True, [wabt, xcb], [pi])
                rr = self.tmpr.next()
                self.act(rr.t[:, 0:n], pr.t[:, 0:n], AF.Exp, [pr], [rr], scale=-1.0, bias=self.dv.t[:, 14 + d, h:h + 1])
                ii = self.tmpr.next()
                self.act(ii.t[:, 0:n], pi.t[:, 0:n], AF.Exp, [pi], [ii], scale=-1.0, bias=self.dv.t[:, 16 + d, h:h + 1])
                self.ts(rr.t[:, 0:n], rr.t[:, 0:n], 1.0, None, ALU.add, ALU.bypass, [rr], [rr])
                self.vop(lambda hd, rr=rr: hd.reciprocal(out=rr.t[:, 0:n], in_=rr.t[:, 0:n]), [rr], [rr])
                self.act(ii.t[:, 0:n], ii.t[:, 0:n], AF.Ln, [ii], [ii], bias=1.0)
                self.act(ii.t[:, 0:n], ii.t[:, 0:n], AF.Exp, [ii], [ii], scale=-1.0)
                aa = self.tmpr.next()
                self.act(aa.t[:, 0:n], rr.t[:, 0:n], AF.Exp, [rr], [aa], scale=self.dv.t[:, 9 + d, h:h + 1])
                a2 = self.tmpr.next()
                self.act(a2.t[:, 0:n], rr.t[:, 0:n], AF.Exp, [rr], [a2], scale=self.dv.t[:, 11 + d, h:h + 1])
                self.act(a2.t[:, 0:n], a2.t[:, 0:n], AF.Ln, [a2], [a2], scale=-1.0, bias=1.0)
                self.act(a2.t[:, 0:n], a2.t[:, 0:n], AF.Exp, [a2], [a2], scale=0.5)
                self.tt(ii.t[:, 0:n], ii.t[:, 0:n], xc.t[:, 0:n], ALU.mult, [ii, xc], [ii])
                self.tt(ii.t[:, 0:n], ii.t[:, 0:n], a2.t[:, 0:n], ALU.mult, [ii, a2], [ii])
                first = 0 if d == 0 else n - 1
                hinp = hin_f if d == 0 else hin_b
                if hinp is not None:
                    self.stt(ii.t[:, first:first + 1], aa.t[:, first:first + 1], hinp(h), ii.t[:, first:first + 1],
                             ALU.mult, ALU.add, [aa, ii, self.hin], [ii])
                hh = self.tmpr.next()
                if d == 0:
                    o_, a_, b_ = hh.t[:, 0:n], aa.t[:, 0:n], ii.t[:, 0:n]
                else:
                    o_, a_, b_ = hh.t[:, 0:n][:, ::-1], aa.t[:, 0:n][:, ::-1], ii.t[:, 0:n][:, ::-1]
                self.vop(lambda hd, o_=o_, a_=a_, b_=b_: hd.tensor_tensor_scan(out=o_, data0=a_, data1=b_, initial=0.0,
                                                                           op0=ALU.mult, op1=ALU.add),
                         [aa, ii], [hh])
                if want_carry:
                    last = n - 1 if d == 0 else 0
                    ctl = self.carry
                    self.vop(lambda hd, rr=rr, d=d, h=h, ctl=ctl: hd.tensor_reduce(out=ctl.t[:, sbi, 2 * d, h:h + 1],
                                                                                 in_=rr.t[:, 0:n], axis=AX.X, op=ALU.add),
                             [rr], [ctl])
                    self.cp(ctl.t[:, sbi, 2 * d + 1, h:h + 1], hh.t[:, last:last + 1], [hh], [ctl])
                hs.append(hh)
            if pass2:
                self.tt(hs[0].t[:, 0:n], hs[0].t[:, 0:n], hs[1].t[:, 0:n], ALU.add, [hs[0], hs[1]], [hs[0]])
                self.tt(YT.t[:, h, 0:n], YT.t[:, h, 0:n], hs[0].t[:, 0:n], ALU.mult, [hs[0], YT], [YT])
        if pass2:
            self.out_proj("w_out", YT, n, 0)

    def lru_p1(self, xsrc, sbi, pass2=False):
        n, rowlen = T, 64
        self.norm_mod(xsrc, n, lambda k: self.A1(0, k), lambda k: self.S1(0, k), self.hT)
        wsr = self.wstream("w_in", (list(range(16)) if pass2 else []) + list(range(16, 32)))
        YT = self.bfA
        if pass2:
            for h in range(16):
                wt = wsr.get()
                pg = self.psbig.next()
                for k in range(KC):
                    self.mm(pg.t[:, 0:n], wt.t[:, k * 128:(k + 1) * 128], self.hT.t[:, k, 0:n], k == 0, k == KC - 1,
                            [wt, self.hT], [pg])
                self.act(YT.t[:, h, 0:n], pg.t[:, 0:n], AF.Gelu_apprx_tanh, [pg], [YT])
        wabs = Stream(self, self.wabring, [(self.ws["wab"][0][h].rearrange("p (a b) -> p a b", a=4), self.ws["wab"][1])
                                           for h in range(16)])
        ctl = self.carry
        G = 2
        R = self.p1ring

        def s0(h, c):
            wt = wsr.get()
            c["px"] = px = self.psbig.next()
            for k in range(KC):
                self.mm(px.t[:, 0:n], wt.t[:, k * 128:(k + 1) * 128], self.hT.t[:, k, 0:n], k == 0, k == KC - 1,
                        [wt, self.hT], [px])

        def s1(h, c):
            c["xc"] = xc = R.next()
            self.act(xc.t[:, 0:n], c["px"].t[:, 0:n], AF.Identity, [c["px"]], [xc], scale=self.V(7, h), bias=self.V(9, h))

        def s2(h, c):
            px, xc = c["px"], c["xc"]
            pv = px.t[:, 0:n].rearrange("p (r c) -> p r c", c=rowlen)
            xv = xc.t[:, 0:n].rearrange("p (r c) -> p r c", c=rowlen)
            for tap, off in ((6, -1), (5, -2), (8, 1)):
                if off < 0:
                    o_, i_ = xv[:, :, -off:], pv[:, :, :rowlen + off]
                else:
                    o_, i_ = xv[:, :, :rowlen - off], pv[:, :, off:]
                self.stt(o_, i_, self.V(tap, h), o_, ALU.mult, ALU.add, [px, xc], [xc])

        def s3(h, c):
            c["xcb"] = xcb = self.tmpb.next()
            self.act(xcb.t[:, 0:n], c["xc"].t[:, 0:n], AF.Copy, [c["xc"]], [xcb])

        def s4(h, c):
            wabt = wabs.get()
            xcb = c["xcb"]
            for d in range(2):
                c["pr", d] = pr = self.psbig.next()
                self.mm(pr.t[:, 0:n], wabt.t[:, d * 2 + 0, :], xcb.t[:, 0:n], True, True, [wabt, xcb], [pr])
                c["pi", d] = pi = self.psbig.next()
                self.mm(pi.t[:, 0:n], wabt.t[:, d * 2 + 1, :], xcb.t[:, 0:n], True, True, [wabt, xcb], [pi])

        def s5(h, c):
            for d in range(2):
                c["rr", d] = rr = R.next()
                self.act(rr.t[:, 0:n], c["pr", d].t[:, 0:n], AF.Exp, [c["pr", d]], [rr], scale=-1.0,
                         bias=self.dv.t[:, 14 + d, h:h + 1])
                c["ii", d] = ii = R.next()
                self.act(ii.t[:, 0:n], c["pi", d].t[:, 0:n], AF.Exp, [c["pi", d]], [ii], scale=-1.0,
                         bias=self.dv.t[:, 16 + d, h:h + 1])

        def s6(h, c):
            for d in range(2):
                rr, ii = c["rr", d], c["ii", d]
                self.act(rr.t[:, 0:n], rr.t[:, 0:n], AF.Ln, [rr], [rr], bias=1.0)
                self.act(ii.t[:, 0:n], ii.t[:, 0:n], AF.Ln, [ii], [ii], bias=1.0)

        def s7(h, c):
            for d in range(2):
                rr, ii = c["rr", d], c["ii", d]
                self.act(rr.t[:, 0:n], rr.t[:, 0:n], AF.Exp, [rr], [rr], scale=-1.0)
                self.act(ii.t[:, 0:n], ii.t[:, 0:n], AF.Exp, [ii], [ii], scale=-1.0)

        def s8(h, c):
            for d in range(2):
                rr = c["rr", d]
                c["aa", d] = aa = R.next()
                self.act(aa.t[:, 0:n], rr.t[:, 0:n], AF.Exp, [rr], [aa], scale=self.dv.t[:, 9 + d, h:h + 1])
                c["a2", d] = a2 = R.next()
                self.act(a2.t[:, 0:n], rr.t[:, 0:n], AF.Exp, [rr], [a2], scale=self.dv.t[:, 11 + d, h:h + 1])
                if not pass2:
                    self.vop(lambda hd, rr=rr, d=d, h=h: hd.tensor_reduce(out=ctl.t[:, sbi, 2 * d, h:h + 1],
                                                                        in_=rr.t[:, 0:n], axis=AX.X, op=ALU.add),
                             [rr], [ctl])
                ii = c["ii", d]
                self.tt(ii.t[:, 0:n], ii.t[:, 0:n], c["xc"].t[:, 0:n], ALU.mult, [ii, c["xc"]], [ii])

        def s9(h, c):
            for d in range(2):
                a2 = c["a2", d]
                self.act(a2.t[:, 0:n], a2.t[:, 0:n], AF.Ln, [a2], [a2], scale=-1.0, bias=1.0)

        def s10(h, c):
            for d in range(2):
                a2 = c["a2", d]
                self.act(a2.t[:, 0:n], a2.t[:, 0:n], AF.Exp, [a2], [a2], scale=0.5)

        def s11(h, c):
            for d in range(2):
                ii, a2 = c["ii", d], c["a2", d]
                self.tt(ii.t[:, 0:n], ii.t[:, 0:n], a2.t[:, 0:n], ALU.mult, [ii, a2], [ii])

        def s12(h, c):
            for d in range(2):
                aa, ii, hh = c["aa", d], c["ii", d], c["a2", d]
                if pass2:
                    first = 0 if d == 0 else n - 1
                    hv = self.hin.t[:, 0, sbi, h:h + 1] if d == 0 else self.hin.t[:, 1, sbi + 1, h:h + 1]
                    self.stt(ii.t[:, first:first + 1], aa.t[:, first:first + 1], hv, ii.t[:, first:first + 1],
                             ALU.mult, ALU.add, [aa, ii, self.hin], [ii])
                if d == 0:
                    o_, a_, b_ = hh.t[:, 0:n], aa.t[:, 0:n], ii.t[:, 0:n]
                else:
                    o_, a_, b_ = hh.t[:, 0:n][:, ::-1], aa.t[:, 0:n][:, ::-1], ii.t[:, 0:n][:, ::-1]
                self.vop(lambda hd, o_=o_, a_=a_, b_=b_: hd.tensor_tensor_scan(out=o_, data0=a_, data1=b_, initial=0.0,
                                                                           op0=ALU.mult, op1=ALU.add),
                         [aa, ii], [hh])

        def s13(h, c):
            if pass2:
                h0_, h1_ = c["a2", 0], c["a2", 1]
                self.tt(h0_.t[:, 0:n], h0_.t[:, 0:n], h1_.t[:, 0:n], ALU.add, [h0_, h1_], [h0_])
                self.tt(YT.t[:, h, 0:n], YT.t[:, h, 0:n], h0_.t[:, 0:n], ALU.mult, [h0_, YT], [YT])
                return
            for d in range(2):
                hh = c["a2", d]
                last = n - 1 if d == 0 else 0
                self.cp(ctl.t[:, sbi, 2 * d + 1, h:h + 1], hh.t[:, last:last + 1], [hh], [ctl])

        stages = [s0, s1, s2, s3, s4, s5, s6, s7, s8, s9, s10, s11, s12, s13]
        SK = 5
        cx = {h: {} for h in range(16)}
        ns = len(stages)
        for step in range(ns + 15 * SK):
            for h in range(16):
                si = step - h * SK
                if 0 <= si < ns:
                    stages[si](h, cx[h])
        if pass2:
            self.out_proj("w_out", YT, n, 0)

    def out_proj(self, key, YT, n, layer):
        wsr = self.wstream(key, list(range(16)))
        for m in range(16):
            wt = wsr.get()
            po = self.psring.next()
            for k in range(KC):
                self.mm(po.t[:, 0:n], wt.t[:, k * 128:(k + 1) * 128], YT.t[:, k, 0:n], k == 0, k == KC - 1,
                        [wt, YT], [po])
            self.stt(self.xres.t[:, m, 0:n], po.t[:, 0:n], self.G1(layer, m), self.xres.t[:, m, 0:n],
                     ALU.mult, ALU.add, [po, self.modt[layer], self.xres], [self.xres])

    def sc_block(self, n):
        self.norm_mod(self.xres, n, lambda k: self.A1(1, k), lambda k: self.S1(1, k), self.hT)
        ccs = []
        for ch in range(16):
            ccs += [ch, 16 + ch, 32 + ch]
        wsr = self.wstream("sc_w_in", ccs)
        YT = self.bfA
        rowlen = 64
        for ch in range(16):
            pp = []
            for j in range(3):
                wt = wsr.get()
                p_ = self.psring.next()
                for k in range(KC):
                    self.mm(p_.t[:, 0:n], wt.t[:, k * 128:(k + 1) * 128], self.hT.t[:, k, 0:n], k == 0, k == KC - 1,
                            [wt, self.hT], [p_])
                pp.append(p_)
            vs = self.tmpr.next()
            self.act(vs.t[:, 0:n], pp[2].t[:, 0:n], AF.Copy, [pp[2]], [vs])
            cvt = self.tmpr.next()
            self.tt(cvt.t[:, 0:n], pp[1].t[:, 0:n], vs.t[:, 0:n], ALU.mult, [pp[1], vs], [cvt])
            yc = self.tmpr.next()
            self.act(yc.t[:, 0:n], cvt.t[:, 0:n], AF.Identity, [cvt], [yc], scale=self.V(17, ch), bias=self.V(19, ch))
            cv_ = cvt.t[:, 0:n].rearrange("p (r c) -> p r c", c=rowlen)
            yv = yc.t[:, 0:n].rearrange("p (r c) -> p r c", c=rowlen)
            for tap, off in ((16, -1), (18, 1)):
                if off < 0:
                    o_, i_ = yv[:, :, -off:], cv_[:, :, :rowlen + off]
                else:
                    o_, i_ = yv[:, :, :rowlen - off], cv_[:, :, off:]
                self.stt(o_, i_, self.V(tap, ch), o_, ALU.mult, ALU.add, [cvt, yc], [yc])
            self.tt(YT.t[:, ch, 0:n], pp[0].t[:, 0:n], yc.t[:, 0:n], ALU.mult, [pp[0], yc], [YT])
        self.out_proj("sc_w_out", YT, n, 1)

    def peer_block(self, layer):
        n = T
        self.norm_mod(self.xres, n, lambda k: self.A2(layer, k), lambda k: self.S2(layer, k), self.hT)
        qT = self.bfA
        wsr = self.wstream(f"wq{layer}", list(range(16)))
        for hp in range(16):
            wt = wsr.get()
            pq = self.psring.next()
            for k in range(KC):
                self.mm(pq.t[:, 0:n], wt.t[:, k * 128:(k + 1) * 128], self.hT.t[:, k, 0:n], k == 0, k == KC - 1,
                        [wt, self.hT], [pq])
            self.act(qT.t[:, hp, 0:n], pq.t[:, 0:n], AF.Copy, [pq], [qT])
        for ts_ in range(T // 128):
            self.topk_sub(layer, qT, ts_)
            self.scatter_sub(ts_)
        self.sweep(layer)

    def topk_sub(self, layer, qT, ts_):
        t0 = ts_ * 128
        skT = self.skT[layer]
        s_sb = Tl(self.X1.t[:].rearrange("p (a b) -> p a b", b=128), "x"); s_sb.b = self.X1.b
        s_wk = Tl(self.X2.t[:].rearrange("p (a b) -> p a b", b=128), "x"); s_wk.b = self.X2.b
        for hp in range(16):
            pS = self.psS[hp // 4]
            self.mm(pS.t[:, (hp % 4) * 128:(hp % 4 + 1) * 128], qT.t[:, hp, t0:t0 + 128], skT.t[:, hp, :], True, True,
                    [qT, skT], [pS])
        for b4 in range(4):
            self.act(s_sb.t[:, b4 * 4:(b4 + 1) * 4, :].rearrange("p a b -> p (a b)"), self.psS[b4].t[:], AF.Copy,
                     [self.psS[b4]], [s_sb])
        sv, si_u = self.sv, self.si_u
        for hp in range(16):
            self.vop(lambda h, hp=hp: h.max(out=sv.t[:, hp, 0:8], in_=s_sb.t[:, hp, :]), [s_sb], [sv])
        for hp in range(16):
            self.vop(lambda h, hp=hp: h.max_index(out=si_u.t[:, hp, 0:8], in_max=sv.t[:, hp, 0:8],
                                                  in_values=s_sb.t[:, hp, :]), [s_sb, sv], [si_u])
        for hp in range(16):
            self.vop(lambda h, hp=hp: h.match_replace(out=s_wk.t[:, hp, :], in_to_replace=sv.t[:, hp, 0:8],
                                                      in_values=s_sb.t[:, hp, :], imm_value=NEG), [s_sb, sv], [s_wk])
        for hp in range(16):
            self.vop(lambda h, hp=hp: h.max(out=sv.t[:, hp, 8:16], in_=s_wk.t[:, hp, :]), [s_wk], [sv])
        for hp in range(16):
            self.vop(lambda h, hp=hp: h.max_index(out=si_u.t[:, hp, 8:16], in_max=sv.t[:, hp, 8:16],
                                                  in_values=s_wk.t[:, hp, :]), [s_wk, sv], [si_u])
        self.cp(self.si_f.t[:], si_u.t[:], [si_u], [self.si_f])
        cand = Tl(self.X1.t[:].rearrange("p (h a b) -> p h a b", h=8, a=16), "x"); cand.b = self.X1.b
        candw = Tl(self.X2.t[:].rearrange("p (h c) -> p h c", h=8), "x"); candw.b = self.X2.b
        self.tt(cand.t, sv.t[:, 0::2, :].unsqueeze(3).broadcast_to([P, 8, 16, 16]),
                sv.t[:, 1::2, :].unsqueeze(2).broadcast_to([P, 8, 16, 16]), ALU.add, [sv], [cand])
        c2 = Tl(self.X1.t[:].rearrange("p (h c) -> p h c", h=8), "x"); c2.b = self.X1.b
        cvv, ci_u = self.cvv, self.ci_u
        for h_ in range(8):
            self.vop(lambda h, h_=h_: h.max(out=cvv.t[:, h_, 0:8], in_=c2.t[:, h_, :]), [c2], [cvv])
        for h_ in range(8):
            self.vop(lambda h, h_=h_: h.max_index(out=ci_u.t[:, h_, 0:8], in_max=cvv.t[:, h_, 0:8],
                                                  in_values=c2.t[:, h_, :]), [c2, cvv], [ci_u])
        for h_ in range(8):
            self.vop(lambda h, h_=h_: h.match_replace(out=candw.t[:, h_, :], in_to_replace=cvv.t[:, h_, 0:8],
                                                      in_values=c2.t[:, h_, :], imm_value=NEG), [c2, cvv], [candw])
        for h_ in range(8):
            self.vop(lambda h, h_=h_: h.max(out=cvv.t[:, h_, 8:16], in_=candw.t[:, h_, :]), [candw], [cvv])
        for h_ in range(8):
            self.vop(lambda h, h_=h_: h.max_index(out=ci_u.t[:, h_, 8:16], in_max=cvv.t[:, h_, 8:16],
                                                  in_values=candw.t[:, h_, :]), [candw, cvv], [ci_u])
        self.ts(self.ca_u.t[:], ci_u.t[:], 4, None, ALU.logical_shift_right, ALU.bypass, [ci_u], [self.ca_u])
        self.ts(self.cb_u.t[:], ci_u.t[:], 15, None, ALU.bitwise_and, ALU.bypass, [ci_u], [self.cb_u])
        self.cp(self.caf.t[:], self.ca_u.t[:], [self.ca_u], [self.caf])
        self.cp(self.cbf.t[:], self.cb_u.t[:], [self.cb_u], [self.cbf])
        oh = Tl(self.X1.t[:].rearrange("p (h k a) -> p h k a", h=8, k=16), "x"); oh.b = self.X1.b
        io16 = self.iota[:, 0:16].unsqueeze(1).unsqueeze(1).broadcast_to([P, 8, 16, 16])
        for j, (cf, par) in enumerate(((self.caf, 0), (self.cbf, 1))):
            self.tt(oh.t, cf.t[:].unsqueeze(3).broadcast_to([P, 8, 16, 16]), io16, ALU.is_equal, [cf, self.cst], [oh])
            self.tt(oh.t, oh.t, self.si_f.t[:, par::2, :].unsqueeze(2).broadcast_to([P, 8, 16, 16]), ALU.mult,
                    [oh, self.si_f], [oh])
            self.vop(lambda h, j=j: h.tensor_reduce(out=self.slot3.t[:, j, :].rearrange("p (h k) -> p h k", h=8),
                                                    in_=oh.t, axis=AX.X, op=ALU.add), [oh], [self.slot3])
        zs = self.zs
        g3 = self.slot3.t[:, 2, :].rearrange("p (h k) -> p h k", h=8)
        self.tt(g3, cvv.t[:], cvv.t[:, :, 0:1].broadcast_to([P, 8, 16]), ALU.subtract, [cvv], [self.slot3])
        self.act(g3, g3, AF.Exp, [self.slot3], [self.slot3])
        self.vop(lambda h: h.tensor_reduce(out=zs.t[:, 0, :], in_=g3, axis=AX.X, op=ALU.add), [self.slot3], [zs])
        self.vop(lambda h: h.reciprocal(out=zs.t[:, 1, :], in_=zs.t[:, 0, :]), [zs], [zs])
        self.tt(g3, g3, zs.t[:, 1, :].unsqueeze(2).broadcast_to([P, 8, 16]), ALU.mult, [self.slot3, zs], [self.slot3])
        pm = self.psM
        for j in range(3):
            self.vop(lambda h, j=j: h.transpose(out=pm.t[:, j * 128:(j + 1) * 128], in_=self.slot3.t[:, j, :],
                                                identity=self.ident), [self.slot3, self.cst], [pm], eng="pe")
        self.cp(self.slotT.t[:, :, ts_ * 128:(ts_ + 1) * 128], pm.t[:, 0:384].rearrange("p (j t) -> p j t", j=3),
                [pm], [self.slotT])

    def scatter_sub(self, ts_):
        TG = 32
        WT = self.WT
        for g_ in range(128 // TG):
            t0 = ts_ * 128 + g_ * TG
            Bt = Tl(self.X1.t[:].bitcast(BF16).rearrange("p (t i) -> p t i", i=128)[:, 0:TG, :], "x"); Bt.b = self.X1.b
            At = Tl(self.X2.t[:].bitcast(BF16).rearrange("p (t i) -> p t i", i=128)[:, 0:TG, :], "x"); At.b = self.X2.b
            io = self.iota.unsqueeze(1).broadcast_to([P, TG, 128])
            i1v = self.slotT.t[:, 0, t0:t0 + TG].unsqueeze(2).broadcast_to([P, TG, 128])
            i2v = self.slotT.t[:, 1, t0:t0 + TG].unsqueeze(2).broadcast_to([P, TG, 128])
            gv = self.slotT.t[:, 2, t0:t0 + TG].unsqueeze(2).broadcast_to([P, TG, 128])
            self.tt(Bt.t, io, i2v, ALU.is_equal, [self.cst, self.slotT], [Bt])
            self.tt(At.t, io, i1v, ALU.is_equal, [self.cst, self.slotT], [At])
            self.tt(At.t, At.t, gv, ALU.mult, [At, self.slotT], [At], eng="pool")
            for q4 in range(TG // 4):
                pw = self.psS[(g_ * (TG // 4) + q4) % 4]
                for tt_ in range(4):
                    tl = q4 * 4 + tt_
                    self.mm(pw.t[:, tt_ * 128:(tt_ + 1) * 128], Bt.t[:, tl, :], At.t[:, tl, :], True, True,
                            [Bt, At], [pw])
                ta = t0 + q4 * 4
                src_ = pw.t[:].rearrange("p (t i) -> p i t", t=4)
                if q4 % 2:
                    self.act(WT.t[:, :, ta:ta + 4], src_, AF.Copy, [pw], [WT])
                else:
                    self.cp(WT.t[:, :, ta:ta + 4], src_, [pw], [WT])

    def sweep(self, layer):
        n = T
        PC = 4
        WT = self.WT
        udst, ubuf = self.ws[f"u{layer}"]
        vdst, vbuf = self.ws[f"v{layer}"]
        ust = Stream(self, self.wring, [(udst[c], ubuf) for c in range(NEXP_C)])
        vst = Stream(self, self.vring, [(vdst[c0:c0 + PC].rearrange("c p d -> p c d"), vbuf)
                                        for c0 in range(0, NEXP_C, PC)], queue="pool")
        xb_ = self.xres.b
        subs = []
        for m in range(16):
            sb_ = Buf(f"xres_m{m}")
            sb_.writers = dict(xb_.writers)
            sb_.readers = list(xb_.readers)
            subs.append(sb_)
        modb = self.modt[layer].b
        wb_ = WT.b
        wtb = []
        for c in range(NEXP_C):
            b_ = Buf(f"wt_c{c}")
            b_.writers = dict(wb_.writers)
            b_.readers = list(wb_.readers)
            wtb.append(b_)

        def u_side(part):
            for cc in range(PC):
                c = part * PC + cc
                ut = ust.get()
                pu = self.psring.next()
                for k in range(KC):
                    self.mm(pu.t[:, 0:n], ut.t[:, k * 128:(k + 1) * 128], self.hT.t[:, k, 0:n], k == 0, k == KC - 1,
                            [ut, self.hT], [pu])
                ab = self.tmpb.next()
                self.act(ab.t[:, 0:n], pu.t[:, 0:n], AF.Gelu_apprx_tanh, [pu], [ab])
                self.S.op("dve", lambda h, c=c, ab=ab: h.tensor_tensor(out=WT.t[:, c, :], in0=WT.t[:, c, :],
                                                                         in1=ab.t[:, 0:n], op=ALU.mult),
                          [wtb[c], ab.b], [wtb[c]])

        def v_side(part):
            vt = vst.get()
            for m in range(16):
                pv = self.psring.next()
                for cc in range(PC):
                    c = part * PC + cc
                    self.S.op("pe", lambda h, pv=pv, vt=vt, cc=cc, m=m, c=c: h.matmul(
                        pv.t[:, 0:n], lhsT=vt.t[:, cc, m * 128:(m + 1) * 128], rhs=WT.t[:, c, :],
                        start=(cc == 0), stop=(cc == PC - 1)), [vt.b, wtb[c]], [pv.b])
                xo = self.xres.t[:, m, 0:n]
                if m % 2 == 0:
                    self.S.op("dve", lambda h, xo=xo, pv=pv, m=m: h.scalar_tensor_tensor(
                        out=xo, in0=pv.t[:, 0:n], scalar=self.G2(layer, m), in1=xo, op0=ALU.mult, op1=ALU.add),
                        [pv.b, modb, subs[m]], [subs[m]])
                else:
                    tm = self.tmpr.next()
                    self.act(tm.t[:, 0:n], pv.t[:, 0:n], AF.Identity, [pv], [tm], scale=self.G2(layer, m))
                    self.S.op("pool", lambda h, xo=xo, tm=tm: h.tensor_tensor(out=xo, in0=xo, in1=tm.t[:, 0:n], op=ALU.add),
                              [tm.b, subs[m]], [subs[m]])

        nparts = NEXP_C // PC
        u_side(0)
        for part in range(nparts):
            if part + 1 < nparts:
                u_side(part + 1)
            v_side(part)
        for sb_ in subs:
            for k_, tok in sb_.writers.items():
                if k_ not in xb_.writers or xb_.writers[k_][1] < tok[1]:
                    xb_.writers[k_] = tok
            xb_.readers.extend(sb_.readers)
        for b_ in wtb:
            for k_, tok in b_.writers.items():
                if k_ not in wb_.writers or wb_.writers[k_][1] < tok[1]:
                    wb_.writers[k_] = tok
            wb_.readers.extend(b_.readers)

    def pass1(self, xsrc, base):
        S = self.S
        if not hasattr(self, "ldx1"):
            self.ldx1 = S.slot("ldx1")
        ldx = self.ldx1
        for sbi in range(NSB):
            self.dma(self.xres.t[:], xsrc[:, :, base + sbi * T:base + (sbi + 1) * T], ldx, writes=[self.xres])
            self.lru_p1(self.xres, sbi)
            if getattr(self, "bg_per_block", 0):
                self.precast_run(self.bg_per_block, bg=True)
        self.S.barrier()
        self.chunk_carry()
        if self.mode == "p1":
            so = S.slot("st_carry")
            self.dma(self.carry_out, self.carry.t[:], so, reads=[self.carry])
        self.S.barrier()

    def exchange(self):
        S = self.S
        cc_in = self.dscr("cc_in", [P, 4 * KC], F32)
        cc_out = self.dscr("cc_out", [NCORE * P, 4 * KC], F32)
        bi, bo = Buf("cc_in"), Buf("cc_out")
        s1, s2, s3 = S.slot("cc1"), S.slot("cc2"), S.slot("cc3")
        self.dma(cc_in, self.carry.t[:, NSB, :, :].rearrange("p a b -> p (a b)"), s1, reads=[self.carry], writes=[bi])
        S.custom("pool", lambda h: h.collective_compute("AllGather", ALU.bypass, replica_groups=[list(range(NCORE))],
                                                        ins=[cc_in], outs=[cc_out]), s2, reads=[bi], writes=[bo])
        self.dma(self.call.t[:].rearrange("p r a b -> p r (a b)"), cc_out.rearrange("(r p) f -> p r f", p=P), s3,
                 reads=[bo], writes=[self.call])
        self.S.barrier()

    def chunk_carry(self):
        c = self.carry
        for d in range(2):
            self.tt(c.t[:, 0:NSB, 2 * d, :], c.t[:, 0:NSB, 2 * d, :],
                    self.dv.t[:, 9 + d, :].unsqueeze(1).broadcast_to([P, NSB, KC]), ALU.mult, [c, self.dv], [c])
            self.act(c.t[:, 0:NSB, 2 * d, :], c.t[:, 0:NSB, 2 * d, :], AF.Exp, [c], [c])
        self.cp(c.t[:, NSB, 0, :], c.t[:, 0, 0, :], [c], [c])
        self.cp(c.t[:, NSB, 1, :], c.t[:, 0, 1, :], [c], [c])
        for sb in range(1, NSB):
            self.tt(c.t[:, NSB, 1, :], c.t[:, NSB, 1, :], c.t[:, sb, 0, :], ALU.mult, [c], [c])
            self.tt(c.t[:, NSB, 1, :], c.t[:, NSB, 1, :], c.t[:, sb, 1, :], ALU.add, [c], [c])
            self.tt(c.t[:, NSB, 0, :], c.t[:, NSB, 0, :], c.t[:, sb, 0, :], ALU.mult, [c], [c])
        self.cp(c.t[:, NSB, 2, :], c.t[:, NSB - 1, 2, :], [c], [c])
        self.cp(c.t[:, NSB, 3, :], c.t[:, NSB - 1, 3, :], [c], [c])
        for sb in range(NSB - 2, -1, -1):
            self.tt(c.t[:, NSB, 3, :], c.t[:, NSB, 3, :], c.t[:, sb, 2, :], ALU.mult, [c], [c])
            self.tt(c.t[:, NSB, 3, :], c.t[:, NSB, 3, :], c.t[:, sb, 3, :], ALU.add, [c], [c])
            self.tt(c.t[:, NSB, 2, :], c.t[:, NSB, 2, :], c.t[:, sb, 2, :], ALU.mult, [c], [c])

    def ctx_and_carries(self):
        NC_ = NSB
        ldc = self.S.slot("ld_ctx")
        self.dma(self.ctxx.t, self.ctxT, ldc, writes=[self.ctxx])
        saved = self.carry
        ctxc = self.sb("ctxcarry", [P, 1, 4, KC])
        self.carry = ctxc
        self.lru_block(self.ctxx, 256, 256, lambda k: self.dv.t[:, 8, k:k + 1],
                       lambda k: self.modt[0].t[:, k, 1:2], False, 0)
        self.carry = saved
        hin = self.hin
        call = self.call
        cm = self.cmk
        sm = self.small
        self.cp(hin.t[:, 0, 0, :], ctxc.t[:, 0, 1, :], [ctxc], [hin])
        self.cp(hin.t[:, 1, NSB, :], ctxc.t[:, 0, 3, :], [ctxc], [hin])
        ns_ = self.nslots
        for d, order in ((0, range(ns_)), (1, range(ns_ - 1, -1, -1))):
            hsl = hin.t[:, 0, 0, :] if d == 0 else hin.t[:, 1, NSB, :]
            for cp_ in order:
                m = cm.t[:, d, cp_:cp_ + 1]
                self.ts(sm.t[:, 0, :], call.t[:, cp_, 2 * d, :], -1.0, m, ALU.add, ALU.mult, [call, cm], [sm])
                self.ts(sm.t[:, 0, :], sm.t[:, 0, :], 1.0, None, ALU.add, ALU.bypass, [sm], [sm])
                self.ts(sm.t[:, 1, :], call.t[:, cp_, 2 * d + 1, :], m, None, ALU.mult, ALU.bypass, [call, cm], [sm])
                self.tt(hsl, hsl, sm.t[:, 0, :], ALU.mult, [hin, sm], [hin])
                self.tt(hsl, hsl, sm.t[:, 1, :], ALU.add, [hin, sm], [hin])
        c = self.carry
        for sb in range(NSB):
            self.tt(hin.t[:, 0, sb + 1, :], hin.t[:, 0, sb, :], c.t[:, sb, 0, :], ALU.mult, [hin, c], [hin])
            self.tt(hin.t[:, 0, sb + 1, :], hin.t[:, 0, sb + 1, :], c.t[:, sb, 1, :], ALU.add, [hin, c], [hin])
        for sb in range(NSB - 1, -1, -1):
            self.tt(hin.t[:, 1, sb, :], hin.t[:, 1, sb + 1, :], c.t[:, sb, 2, :], ALU.mult, [hin, c], [hin])
            self.tt(hin.t[:, 1, sb, :], hin.t[:, 1, sb, :], c.t[:, sb, 3, :], ALU.add, [hin, c], [hin])
        self.S.barrier()

    def pass2(self):
        S = self.S
        ldx = S.slot("ldx")
        sto = S.slot("sto")
        for sbi in range(self.nsb_run):
            self.dma(self.xres.t[:], self.xT[:, :, sbi * T:(sbi + 1) * T], ldx, writes=[self.xres])
            self.S.barrier()
            self.lru_p1(self.xres, sbi, pass2=True)
            self.S.barrier()
            if self.dbg != "x1":
                self.peer_block(0)
            if self.dbg is None or self.dbg in ("x3", "x4"):
                self.sc_block(T)
            if self.dbg is None or self.dbg == "x4":
                self.peer_block(1)
            if self.dbg is None:
                self.norm_mod(self.xres, T, lambda k: self.V(4, k), lambda k: self.dv.t[:, 13, k:k + 1], self.outT)
                self.dma(self.yT[:, :, sbi * T:(sbi + 1) * T], self.outT.t[:], sto, reads=[self.outT])
            else:
                self.dma(self.yT[:, :, sbi * T:(sbi + 1) * T], self.xres.t[:], sto, reads=[self.xres])
        self.S.barrier()


def _fm(v):
    v = np.asarray(v, np.float32)
    return np.ascontiguousarray(np.moveaxis(v.reshape(v.shape[:-1] + (KC, P)), -1, 0))


def _consts():
    c = np.zeros((P, 384), np.float32)
    c[:, 0:128] = np.eye(P, dtype=np.float32)
    c[:, 128:256] = np.arange(128, dtype=np.float32)[None, :]
    c[:, 256:384] = 1.0
    return c


_NC_CACHE = {}


def _get_nc(mode, nsb_run=NSB, dbg=None):
    key = (mode, nsb_run, dbg)
    if key not in _NC_CACHE:
        _NC_CACHE[key] = KB(mode, nsb_run, dbg).build()
    return _NC_CACHE[key]


def _prep(inputs):
    f = lambda k: np.asarray(inputs[k], np.float32)
    x = f("x")
    shared = {}
    shared["consts"] = _consts()
    vec = np.zeros((NV, D), np.float32)
    vec[0] = f("norm_mix_g")[0]; vec[1] = f("norm_ffn_g")[0]
    vec[2] = f("norm_mix_g")[1]; vec[3] = f("norm_ffn_g")[1]
    vec[4] = f("norm_final_g")
    vec[5:9] = f("lru_conv_w")[0]; vec[9] = f("lru_conv_b")[0]
    vec[10:12] = f("lru_b_a")[0]; vec[12:14] = f("lru_b_x")[0]; vec[14:16] = f("lru_lambda")[0]
    vec[16:19] = f("sc_conv_w")[0]; vec[19] = f("sc_conv_b")[0]
    shared["vecs"] = _fm(vec)
    shared["w_mod"] = f("w_mod")
    shared["b_modT"] = np.ascontiguousarray(f("b_mod").reshape(2, 96, P).transpose(2, 0, 1))
    shared["lru_w_in"] = f("lru_w_in")[0]
    wab = np.stack([f("lru_w_a")[0], f("lru_w_x")[0]], axis=1)
    shared["wab"] = np.ascontiguousarray(wab.transpose(2, 3, 0, 1, 4)).reshape(16, P, 512)
    per_core = []
    for c in range(NCORE):
        b, j = divmod(c, 4)
        m = {}
        xs = x[b, j * NTOK:(j + 1) * NTOK]
        m["xT"] = np.ascontiguousarray(xs.reshape(NTOK, KC, P).transpose(2, 1, 0))
        m["cvec"] = np.ascontiguousarray(np.stack([_fm(f("c")[b]), _fm(f("c_ctx"))], axis=-1))
        per_core.append(m)
    return shared, per_core


def _prep2(inputs):
    f = lambda k: np.asarray(inputs[k], np.float32)
    sh = {}
    sh["lru_w_out"] = f("lru_w_out")[0]
    sh["sc_w_in"] = f("sc_w_in")[0]
    sh["sc_w_out"] = f("sc_w_out")[0]
    sh["peer_w_q"] = f("peer_w_q")
    sk = f("peer_sub_keys")
    sh["skT"] = np.ascontiguousarray(sk.reshape(2, 16, 128, 128).transpose(0, 3, 1, 2))
    u = f("peer_u")
    sh["u_l"] = np.ascontiguousarray(u.reshape(2, NEXP_C, P, KC, P).transpose(0, 1, 4, 3, 2)).reshape(2, NEXP_C, P, D)
    sh["v_l"] = f("peer_v").reshape(2, NEXP_C, P, D)
    ctx = f("ctx")
    pc = []
    for c in range(NCORE):
        b, j = divmod(c, 4)
        m = {}
        m["ctxT"] = np.ascontiguousarray(ctx[b].reshape(256, KC, P).transpose(2, 1, 0))
        cm = np.zeros((P, 2, NCORE), np.float32)
        for c2 in range(NCORE):
            b2, j2 = divmod(c2, 4)
            if b2 == b and j2 < j:
                cm[:, 0, c2] = 1.0
            if b2 == b and j2 > j:
                cm[:, 1, c2] = 1.0
        m["cmask"] = cm
        pc.append(m)
    return sh, pc


def _xT_chunk(x, b, j):
    xs = x[b, j * NTOK:(j + 1) * NTOK]
    return xs.reshape(NTOK, KC, P).transpose(2, 1, 0)


def kernel(**inputs):
    shared, pc = _prep(inputs)
    sh2, pc2 = _prep2(inputs)
    x = np.asarray(inputs["x"], np.float32)
    in_maps = []
    for c in range(NCORE):
        b, j = divmod(c, 4)
        others = [jj for jj in range(4) if jj != j]
        xTo = np.ascontiguousarray(np.concatenate([_xT_chunk(x, b, jj) for jj in others], axis=2))
        cm = np.zeros((P, 2, 3), np.float32)
        for s_, jj in enumerate(others):
            cm[:, 0, s_] = 1.0 if jj < j else 0.0
            cm[:, 1, s_] = 1.0 if jj > j else 0.0
        m = {**shared, **pc[c], **sh2, **pc2[c], "xTo": xTo, "cmask": cm}
        in_maps.append(m)
    nc = _get_nc("f")
    r = run_bass_kernel_spmd(nc, in_maps, core_ids=list(range(NCORE)))
    out = np.zeros((2, 4 * NTOK, D), np.float32)
    for c in range(NCORE):
        b, j = divmod(c, 4)
        yT = np.asarray(r.results[c]["yT"], np.float32)
        out[b, j * NTOK:(j + 1) * NTOK] = yT.transpose(2, 1, 0).reshape(NTOK, D)
    return out
```

```python
import math
import numpy as np
from contextlib import ExitStack
import concourse.bass as bass
import concourse.mybir as mybir
from concourse.bass_utils import run_bass_kernel_spmd

F32 = mybir.dt.float32
BF16 = mybir.dt.bfloat16
U32 = mybir.dt.uint32
ALU = mybir.AluOpType
AF = mybir.ActivationFunctionType
AX = mybir.AxisListType

P = 128
D = 2048
KC = 16
T = 256
NTOK = 4096
NSB = NTOK // T
NCORE = 8
NEXP_C = 128
EPS = 1e-6
NV = 20
NEG = -1.0e30


class Buf:
    __slots__ = ("name", "writers", "readers")

    def __init__(self, name=""):
        self.name = name
        self.writers = {}
        self.readers = []


class Eng:
    def __init__(self, name, sem, self_sync):
        self.name = name
        self.sem = sem
        self.count = 0
        self.known = {}
        self.self_sync = self_sync
        self.prog = []


class Sched:
    def __init__(self, nc, stack):
        self.nc = nc
        self.stack = stack
        self.engs = {}
        self.slots = []
        for name, ss in (("pe", False), ("act", True), ("dve", True), ("pool", True), ("sp", False)):
            self.engs[name] = Eng(name, self.new_sem("e_" + name), ss)
        self.ninst = 0

    def new_sem(self, name):
        return self.stack.enter_context(self.nc.semaphore(name))

    def _waits(self, E, reads, writes):
        need = {}

        def add(tok):
            sem, val, en = tok
            if en == E.name and not E.self_sync:
                return
            k = id(sem)
            if k not in need or need[k][1] < val:
                need[k] = (sem, val)
        for b in reads:
            for tok in b.writers.values():
                add(tok)
        for b in writes:
            for tok in b.writers.values():
                add(tok)
            for tok in b.readers:
                add(tok)
        for k, (sem, val) in need.items():
            if E.known.get(k, 0) < val:
                E.prog.append(("w", sem, val))
                E.known[k] = val

    def _mark(self, tok, key, reads, writes):
        for b in reads:
            b.readers.append(tok)
        for b in writes:
            b.writers[key] = tok
            b.readers = []

    def op(self, ename, build, reads=(), writes=()):
        E = self.engs[ename]
        self._waits(E, reads, writes)
        E.count += 1
        E.prog.append(("o", build, E.sem, 1))
        self._mark((E.sem, E.count, E.name), E.name, reads, writes)
        self.ninst += 1

    def slot(self, name):
        s = {"sem": self.new_sem(name), "n": 0, "name": name}
        self.slots.append(s)
        return s

    def dma(self, qname, out, in_, slot, reads=(), writes=()):
        E = self.engs[qname]
        self._waits(E, reads, writes)
        E.prog.append(("o", (lambda h: h.dma_start(out=out, in_=in_)), slot["sem"], 16))
        slot["n"] += 1
        self._mark((slot["sem"], 16 * slot["n"], "dma"), "dma_" + slot["name"], reads, writes)
        self.ninst += 1

    def custom(self, qname, build, slot, reads=(), writes=(), inc=16):
        E = self.engs[qname]
        self._waits(E, reads, writes)
        E.prog.append(("o", build, slot["sem"], inc))
        slot["n"] += 1
        self._mark((slot["sem"], inc * slot["n"], "dma"), "dma_" + slot["name"], reads, writes)
        self.ninst += 1

    def barrier(self):
        for E in self.engs.values():
            for Fe in self.engs.values():
                if Fe is E or Fe.count == 0:
                    continue
                k = id(Fe.sem)
                if E.known.get(k, 0) < Fe.count:
                    E.prog.append(("w", Fe.sem, Fe.count))
                    E.known[k] = Fe.count
            for s in self.slots:
                k = id(s["sem"])
                if s["n"] and E.known.get(k, 0) < 16 * s["n"]:
                    E.prog.append(("w", s["sem"], 16 * s["n"]))
                    E.known[k] = 16 * s["n"]

    def emit(self):
        with self.nc.Block() as block:
            def run(E):
                def f(h):
                    for it in E.prog:
                        if it[0] == "w":
                            h.wait_ge(it[1], it[2])
                        else:
                            it[1](h).then_inc(it[2], it[3])
                return f
            block.tensor(run(self.engs["pe"]))
            block.scalar(run(self.engs["act"]))
            block.vector(run(self.engs["dve"]))
            block.gpsimd(run(self.engs["pool"]))
            block.sync(run(self.engs["sp"]))


class Tl:
    def __init__(self, t, name):
        self.t = t
        self.b = Buf(name)


class Ring:
    def __init__(self, tiles):
        self.tiles = tiles
        self.i = 0

    def next(self):
        t = self.tiles[self.i % len(self.tiles)]
        self.i += 1
        return t


class Stream:
    def __init__(self, kb, ring, srcs, queue="sp"):
        self.kb = kb
        self.ring = ring
        self.srcs = srcs
        self.R = len(ring)
        for tl in ring:
            if not hasattr(tl, "slot"):
                tl.slot = kb.S.slot("r_" + tl.b.name)
        self.issued = 0
        self.taken = 0
        self.queue = queue

    def _issue(self):
        n = self.issued
        tl = self.ring[n % self.R]
        src, sbuf = self.srcs[n]
        self.kb.S.dma(self.queue, tl.t[:], src, tl.slot,
                      reads=[sbuf] if sbuf is not None else [], writes=[tl.b])
        self.issued += 1

    def get(self):
        while self.issued < min(self.taken + self.R, len(self.srcs)):
            self._issue()
        tl = self.ring[self.taken % self.R]
        self.taken += 1
        return tl


class KB:
    def __init__(self, mode, nsb_run=NSB, dbg=None):
        self.mode = mode
        self.nsb_run = nsb_run
        self.dbg = dbg
        self.nc = bass.Bass("TRN2", target_bir_lowering=False)
        self.st = ExitStack()
        full = ["w_in", "wab", "w_out", "sc_w_in", "sc_w_out", "wq0", "wq1", "u0", "v0", "u1", "v1"]
        sub = {"s_setup": [], "x1": full[:3], "s_ctx": full[:3], "x2": full[:3] + ["wq0", "u0", "v0"],
               "x3": full[:5] + ["wq0", "u0", "v0"]}
        self.wlist = ["w_in_xb", "wab"] if mode == "p1" else sub.get(dbg, full)

    def din(self, name, shape, dt=F32):
        return self.nc.dram_tensor(name, list(shape), dt, kind="ExternalInput").ap()

    def dout(self, name, shape, dt=F32):
        return self.nc.dram_tensor(name, list(shape), dt, kind="ExternalOutput").ap()

    def dscr(self, name, shape, dt=BF16):
        return self.nc.dram_tensor(name, list(shape), dt, kind="Internal").ap()

    def sb(self, name, shape, dt=F32):
        return Tl(self.st.enter_context(self.nc.sbuf_tensor(name, list(shape), dt)), name)

    def ps(self, name, shape, dt=F32):
        return Tl(self.st.enter_context(self.nc.psum_tensor(name, list(shape), dt)), name)

    def act(self, out, in_, func, reads, writes, scale=1.0, bias=0.0, eng="act"):
        self.S.op(eng, lambda h: h.activation(out=out, in_=in_, func=func, scale=scale, bias=bias),
                  [r.b for r in reads], [w.b for w in writes])

    def tt(self, out, in0, in1, op, reads, writes, eng="dve"):
        self.S.op(eng, lambda h: h.tensor_tensor(out=out, in0=in0, in1=in1, op=op),
                  [r.b for r in reads], [w.b for w in writes])

    def ts(self, out, in0, s1, s2, op0, op1, reads, writes, eng="dve"):
        self.S.op(eng, lambda h: h.tensor_scalar(out=out, in0=in0, scalar1=s1, scalar2=s2, op0=op0, op1=op1),
                  [r.b for r in reads], [w.b for w in writes])

    def stt(self, out, in0, scalar, in1, op0, op1, reads, writes):
        self.S.op("dve", lambda h: h.scalar_tensor_tensor(out=out, in0=in0, scalar=scalar, in1=in1, op0=op0, op1=op1),
                  [r.b for r in reads], [w.b for w in writes])

    def cp(self, out, in_, reads, writes, eng="dve"):
        self.S.op(eng, lambda h: h.tensor_copy(out=out, in_=in_),
                  [r.b for r in reads], [w.b for w in writes])

    def mm(self, out, lhsT, rhs, start, stop, reads, writes):
        self.S.op("pe", lambda h: h.matmul(out, lhsT=lhsT, rhs=rhs, start=start, stop=stop),
                  [r.b for r in reads], [w.b for w in writes])

    def vop(self, fn, reads, writes, eng="dve"):
        self.S.op(eng, fn, [r.b for r in reads], [w.b for w in writes])

    def dma(self, out, in_, slot, reads=(), writes=(), q="sp"):
        self.S.dma(q, out, in_, slot, [r if isinstance(r, Buf) else r.b for r in reads],
                   [w if isinstance(w, Buf) else w.b for w in writes])

    def build(self):
        nc = self.nc
        mode = self.mode
        with self.st:
            self.S = Sched(nc, self.st)
            self.declare_io()
            self.alloc()
            self.setup()
            if mode == "p1":
                self.precast(["w_in_xb", "wab"])
                self.precast_finish()
                self.S.barrier()
                self.pass1(self.xT, 0)
            elif mode == "f":
                self.precast(self.wlist)
                self.precast_run(36, bg=False)
                self.S.barrier()
                self.bg_per_block = -(-(len(self.pc_jobs) - 36) // (4 * NSB))
                for s_ in range(3):
                    self.pass1(self.xTo, s_ * NTOK)
                    self.cp(self.call.t[:, s_, :, :], self.carry.t[:, NSB, :, :], [self.carry], [self.call])
                    self.S.barrier()
                self.pass1(self.xT, 0)
                self.precast_finish()
                self.S.barrier()
                self.ctx_and_carries()
                self.pass2()
            else:
                self.precast(self.wlist)
                self.precast_finish()
                self.S.barrier()
                if self.dbg not in ("s_setup", "s_precast"):
                    self.ctx_and_carries()
                    if self.dbg != "s_ctx":
                        self.pass2()
            self.S.barrier()
            self.S.emit()
        return nc

    def declare_io(self):
        self.xT = self.din("xT", [P, KC, NTOK])
        self.consts = self.din("consts", [P, 3 * 128])
        self.vecs = self.din("vecs", [P, NV, KC])
        self.cvec = self.din("cvec", [P, KC, 2])
        self.w_mod = self.din("w_mod", [2, D, 6 * D])
        self.b_modT = self.din("b_modT", [P, 2, 96])
        self.lru_w_in = self.din("lru_w_in", [D, 2 * D])
        self.wab_in = self.din("wab", [16, P, 512])
        if self.mode == "p1":
            self.carry_out = self.dout("carry", [P, NSB + 1, 4, KC])
        else:
            self.ctxT = self.din("ctxT", [P, KC, 256])
            if self.mode == "f":
                self.xTo = self.din("xTo", [P, KC, 3 * NTOK])
            if self.mode == "p2":
                self.carry_own = self.din("carry_own", [P, NSB + 1, 4, KC])
                self.carry_all = self.din("carry_all", [P, NCORE, 4, KC])
            self.nslots = 3 if self.mode == "f" else NCORE
            self.cmask = self.din("cmask", [P, 2, self.nslots])
            wl = self.wlist
            if "w_out" in wl:
                self.lru_w_out = self.din("lru_w_out", [D, D])
            if "sc_w_in" in wl:
                self.sc_w_in = self.din("sc_w_in", [D, 3 * D])
                self.sc_w_out = self.din("sc_w_out", [D, D])
            self.has_peer = "wq0" in wl
            if self.has_peer:
                self.peer_w_q = self.din("peer_w_q", [2, D, D])
                self.skT_in = self.din("skT", [2, P, 16, 128])
                self.u_l = self.din("u_l", [2, NEXP_C, P, D])
                self.v_l = self.din("v_l", [2, NEXP_C, P, D])
            self.yT = self.dout("yT", [P, KC, NTOK])
        self.ws = {}
        self.ws["w_in"] = (self.dscr("ws_w_in", [32, P, D]), Buf("ws_w_in"))
        self.ws["wab"] = (self.dscr("ws_wab", [16, P, 512]), Buf("ws_wab"))
        if self.mode != "p1":
            self.ws["w_out"] = (self.dscr("ws_w_out", [16, P, D]), Buf("ws_w_out"))
            self.ws["sc_w_in"] = (self.dscr("ws_sc_w_in", [48, P, D]), Buf("ws_sc_w_in"))
            self.ws["sc_w_out"] = (self.dscr("ws_sc_w_out", [16, P, D]), Buf("ws_sc_w_out"))
            for i in range(2):
                self.ws[f"wq{i}"] = (self.dscr(f"ws_wq{i}", [16, P, D]), Buf(f"ws_wq{i}"))
                self.ws[f"u{i}"] = (self.dscr(f"ws_u{i}", [NEXP_C, P, D]), Buf(f"ws_u{i}"))
                self.ws[f"v{i}"] = (self.dscr(f"ws_v{i}", [NEXP_C, P, D]), Buf(f"ws_v{i}"))

    def alloc(self):
        sb, ps = self.sb, self.ps
        self.cst = sb("cst", [P, 3 * 128])
        self.ident = self.cst.t[:, 0:128]
        self.iota = self.cst.t[:, 128:256]
        self.ones_bf = sb("ones_bf", [P, 128], BF16)
        self.iota_bf = sb("iota_bf", [P, 128], BF16)
        self.vec = sb("vec", [P, NV, KC])
        self.cv = sb("cv", [P, KC, 2])
        self.modt = [sb(f"modt{i}", [P, 96, 2]) for i in range(2)]
        self.bmod = sb("bmod", [P, 2, 96])
        self.dv = sb("dv", [P, 24, KC])
        self.wabring = [sb(f"wabr{i}", [P, 4, 128], BF16) for i in range(2)]
        self.xres = sb("xres", [P, KC, T])
        self.hT = sb("hT", [P, KC, T], BF16)
        self.bfA = sb("bfA", [P, KC, T], BF16)
        self.rstd = sb("rstd", [P, T])
        self.tmpr = Ring([sb(f"tmp{i}", [P, T]) for i in range(11)])
        self.tmpb = Ring([sb(f"tmpb{i}", [P, T], BF16) for i in range(4)])
        self.arena = sb("arena", [P, 16384])
        self.wring = [sb(f"wr{i}", [P, D], BF16) for i in range(3)]
        self.carry = sb("carryt", [P, NSB + 1, 4, KC])
        self.hin = sb("hin", [P, 2, NSB + 1, KC])
        self.small = sb("small", [P, 8, KC])
        if self.mode != "p1":
            self.vring = [sb(f"vr{i}", [P, 4, D], BF16) for i in range(2)]
            self.skT = [sb(f"skT{i}", [P, 16, 128], BF16) for i in range(2)]
            self.X1 = sb("X1", [P, 2048])
            self.X2 = sb("X2", [P, 2048])
            self.sv = sb("sv", [P, 16, 16])
            self.si_u = sb("si_u", [P, 16, 16], U32)
            self.si_f = sb("si_f", [P, 16, 16])
            self.cvv = sb("cvv", [P, 8, 16])
            self.ci_u = sb("ci_u", [P, 8, 16], U32)
            self.ca_u = sb("ca_u", [P, 8, 16], U32)
            self.cb_u = sb("cb_u", [P, 8, 16], U32)
            self.caf = sb("caf", [P, 8, 16])
            self.cbf = sb("cbf", [P, 8, 16])
            self.slot3 = sb("slot3", [P, 3, 128])
            self.slotT = sb("slotT", [P, 3, T])
            self.zs = sb("zs", [P, 4, 8])
            self.cmk = sb("cmk", [P, 2, self.nslots])
            self.call = sb("call", [P, NCORE, 4, KC])
            self.WT = Tl(self.arena.t[:].bitcast(BF16).rearrange("p (c t) -> p c t", t=T), "WT")
            self.WT.b = self.arena.b
            self.ctxx = Tl(self.arena.t[:, 0:KC * 256].rearrange("p (k t) -> p k t", k=KC), "ctxx")
            self.ctxx.b = self.arena.b
            self.outT = Tl(self.arena.t[:, 0:KC * T].rearrange("p (k t) -> p k t", k=KC), "outT")
            self.outT.b = self.arena.b
        pa = [ps(f"psA{i}", [P, 512]) for i in range(3)]
        pst = []
        for half in range(2):
            for bnk in range(3):
                t_ = Tl(pa[bnk].t[:, half * 256:half * 256 + 256], f"psh{bnk}_{half}")
                t_.b = pa[bnk].b
                pst.append(t_)
        self.psring = Ring(pst)
        self._pa = pa
        self.psS = [ps(f"psS{i}", [P, 512]) for i in range(4)]
        self.psM = ps("psM", [P, 512])
        big = []
        banks = self._pa + self.psS
        for half in range(2):
            for bk in banks:
                t_ = Tl(bk.t[:, half * 256:half * 256 + 256], f"pb_{bk.b.name}_{half}")
                t_.b = bk.b
                big.append(t_)
        self.psbig = Ring(big)
        self.p1ring = Ring([Tl(self.arena.t[:, 9216 + i * 256:9216 + (i + 1) * 256], f"p1t{i}") for i in range(28)])

    def setup(self):
        S = self.S
        self._nld = 0

        def ld1(o, i, tl):
            self._nld += 1
            self.dma(o, i, S.slot(f"ld1_{self._nld}"), writes=[tl])
        self.ld1 = ld1
        ld1(self.cst.t[:], self.consts, self.cst)
        ld1(self.vec.t[:], self.vecs, self.vec)
        ld1(self.cv.t[:], self.cvec, self.cv)
        ld1(self.bmod.t[:], self.b_modT, self.bmod)
        self.cp(self.ones_bf.t[:], self.cst.t[:, 256:384], [self.cst], [self.ones_bf])
        self.cp(self.iota_bf.t[:], self.cst.t[:, 128:256], [self.cst], [self.iota_bf])
        self.act(self.cv.t[:], self.cv.t[:], AF.Silu, [self.cv], [self.cv])
        nlay = 1 if self.mode == "p1" else 2
        NB = 256
        wmr = [Tl(self.arena.t[:, i * 4096:(i + 1) * 4096].rearrange("p (k n) -> p k n", k=KC), f"wm{i}")
               for i in range(3)]
        wslots = [S.slot(f"wm{i}") for i in range(3)]
        cnt = 0
        for i in range(nlay):
            nblk = (3 * D // NB) if self.mode == "p1" else (6 * D // NB)
            for nb in range(nblk):
                wt = wmr[cnt % 3]
                src = self.w_mod[i, :, nb * NB:(nb + 1) * NB].rearrange("(k p) n -> p k n", p=P)
                self.dma(wt.t, src, wslots[cnt % 3], writes=[wt])
                cnt += 1
                for mi in range(NB // 128):
                    m = nb * (NB // 128) + mi
                    for k in range(KC):
                        self.mm(self.psM.t[:, 2 * m:2 * m + 2], wt.t[:, k, mi * 128:(mi + 1) * 128],
                                self.cv.t[:, k, :], k == 0, k == KC - 1, [wt, self.cv], [self.psM])
            nm = nblk * NB // 128
            self.tt(self.modt[i].t[:, 0:nm, :], self.psM.t[:, 0:2 * nm].rearrange("p (m c) -> p m c", c=2),
                    self.bmod.t[:, i, 0:nm].unsqueeze(2).broadcast_to([P, nm, 2]), ALU.add,
                    [self.psM, self.bmod], [self.modt[i]])
        dv = self.dv
        for i in range(nlay):
            self.stt(dv.t[:, 2 * i + 0, :], self.modt[i].t[:, 16:32, 0], 1.0, self.vec.t[:, 2 * i + 0, :],
                     ALU.add, ALU.mult, [self.modt[i], self.vec], [dv])
            if self.mode != "p1":
                self.stt(dv.t[:, 2 * i + 1, :], self.modt[i].t[:, 64:80, 0], 1.0, self.vec.t[:, 2 * i + 1, :],
                         ALU.add, ALU.mult, [self.modt[i], self.vec], [dv])
        self.stt(dv.t[:, 8, :], self.modt[0].t[:, 16:32, 1], 1.0, self.vec.t[:, 0, :],
                 ALU.add, ALU.mult, [self.modt[0], self.vec], [dv])
        self.act(dv.t[:, 9:11, :], self.vec.t[:, 14:16, :], AF.Exp, [self.vec], [dv], scale=-1.0)
        self.act(dv.t[:, 9:11, :], dv.t[:, 9:11, :], AF.Ln, [dv], [dv], bias=1.0)
        self.ts(dv.t[:, 9:11, :], dv.t[:, 9:11, :], -8.0, None, ALU.mult, ALU.bypass, [dv], [dv])
        self.ts(dv.t[:, 11:13, :], dv.t[:, 9:11, :], 2.0, None, ALU.mult, ALU.bypass, [dv], [dv])
        self.ts(dv.t[:, 14:16, :], self.vec.t[:, 10:12, :], -1.0, None, ALU.mult, ALU.bypass, [self.vec], [dv])
        self.ts(dv.t[:, 16:18, :], self.vec.t[:, 12:14, :], -1.0, None, ALU.mult, ALU.bypass, [self.vec], [dv])
        self.ts(dv.t[:, 18:20, :], dv.t[:, 9:11, :], 0.5, None, ALU.mult, ALU.bypass, [dv], [dv])
        self.ts(dv.t[:, 20:22, :], dv.t[:, 9:11, :], 0.5 * T, None, ALU.mult, ALU.bypass, [dv], [dv])
        self.vop(lambda h: h.memset(dv.t[:, 13, :], 0.0), [], [dv], eng="pool")
        if self.mode != "p1":
            ld1(self.cmk.t[:], self.cmask, self.cmk)
            if self.mode == "p2":
                ld1(self.call.t[:], self.carry_all, self.call)
                ld1(self.carry.t[:], self.carry_own, self.carry)
        self.S.barrier()

    def A1(self, i, k): return self.dv.t[:, 2 * i, k:k + 1]
    def A2(self, i, k): return self.dv.t[:, 2 * i + 1, k:k + 1]
    def S1(self, i, k): return self.modt[i].t[:, 0 + k, 0:1]
    def G1(self, i, k): return self.modt[i].t[:, 32 + k, 0:1]
    def S2(self, i, k): return self.modt[i].t[:, 48 + k, 0:1]
    def G2(self, i, k): return self.modt[i].t[:, 80 + k, 0:1]
    def V(self, idx, k): return self.vec.t[:, idx, k:k + 1]

    def precast(self, which):
        S = self.S
        NST = 3
        fin = [Tl(self.arena.t[:, i * 2048:(i + 1) * 2048], f"pcin{i}") for i in range(NST)]
        fob = [Tl(self.arena.t[:, 6144 + i * 1024:6144 + (i + 1) * 1024].bitcast(BF16), f"pcout{i}")
               for i in range(NST)]
        sl_in = [S.slot(f"pci{i}") for i in range(NST)]
        sl_out = [S.slot(f"pco{i}") for i in range(NST)]
        engs = ["dve", "act", "pool"]
        jobs = []

        def add_W(W, key, cc_lo, cc_hi, base=0):
            dst, dbuf = self.ws[key]
            for cc in range(cc_lo, cc_hi):
                src = W[:, cc * 128:(cc + 1) * 128].rearrange("(k p) n -> p k n", p=P)
                jobs.append((src, dst[cc - base], dbuf, 16))

        def add_rows(R, key):
            dst, dbuf = self.ws[key]
            for c in range(NEXP_C):
                jobs.append((R[c], dst[c], dbuf, 0))
        for w in which:
            if w == "w_in_xb":
                add_W(self.lru_w_in, "w_in", 16, 32)
            elif w == "w_in":
                add_W(self.lru_w_in, "w_in", 0, 32)
            elif w == "wab":
                dst, dbuf = self.ws["wab"]
                for j in range(4):
                    jobs.append((self.wab_in[4 * j:4 * j + 4].rearrange("h p f -> p h f"),
                                 dst[4 * j:4 * j + 4].rearrange("h p f -> p h f"), dbuf, 4))
            elif w == "w_out":
                add_W(self.lru_w_out, "w_out", 0, 16)
            elif w == "sc_w_in":
                add_W(self.sc_w_in, "sc_w_in", 0, 48)
            elif w == "sc_w_out":
                add_W(self.sc_w_out, "sc_w_out", 0, 16)
            elif w in ("wq0", "wq1"):
                add_W(self.peer_w_q[int(w[2])], w, 0, 16)
            elif w in ("u0", "u1"):
                add_rows(self.u_l[int(w[1])], w)
            elif w in ("v0", "v1"):
                add_rows(self.v_l[int(w[1])], w)
        self.pc_jobs = jobs
        self.pc_n = 0
        self.pc_in = 0
        self.pc_st = (fin, fob, sl_in, sl_out, NST)
        self.pc_st_bg = ([S.slot(f"pcib{i}") for i in range(NST)], [S.slot(f"pcob{i}") for i in range(NST)])

    def precast_run(self, k, bg):
        fin, fob, sl_in, sl_out, NST = self.pc_st
        if bg:
            sl_in, sl_out = self.pc_st_bg
        jobs = self.pc_jobs
        engs = ["pool"] if bg else ["dve", "act", "pool"]
        q = "pool" if bg else "sp"
        stop = min(self.pc_n + k, len(jobs))
        while self.pc_n < stop:
            n = self.pc_n
            while self.pc_in < min(n + NST, len(jobs)):
                m = self.pc_in
                src, dst, dbuf, is3 = jobs[m]
                i = m % NST
                o = fin[i].t.rearrange("p (k n) -> p k n", k=is3) if is3 else fin[i].t
                self.dma(o, src, sl_in[i], writes=[fin[i]], q=q)
                self.pc_in += 1
            src, dst, dbuf, is3 = jobs[n]
            i = n % NST
            e = engs[n % len(engs)]
            if e == "act":
                self.act(fob[i].t, fin[i].t, AF.Copy, [fin[i]], [fob[i]])
            else:
                self.cp(fob[i].t, fin[i].t, [fin[i]], [fob[i]], eng=e)
            oo = fob[i].t.rearrange("p (k n) -> p k n", k=4) if is3 == 4 else fob[i].t
            self.dma(dst, oo, sl_out[i], reads=[fob[i]], writes=[dbuf], q=q)
            self.pc_n += 1

    def precast_finish(self):
        self.precast_run(len(self.pc_jobs), bg=False)
        self.S.barrier()
        S = self.S
        if self.mode != "p1" and self.has_peer:
            for i in range(2):
                st = Tl(self.arena.t[:, i * 2048:(i + 1) * 2048], f"sks{i}")
                self.ld1(st.t, self.skT_in[i].rearrange("p a b -> p (a b)"), st)
                self.cp(self.skT[i].t[:].rearrange("p a b -> p (a b)"), st.t, [st], [self.skT[i]])

    def wstream(self, key, ccs):
        dst, dbuf = self.ws[key]
        return Stream(self, self.wring, [(dst[cc], dbuf) for cc in ccs])

    def norm_mod(self, xsrc, n, Afn, Sfn, out_tl, out_is_f32=False):
        sq = self.bfA
        self.act(sq.t[:, :, 0:n], xsrc.t[:, :, 0:n], AF.Square, [xsrc], [sq])
        pm = self.psM
        for k in range(KC):
            self.mm(pm.t[:, 0:n], self.ones_bf.t[:], sq.t[:, k, 0:n], k == 0, k == KC - 1,
                    [self.ones_bf, sq], [pm])
        r = self.rstd
        self.act(r.t[:, 0:n], pm.t[:, 0:n], AF.Ln, [pm], [r], scale=1.0 / D, bias=EPS)
        self.act(r.t[:, 0:n], r.t[:, 0:n], AF.Exp, [r], [r], scale=-0.5)
        for k in range(KC):
            tm = self.tmpr.next()
            self.tt(tm.t[:, 0:n], xsrc.t[:, k, 0:n], r.t[:, 0:n], ALU.mult, [xsrc, r], [tm])
            self.act(out_tl.t[:, k, 0:n], tm.t[:, 0:n], AF.Identity, [tm], [out_tl], scale=Afn(k), bias=Sfn(k))

    def lru_block(self, xsrc, n, rowlen, Afn, Sfn, pass2, sbi, hin_f=None, hin_b=None, want_carry=True):
        self.norm_mod(xsrc, n, Afn, Sfn, self.hT)
        ccs = (list(range(16)) if pass2 else []) + list(range(16, 32))
        wsr = self.wstream("w_in", ccs)
        wabs = Stream(self, self.wabring, [(self.ws["wab"][0][h].rearrange("p (a b) -> p a b", a=4), self.ws["wab"][1])
                                           for h in range(16)])
        YT = self.bfA
        if pass2:
            for h in range(16):
                wt = wsr.get()
                pg = self.psring.next()
                for k in range(KC):
                    self.mm(pg.t[:, 0:n], wt.t[:, k * 128:(k + 1) * 128], self.hT.t[:, k, 0:n], k == 0, k == KC - 1,
                            [wt, self.hT], [pg])
                self.act(YT.t[:, h, 0:n], pg.t[:, 0:n], AF.Gelu_apprx_tanh, [pg], [YT])
        LN_HALF = math.log(0.5)
        for h in range(16):
            wabt = wabs.get()
            wt = wsr.get()
            px = self.psring.next()
            for k in range(KC):
                self.mm(px.t[:, 0:n], wt.t[:, k * 128:(k + 1) * 128], self.hT.t[:, k, 0:n], k == 0, k == KC - 1,
                        [wt, self.hT], [px])
            xc = self.tmpr.next()
            self.act(xc.t[:, 0:n], px.t[:, 0:n], AF.Identity, [px], [xc], scale=self.V(7, h), bias=self.V(9, h))
            pv = px.t[:, 0:n].rearrange("p (r c) -> p r c", c=rowlen)
            xv = xc.t[:, 0:n].rearrange("p (r c) -> p r c", c=rowlen)
            for tap, off in ((6, -1), (5, -2), (8, 1)):
                if off < 0:
                    o_, i_ = xv[:, :, -off:], pv[:, :, :rowlen + off]
                else:
                    o_, i_ = xv[:, :, :rowlen - off], pv[:, :, off:]
                self.stt(o_, i_, self.V(tap, h), o_, ALU.mult, ALU.add, [px, xc], [xc])
            xcb = self.tmpb.next()
            self.act(xcb.t[:, 0:n], xc.t[:, 0:n], AF.Copy, [xc], [xcb])
            hs = []
            for d in range(2):
                pr = self.psring.next()
                self.mm(pr.t[:, 0:n], wabt.t[:, d * 2 + 0, :], xcb.t[:, 0:n], True, True, [wabt, xcb], [pr])
                pi = self.psring.next()
                self.mm(pi.t[:, 0:n], wabt.t[:, d * 2 + 1, :], xcb.t[:, 0:n], True, True, [wabt, xcb], [pi])
                rr = self.tmpr.next()
                self.act(rr.t[:, 0:n], pr.t[:, 0:n], AF.Exp, [pr], [rr], scale=-1.0, bias=self.dv.t[:, 14 + d, h:h + 1])
                ii = self.tmpr.next()
                self.act(ii.t[:, 0:n], pi.t[:, 0:n], AF.Exp, [pi], [ii], scale=-1.0, bias=self.dv.t[:, 16 + d, h:h + 1])
                self.ts(rr.t[:, 0:n], rr.t[:, 0:n], 1.0, None, ALU.add, ALU.bypass, [rr], [rr])
                self.vop(lambda hd, rr=rr: hd.reciprocal(out=rr.t[:, 0:n], in_=rr.t[:, 0:n]), [rr], [rr])
                self.act(ii.t[:, 0:n], ii.t[:, 0:n], AF.Ln, [ii], [ii], bias=1.0)
                self.act(ii.t[:, 0:n], ii.t[:, 0:n], AF.Exp, [ii], [ii], scale=-1.0)
                aa = self.tmpr.next()
                self.act(aa.t[:, 0:n], rr.t[:, 0:n], AF.Exp, [rr], [aa], scale=self.dv.t[:, 9 + d, h:h + 1])
                a2 = self.tmpr.next()
                self.act(a2.t[:, 0:n], rr.t[:, 0:n], AF.Exp, [rr], [a2], scale=self.dv.t[:, 11 + d, h:h + 1])
                self.act(a2.t[:, 0:n], a2.t[:, 0:n], AF.Ln, [a2], [a2], scale=-1.0, bias=1.0)
                self.act(a2.t[:, 0:n], a2.t[:, 0:n], AF.Exp, [a2], [a2], scale=0.5)
                self.tt(ii.t[:, 0:n], ii.t[:, 0:n], xc.t[:, 0:n], ALU.mult, [ii, xc], [ii])
                self.tt(ii.t[:, 0:n], ii.t[:, 0:n], a2.t[:, 0:n], ALU.mult, [ii, a2], [ii])
                first = 0 if d == 0 else n - 1
                hinp = hin_f if d == 0 else hin_b
                if hinp is not None:
                    self.stt(ii.t[:, first:first + 1], aa.t[:, first:first + 1], hinp(h), ii.t[:, first:first + 1],
                             ALU.mult, ALU.add, [aa, ii, self.hin], [ii])
                hh = self.tmpr.next()
                if d == 0:
                    o_, a_, b_ = hh.t[:, 0:n], aa.t[:, 0:n], ii.t[:, 0:n]
                else:
                    o_, a_, b_ = hh.t[:, 0:n][:, ::-1], aa.t[:, 0:n][:, ::-1], ii.t[:, 0:n][:, ::-1]
                self.vop(lambda hd, o_=o_, a_=a_, b_=b_: hd.tensor_tensor_scan(out=o_, data0=a_, data1=b_, initial=0.0,
                                                                           op0=ALU.mult, op1=ALU.add),
                         [aa, ii], [hh])
                if want_carry:
                    last = n - 1 if d == 0 else 0
                    ctl = self.carry
                    self.vop(lambda hd, rr=rr, d=d, h=h, ctl=ctl: hd.tensor_reduce(out=ctl.t[:, sbi, 2 * d, h:h + 1],
                                                                                 in_=rr.t[:, 0:n], axis=AX.X, op=ALU.add),
                             [rr], [ctl])
                    self.cp(ctl.t[:, sbi, 2 * d + 1, h:h + 1], hh.t[:, last:last + 1], [hh], [ctl])
                hs.append(hh)
            if pass2:
                self.tt(hs[0].t[:, 0:n], hs[0].t[:, 0:n], hs[1].t[:, 0:n], ALU.add, [hs[0], hs[1]], [hs[0]])
                self.tt(YT.t[:, h, 0:n], YT.t[:, h, 0:n], hs[0].t[:, 0:n], ALU.mult, [hs[0], YT], [YT])
        if pass2:
            self.out_proj("w_out", YT, n, 0)

    def lru_p1(self, xsrc, sbi, pass2=False):
        n, rowlen = T, 64
        self.norm_mod(xsrc, n, lambda k: self.A1(0, k), lambda k: self.S1(0, k), self.hT)
        wsr = self.wstream("w_in", (list(range(16)) if pass2 else []) + list(range(16, 32)))
        YT = self.bfA
        if pass2:
            for h in range(16):
                wt = wsr.get()
                pg = self.psbig.next()
                for k in range(KC):
                    self.mm(pg.t[:, 0:n], wt.t[:, k * 128:(k + 1) * 128], self.hT.t[:, k, 0:n], k == 0, k == KC - 1,
                            [wt, self.hT], [pg])
                self.act(YT.t[:, h, 0:n], pg.t[:, 0:n], AF.Gelu_apprx_tanh, [pg], [YT])
        wabs = Stream(self, self.wabring, [(self.ws["wab"][0][h].rearrange("p (a b) -> p a b", a=4), self.ws["wab"][1])
                                           for h in range(16)])
        ctl = self.carry
        G = 2
        R = self.p1ring

        def s0(h, c):
            wt = wsr.get()
            c["px"] = px = self.psbig.next()
            for k in range(KC):
                self.mm(px.t[:, 0:n], wt.t[:, k * 128:(k + 1) * 128], self.hT.t[:, k, 0:n], k == 0, k == KC - 1,
                        [wt, self.hT], [px])

        def s1(h, c):
            c["xc"] = xc = R.next()
            self.act(xc.t[:, 0:n], c["px"].t[:, 0:n], AF.Identity, [c["px"]], [xc], scale=self.V(7, h), bias=self.V(9, h))

        def s2(h, c):
            px, xc = c["px"], c["xc"]
            pv = px.t[:, 0:n].rearrange("p (r c) -> p r c", c=rowlen)
            xv = xc.t[:, 0:n].rearrange("p (r c) -> p r c", c=rowlen)
            for tap, off in ((6, -1), (5, -2), (8, 1)):
                if off < 0:
                    o_, i_ = xv[:, :, -off:], pv[:, :, :rowlen + off]
                else:
                    o_, i_ = xv[:, :, :rowlen - off], pv[:, :, off:]
                self.stt(o_, i_, self.V(tap, h), o_, ALU.mult, ALU.add, [px, xc], [xc])

        def s3(h, c):
            c["xcb"] = xcb = self.tmpb.next()
            self.act(xcb.t[:, 0:n], c["xc"].t[:, 0:n], AF.Copy, [c["xc"]], [xcb])

        def s4(h, c):
            wabt = wabs.get()
            xcb = c["xcb"]
            for d in range(2):
                c["pr", d] = pr = self.psbig.next()
                self.mm(pr.t[:, 0:n], wabt.t[:, d * 2 + 0, :], xcb.t[:, 0:n], True, True, [wabt, xcb], [pr])
                c["pi", d] = pi = self.psbig.next()
                self.mm(pi.t[:, 0:n], wabt.t[:, d * 2 + 1, :], xcb.t[:, 0:n], True, True, [wabt, xcb], [pi])

        def s5(h, c):
            for d in range(2):
                c["rr", d] = rr = R.next()
                self.act(rr.t[:, 0:n], c["pr", d].t[:, 0:n], AF.Exp, [c["pr", d]], [rr], scale=-1.0,
                         bias=self.dv.t[:, 14 + d, h:h + 1])
                c["ii", d] = ii = R.next()
                self.act(ii.t[:, 0:n], c["pi", d].t[:, 0:n], AF.Exp, [c["pi", d]], [ii], scale=-1.0,
                         bias=self.dv.t[:, 16 + d, h:h + 1])

        def s6(h, c):
            for d in range(2):
                rr, ii = c["rr", d], c["ii", d]
                self.act(rr.t[:, 0:n], rr.t[:, 0:n], AF.Ln, [rr], [rr], bias=1.0)
                self.act(ii.t[:, 0:n], ii.t[:, 0:n], AF.Ln, [ii], [ii], bias=1.0)

        def s7(h, c):
            for d in range(2):
                rr, ii = c["rr", d], c["ii", d]
                self.act(rr.t[:, 0:n], rr.t[:, 0:n], AF.Exp, [rr], [rr], scale=-1.0)
                self.act(ii.t[:, 0:n], ii.t[:, 0:n], AF.Exp, [ii], [ii], scale=-1.0)

        def s8(h, c):
            for d in range(2):
                rr = c["rr", d]
                c["aa", d] = aa = R.next()
                self.act(aa.t[:, 0:n], rr.t[:, 0:n], AF.Exp, [rr], [aa], scale=self.dv.t[:, 9 + d, h:h + 1])
                c["a2", d] = a2 = R.next()
                self.act(a2.t[:, 0:n], rr.t[:, 0:n], AF.Exp, [rr], [a2], scale=self.dv.t[:, 11 + d, h:h + 1])
                if not pass2:
                    self.vop(lambda hd, rr=rr, d=d, h=h: hd.tensor_reduce(out=ctl.t[:, sbi, 2 * d, h:h + 1],
                                                                        in_=rr.t[:, 0:n], axis=AX.X, op=ALU.add),
                             [rr], [ctl])
                ii = c["ii", d]
                self.tt(ii.t[:, 0:n], ii.t[:, 0:n], c["xc"].t[:, 0:n], ALU.mult, [ii, c["xc"]], [ii])

        def s9(h, c):
            for d in range(2):
                a2 = c["a2", d]
                self.act(a2.t[:, 0:n], a2.t[:, 0:n], AF.Ln, [a2], [a2], scale=-1.0, bias=1.0)

        def s10(h, c):
            for d in range(2):
                a2 = c["a2", d]
                self.act(a2.t[:, 0:n], a2.t[:, 0:n], AF.Exp, [a2], [a2], scale=0.5)

        def s11(h, c):
            for d in range(2):
                ii, a2 = c["ii", d], c["a2", d]
                self.tt(ii.t[:, 0:n], ii.t[:, 0:n], a2.t[:, 0:n], ALU.mult, [ii, a2], [ii])

        def s12(h, c):
            for d in range(2):
                aa, ii, hh = c["aa", d], c["ii", d], c["a2", d]
                if pass2:
                    first = 0 if d == 0 else n - 1
                    hv = self.hin.t[:, 0, sbi, h:h + 1] if d == 0 else self.hin.t[:, 1, sbi + 1, h:h + 1]
                    self.stt(ii.t[:, first:first + 1], aa.t[:, first:first + 1], hv, ii.t[:, first:first + 1],
                             ALU.mult, ALU.add, [aa, ii, self.hin], [ii])
                if d == 0:
                    o_, a_, b_ = hh.t[:, 0:n], aa.t[:, 0:n], ii.t[:, 0:n]
                else:
                    o_, a_, b_ = hh.t[:, 0:n][:, ::-1], aa.t[:, 0:n][:, ::-1], ii.t[:, 0:n][:, ::-1]
                self.vop(lambda hd, o_=o_, a_=a_, b_=b_: hd.tensor_tensor_scan(out=o_, data0=a_, data1=b_, initial=0.0,
                                                                           op0=ALU.mult, op1=ALU.add),
                         [aa, ii], [hh])

        def s13(h, c):
            if pass2:
                h0_, h1_ = c["a2", 0], c["a2", 1]
                self.tt(h0_.t[:, 0:n], h0_.t[:, 0:n], h1_.t[:, 0:n], ALU.add, [h0_, h1_], [h0_])
                self.tt(YT.t[:, h, 0:n], YT.t[:, h, 0:n], h0_.t[:, 0:n], ALU.mult, [h0_, YT], [YT])
                return
            for d in range(2):
                hh = c["a2", d]
                last = n - 1 if d == 0 else 0
                self.cp(ctl.t[:, sbi, 2 * d + 1, h:h + 1], hh.t[:, last:last + 1], [hh], [ctl])

        stages = [s0, s1, s2, s3, s4, s5, s6, s7, s8, s9, s10, s11, s12, s13]
        SK = 5
        cx = {h: {} for h in range(16)}
        ns = len(stages)
        for step in range(ns + 15 * SK):
            for h in range(16):
                si = step - h * SK
                if 0 <= si < ns:
                    stages[si](h, cx[h])
        if pass2:
            self.out_proj("w_out", YT, n, 0)

    def out_proj(self, key, YT, n, layer):
        wsr = self.wstream(key, list(range(16)))
        for m in range(16):
            wt = wsr.get()
            po = self.psring.next()
            for k in range(KC):
                self.mm(po.t[:, 0:n], wt.t[:, k * 128:(k + 1) * 128], YT.t[:, k, 0:n], k == 0, k == KC - 1,
                        [wt, YT], [po])
            self.stt(self.xres.t[:, m, 0:n], po.t[:, 0:n], self.G1(layer, m), self.xres.t[:, m, 0:n],
                     ALU.mult, ALU.add, [po, self.modt[layer], self.xres], [self.xres])

    def sc_block(self, n):
        self.norm_mod(self.xres, n, lambda k: self.A1(1, k), lambda k: self.S1(1, k), self.hT)
        ccs = []
        for ch in range(16):
            ccs += [ch, 16 + ch, 32 + ch]
        wsr = self.wstream("sc_w_in", ccs)
        YT = self.bfA
        rowlen = 64
        for ch in range(16):
            pp = []
            for j in range(3):
                wt = wsr.get()
                p_ = self.psring.next()
                for k in range(KC):
                    self.mm(p_.t[:, 0:n], wt.t[:, k * 128:(k + 1) * 128], self.hT.t[:, k, 0:n], k == 0, k == KC - 1,
                            [wt, self.hT], [p_])
                pp.append(p_)
            vs = self.tmpr.next()
            self.act(vs.t[:, 0:n], pp[2].t[:, 0:n], AF.Copy, [pp[2]], [vs])
            cvt = self.tmpr.next()
            self.tt(cvt.t[:, 0:n], pp[1].t[:, 0:n], vs.t[:, 0:n], ALU.mult, [pp[1], vs], [cvt])
            yc = self.tmpr.next()
            self.act(yc.t[:, 0:n], cvt.t[:, 0:n], AF.Identity, [cvt], [yc], scale=self.V(17, ch), bias=self.V(19, ch))
            cv_ = cvt.t[:, 0:n].rearrange("p (r c) -> p r c", c=rowlen)
            yv = yc.t[:, 0:n].rearrange("p (r c) -> p r c", c=rowlen)
            for tap, off in ((16, -1), (18, 1)):
                if off < 0:
                    o_, i_ = yv[:, :, -off:], cv_[:, :, :rowlen + off]
                else:
                    o_, i_ = yv[:, :, :rowlen - off], cv_[:, :, off:]
                self.stt(o_, i_, self.V(tap, ch), o_, ALU.mult, ALU.add, [cvt, yc], [yc])
            self.tt(YT.t[:, ch, 0:n], pp[0].t[:, 0:n], yc.t[:, 0:n], ALU.mult, [pp[0], yc], [YT])
        self.out_proj("sc_w_out", YT, n, 1)

    def peer_block(self, layer):
        n = T
        self.norm_mod(self.xres, n, lambda k: self.A2(layer, k), lambda k: self.S2(layer, k), self.hT)
        qT = self.bfA
        wsr = self.wstream(f"wq{layer}", list(range(16)))
        for hp in range(16):
            wt = wsr.get()
            pq = self.psring.next()
            for k in range(KC):
                self.mm(pq.t[:, 0:n], wt.t[:, k * 128:(k + 1) * 128], self.hT.t[:, k, 0:n], k == 0, k == KC - 1,
                        [wt, self.hT], [pq])
            self.act(qT.t[:, hp, 0:n], pq.t[:, 0:n], AF.Copy, [pq], [qT])
        for ts_ in range(T // 128):
            self.topk_sub(layer, qT, ts_)
            self.scatter_sub(ts_)
        self.sweep(layer)

    def topk_sub(self, layer, qT, ts_):
        t0 = ts_ * 128
        skT = self.skT[layer]
        s_sb = Tl(self.X1.t[:].rearrange("p (a b) -> p a b", b=128), "x"); s_sb.b = self.X1.b
        s_wk = Tl(self.X2.t[:].rearrange("p (a b) -> p a b", b=128), "x"); s_wk.b = self.X2.b
        for hp in range(16):
            pS = self.psS[hp // 4]
            self.mm(pS.t[:, (hp % 4) * 128:(hp % 4 + 1) * 128], qT.t[:, hp, t0:t0 + 128], skT.t[:, hp, :], True, True,
                    [qT, skT], [pS])
        for b4 in range(4):
            self.act(s_sb.t[:, b4 * 4:(b4 + 1) * 4, :].rearrange("p a b -> p (a b)"), self.psS[b4].t[:], AF.Copy,
                     [self.psS[b4]], [s_sb])
        sv, si_u = self.sv, self.si_u
        for hp in range(16):
            self.vop(lambda h, hp=hp: h.max(out=sv.t[:, hp, 0:8], in_=s_sb.t[:, hp, :]), [s_sb], [sv])
        for hp in range(16):
            self.vop(lambda h, hp=hp: h.max_index(out=si_u.t[:, hp, 0:8], in_max=sv.t[:, hp, 0:8],
                                                  in_values=s_sb.t[:, hp, :]), [s_sb, sv], [si_u])
        for hp in range(16):
            self.vop(lambda h, hp=hp: h.match_replace(out=s_wk.t[:, hp, :], in_to_replace=sv.t[:, hp, 0:8],
                                                      in_values=s_sb.t[:, hp, :], imm_value=NEG), [s_sb, sv], [s_wk])
        for hp in range(16):
            self.vop(lambda h, hp=hp: h.max(out=sv.t[:, hp, 8:16], in_=s_wk.t[:, hp, :]), [s_wk], [sv])
        for hp in range(16):
            self.vop(lambda h, hp=hp: h.max_index(out=si_u.t[:, hp, 8:16], in_max=sv.t[:, hp, 8:16],
                                                  in_values=s_wk.t[:, hp, :]), [s_wk, sv], [si_u])
        self.cp(self.si_f.t[:], si_u.t[:], [si_u], [self.si_f])
        cand = Tl(self.X1.t[:].rearrange("p (h a b) -> p h a b", h=8, a=16), "x"); cand.b = self.X1.b
        candw = Tl(self.X2.t[:].rearrange("p (h c) -> p h c", h=8), "x"); candw.b = self.X2.b
        self.tt(cand.t, sv.t[:, 0::2, :].unsqueeze(3).broadcast_to([P, 8, 16, 16]),
                sv.t[:, 1::2, :].unsqueeze(2).broadcast_to([P, 8, 16, 16]), ALU.add, [sv], [cand])
        c2 = Tl(self.X1.t[:].rearrange("p (h c) -> p h c", h=8), "x"); c2.b = self.X1.b
        cvv, ci_u = self.cvv, self.ci_u
        for h_ in range(8):
            self.vop(lambda h, h_=h_: h.max(out=cvv.t[:, h_, 0:8], in_=c2.t[:, h_, :]), [c2], [cvv])
        for h_ in range(8):
            self.vop(lambda h, h_=h_: h.max_index(out=ci_u.t[:, h_, 0:8], in_max=cvv.t[:, h_, 0:8],
                                                  in_values=c2.t[:, h_, :]), [c2, cvv], [ci_u])
        for h_ in range(8):
            self.vop(lambda h, h_=h_: h.match_replace(out=candw.t[:, h_, :], in_to_replace=cvv.t[:, h_, 0:8],
                                                      in_values=c2.t[:, h_, :], imm_value=NEG), [c2, cvv], [candw])
        for h_ in range(8):
            self.vop(lambda h, h_=h_: h.max(out=cvv.t[:, h_, 8:16], in_=candw.t[:, h_, :]), [candw], [cvv])
        for h_ in range(8):
            self.vop(lambda h, h_=h_: h.max_index(out=ci_u.t[:, h_, 8:16], in_max=cvv.t[:, h_, 8:16],
                                                  in_values=candw.t[:, h_, :]), [candw, cvv], [ci_u])
        self.ts(self.ca_u.t[:], ci_u.t[:], 4, None, ALU.logical_shift_right, ALU.bypass, [ci_u], [self.ca_u])
        self.ts(self.cb_u.t[:], ci_u.t[:], 15, None, ALU.bitwise_and, ALU.bypass, [ci_u], [self.cb_u])
        self.cp(self.caf.t[:], self.ca_u.t[:], [self.ca_u], [self.caf])
        self.cp(self.cbf.t[:], self.cb_u.t[:], [self.cb_u], [self.cbf])
        oh = Tl(self.X1.t[:].rearrange("p (h k a) -> p h k a", h=8, k=16), "x"); oh.b = self.X1.b
        io16 = self.iota[:, 0:16].unsqueeze(1).unsqueeze(1).broadcast_to([P, 8, 16, 16])
        for j, (cf, par) in enumerate(((self.caf, 0), (self.cbf, 1))):
            self.tt(oh.t, cf.t[:].unsqueeze(3).broadcast_to([P, 8, 16, 16]), io16, ALU.is_equal, [cf, self.cst], [oh])
            self.tt(oh.t, oh.t, self.si_f.t[:, par::2, :].unsqueeze(2).broadcast_to([P, 8, 16, 16]), ALU.mult,
                    [oh, self.si_f], [oh])
            self.vop(lambda h, j=j: h.tensor_reduce(out=self.slot3.t[:, j, :].rearrange("p (h k) -> p h k", h=8),
                                                    in_=oh.t, axis=AX.X, op=ALU.add), [oh], [self.slot3])
        zs = self.zs
        g3 = self.slot3.t[:, 2, :].rearrange("p (h k) -> p h k", h=8)
        self.tt(g3, cvv.t[:], cvv.t[:, :, 0:1].broadcast_to([P, 8, 16]), ALU.subtract, [cvv], [self.slot3])
        self.act(g3, g3, AF.Exp, [self.slot3], [self.slot3])
        self.vop(lambda h: h.tensor_reduce(out=zs.t[:, 0, :], in_=g3, axis=AX.X, op=ALU.add), [self.slot3], [zs])
        self.vop(lambda h: h.reciprocal(out=zs.t[:, 1, :], in_=zs.t[:, 0, :]), [zs], [zs])
        self.tt(g3, g3, zs.t[:, 1, :].unsqueeze(2).broadcast_to([P, 8, 16]), ALU.mult, [self.slot3, zs], [self.slot3])
        pm = self.psM
        for j in range(3):
            self.vop(lambda h, j=j: h.transpose(out=pm.t[:, j * 128:(j + 1) * 128], in_=self.slot3.t[:, j, :],
                                                identity=self.ident), [self.slot3, self.cst], [pm], eng="pe")
        self.cp(self.slotT.t[:, :, ts_ * 128:(ts_ + 1) * 128], pm.t[:, 0:384].rearrange("p (j t) -> p j t", j=3),
                [pm], [self.slotT])

    def scatter_sub(self, ts_):
        TG = 16
        WT = self.WT
        if not hasattr(self, "ohb"):
            self.ohb = []
            for X in (self.X1, self.X2):
                v = X.t[:].bitcast(BF16).rearrange("p (t i) -> p t i", i=128)
                self.ohb.append([Tl(v[:, j * TG:(j + 1) * TG, :], f"oh_{X.b.name}_{j}") for j in range(2)])
        for X, pair in zip((self.X1, self.X2), self.ohb):
            for tl in pair:
                for k_, tok in X.b.writers.items():
                    if k_ not in tl.b.writers or tl.b.writers[k_][1] < tok[1]:
                        tl.b.writers[k_] = tok
                tl.b.readers.extend(X.b.readers)
        for g_ in range(128 // TG):
            t0 = ts_ * 128 + g_ * TG
            Bt = self.ohb[0][g_ % 2]
            At = self.ohb[1][g_ % 2]
            io = self.iota.unsqueeze(1).broadcast_to([P, TG, 128])
            i1v = self.slotT.t[:, 0, t0:t0 + TG].unsqueeze(2).broadcast_to([P, TG, 128])
            i2v = self.slotT.t[:, 1, t0:t0 + TG].unsqueeze(2).broadcast_to([P, TG, 128])
            gv = self.slotT.t[:, 2, t0:t0 + TG].unsqueeze(2).broadcast_to([P, TG, 128])
            self.tt(Bt.t, io, i2v, ALU.is_equal, [self.cst, self.slotT], [Bt])
            self.tt(At.t, io, i1v, ALU.is_equal, [self.cst, self.slotT], [At])
            self.tt(At.t, At.t, gv, ALU.mult, [At, self.slotT], [At])
            for q4 in range(TG // 4):
                pw = self.psS[(g_ * (TG // 4) + q4) % 4]
                for tt_ in range(4):
                    tl = q4 * 4 + tt_
                    self.mm(pw.t[:, tt_ * 128:(tt_ + 1) * 128], Bt.t[:, tl, :], At.t[:, tl, :], True, True,
                            [Bt, At], [pw])
                ta = t0 + q4 * 4
                src_ = pw.t[:].rearrange("p (t i) -> p i t", t=4)
                self.act(WT.t[:, :, ta:ta + 4], src_, AF.Copy, [pw], [WT])
        for X, pair in zip((self.X1, self.X2), self.ohb):
            for tl in pair:
                for k_, tok in tl.b.writers.items():
                    if k_ not in X.b.writers or X.b.writers[k_][1] < tok[1]:
                        X.b.writers[k_] = tok
                X.b.readers.extend(tl.b.readers)

    def sweep(self, layer):
        n = T
        PC = 4
        WT = self.WT
        udst, ubuf = self.ws[f"u{layer}"]
        vdst, vbuf = self.ws[f"v{layer}"]
        ust = Stream(self, self.wring, [(udst[c], ubuf) for c in range(NEXP_C)])
        vst = Stream(self, self.vring, [(vdst[c0:c0 + PC].rearrange("c p d -> p c d"), vbuf)
                                        for c0 in range(0, NEXP_C, PC)], queue="pool")
        xb_ = self.xres.b
        subs = []
        for m in range(16):
            sb_ = Buf(f"xres_m{m}")
            sb_.writers = dict(xb_.writers)
            sb_.readers = list(xb_.readers)
            subs.append(sb_)
        modb = self.modt[layer].b
        wb_ = WT.b
        wtb = []
        for c in range(NEXP_C):
            b_ = Buf(f"wt_c{c}")
            b_.writers = dict(wb_.writers)
            b_.readers = list(wb_.readers)
            wtb.append(b_)

        def u_side(part):
            for cc in range(PC):
                c = part * PC + cc
                ut = ust.get()
                pu = self.psring.next()
                for k in range(KC):
                    self.mm(pu.t[:, 0:n], ut.t[:, k * 128:(k + 1) * 128], self.hT.t[:, k, 0:n], k == 0, k == KC - 1,
                            [ut, self.hT], [pu])
                ab = self.tmpb.next()
                self.act(ab.t[:, 0:n], pu.t[:, 0:n], AF.Gelu_apprx_tanh, [pu], [ab])
                self.S.op("dve", lambda h, c=c, ab=ab: h.tensor_tensor(out=WT.t[:, c, :], in0=WT.t[:, c, :],
                                                                         in1=ab.t[:, 0:n], op=ALU.mult),
                          [wtb[c], ab.b], [wtb[c]])

        def v_side(part):
            vt = vst.get()
            for m in range(16):
                pv = self.psring.next()
                for cc in range(PC):
                    c = part * PC + cc
                    self.S.op("pe", lambda h, pv=pv, vt=vt, cc=cc, m=m, c=c: h.matmul(
                        pv.t[:, 0:n], lhsT=vt.t[:, cc, m * 128:(m + 1) * 128], rhs=WT.t[:, c, :],
                        start=(cc == 0), stop=(cc == PC - 1)), [vt.b, wtb[c]], [pv.b])
                xo = self.xres.t[:, m, 0:n]
                if m % 2 == 0:
                    self.S.op("dve", lambda h, xo=xo, pv=pv, m=m: h.scalar_tensor_tensor(
                        out=xo, in0=pv.t[:, 0:n], scalar=self.G2(layer, m), in1=xo, op0=ALU.mult, op1=ALU.add),
                        [pv.b, modb, subs[m]], [subs[m]])
                else:
                    tm = self.tmpr.next()
                    self.act(tm.t[:, 0:n], pv.t[:, 0:n], AF.Identity, [pv], [tm], scale=self.G2(layer, m))
                    self.S.op("pool", lambda h, xo=xo, tm=tm: h.tensor_tensor(out=xo, in0=xo, in1=tm.t[:, 0:n], op=ALU.add),
                              [tm.b, subs[m]], [subs[m]])

        nparts = NEXP_C // PC
        u_side(0)
        for part in range(nparts):
            if part + 1 < nparts:
                u_side(part + 1)
            v_side(part)
        for sb_ in subs:
            for k_, tok in sb_.writers.items():
                if k_ not in xb_.writers or xb_.writers[k_][1] < tok[1]:
                    xb_.writers[k_] = tok
            xb_.readers.extend(sb_.readers)
        for b_ in wtb:
            for k_, tok in b_.writers.items():
                if k_ not in wb_.writers or wb_.writers[k_][1] < tok[1]:
                    wb_.writers[k_] = tok
            wb_.readers.extend(b_.readers)

    def pass1(self, xsrc, base):
        S = self.S
        if not hasattr(self, "ldx1"):
            self.ldx1 = S.slot("ldx1")
        ldx = self.ldx1
        for sbi in range(NSB):
            self.dma(self.xres.t[:], xsrc[:, :, base + sbi * T:base + (sbi + 1) * T], ldx, writes=[self.xres])
            self.lru_p1(self.xres, sbi)
            if getattr(self, "bg_per_block", 0):
                self.precast_run(self.bg_per_block, bg=True)
        self.S.barrier()
        self.chunk_carry()
        if self.mode == "p1":
            so = S.slot("st_carry")
            self.dma(self.carry_out, self.carry.t[:], so, reads=[self.carry])
        self.S.barrier()

    def exchange(self):
        S = self.S
        cc_in = self.dscr("cc_in", [P, 4 * KC], F32)
        cc_out = self.dscr("cc_out", [NCORE * P, 4 * KC], F32)
        bi, bo = Buf("cc_in"), Buf("cc_out")
        s1, s2, s3 = S.slot("cc1"), S.slot("cc2"), S.slot("cc3")
        self.dma(cc_in, self.carry.t[:, NSB, :, :].rearrange("p a b -> p (a b)"), s1, reads=[self.carry], writes=[bi])
        S.custom("pool", lambda h: h.collective_compute("AllGather", ALU.bypass, replica_groups=[list(range(NCORE))],
                                                        ins=[cc_in], outs=[cc_out]), s2, reads=[bi], writes=[bo])
        self.dma(self.call.t[:].rearrange("p r a b -> p r (a b)"), cc_out.rearrange("(r p) f -> p r f", p=P), s3,
                 reads=[bo], writes=[self.call])
        self.S.barrier()

    def chunk_carry(self):
        c = self.carry
        for d in range(2):
            self.tt(c.t[:, 0:NSB, 2 * d, :], c.t[:, 0:NSB, 2 * d, :],
                    self.dv.t[:, 9 + d, :].unsqueeze(1).broadcast_to([P, NSB, KC]), ALU.mult, [c, self.dv], [c])
            self.act(c.t[:, 0:NSB, 2 * d, :], c.t[:, 0:NSB, 2 * d, :], AF.Exp, [c], [c])
        self.cp(c.t[:, NSB, 0, :], c.t[:, 0, 0, :], [c], [c])
        self.cp(c.t[:, NSB, 1, :], c.t[:, 0, 1, :], [c], [c])
        for sb in range(1, NSB):
            self.tt(c.t[:, NSB, 1, :], c.t[:, NSB, 1, :], c.t[:, sb, 0, :], ALU.mult, [c], [c])
            self.tt(c.t[:, NSB, 1, :], c.t[:, NSB, 1, :], c.t[:, sb, 1, :], ALU.add, [c], [c])
            self.tt(c.t[:, NSB, 0, :], c.t[:, NSB, 0, :], c.t[:, sb, 0, :], ALU.mult, [c], [c])
        self.cp(c.t[:, NSB, 2, :], c.t[:, NSB - 1, 2, :], [c], [c])
        self.cp(c.t[:, NSB, 3, :], c.t[:, NSB - 1, 3, :], [c], [c])
        for sb in range(NSB - 2, -1, -1):
            self.tt(c.t[:, NSB, 3, :], c.t[:, NSB, 3, :], c.t[:, sb, 2, :], ALU.mult, [c], [c])
            self.tt(c.t[:, NSB, 3, :], c.t[:, NSB, 3, :], c.t[:, sb, 3, :], ALU.add, [c], [c])
            self.tt(c.t[:, NSB, 2, :], c.t[:, NSB, 2, :], c.t[:, sb, 2, :], ALU.mult, [c], [c])

    def ctx_and_carries(self):
        NC_ = NSB
        ldc = self.S.slot("ld_ctx")
        self.dma(self.ctxx.t, self.ctxT, ldc, writes=[self.ctxx])
        saved = self.carry
        ctxc = self.sb("ctxcarry", [P, 1, 4, KC])
        self.carry = ctxc
        self.lru_block(self.ctxx, 256, 256, lambda k: self.dv.t[:, 8, k:k + 1],
                       lambda k: self.modt[0].t[:, k, 1:2], False, 0)
        self.carry = saved
        hin = self.hin
        call = self.call
        cm = self.cmk
        sm = self.small
        self.cp(hin.t[:, 0, 0, :], ctxc.t[:, 0, 1, :], [ctxc], [hin])
        self.cp(hin.t[:, 1, NSB, :], ctxc.t[:, 0, 3, :], [ctxc], [hin])
        ns_ = self.nslots
        for d, order in ((0, range(ns_)), (1, range(ns_ - 1, -1, -1))):
            hsl = hin.t[:, 0, 0, :] if d == 0 else hin.t[:, 1, NSB, :]
            for cp_ in order:
                m = cm.t[:, d, cp_:cp_ + 1]
                self.ts(sm.t[:, 0, :], call.t[:, cp_, 2 * d, :], -1.0, m, ALU.add, ALU.mult, [call, cm], [sm])
                self.ts(sm.t[:, 0, :], sm.t[:, 0, :], 1.0, None, ALU.add, ALU.bypass, [sm], [sm])
                self.ts(sm.t[:, 1, :], call.t[:, cp_, 2 * d + 1, :], m, None, ALU.mult, ALU.bypass, [call, cm], [sm])
                self.tt(hsl, hsl, sm.t[:, 0, :], ALU.mult, [hin, sm], [hin])
                self.tt(hsl, hsl, sm.t[:, 1, :], ALU.add, [hin, sm], [hin])
        c = self.carry
        for sb in range(NSB):
            self.tt(hin.t[:, 0, sb + 1, :], hin.t[:, 0, sb, :], c.t[:, sb, 0, :], ALU.mult, [hin, c], [hin])
            self.tt(hin.t[:, 0, sb + 1, :], hin.t[:, 0, sb + 1, :], c.t[:, sb, 1, :], ALU.add, [hin, c], [hin])
        for sb in range(NSB - 1, -1, -1):
            self.tt(hin.t[:, 1, sb, :], hin.t[:, 1, sb + 1, :], c.t[:, sb, 2, :], ALU.mult, [hin, c], [hin])
            self.tt(hin.t[:, 1, sb, :], hin.t[:, 1, sb, :], c.t[:, sb, 3, :], ALU.add, [hin, c], [hin])
        self.S.barrier()

    def pass2(self):
        S = self.S
        ldx = S.slot("ldx")
        sto = S.slot("sto")
        for sbi in range(self.nsb_run):
            self.dma(self.xres.t[:], self.xT[:, :, sbi * T:(sbi + 1) * T], ldx, writes=[self.xres])
            self.S.barrier()
            self.lru_p1(self.xres, sbi, pass2=True)
            self.S.barrier()
            if self.dbg != "x1":
                self.peer_block(0)
            if self.dbg is None or self.dbg in ("x3", "x4"):
                self.sc_block(T)
            if self.dbg is None or self.dbg == "x4":
                self.peer_block(1)
            if self.dbg is None:
                self.norm_mod(self.xres, T, lambda k: self.V(4, k), lambda k: self.dv.t[:, 13, k:k + 1], self.outT)
                self.dma(self.yT[:, :, sbi * T:(sbi + 1) * T], self.outT.t[:], sto, reads=[self.outT])
            else:
                self.dma(self.yT[:, :, sbi * T:(sbi + 1) * T], self.xres.t[:], sto, reads=[self.xres])
        self.S.barrier()


def _fm(v):
    v = np.asarray(v, np.float32)
    return np.ascontiguousarray(np.moveaxis(v.reshape(v.shape[:-1] + (KC, P)), -1, 0))


def _consts():
    c = np.zeros((P, 384), np.float32)
    c[:, 0:128] = np.eye(P, dtype=np.float32)
    c[:, 128:256] = np.arange(128, dtype=np.float32)[None, :]
    c[:, 256:384] = 1.0
    return c


_NC_CACHE = {}


def _get_nc(mode, nsb_run=NSB, dbg=None):
    key = (mode, nsb_run, dbg)
    if key not in _NC_CACHE:
        _NC_CACHE[key] = KB(mode, nsb_run, dbg).build()
    return _NC_CACHE[key]


def _prep(inputs):
    f = lambda k: np.asarray(inputs[k], np.float32)
    x = f("x")
    shared = {}
    shared["consts"] = _consts()
    vec = np.zeros((NV, D), np.float32)
    vec[0] = f("norm_mix_g")[0]; vec[1] = f("norm_ffn_g")[0]
    vec[2] = f("norm_mix_g")[1]; vec[3] = f("norm_ffn_g")[1]
    vec[4] = f("norm_final_g")
    vec[5:9] = f("lru_conv_w")[0]; vec[9] = f("lru_conv_b")[0]
    vec[10:12] = f("lru_b_a")[0]; vec[12:14] = f("lru_b_x")[0]; vec[14:16] = f("lru_lambda")[0]
    vec[16:19] = f("sc_conv_w")[0]; vec[19] = f("sc_conv_b")[0]
    shared["vecs"] = _fm(vec)
    shared["w_mod"] = f("w_mod")
    shared["b_modT"] = np.ascontiguousarray(f("b_mod").reshape(2, 96, P).transpose(2, 0, 1))
    shared["lru_w_in"] = f("lru_w_in")[0]
    wab = np.stack([f("lru_w_a")[0], f("lru_w_x")[0]], axis=1)
    shared["wab"] = np.ascontiguousarray(wab.transpose(2, 3, 0, 1, 4)).reshape(16, P, 512)
    per_core = []
    for c in range(NCORE):
        b, j = divmod(c, 4)
        m = {}
        xs = x[b, j * NTOK:(j + 1) * NTOK]
        m["xT"] = np.ascontiguousarray(xs.reshape(NTOK, KC, P).transpose(2, 1, 0))
        m["cvec"] = np.ascontiguousarray(np.stack([_fm(f("c")[b]), _fm(f("c_ctx"))], axis=-1))
        per_core.append(m)
    return shared, per_core


def _prep2(inputs):
    f = lambda k: np.asarray(inputs[k], np.float32)
    sh = {}
    sh["lru_w_out"] = f("lru_w_out")[0]
    sh["sc_w_in"] = f("sc_w_in")[0]
    sh["sc_w_out"] = f("sc_w_out")[0]
    sh["peer_w_q"] = f("peer_w_q")
    sk = f("peer_sub_keys")
    sh["skT"] = np.ascontiguousarray(sk.reshape(2, 16, 128, 128).transpose(0, 3, 1, 2))
    u = f("peer_u")
    sh["u_l"] = np.ascontiguousarray(u.reshape(2, NEXP_C, P, KC, P).transpose(0, 1, 4, 3, 2)).reshape(2, NEXP_C, P, D)
    sh["v_l"] = f("peer_v").reshape(2, NEXP_C, P, D)
    ctx = f("ctx")
    pc = []
    for c in range(NCORE):
        b, j = divmod(c, 4)
        m = {}
        m["ctxT"] = np.ascontiguousarray(ctx[b].reshape(256, KC, P).transpose(2, 1, 0))
        cm = np.zeros((P, 2, NCORE), np.float32)
        for c2 in range(NCORE):
            b2, j2 = divmod(c2, 4)
            if b2 == b and j2 < j:
                cm[:, 0, c2] = 1.0
            if b2 == b and j2 > j:
                cm[:, 1, c2] = 1.0
        m["cmask"] = cm
        pc.append(m)
    return sh, pc


def _xT_chunk(x, b, j):
    xs = x[b, j * NTOK:(j + 1) * NTOK]
    return xs.reshape(NTOK, KC, P).transpose(2, 1, 0)


def kernel(**inputs):
    shared, pc = _prep(inputs)
    sh2, pc2 = _prep2(inputs)
    x = np.asarray(inputs["x"], np.float32)
    in_maps = []
    for c in range(NCORE):
        b, j = divmod(c, 4)
        others = [jj for jj in range(4) if jj != j]
        xTo = np.ascontiguousarray(np.concatenate([_xT_chunk(x, b, jj) for jj in others], axis=2))
        cm = np.zeros((P, 2, 3), np.float32)
        for s_, jj in enumerate(others):
            cm[:, 0, s_] = 1.0 if jj < j else 0.0
            cm[:, 1, s_] = 1.0 if jj > j else 0.0
        m = {**shared, **pc[c], **sh2, **pc2[c], "xTo": xTo, "cmask": cm}
        in_maps.append(m)
    nc = _get_nc("f")
    r = run_bass_kernel_spmd(nc, in_maps, core_ids=list(range(NCORE)))
    out = np.zeros((2, 4 * NTOK, D), np.float32)
    for c in range(NCORE):
        b, j = divmod(c, 4)
        yT = np.asarray(r.results[c]["yT"], np.float32)
        out[b, j * NTOK:(j + 1) * NTOK] = yT.transpose(2, 1, 0).reshape(NTOK, D)
    return out
```
